# Optimizing a Trainium2 kernel written in Bass

```python
import math
import jax, jax.numpy as jnp
from jax import lax
import numpy as np

D_MODEL = 1024
BATCH = 8
SEQ = 4096
DEPTH = 2

HEAD_DIM = 64
A_HEADS = 4
A_QK_DIM = HEAD_DIM // 2
B_HEADS = 6
B_PATTERNS = ((128, 1), (512, 4), (2048, 16))
C_HEADS = 6
C_KV_HEADS = 2
C_HALF_WINDOW = 128
C_BLOCK = 128
Q_BLOCK = 128
N_ATTN_HEADS = A_HEADS + B_HEADS + C_HEADS
FFN_HIDDEN = -(-(8 * D_MODEL) // (3 * 256)) * 256
LN_EPS = 1e-5
NEG = -1e30

A_Q = A_HEADS * 2 * A_QK_DIM
A_K = A_HEADS * 2 * A_QK_DIM
A_V = A_HEADS * HEAD_DIM
B_Q = B_HEADS * HEAD_DIM
B_K = B_HEADS * HEAD_DIM
B_V = B_HEADS * HEAD_DIM
C_Q = C_HEADS * HEAD_DIM
C_K = C_KV_HEADS * HEAD_DIM
C_V = C_KV_HEADS * HEAD_DIM
IN_SPLITS = (A_Q, A_K, A_V, B_Q, B_K, B_V, C_Q, C_K, C_V)
IN_WIDTH = A_Q + A_K + A_V + B_Q + B_K + B_V + C_Q + C_K + C_V
VALUE_SLOTS = (2, 5, 8)

kernel_name = "hybrid_diff_dilated_swa_deepnorm_encoder"


def split_points():
    pts, acc = [], 0
    for w in IN_SPLITS[:-1]:
        acc += w
        pts.append(acc)
    return pts


def alibi_slopes():
    s = (2.0 ** (-8.0 * np.arange(1, N_ATTN_HEADS + 1) / N_ATTN_HEADS)).astype(np.float32)
    s_c = jnp.asarray(s[:C_HEADS])
    s_a = jnp.asarray(s[C_HEADS:C_HEADS + A_HEADS])
    s_b = jnp.asarray(s[C_HEADS + A_HEADS:])
    return s_a, s_b, s_c


def layer_norm(x, g, b):
    xf = x.astype(jnp.float32)
    mu = jnp.mean(xf, axis=-1, keepdims=True)
    xc = xf - mu
    var = jnp.mean(xc * xc, axis=-1, keepdims=True)
    return (xc * lax.rsqrt(var + LN_EPS) * g + b).astype(x.dtype)


def rms_norm(x, g):
    xf = x.astype(jnp.float32)
    return xf * lax.rsqrt(jnp.mean(xf * xf, axis=-1, keepdims=True) + LN_EPS) * g


def banded_attention_stats(q, k, v, half, blk, slopes, dist_scale):
    bsz, g, r, L, dh = q.shape
    nb = -(-L // blk)
    lp = nb * blk
    qb = jnp.pad(q, ((0, 0), (0, 0), (0, 0), (0, lp - L), (0, 0))).reshape(bsz, g, r, nb, blk, dh)
    pad = ((0, 0), (0, 0), (blk, lp - L + blk), (0, 0))
    kp = jnp.pad(k, pad)
    vp = jnp.pad(v, pad)

    def band(a):
        return jnp.concatenate(
            [a[:, :, o:o + lp].reshape(bsz, g, nb, blk, dh) for o in (0, blk, 2 * blk)], axis=3)

    kb = band(kp)
    vb = band(vp).astype(jnp.float32)
    s = jnp.einsum('bgrnqd,bgnkd->bgrnqk', qb, kb).astype(jnp.float32) * (dh ** -0.5)
    blk_start = jnp.arange(nb)[:, None] * blk
    qpos = blk_start + jnp.arange(blk)[None]
    kpos = blk_start - blk + jnp.arange(3 * blk)[None]
    dist = jnp.abs(qpos[:, :, None] - kpos[:, None, :])
    valid = (dist <= half) & (kpos >= 0)[:, None, :] & (kpos < L)[:, None, :]
    bias = -slopes.astype(jnp.float32)[:, :, None, None, None] * (dist.astype(jnp.float32) * dist_scale)
    s = jnp.where(valid, s + bias, NEG)
    m = jnp.max(s, axis=-1)
    p = jnp.exp(s - m[..., None])
    l = jnp.sum(p, axis=-1)
    o = jnp.einsum('bgrnqk,bgnkd->bgrnqd', p, vb)
    m = m.reshape(bsz, g, r, lp)[..., :L]
    l = l.reshape(bsz, g, r, lp)[..., :L]
    o = o.reshape(bsz, g, r, lp, dh)[:, :, :, :L]
    return m, l, o


def diff_attention(q, k, v, lam, slopes, subln_g, lambda_init):
    bsz, h, T, _, dq = q.shape
    dv = v.shape[-1]
    nb = T // Q_BLOCK
    qb = q.reshape(bsz, h, nb, Q_BLOCK, 2, dq).transpose(2, 0, 1, 3, 4, 5)
    vf = v.astype(jnp.float32)
    kpos = jnp.arange(T)
    sl = slopes.astype(jnp.float32)[None, :, None, None, None]

    def block(args):
        qi, i = args
        s = jnp.einsum('bhqmd,bhkmd->bhmqk', qi, k).astype(jnp.float32) * (dq ** -0.5)
        qpos = i * Q_BLOCK + jnp.arange(Q_BLOCK)
        dist = jnp.abs(qpos[:, None] - kpos[None, :]).astype(jnp.float32)
        p = jax.nn.softmax(s - sl * dist, axis=-1)
        w = p[:, :, 0] - lam * p[:, :, 1]
        return jnp.einsum('bhqk,bhkd->bhqd', w, vf)

    o = lax.map(block, (qb, jnp.arange(nb)))
    o = o.transpose(1, 2, 0, 3, 4).reshape(bsz, h, T, dv)
    return rms_norm(o, subln_g) * (1.0 - lambda_init)


def dilated_attention(q, k, v, slopes):
    bsz, h, T, dh = q.shape
    ms, ls, os_ = [], [], []
    for window, dil in B_PATTERNS:
        half = (window // 2) // dil
        L = T // dil

        def stride(a):
            return a.reshape(bsz, h, L, dil, dh).transpose(0, 1, 3, 2, 4).reshape(bsz, h * dil, L, dh)

        m, l, o = banded_attention_stats(stride(q)[:, :, None], stride(k), stride(v), half, half,
                                         jnp.repeat(slopes, dil)[:, None], float(dil))
        ms.append(m[:, :, 0].reshape(bsz, h, dil, L).transpose(0, 1, 3, 2).reshape(bsz, h, T))
        ls.append(l[:, :, 0].reshape(bsz, h, dil, L).transpose(0, 1, 3, 2).reshape(bsz, h, T))
        os_.append(o[:, :, 0].reshape(bsz, h, dil, L, dh).transpose(0, 1, 3, 2, 4).reshape(bsz, h, T, dh))
    m_all = jnp.stack(ms)
    l_all = jnp.stack(ls)
    o_all = jnp.stack(os_)
    w = jnp.exp(m_all - jnp.max(m_all, axis=0, keepdims=True))
    return jnp.sum(w[..., None] * o_all, axis=0) / jnp.sum(w * l_all, axis=0)[..., None]


def sink_window_gqa(q, k, v, sink, slopes):
    bsz, T, _ = q.shape
    rep = C_HEADS // C_KV_HEADS
    q = q.reshape(bsz, T, C_KV_HEADS, rep, HEAD_DIM).transpose(0, 2, 3, 1, 4)
    k = k.reshape(bsz, T, C_KV_HEADS, HEAD_DIM).transpose(0, 2, 1, 3)
    v = v.reshape(bsz, T, C_KV_HEADS, HEAD_DIM).transpose(0, 2, 1, 3)
    m, l, o = banded_attention_stats(q, k, v, C_HALF_WINDOW, C_BLOCK,
                                     slopes.reshape(C_KV_HEADS, rep), 1.0)
    sk = sink.astype(jnp.float32).reshape(C_KV_HEADS, rep)[None, :, :, None]
    mx = jnp.maximum(m, sk)
    a = jnp.exp(m - mx)
    out = o * a[..., None] / (l * a + jnp.exp(sk - mx))[..., None]
    return out.transpose(0, 3, 1, 2, 4).reshape(bsz, T, C_HEADS * HEAD_DIM)


def token_mixer(h, w_in, lam, subln_g, sink, w_out, lambda_init):
    bsz, T, _ = h.shape
    proj = h @ w_in
    qa, ka, va, qb, kb, vb, qc, kc, vc = jnp.split(proj, split_points(), axis=-1)
    s_a, s_b, s_c = alibi_slopes()

    qa = qa.reshape(bsz, T, A_HEADS, 2, A_QK_DIM).transpose(0, 2, 1, 3, 4)
    ka = ka.reshape(bsz, T, A_HEADS, 2, A_QK_DIM).transpose(0, 2, 1, 3, 4)
    va = va.reshape(bsz, T, A_HEADS, HEAD_DIM).transpose(0, 2, 1, 3)
    lf = lam.astype(jnp.float32)
    lam_full = jnp.exp(jnp.sum(lf[0] * lf[1])) - jnp.exp(jnp.sum(lf[2] * lf[3])) + lambda_init
    oa = diff_attention(qa, ka, va, lam_full, s_a, subln_g.astype(jnp.float32), lambda_init)
    oa = oa.transpose(0, 2, 1, 3).reshape(bsz, T, A_V)

    def heads(a):
        return a.reshape(bsz, T, B_HEADS, HEAD_DIM).transpose(0, 2, 1, 3)
    ob = dilated_attention(heads(qb), heads(kb), heads(vb), s_b)
    ob = ob.transpose(0, 2, 1, 3).reshape(bsz, T, B_V)

    oc = sink_window_gqa(qc, kc, vc, sink, s_c)

    mixed = jnp.concatenate([oa, ob, oc], axis=-1).astype(h.dtype)
    return mixed @ w_out


def swiglu(h, w_gu, w_down):
    g, u = jnp.split(h @ w_gu, 2, axis=-1)
    return (jax.nn.silu(g) * u) @ w_down


def setup_inputs(seed: int = 0) -> dict:
    key = jax.random.key(seed)
    ks = jax.random.split(key, 13)
    beta = (8 * DEPTH) ** -0.25
    f32 = jnp.float32
    x = jax.random.normal(ks[0], (BATCH, SEQ, D_MODEL), f32)
    c = jax.random.normal(ks[1], (BATCH, D_MODEL), f32)
    w_ada = jax.random.normal(ks[2], (DEPTH, D_MODEL, 6 * D_MODEL), f32) * (0.1 * D_MODEL ** -0.5)
    b_ada = 0.02 * jax.random.normal(ks[3], (DEPTH, 6 * D_MODEL), f32)
    col_scale = np.ones((IN_WIDTH,), np.float32)
    off = 0
    for i, w in enumerate(IN_SPLITS):
        if i in VALUE_SLOTS:
            col_scale[off:off + w] = beta
        off += w
    w_in = jax.random.normal(ks[4], (DEPTH, D_MODEL, IN_WIDTH), f32) * (D_MODEL ** -0.5) * jnp.asarray(col_scale)
    lam = 0.1 * jax.random.normal(ks[5], (DEPTH, 4, A_QK_DIM), f32)
    subln_g = 1.0 + 0.02 * jax.random.normal(ks[6], (DEPTH, HEAD_DIM), f32)
    sink = 0.5 * jax.random.normal(ks[7], (DEPTH, C_HEADS), f32)
    w_out = jax.random.normal(ks[8], (DEPTH, D_MODEL, D_MODEL), f32) * (D_MODEL ** -0.5) * beta
    ln_g = 1.0 + 0.02 * jax.random.normal(ks[9], (DEPTH, 2, D_MODEL), f32)
    ln_b = 0.02 * jax.random.normal(ks[10], (DEPTH, 2, D_MODEL), f32)
    w_gu = jax.random.normal(ks[11], (DEPTH, D_MODEL, 2 * FFN_HIDDEN), f32) * (D_MODEL ** -0.5) * beta
    w_down = jax.random.normal(ks[12], (DEPTH, FFN_HIDDEN, D_MODEL), f32) * (FFN_HIDDEN ** -0.5) * beta
    return {"x": x, "c": c, "w_ada": w_ada, "b_ada": b_ada, "w_in": w_in, "lam": lam,
            "subln_g": subln_g, "sink": sink, "w_out": w_out, "ln_g": ln_g, "ln_b": ln_b,
            "w_gu": w_gu, "w_down": w_down}


def reference(x, c, w_ada, b_ada, w_in, lam, subln_g, sink, w_out, ln_g, ln_b, w_gu, w_down):
    alpha = (2 * DEPTH) ** 0.25
    for layer in range(DEPTH):
        lambda_init = 0.8 - 0.6 * math.exp(-0.3 * layer)
        mod = jax.nn.silu(c) @ w_ada[layer] + b_ada[layer]
        sh1, sc1, g1, sh2, sc2, g2 = [m[:, None, :] for m in jnp.split(mod, 6, axis=-1)]
        h = x * (1.0 + sc1) + sh1
        mix = token_mixer(h, w_in[layer], lam[layer], subln_g[layer], sink[layer], w_out[layer], lambda_init)
        x = layer_norm(alpha * x + (1.0 + g1) * mix, ln_g[layer, 0], ln_b[layer, 0])
        h = x * (1.0 + sc2) + sh2
        x = layer_norm(alpha * x + (1.0 + g2) * swiglu(h, w_gu[layer], w_down[layer]), ln_g[layer, 1], ln_b[layer, 1])
    return x
```

```python
import contextlib
import math
import numpy as np
import ml_dtypes
import concourse.bass as bass
import concourse.mybir as mybir
from concourse.bass_utils import run_bass_kernel_spmd

F32 = mybir.dt.float32
BF16 = mybir.dt.bfloat16
AF = mybir.ActivationFunctionType
ALU = mybir.AluOpType
AX = mybir.AxisListType

T = 4096
D = 1024
DEPTH = 2
NCORES = 8
FFN = 2816
NJ = FFN // 128
INW = 2560
LN_EPS = 1e-5
ALPHA = (2 * DEPTH) ** 0.25
NEG = -1.0e30
POOLW = 45500

SL = (2.0 ** (-8.0 * np.arange(1, 17) / 16)).astype(np.float32)
S_C = SL[0:6]
S_A = SL[6:10]
S_B = SL[10:16]
B_PATTERNS = ((128, 1), (512, 4), (2048, 16))


class Sched:
    ENGS = ("pe", "act", "dve", "pool", "sp")

    def __init__(self, nc, n_dma_sems=48):
        self.nc = nc
        self.ops = []
        self.last_w = {}
        self.readers = {}
        self.dma_last = {}
        self.dma_slot = {}
        self.n_dma_sems = n_dma_sems
        self.barrier_deps = []
        self.n_bg = 4
        self.bg_slot = {}
        self.persist = set()

    def add(self, eng, fn, r=(), w=(), dma=None, bg=False):
        idx = len(self.ops)
        raw = set()
        other = set()
        for k in r:
            if k in self.last_w:
                raw.add(self.last_w[k])
        for k in w:
            if k in self.last_w:
                other.add(self.last_w[k])
            other.update(self.readers.get(k, ()))
        slot = None
        if dma is not None:
            if bg:
                if dma not in self.bg_slot:
                    self.bg_slot[dma] = self.n_dma_sems + len(self.bg_slot) % self.n_bg
                slot = self.bg_slot[dma]
                self.persist.update(w)
            else:
                if dma not in self.dma_slot:
                    self.dma_slot[dma] = len(self.dma_slot) % self.n_dma_sems
                slot = self.dma_slot[dma]
            if slot in self.dma_last:
                raw.add(self.dma_last[slot])
            self.dma_last[slot] = idx
        for k in r:
            self.readers.setdefault(k, []).append(idx)
        for k in w:
            self.last_w[k] = idx
            self.readers[k] = []
        deps = set() if bg else set(self.barrier_deps)
        for d in raw | other:
            p = self.ops[d]
            if p["slot"] is None and slot is None and p["eng"] == eng:
                if eng == "pe" or d not in raw:
                    continue
            deps.add(d)
        deps.discard(idx)
        self.ops.append(dict(eng=eng, fn=fn, deps=deps, slot=slot, sem=None, val=0))
        return idx

    def barrier(self, final=False):
        last = {}
        for i, op in enumerate(self.ops):
            if op["slot"] is not None and op["slot"] >= self.n_dma_sems and not final:
                continue
            key = ("dma", op["slot"]) if op["slot"] is not None else ("eng", op["eng"])
            last[key] = i
        self.barrier_deps = sorted(last.values())
        self.last_w = {k: v for k, v in self.last_w.items() if k in self.persist}
        self.readers = {k: v for k, v in self.readers.items() if k in self.persist}

    def emit(self):
        nc = self.nc
        ops = self.ops
        self.barrier(final=True)
        self.add("sp", None)
        needed = set()
        for op in ops:
            needed.update(op["deps"])
        with contextlib.ExitStack() as st:
            eng_sem = {e: st.enter_context(nc.semaphore("s_" + e)) for e in self.ENGS}
            nslots = self.n_dma_sems + self.n_bg
            dma_sems = [st.enter_context(nc.semaphore("d_%d" % i)) for i in range(nslots)]
            cnt_e = {e: 0 for e in self.ENGS}
            cnt_d = [0] * nslots
            for i, op in enumerate(ops):
                if op["slot"] is not None:
                    cnt_d[op["slot"]] += 16
                    op["sem"] = ("d", op["slot"])
                    op["val"] = cnt_d[op["slot"]]
                elif i in needed:
                    cnt_e[op["eng"]] += 1
                    op["sem"] = ("e", op["eng"])
                    op["val"] = cnt_e[op["eng"]]
            per_eng = {e: [] for e in self.ENGS}
            for op in ops:
                per_eng[op["eng"]].append(op)

            def semh(s):
                return eng_sem[s[1]] if s[0] == "e" else dma_sems[s[1]]

            def run(e_obj, ename):
                waited = {}
                for op in per_eng[ename]:
                    for d in sorted(op["deps"]):
                        p = ops[d]
                        if waited.get(p["sem"], 0) >= p["val"]:
                            continue
                        e_obj.wait_ge(semh(p["sem"]), p["val"])
                        waited[p["sem"]] = p["val"]
                    if op["fn"] is None:
                        continue
                    ins = op["fn"](e_obj)
                    if op["sem"] is not None:
                        ins.then_inc(semh(op["sem"]), 16 if op["sem"][0] == "d" else 1)

            with nc.Block() as block:
                @block.sync
                def _(e):
                    run(e, "sp")

                @block.tensor
                def _(e):
                    run(e, "pe")

                @block.scalar
                def _(e):
                    run(e, "act")

                @block.vector
                def _(e):
                    run(e, "dve")

                @block.gpsimd
                def _(e):
                    run(e, "pool")
        return len(ops)


def _hi_lo(v):
    v = np.asarray(v, np.float32)
    hi = v.astype(ml_dtypes.bfloat16).astype(np.float32)
    lo = (v - hi).astype(ml_dtypes.bfloat16).astype(np.float32)
    return hi, lo


def make_consts():
    c = {}
    c["c_ident"] = np.eye(128, dtype=np.float32)
    qaug = np.zeros((4, 8, T), np.float32)
    kaug = np.zeros((4, 2, 8, T), np.float32)
    dp = np.zeros((4, 2, 128, 128), np.float32)
    ii = (np.arange(T) % 512).astype(np.float32)
    jj = (np.arange(T) % 128).astype(np.float32)
    for h in range(4):
        m = np.float32(S_A[h])
        hi, lo = _hi_lo(m * ii)
        qaug[h, 0] = -hi
        qaug[h, 1] = -lo
        qaug[h, 2] = 1.0
        qaug[h, 3] = 1.0
        hi, lo = _hi_lo(m * jj)
        kaug[h, 0, 0] = 1.0
        kaug[h, 0, 1] = 1.0
        kaug[h, 0, 2] = hi
        kaug[h, 0, 3] = lo
        kaug[h, 1] = -kaug[h, 0]
        pj = np.arange(128, dtype=np.float32)[:, None]
        pi = np.arange(128, dtype=np.float32)[None, :]
        dmat = -2.0 * m * np.maximum(pj - pi, 0.0)
        dp[h, 0], dp[h, 1] = _hi_lo(dmat)
    c["c_qaug"] = qaug
    c["c_kaug"] = kaug
    c["c_dp"] = dp
    p = np.arange(128, dtype=np.float32)[:, None]
    i = np.arange(128, dtype=np.float32)[None, :]
    bb = np.zeros((1, 128, 7, 512), np.float32)
    for h in range(1):
        m = np.float32(1.0)
        k = 0
        for pi_, (win, dil) in enumerate(B_PATTERNS):
            d1 = np.abs(i - (p - 64.0))
            d2 = np.abs(i - (p + 64.0))
            t1 = np.where(d1 <= 64.0, -m * dil * d1, -1.0e32).astype(np.float32)
            t2 = np.where(d2 <= 64.0, -m * dil * d2, -1.0e32).astype(np.float32)
            t1f = t1.copy()
            t1f[0:64, :] = -1.0e32
            t2l = t2.copy()
            t2l[64:128, :] = -1.0e32
            mid = np.concatenate([t1, t2, t1, t2], 1)
            first = np.concatenate([t1f, t2, t1, t2], 1)
            last = np.concatenate([t1, t2, t1, t2l], 1)
            both = np.concatenate([t1f, t2, t1, t2l], 1)
            if pi_ < 2:
                bb[h, :, k] = mid
                bb[h, :, k + 1] = first
                bb[h, :, k + 2] = last
                k += 3
            else:
                bb[h, :, k] = both
                k += 1
    c["c_bbias"] = bb[0]
    cb = np.zeros((2, 128, 3, 384), np.float32)
    for g in range(2):
        for rep in range(3):
            m = np.float32(S_C[3 * g + rep])
            for kt in range(3):
                dist = np.abs(i - (p + 128.0 * (kt - 1)))
                cb[g, :, kt, rep * 128:(rep + 1) * 128] = np.where(dist <= 128.0, -m * dist, NEG)
    c["c_cbias"] = cb
    return c


CONST_SHAPES = {
    "c_ident": [128, 128], "c_qaug": [4, 8, T], "c_kaug": [4, 2, 8, T], "c_dp": [4, 2, 128, 128],
    "c_bbias": [128, 7, 512], "c_cbias": [2, 128, 3, 384],
}


def build_program(debug=False, phases=None, layers=(0, 1)):
    nc = bass.Bass("TRN2", target_bir_lowering=False)

    def din(name, shape, dt=F32):
        return nc.dram_tensor(name, shape, dt, kind="ExternalInput").ap()

    def dscr(name, shape, dt):
        return nc.dram_tensor(name, shape, dt, kind=("ExternalOutput" if debug else "Internal")).ap()

    x_in = din("x", [T, D])
    ccol = din("ccol", [128, 8])
    w_ada = din("w_ada", [DEPTH, D, 6 * D])
    b_ada = din("b_ada", [DEPTH, 6 * D])
    w_in = din("w_in", [DEPTH, D, INW])
    lam = din("lam", [DEPTH, 128])
    subln = din("subln_g", [DEPTH, 64])
    sink = din("sink", [DEPTH, 6])
    w_out = din("w_out", [DEPTH, D, D])
    ln_g = din("ln_g", [DEPTH, 2, D])
    ln_b = din("ln_b", [DEPTH, 2, D])
    w_gu = din("w_gu", [DEPTH, D, 2 * FFN])
    w_down = din("w_down", [DEPTH, FFN, D])
    cst = {k: din(k, v) for k, v in CONST_SHAPES.items()}
    y_out = nc.dram_tensor("y", [T, D], F32, kind="ExternalOutput").ap()

    qaT = dscr("qaT", [256, T], BF16)
    kaT = dscr("kaT", [256, T], BF16)
    qbT = dscr("qbT", [384, T], BF16)
    kbT = dscr("kbT", [384, T], BF16)
    qcT = dscr("qcT", [384, T], BF16)
    kcT = dscr("kcT", [128, T], BF16)
    vA = dscr("vA", [T, 260], BF16)
    vB = dscr("vB", [T, 390], BF16)
    vC = dscr("vC", [T, 130], BF16)
    mixT = dscr("mixT", [D, T], BF16)
    x1s = dscr("x1s", [T, D], F32)
    wguS = dscr("wguS", [NJ, 128, 8 * 256], BF16)
    wdnS = dscr("wdnS", [NJ, 128, D], BF16)
    dbg_mod = dscr("dbg_mod", [128, 4096 + 16], F32) if debug else None

    allp = phases is None

    def on(name):
        return allp or name in phases

    with nc.sbuf_tensor("pool", [128, POOLW], F32) as pool, nc.psum_tensor("ps", [128, 4096], F32) as ps:
        S = Sched(nc)

        class Mem:
            def __init__(self, base):
                self.off = base

            def f32(self, n, parts=None):
                v = pool[:, self.off:self.off + n]
                self.off += n
                assert self.off <= POOLW, self.off
                return v

            def bf16(self, n):
                nw = (n + 1) // 2
                v = pool[:, self.off:self.off + nw].bitcast(BF16)[:, 0:n]
                self.off += nw
                assert self.off <= POOLW, self.off
                return v

        def bank(i, n=512):
            return ps[:, i * 512:i * 512 + n]

        def bank_bf(i):
            return ps[:, i * 512:(i + 1) * 512].bitcast(BF16)

        PM = Mem(0)
        ident = PM.f32(128)
        ident_bf = PM.bf16(128)
        ones = PM.f32(128)
        cs_rep = PM.f32(1024).rearrange("p (k m) -> p k m", m=128)
        modcols = PM.f32(16)
        keepR = PM.f32(4096).rearrange("p (a n) -> p a n", n=1024)
        small = PM.f32(64)
        PBASE = PM.off

        S.add("sp", lambda e: e.dma_start(out=ident, in_=cst["c_ident"]), w=["ident"], dma="ident")
        S.add("pool", lambda e: e.dma_start(out=ident_bf, in_=cst["c_ident"]), w=["ident_bf"], dma="ident_bf")
        S.add("dve", lambda e: e.memset(ones, 1.0), w=["ones"])
        cs = small[:, 0:8]
        cs2 = small[:, 8:16]
        S.add("sp", lambda e: e.dma_start(out=cs, in_=ccol), w=["cs"], dma="cs")
        S.add("act", lambda e: e.activation(out=cs2, in_=cs, func=AF.Silu), r=["cs"], w=["cs2"])
        S.add("dve", lambda e: e.tensor_copy(out=cs_rep, in_=cs2.unsqueeze(2).to_broadcast([128, 8, 128])),
              r=["cs2"], w=["cs_rep"])
        S.barrier()

        def phase_M(l):
            M = Mem(PBASE)
            wst = [M.f32(3072) for _ in range(2)]
            bada = M.f32(6144)
            modR = M.f32(6144)
            S.add("sp", lambda e: e.dma_start(out=bada, in_=b_ada[l].partition_broadcast(128)), w=["bada"], dma="bada")
            ld = 0
            for half in range(2):
                for kc in range(8):
                    b = ld % 2
                    ld += 1
                    S.add("sp", lambda e, b=b, kc=kc, half=half: e.dma_start(
                        out=wst[b], in_=w_ada[l, kc * 128:(kc + 1) * 128, half * 3072:(half + 1) * 3072]),
                        w=["wst%d" % b], dma="wst%d" % b)
                    for n in range(6):
                        S.add("pe", lambda e, b=b, kc=kc, n=n: e.matmul(
                            bank(n), lhsT=cs_rep[:, kc, :], rhs=wst[b][:, n * 512:(n + 1) * 512],
                            start=(kc == 0), stop=(kc == 7)),
                            r=["wst%d" % b, "cs_rep"], w=["psM%d" % n])
                for n in range(6):
                    c0 = half * 3072 + n * 512
                    S.add("dve", lambda e, n=n, c0=c0: e.tensor_tensor(
                        out=modR[:, c0:c0 + 512], in0=bank(n), in1=bada[:, c0:c0 + 512], op=ALU.add),
                        r=["psM%d" % n, "bada"], w=["modR%d" % (c0 // 512)])
            allR = ["modR%d" % i for i in range(12)]
            for grp in range(4):
                for t4 in range(4):
                    idx = grp * 4 + t4
                    col0 = (1024 + idx * 128) if idx < 8 else ((idx - 8) * 128)
                    S.add("pe", lambda e, grp=grp, t4=t4, col0=col0: e.transpose(
                        out=bank(6 + grp % 2)[:, t4 * 128:(t4 + 1) * 128], in_=modR[:, col0:col0 + 128], identity=ident),
                        r=allR + ["ident"], w=["psT%d" % (grp % 2)])
                src = bank(6 + grp % 2).rearrange("p (a b) -> p a b", b=128)[:, :, 0]
                addv = 1.0 if grp < 2 else 0.0
                S.add("dve", lambda e, grp=grp, src=src, addv=addv: e.tensor_scalar(
                    out=modcols[:, grp * 4:(grp + 1) * 4], in0=src, scalar1=addv, scalar2=None, op0=ALU.add),
                    r=["psT%d" % (grp % 2)], w=["modcols"])
            S.add("dve", lambda e: e.tensor_scalar(out=keepR[:, 0, :], in0=modR[:, 2048:3072], scalar1=1.0,
                                                    scalar2=1.0 / ALPHA, op0=ALU.add, op1=ALU.mult), r=allR, w=["keep0"])
            S.add("dve", lambda e: e.tensor_scalar(out=keepR[:, 1, :], in0=modR[:, 4096:5120], scalar1=1.0,
                                                    scalar2=None, op0=ALU.add), r=allR, w=["keep1"])
            S.add("dve", lambda e: e.tensor_copy(out=keepR[:, 2, :], in_=modR[:, 3072:4096]), r=allR, w=["keep2"])
            S.add("dve", lambda e: e.tensor_scalar(out=keepR[:, 3, :], in0=modR[:, 5120:6144], scalar1=1.0,
                                                    scalar2=1.0 / ALPHA, op0=ALU.add, op1=ALU.mult), r=allR, w=["keep3"])
            if debug:
                S.add("sp", lambda e: e.dma_start(out=dbg_mod[:, 0:4096], in_=keepR.rearrange("p a n -> p (a n)")),
                      r=["keep0", "keep1", "keep2", "keep3"], dma="dbgm")
                S.add("sp", lambda e: e.dma_start(out=dbg_mod[:, 4096:4112], in_=modcols), r=["modcols"], dma="dbgm2")
            S.barrier()

        def phase_P1(l):
            M = Mem(PBASE)
            wbf = M.bf16(8 * INW).rearrange("p (k n) -> p k n", n=INW)
            xt = [M.f32(4096).rearrange("p (t n) -> p t n", n=1024) for _ in range(2)]
            hT = [M.bf16(4096).rearrange("p (k n) -> p k n", n=512) for _ in range(2)]
            stg = [M.bf16(512) for _ in range(4)]
            vst = [[M.bf16(4 * 65).rearrange("p (h d) -> p h d", d=65),
                    M.bf16(6 * 65).rearrange("p (h d) -> p h d", d=65),
                    M.bf16(2 * 65).rearrange("p (h d) -> p h d", d=65)] for _ in range(2)]
            xsrc = x_in if l == 0 else x1s
            for kc in range(8):
                for hf in range(2):
                    S.add("pool", lambda e, kc=kc, hf=hf: e.dma_start(
                        out=wbf[:, kc, hf * 1280:(hf + 1) * 1280],
                        in_=w_in[l, kc * 128:(kc + 1) * 128, hf * 1280:(hf + 1) * 1280]),
                        w=["wbf%d" % kc], dma="wbf%d_%d" % (kc, hf))
            for b in range(2):
                for gi in range(3):
                    S.add("pool", lambda e, b=b, gi=gi: e.memset(vst[b][gi], 1.0), w=["vst%d_%d" % (b, gi)])
            WB = ["wbf%d" % k for k in range(8)]

            def load_x(c):
                b = c % 2
                S.add("sp", lambda e, b=b, c=c: e.dma_start(
                    out=xt[b], in_=xsrc[c * 512:(c + 1) * 512, :].rearrange("(t p) n -> p t n", p=128)),
                    w=["xt%d" % b], dma="xt%d" % b)

            fm = []
            for i in range(2):
                fm.append((i * 128, qaT, i * 128, 32.0 ** -0.5))
            for i in range(2):
                fm.append((256 + i * 128, kaT, i * 128, 1.0))
            for i in range(3):
                fm.append((768 + i * 128, qbT, i * 128, 0.125))
            for i in range(3):
                fm.append((1152 + i * 128, kbT, i * 128, 1.0))
            for i in range(3):
                fm.append((1920 + i * 128, qcT, i * 128, 0.125))
            fm.append((2304, kcT, 0, 1.0))
            tm = [(512, 256, vA, 4), (1536, 384, vB, 6), (2432, 128, vC, 2)]

            load_x(0)
            ev = 0
            fmn = 0
            tmn = 0
            for c in range(8):
                if c + 1 < 8:
                    load_x(c + 1)
                b = c % 2
                for kc in range(8):
                    pb = kc % 2
                    for t4 in range(4):
                        S.add("pe", lambda e, b=b, kc=kc, t4=t4, pb=pb: e.transpose(
                            out=bank(pb)[:, t4 * 128:(t4 + 1) * 128], in_=xt[b][:, t4, kc * 128:(kc + 1) * 128],
                            identity=ident), r=["xt%d" % b, "ident"], w=["psT%d" % pb])
                    if ev % 2 == 0:
                        S.add("act", lambda e, b=b, kc=kc, pb=pb: e.activation(
                            out=hT[b][:, kc, :], in_=bank(pb), func=AF.Identity,
                            scale=modcols[:, kc:kc + 1], bias=modcols[:, 8 + kc:9 + kc]),
                            r=["psT%d" % pb, "modcols"], w=["hT%d_%d" % (b, kc)])
                    else:
                        S.add("dve", lambda e, b=b, kc=kc, pb=pb: e.tensor_scalar(
                            out=hT[b][:, kc, :], in0=bank(pb), scalar1=modcols[:, kc:kc + 1],
                            scalar2=modcols[:, 8 + kc:9 + kc], op0=ALU.mult, op1=ALU.add),
                            r=["psT%d" % pb, "modcols"], w=["hT%d_%d" % (b, kc)])
                    ev += 1
                HT = ["hT%d_%d" % (b, k) for k in range(8)]
                for (wc, dst, r0, scl) in fm:
                    pb = 2 + fmn % 3
                    sb = fmn % 4
                    fmn += 1
                    for kc in range(8):
                        S.add("pe", lambda e, kc=kc, wc=wc, pb=pb, b=b: e.matmul(
                            bank(pb), lhsT=wbf[:, kc, wc:wc + 128], rhs=hT[b][:, kc, :], start=(kc == 0), stop=(kc == 7)),
                            r=[WB[kc], HT[kc]], w=["psF%d" % pb])
                    if ev % 2 == 0:
                        S.add("act", lambda e, pb=pb, sb=sb, scl=scl: e.activation(
                            out=stg[sb], in_=bank(pb), func=AF.Copy, scale=float(scl)), r=["psF%d" % pb], w=["stg%d" % sb])
                    else:
                        S.add("dve", lambda e, pb=pb, sb=sb, scl=scl: e.tensor_scalar(
                            out=stg[sb], in0=bank(pb), scalar1=float(scl), scalar2=None, op0=ALU.mult),
                            r=["psF%d" % pb], w=["stg%d" % sb])
                    ev += 1
                    S.add("sp", lambda e, sb=sb, dst=dst, r0=r0, c=c: e.dma_start(
                        out=dst[r0:r0 + 128, c * 512:(c + 1) * 512], in_=stg[sb]), r=["stg%d" % sb], dma="stg%d" % sb)
                for t4 in range(4):
                    vb_ = tmn % 2
                    tmn += 1
                    for gi, (wc, ncol, dst, nh) in enumerate(tm):
                        pb = 5 + gi
                        for kc in range(8):
                            S.add("pe", lambda e, kc=kc, wc=wc, ncol=ncol, pb=pb, b=b, t4=t4: e.matmul(
                                bank(pb, ncol), lhsT=hT[b][:, kc, t4 * 128:(t4 + 1) * 128], rhs=wbf[:, kc, wc:wc + ncol],
                                start=(kc == 0), stop=(kc == 7)), r=[WB[kc], HT[kc]], w=["psV%d" % pb])
                        src = bank(pb, ncol).rearrange("p (h d) -> p h d", d=64)
                        if ev % 2 == 0:
                            S.add("act", lambda e, src=src, vb_=vb_, gi=gi: e.copy(out=vst[vb_][gi][:, :, 0:64], in_=src),
                                  r=["psV%d" % pb], w=["vst%d_%d" % (vb_, gi)])
                        else:
                            S.add("dve", lambda e, src=src, vb_=vb_, gi=gi: e.tensor_copy(out=vst[vb_][gi][:, :, 0:64], in_=src),
                                  r=["psV%d" % pb], w=["vst%d_%d" % (vb_, gi)])
                        ev += 1
                        t0 = c * 512 + t4 * 128
                        S.add("sp", lambda e, vb_=vb_, gi=gi, dst=dst, t0=t0: e.dma_start(
                            out=dst[t0:t0 + 128, :], in_=vst[vb_][gi].rearrange("p h d -> p (h d)")),
                            r=["vst%d_%d" % (vb_, gi)], dma="vst%d_%d" % (vb_, gi))
            S.barrier()

        def norm_store(osb_num, rl_in, rl_buf, bc_bank, obf, dst, keys_r, tag, n=512, bc_key=None):
            S.add("act", lambda e: e.activation(out=rl_buf[64:65, 0:n], in_=rl_in, func=AF.Ln), r=keys_r, w=["rl" + tag])
            S.add("act", lambda e: e.activation(out=rl_buf[64:65, 0:n], in_=rl_buf[64:65, 0:n], func=AF.Exp, scale=-1.0),
                  r=["rl" + tag], w=["rl" + tag])
            bck = bc_key if bc_key is not None else "bc" + tag
            S.add("pe", lambda e: e.matmul(bc_bank[0:64, 0:n], lhsT=ones[64:65, 0:64], rhs=rl_buf[64:65, 0:n], start=True, stop=True),
                  r=["rl" + tag, "ones"], w=[bck])
            S.add("dve", lambda e: e.tensor_tensor(out=obf[0:64, 0:n], in0=bc_bank[0:64, 0:n], in1=osb_num, op=ALU.mult),
                  r=keys_r + [bck], w=["obf" + tag])
            S.add("sp", lambda e: e.dma_start(out=dst, in_=obf[0:64, 0:n]), r=["obf" + tag], dma="obf" + tag)

        def phase_A(l):
            lam_init = 0.8 - 0.6 * math.exp(-0.3 * l)
            M = Mem(PBASE)
            qa = [M.bf16(2 * T).rearrange("p (m t) -> p m t", t=T) for _ in range(2)]
            ka = [M.bf16(4 * T).rearrange("p (s m t) -> p s m t", m=2, t=T) for _ in range(2)]
            vah = [M.bf16(32 * 128).rearrange("p (k n) -> p k n", n=128) for _ in range(2)]
            dpt = M.bf16(4 * 2 * 128).rearrange("p (h s i) -> p h s i", s=2, i=128)
            pT = [M.bf16(1024).rearrange("p (m t) -> p m t", t=512) for _ in range(3)]
            osb = M.f32(1024).rearrange("p (m t) -> p m t", t=512)
            rl = M.f32(1024)
            t0 = M.f32(512)
            t1 = M.f32(512)
            dd = M.f32(512)
            sq = M.f32(512)
            tmpv = M.f32(512)
            rstd = M.f32(512)
            negh = M.f32(512)
            obf = M.bf16(512)
            lamt = M.f32(128)
            lw = M.f32(64)
            sm = M.f32(16)
            S.add("sp", lambda e: e.dma_start(out=lamt, in_=lam[l].partition_broadcast(128)), w=["lamt"], dma="lamt")
            S.add("dve", lambda e: e.tensor_tensor(out=lw[:, 0:32], in0=lamt[:, 0:32], in1=lamt[:, 32:64], op=ALU.mult), r=["lamt"], w=["lw0"])
            S.add("dve", lambda e: e.tensor_tensor(out=lw[:, 32:64], in0=lamt[:, 64:96], in1=lamt[:, 96:128], op=ALU.mult), r=["lamt"], w=["lw1"])
            S.add("dve", lambda e: e.tensor_reduce(out=sm[:, 0:2], in_=lw.rearrange("p (a b) -> p a b", b=32), axis=AX.X, op=ALU.add),
                  r=["lw0", "lw1"], w=["sm01"])
            S.add("act", lambda e: e.activation(out=sm[:, 2:4], in_=sm[:, 0:2], func=AF.Exp), r=["sm01"], w=["sm23"])
            S.add("dve", lambda e: e.scalar_tensor_tensor(out=sm[:, 4:5], in0=sm[:, 3:4], scalar=-lam_init, in1=sm[:, 2:3],
                                                          op0=ALU.add, op1=ALU.subtract), r=["sm23"], w=["lamneg"])
            S.add("sp", lambda e: e.dma_start(out=sm[0:64, 8:9], in_=subln[l].rearrange("(p o) -> p o", o=1)), w=["gc0"], dma="gc0")
            S.add("dve", lambda e: e.tensor_scalar(out=sm[0:64, 9:10], in0=sm[0:64, 8:9], scalar1=float(1.0 - lam_init), scalar2=None,
                                                    op0=ALU.mult), r=["gc0"], w=["gcol"])
            lamneg = sm[0:64, 4:5]
            gcol = sm[0:64, 9:10]
            epsc = sm[:, 12:13]
            S.add("pool", lambda e: e.memset(sm[:, 12:13], LN_EPS), w=["epsc"])
            for b2 in range(2):
                for m in range(2):
                    S.add("dve", lambda e, b2=b2, m=m: e.memset(qa[b2][:, m, :], 0.0), w=["qa%d" % b2])
                    for s_ in range(2):
                        S.add("dve", lambda e, b2=b2, m=m, s_=s_: e.memset(ka[b2][:, s_, m, :], 0.0), w=["ka%d" % b2])
                S.add("dve", lambda e, b2=b2: e.memset(vah[b2].rearrange("p k n -> p (k n)"), 0.0), w=["va%d" % b2])
            for h in range(4):
                S.add("pool", lambda e, h=h: e.dma_start(out=dpt[:, h, :, :], in_=cst["c_dp"][h].rearrange("s p i -> p s i")),
                      w=["dpt"], dma="dpt%d" % h)
            def load_head(h):
                hb = h % 2
                for q4 in range(4):
                    S.add("sp", lambda e, q4=q4, hb=hb, h=h: e.dma_start(
                        out=vah[hb][:, q4 * 8:(q4 + 1) * 8, 0:65],
                        in_=vA[q4 * 1024:(q4 + 1) * 1024, h * 65:(h + 1) * 65].rearrange("(k p) n -> p k n", p=128)),
                        w=["va%d" % hb], dma="va%d_%d" % (hb, q4))
                for m in range(2):
                    r0 = h * 64 + m * 32
                    S.add("sp", lambda e, hb=hb, m=m, r0=r0: e.dma_start(out=qa[hb][0:32, m, :], in_=qaT[r0:r0 + 32, :]),
                          w=["qa%d" % hb], dma="qa%d_%d" % (hb, m))
                    S.add("pool", lambda e, hb=hb, m=m, h=h: e.dma_start(
                        out=qa[hb][32:40, m, :].rearrange("p (a b) -> p a b", b=2048),
                        in_=cst["c_qaug"][h].rearrange("p (a b) -> p a b", b=2048)), w=["qa%d" % hb], dma="qg%d_%d" % (hb, m))
                    for s_ in range(2):
                        S.add("sp", lambda e, hb=hb, m=m, r0=r0, s_=s_: e.dma_start(
                            out=ka[hb][0:32, s_, m, :], in_=kaT[r0:r0 + 32, :]), w=["ka%d" % hb], dma="ka%d_%d_%d" % (hb, m, s_))
                        S.add("pool", lambda e, hb=hb, m=m, h=h, s_=s_: e.dma_start(
                            out=ka[hb][32:40, s_, m, :].rearrange("p (a b) -> p a b", b=2048),
                            in_=cst["c_kaug"][h, s_].rearrange("p (a b) -> p a b", b=2048)), w=["ka%d" % hb],
                            dma="kg%d_%d_%d" % (hb, m, s_))

            EPI_AT = [1, 2, 12, 13, 14, 15, 16, 17, 18, 19]
            load_head(0)
            it = 0
            pend_pv = None
            pend_epi = []
            for h in range(4):
                hb = h % 2
                if pend_pv is not None:
                    pend_pv()
                    pend_pv = None
                if h + 1 < 4:
                    load_head(h + 1)
                if h == 0:
                    for j in range(NJ):
                        for gu in range(2):
                            src = w_gu[l][:, gu * FFN + j * 128:gu * FFN + (j + 1) * 128].rearrange("(k p) c -> p k c", p=128)
                            dst = wguS[j].rearrange("p (k n) -> p k n", n=256)[:, :, gu * 128:(gu + 1) * 128]
                            S.add("pool", lambda e, src=src, dst=dst: e.dma_start(out=dst, in_=src), w=["wguS%d" % j],
                                  dma="wguS%d" % ((2 * j + gu) % 4), bg=True)
                mh = float(S_A[h])
                QK = ["qa%d" % hb, "ka%d" % hb]
                for qc in range(8):
                    ab = (h * 8 + qc) % 2
                    acc = ps[:, (4 + 2 * ab) * 512:(6 + 2 * ab) * 512].rearrange("p (m t) -> p m t", t=512)
                    acck = "acc%d" % ab
                    q0 = qc * 512
                    for kb in range(32):
                        sb = it % 2
                        pb = it % 3
                        Sv = ps[:, sb * 1024:(sb + 1) * 1024].rearrange("p (m t) -> p m t", t=512)
                        sk = "S%d" % sb
                        dl = kb - 4 * qc
                        k0 = kb * 128
                        for m in range(2):
                            if dl < 0 or dl > 3:
                                s_ = 0 if dl < 0 else 1
                                S.add("pe", lambda e, m=m, s_=s_, k0=k0, Sv=Sv, hb=hb, q0=q0: e.matmul(
                                    Sv[:, m, :], lhsT=ka[hb][:, s_, m, k0:k0 + 128], rhs=qa[hb][:, m, q0:q0 + 512],
                                    start=True, stop=True), r=QK, w=[sk])
                            else:
                                c0 = 128 * dl
                                if dl > 0:
                                    S.add("pe", lambda e, m=m, k0=k0, Sv=Sv, hb=hb, q0=q0, c0=c0: e.matmul(
                                        Sv[:, m, 0:c0], lhsT=ka[hb][:, 1, m, k0:k0 + 128], rhs=qa[hb][:, m, q0:q0 + c0],
                                        start=True, stop=True), r=QK, w=[sk])
                                S.add("pe", lambda e, m=m, k0=k0, Sv=Sv, hb=hb, q0=q0, c0=c0: e.matmul(
                                    Sv[:, m, c0:512], lhsT=ka[hb][:, 0, m, k0:k0 + 128], rhs=qa[hb][:, m, q0 + c0:q0 + 512],
                                    start=True, stop=False), r=QK, w=[sk])
                                for hl in range(2):
                                    S.add("pe", lambda e, m=m, Sv=Sv, c0=c0, hl=hl, h=h: e.matmul(
                                        Sv[:, m, c0:c0 + 128], lhsT=ident_bf, rhs=dpt[:, h, hl, :],
                                        start=False, stop=(hl == 1)), r=["dpt", "ident_bf"], w=[sk])
                        if pend_pv is not None:
                            pend_pv()
                            pend_pv = None
                        if dl < 0 or dl > 3:
                            bias = -mh * abs(512 * qc - 128 * kb)
                            S.add("act", lambda e, Sv=Sv, pb=pb, bias=bias: e.activation(
                                out=pT[pb].rearrange("p m t -> p (m t)"), in_=Sv.rearrange("p m t -> p (m t)"),
                                func=AF.Exp, bias=float(bias), scale=1.0), r=[sk], w=["pT%d" % pb])
                        else:
                            c0 = 128 * dl
                            if dl > 0:
                                S.add("act", lambda e, Sv=Sv, pb=pb, c0=c0, mh=mh: e.activation(
                                    out=pT[pb][:, :, 0:c0], in_=Sv[:, :, 0:c0], func=AF.Exp, bias=float(-mh * c0), scale=1.0),
                                    r=[sk], w=["pT%d" % pb])
                            S.add("act", lambda e, Sv=Sv, pb=pb, c0=c0, mh=mh: e.activation(
                                out=pT[pb][:, :, c0:512], in_=Sv[:, :, c0:512], func=AF.Exp, bias=float(mh * c0), scale=1.0),
                                r=[sk], w=["pT%d" % pb])

                        def pv(pb=pb, kb=kb, acc=acc, acck=acck, hb=hb):
                            for m in range(2):
                                S.add("pe", lambda e, m=m: e.matmul(
                                    acc[:, m, :], lhsT=vah[hb][:, kb, :], rhs=pT[pb][:, m, :],
                                    start=(kb == 0), stop=(kb == 31)), r=["pT%d" % pb, "va%d" % hb], w=[acck])
                        pend_pv = pv
                        it += 1
                        if pend_epi and kb == EPI_AT[10 - len(pend_epi)]:
                            pend_epi.pop(0)()
                    def mk_epi(acc=acc, acck=acck, h=h, qc=qc):
                        st = []
                        st.append(lambda: S.add("dve", lambda e: e.tensor_copy(out=osb[0:65].rearrange("p m t -> p (m t)"),
                                                                                in_=acc[0:65].rearrange("p m t -> p (m t)")), r=[acck], w=["osb"]))
                        st.append(lambda: S.add("dve", lambda e: e.reciprocal(out=rl[64:65, :], in_=osb[64:65].rearrange("p m t -> p (m t)")),
                                                r=["osb"], w=["rl"]))
                        def bc():
                            for m in range(2):
                                S.add("pe", lambda e, m=m: e.matmul(acc[0:64, m, :], lhsT=ones[64:65, 0:64], rhs=rl[64:65, m * 512:(m + 1) * 512],
                                                                     start=True, stop=True), r=["rl", "ones"], w=[acck])
                        st.append(bc)
                        def mul():
                            S.add("dve", lambda e: e.tensor_tensor(out=t0[0:64], in0=acc[0:64, 0, :], in1=osb[0:64, 0, :], op=ALU.mult),
                                  r=[acck, "osb"], w=["t0"])
                            S.add("dve", lambda e: e.tensor_tensor(out=t1[0:64], in0=acc[0:64, 1, :], in1=osb[0:64, 1, :], op=ALU.mult),
                                  r=[acck, "osb"], w=["t1"])
                            S.add("dve", lambda e: e.scalar_tensor_tensor(out=dd[0:64], in0=t1[0:64], scalar=lamneg, in1=t0[0:64],
                                                                          op0=ALU.mult, op1=ALU.add), r=["t0", "t1", "lamneg"], w=["dd"])
                        st.append(mul)
                        st.append(lambda: S.add("dve", lambda e: e.tensor_tensor(out=sq[0:64], in0=dd[0:64], in1=dd[0:64], op=ALU.mult), r=["dd"], w=["sq"]))
                        st.append(lambda: S.add("pe", lambda e: e.matmul(acc[0:64, 0, :], lhsT=ones[0:64, 0:64], rhs=sq[0:64],
                                                                          start=True, stop=True), r=["sq", "ones"], w=[acck]))
                        st.append(lambda: S.add("act", lambda e: e.activation(out=tmpv[0:64], in_=acc[0:64, 0, :], func=AF.Ln, scale=1.0 / 64.0,
                                                                              bias=epsc[0:64, 0:1]), r=[acck, "epsc"], w=["tmpv"]))
                        st.append(lambda: S.add("act", lambda e: e.activation(out=rstd[0:64], in_=tmpv[0:64], func=AF.Exp, scale=-0.5),
                                                r=["tmpv"], w=["rstd"]))
                        st.append(lambda: S.add("dve", lambda e: e.scalar_tensor_tensor(out=obf[0:64], in0=dd[0:64], scalar=gcol, in1=rstd[0:64],
                                                                                        op0=ALU.mult, op1=ALU.mult),
                                                r=["dd", "rstd", "gcol"], w=["obfA"]))
                        st.append(lambda: S.add("sp", lambda e: e.dma_start(out=mixT[h * 64:(h + 1) * 64, qc * 512:(qc + 1) * 512], in_=obf[0:64]),
                                                r=["obfA"], dma="obfA"))
                        return st
                    while pend_epi:
                        pend_epi.pop(0)()
                    pend_epi = mk_epi()
            if pend_pv is not None:
                pend_pv()
            while pend_epi:
                pend_epi.pop(0)()
            S.barrier()

        def phase_B(l):
            M = Mem(PBASE)
            PAD = 64
            vb = M.bf16(3 * 33 * 390).rearrange("p (a t n) -> p a t n", a=3, t=33)
            qb = M.bf16(T)
            kb_ = M.bf16(T + 2 * PAD)
            qp = M.bf16(T)
            kp = M.bf16(T + 2 * PAD)
            bias = M.f32(7 * 512).rearrange("p (v n) -> p v n", n=512)
            accT = M.f32(T)
            Ssb = [M.f32(512) for _ in range(3)]
            pT = [M.bf16(512) for _ in range(3)]
            rl = [M.f32(512) for _ in range(2)]
            obf = [M.bf16(512) for _ in range(2)]
            S.add("sp", lambda e: e.dma_start(out=bias.rearrange("p v n -> p (v n)"), in_=cst["c_bbias"].rearrange("p v n -> p (v n)")),
                  w=["biasB"], dma="biasB")
            VBK = ["vb_%d" % i for i in range(84)]
            for a3 in range(3):
                for t3 in range(3):
                    S.add("dve", lambda e, a3=a3, t3=t3: e.memset(vb[:, a3, t3 * 11:(t3 + 1) * 11, :], 0.0), w=VBK)
            for (buf, key) in ((qb, "qb"), (kb_, "kb"), (qp, "qp"), (kp, "kp")):
                S.add("dve", lambda e, buf=buf: e.memset(buf, 0.0), w=[key])
            nd = 0
            for pi, (win, d) in enumerate(B_PATTERNS):
                Lc = T // d
                nt = Lc // 128
                for r in range(d):
                    srcB = bass.AP(vB.tensor, r * 390, [[d * 390, 64], [128 * d * 390, nt], [1, 390]])
                    tA = r * nt + 1
                    tB = r * nt
                    ntA = nt
                    if (64 + 128 * (nt - 1) + 63) * d + r >= T:
                        ntA = nt - 1
                    if ntA > 0:
                        srcA = bass.AP(vB.tensor, (64 * d + r) * 390, [[d * 390, 64], [128 * d * 390, ntA], [1, 390]])
                        S.add("sp", lambda e, srcA=srcA, pi=pi, tA=tA, ntA=ntA: e.dma_start(out=vb[0:64, pi, tA:tA + ntA, :], in_=srcA),
                              w=[VBK[nd]], dma="vb%d" % (nd % 4))
                        nd += 1
                    S.add("sp", lambda e, srcB=srcB, pi=pi, tB=tB, nt=nt: e.dma_start(out=vb[64:128, pi, tB:tB + nt, :], in_=srcB),
                          w=[VBK[nd]], dma="vb%d" % (nd % 4))
                    nd += 1
            blocks = [(0, 0), (0, 1), (1, 1), (1, 2)]
            def load_qk(h):
                S.add("sp", lambda e, h=h: e.dma_start(out=qb[0:64, :], in_=qbT[h * 64:(h + 1) * 64, :]), w=["qb"], dma="qb")
                S.add("sp", lambda e, h=h: e.dma_start(out=kb_[0:64, PAD:PAD + T], in_=kbT[h * 64:(h + 1) * 64, :]), w=["kb"], dma="kb")

            load_qk(0)
            for h in range(6):
                mh = float(S_B[h])
                for pi, (win, d) in enumerate(B_PATTERNS):
                    Lc = T // d
                    ntc = Lc // 128
                    ng = ntc // 2
                    if pi == 0:
                        qs_, ks_, qk_, kk_ = qb, kb_, "qb", "kb"
                    else:
                        qs_, ks_, qk_, kk_ = qp, kp, "qp", "kp"
                        S.add("dve", lambda e, d=d: e.tensor_copy(out=qp[0:64, :].rearrange("p (r j) -> p r j", r=d),
                                                                  in_=qb[0:64, :].rearrange("p (j r) -> p r j", r=d)), r=["qb"], w=["qp"])
                        S.add("act", lambda e, d=d: e.copy(out=kp[0:64, PAD:PAD + T].rearrange("p (r j) -> p r j", r=d),
                                                           in_=kb_[0:64, PAD:PAD + T].rearrange("p (j r) -> p r j", r=d)), r=["kb"], w=["kp"])
                        if pi == 2 and h + 1 < 6:
                            load_qk(h + 1)
                    groups = []
                    for r in range(d):
                        for gq in range(ng):
                            b0 = 2 * gq
                            tq = r * ntc + b0
                            if pi == 2:
                                var = 6
                            else:
                                var = 3 * pi + (1 if gq == 0 else (2 if gq == ng - 1 else 0))
                            groups.append((tq, var, 128 * b0 * d + r))

                    def emit_S(gi, groups=groups, qs_=qs_, ks_=ks_, qk_=qk_, kk_=kk_):
                        tq, var, s0 = groups[gi]
                        sbk = gi % 3
                        Sb = bank(sbk)
                        for bi, (qi, ci) in enumerate(blocks):
                            qs = 128 * (tq + qi)
                            ks = PAD + 128 * (tq + ci) - 64
                            S.add("pe", lambda e, bi=bi, qs=qs, ks=ks: e.matmul(
                                Sb[:, bi * 128:(bi + 1) * 128], lhsT=ks_[:, ks:ks + 128], rhs=qs_[:, qs:qs + 128],
                                start=True, stop=True), r=[qk_, kk_], w=["SB%d" % sbk])

                    def emit_rest(gi, groups=groups, h=h, pi=pi, d=d, mh=mh):
                        tq, var, s0 = groups[gi]
                        sbk = gi % 3
                        obk = gi % 2
                        Sb = bank(sbk)
                        ob = bank(3 + obk)
                        S.add("dve", lambda e: e.scalar_tensor_tensor(out=Ssb[sbk], in0=bias[:, var, :], scalar=mh, in1=Sb,
                                                                      op0=ALU.mult, op1=ALU.add),
                              r=["SB%d" % sbk, "biasB"], w=["Ssb%d" % sbk])
                        S.add("act", lambda e: e.activation(out=pT[sbk], in_=Ssb[sbk], func=AF.Exp), r=["Ssb%d" % sbk], w=["pTB%d" % sbk])
                        for bi, (qi, ci) in enumerate(blocks):
                            S.add("pe", lambda e, bi=bi, qi=qi, ci=ci: e.matmul(
                                ob[0:65, qi * 128:(qi + 1) * 128], lhsT=vb[:, pi, tq + ci, h * 65:(h + 1) * 65],
                                rhs=pT[sbk][:, bi * 128:(bi + 1) * 128], start=(bi % 2 == 0), stop=(bi % 2 == 1)),
                                r=["pTB%d" % sbk] + VBK, w=["oB%d" % obk])

                    def emit_tail(gi, groups=groups, pi=pi, d=d):
                        tq, var, s0 = groups[gi]
                        obk = gi % 2
                        ob = bank(3 + obk)
                        dst = accT[0:65, s0:s0 + 255 * d + 1:d]
                        if pi == 0:
                            S.add("act", lambda e: e.copy(out=dst, in_=ob[0:65, 0:256]), r=["oB%d" % obk], w=["accT"])
                        else:
                            S.add("dve", lambda e: e.tensor_tensor(out=dst, in0=ob[0:65, 0:256], in1=dst, op=ALU.add),
                                  r=["oB%d" % obk, "accT"], w=["accT"])

                    emit_S(0)
                    emit_S(1)
                    for gi in range(len(groups)):
                        if gi + 2 < len(groups):
                            emit_S(gi + 2)
                        emit_rest(gi)
                        if gi >= 1:
                            emit_tail(gi - 1)
                    emit_tail(len(groups) - 1)
                for c in range(8):
                    norm_store(accT[0:64, c * 512:(c + 1) * 512], accT[64:65, c * 512:(c + 1) * 512], rl[c % 2], bank(5 + c % 2), obf[c % 2],
                               mixT[256 + h * 64:256 + (h + 1) * 64, c * 512:(c + 1) * 512], ["accT"], "B%d" % (c % 2))
            S.barrier()

        def phase_C(l):
            M = Mem(PBASE)
            PADC = 128
            G = []
            for g in range(2):
                d_ = dict(
                    kc=M.bf16(T + 2 * PADC), qc=M.bf16(3 * T).rearrange("p (r t) -> p r t", t=T),
                    vc=M.bf16(32 * 65).rearrange("p (k n) -> p k n", n=65), cb=M.f32(3 * 384).rearrange("p (k n) -> p k n", n=384),
                    Ssb=M.f32(3 * 384).rearrange("p (k n) -> p k n", n=384), pT=M.bf16(3 * 384).rearrange("p (k n) -> p k n", n=384),
                    osbc=[M.f32(3 * 512).rearrange("p (r t) -> p r t", t=512) for _ in range(2)],
                    rlin=M.f32(512), rl=M.f32(512), obf=M.bf16(512))
                G.append(d_)
            es = M.f32(16)
            S.add("sp", lambda e: e.dma_start(out=es[64:65, 0:6], in_=sink[l:l + 1, :]), w=["es0"], dma="es0")
            S.add("act", lambda e: e.activation(out=es[64:65, 8:14], in_=es[64:65, 0:6], func=AF.Exp), r=["es0"], w=["es"])
            for g in range(2):
                d_ = G[g]
                S.add("dve", lambda e, d_=d_: e.memset(d_["kc"], 0.0), w=["kc%d" % g])
                S.add("dve", lambda e, d_=d_: e.memset(d_["qc"][64:128].rearrange("p r t -> p (r t)"), 0.0), w=["qcz%d" % g])
                S.add("sp", lambda e, g=g, d_=d_: e.dma_start(out=d_["kc"][0:64, PADC:PADC + T], in_=kcT[g * 64:(g + 1) * 64, :]),
                      w=["kc%d" % g], dma="kc%d" % g)
                for rep in range(3):
                    hq = 3 * g + rep
                    S.add("sp", lambda e, rep=rep, hq=hq, d_=d_: e.dma_start(out=d_["qc"][0:64, rep, :], in_=qcT[hq * 64:(hq + 1) * 64, :]),
                          w=["qc%d_%d" % (g, rep)], dma="qc%d_%d" % (g, rep))
                for q4 in range(4):
                    S.add("sp", lambda e, g=g, q4=q4, d_=d_: e.dma_start(
                        out=d_["vc"][:, q4 * 8:(q4 + 1) * 8, :],
                        in_=vC[q4 * 1024:(q4 + 1) * 1024, g * 65:(g + 1) * 65].rearrange("(k p) n -> p k n", p=128)),
                        w=["vc%d_%d" % (g, q4)], dma="vc%d_%d" % (g, q4))
                S.add("sp", lambda e, g=g, d_=d_: e.dma_start(out=d_["cb"].rearrange("p k n -> p (k n)"),
                                                              in_=cst["c_cbias"][g].rearrange("p k n -> p (k n)")), w=["cb%d" % g], dma="cb%d" % g)

            def kts_of(qb_):
                return [kt for kt in range(3) if 0 <= qb_ + kt - 1 <= 31]

            def c_S(g, qb_):
                d_ = G[g]
                for kt in kts_of(qb_):
                    kbk = qb_ + kt - 1
                    Sb = bank(3 * g + kt, 384)
                    S.add("pe", lambda e, Sb=Sb, kbk=kbk: e.matmul(
                        Sb, lhsT=d_["kc"][:, PADC + kbk * 128:PADC + (kbk + 1) * 128], rhs=d_["qc"][:, :, qb_ * 128:(qb_ + 1) * 128],
                        start=True, stop=True), r=["kc%d" % g, "qcz%d" % g] + ["qc%d_%d" % (g, r_) for r_ in range(3)], w=["SC%d_%d" % (g, kt)])

            def c_exp(g, qb_):
                d_ = G[g]
                kts = kts_of(qb_)
                for kt in kts:
                    Sb = bank(3 * g + kt, 384)
                    S.add("dve", lambda e, Sb=Sb, kt=kt: e.tensor_tensor(out=d_["Ssb"][:, kt, :], in0=Sb, in1=d_["cb"][:, kt, :], op=ALU.add),
                          r=["SC%d_%d" % (g, kt), "cb%d" % g], w=["SsbC%d" % g])
                lo, hi = kts[0], kts[-1] + 1
                S.add("act", lambda e: e.activation(out=d_["pT"][:, lo:hi, :], in_=d_["Ssb"][:, lo:hi, :], func=AF.Exp),
                      r=["SsbC%d" % g], w=["pTC%d" % g])

            def c_pv(g, qb_):
                d_ = G[g]
                kts = kts_of(qb_)
                lo, hi = kts[0], kts[-1] + 1
                for kt in kts:
                    kbk = qb_ + kt - 1
                    S.add("pe", lambda e, kt=kt, kbk=kbk: e.matmul(
                        bank(6 + g, 384)[0:65, :], lhsT=d_["vc"][:, kbk, :], rhs=d_["pT"][:, kt, :], start=(kt == lo), stop=(kt == hi - 1)),
                        r=["pTC%d" % g] + ["vc%d_%d" % (g, q4) for q4 in range(4)], w=["oC%d" % g])

            def c_tail(g, qb_):
                d_ = G[g]
                ch = qb_ // 4
                obk = ch % 2
                S.add("act", lambda e: e.copy(
                    out=d_["osbc"][obk][0:65, :, (qb_ % 4) * 128:(qb_ % 4 + 1) * 128],
                    in_=bank(6 + g, 384)[0:65, :].rearrange("p (r t) -> p r t", t=128)), r=["oC%d" % g], w=["osbc%d_%d" % (g, obk)])
                if qb_ % 4 == 3:
                    for rep in range(3):
                        hq = 3 * g + rep
                        S.add("dve", lambda e, rep=rep, hq=hq: e.tensor_scalar(
                            out=d_["rlin"][64:65, :], in0=d_["osbc"][obk][64:65, rep, :], scalar1=es[64:65, 8 + hq:9 + hq], scalar2=None, op0=ALU.add),
                            r=["osbc%d_%d" % (g, obk), "es"], w=["rlin%d" % g])
                        norm_store(d_["osbc"][obk][0:64, rep, :], d_["rlin"][64:65, :], d_["rl"], bank(6 + g), d_["obf"],
                                   mixT[640 + hq * 64:640 + (hq + 1) * 64, ch * 512:(ch + 1) * 512],
                                   ["rlin%d" % g, "osbc%d_%d" % (g, obk)], "C%d" % g, bc_key="oC%d" % g)

            c_S(0, 0)
            c_S(1, 0)
            for qb_ in range(32):
                for g in range(2):
                    c_exp(g, qb_)
                    if qb_ + 1 < 32:
                        c_S(g, qb_ + 1)
                for g in range(2):
                    c_pv(g, qb_)
                for g in range(2):
                    c_tail(g, qb_)
            S.barrier()

        def phase_P3(l):
            EPS1 = LN_EPS / (ALPHA * ALPHA)
            M = Mem(PBASE)
            wob = M.bf16(8 * 1024).rearrange("p (k n) -> p k n", n=1024)
            lnt = M.f32(6 * 1024).rearrange("p (a n) -> p a n", n=1024)
            mxT = M.bf16(8 * 512).rearrange("p (k t) -> p k t", t=512)
            xt = [M.f32(4096).rearrange("p (t n) -> p t n", n=1024) for _ in range(2)]
            zn = M.f32(1024)
            h2 = M.bf16(4 * 1024).rearrange("p (t n) -> p t n", n=1024)
            h2T = M.bf16(8 * 512).rearrange("p (k t) -> p k t", t=512)
            actT = M.bf16(NJ * 512).rearrange("p (j t) -> p j t", t=512)
            sg = [M.f32(512) for _ in range(2)]
            wgu = [M.bf16(8 * 256).rearrange("p (k n) -> p k n", n=256) for _ in range(3)]
            wdn = [M.bf16(1024) for _ in range(3)]
            wst = [M.f32(1024) for _ in range(2)]
            stt = M.f32(8 * 24).rearrange("p (s n) -> p s n", n=24)
            negh1 = M.f32(2)
            xdst = x1s if l == 0 else y_out
            S.add("pool", lambda e: e.memset(negh1, -0.5), w=["negh1"])
            for (a, src) in ((0, ln_g[l, 0]), (1, ln_b[l, 0]), (4, ln_g[l, 1]), (5, ln_b[l, 1])):
                S.add("sp", lambda e, a=a, src=src: e.dma_start(out=lnt[:, a, :], in_=src.partition_broadcast(128)), w=["lnt%d" % a], dma="lnt%d" % a)
            S.add("dve", lambda e: e.tensor_tensor(out=lnt[:, 2, :], in0=lnt[:, 0, :], in1=keepR[:, 1, :], op=ALU.mult), r=["lnt0"], w=["lnt2"])
            S.add("dve", lambda e: e.tensor_tensor(out=lnt[:, 3, :], in0=lnt[:, 1, :], in1=keepR[:, 1, :], op=ALU.mult), r=["lnt1"], w=["lnt3"])
            S.add("dve", lambda e: e.tensor_tensor(out=lnt[:, 3, :], in0=lnt[:, 3, :], in1=keepR[:, 2, :], op=ALU.add), r=["lnt3"], w=["lnt3"])
            for kc in range(8):
                b = kc % 2
                S.add("sp", lambda e, kc=kc, b=b: e.dma_start(out=wst[b], in_=w_out[l, kc * 128:(kc + 1) * 128, :]), w=["wst%d" % b], dma="wst%d" % b)
                S.add("dve", lambda e, kc=kc, b=b: e.tensor_tensor(out=wob[:, kc, :], in0=wst[b], in1=keepR[:, 0, :], op=ALU.mult),
                      r=["wst%d" % b], w=["wob%d" % kc])
            for j in range(NJ):
                b = j % 2
                wb_ = j % 3
                S.add("sp", lambda e, j=j, b=b: e.dma_start(out=wst[b], in_=w_down[l, j * 128:(j + 1) * 128, :]), w=["wst%d" % b], dma="wst%d" % b)
                S.add("dve", lambda e, b=b, wb_=wb_: e.tensor_tensor(out=wdn[wb_], in0=wst[b], in1=keepR[:, 3, :], op=ALU.mult),
                      r=["wst%d" % b], w=["wdn%d" % wb_])
                S.add("sp", lambda e, j=j, wb_=wb_: e.dma_start(out=wdnS[j], in_=wdn[wb_]), r=["wdn%d" % wb_], w=["wdnS%d" % j], dma="wdn%d" % wb_)
            WOB = ["wob%d" % k for k in range(8)]

            def load_mx(c):
                S.add("sp", lambda e, c=c: e.dma_start(out=mxT, in_=mixT[:, c * 512:(c + 1) * 512].rearrange("(k p) t -> p k t", p=128)),
                      w=["mxT"], dma="mxT")

            def load_x(c):
                cb_ = c % 2
                xsrc = x_in if l == 0 else x1s
                S.add("sp", lambda e, c=c, cb_=cb_: e.dma_start(
                    out=xt[cb_], in_=xsrc[c * 512:(c + 1) * 512, :].rearrange("(t p) n -> p t n", p=128)), w=["xt%d_%d" % (cb_, t_) for t_ in range(4)],
                    dma="xt%d" % cb_)

            cnt = {"tm": 0, "ev": 0, "zn": 0}

            def wout(c):
                cb_ = c % 2
                for tb in range(4):
                    for n in range(2):
                        pb = cnt["tm"] % 2
                        cnt["tm"] += 1
                        for kc in range(8):
                            S.add("pe", lambda e, tb=tb, n=n, kc=kc, pb=pb: e.matmul(
                                bank(pb), lhsT=mxT[:, kc, tb * 128:(tb + 1) * 128], rhs=wob[:, kc, n * 512:(n + 1) * 512],
                                start=(kc == 0), stop=(kc == 7)), r=["mxT", WOB[kc]], w=["psO%d" % pb])
                        S.add("dve", lambda e, tb=tb, n=n, pb=pb, cb_=cb_: e.tensor_tensor(
                            out=xt[cb_][:, tb, n * 512:(n + 1) * 512], in0=bank(pb), in1=xt[cb_][:, tb, n * 512:(n + 1) * 512], op=ALU.add),
                            r=["psO%d" % pb, "xt%d_%d" % (cb_, tb)], w=["xt%d_%d" % (cb_, tb)])

            def layer_norm(c, tb, which):
                cb_ = c % 2
                y = xt[cb_][:, tb, :]
                yk = "xt%d_%d" % (cb_, tb)
                st = stt[:, tb + 4 * (which - 1), :]
                sk = "st%d" % (tb + 4 * (which - 1))
                S.add("dve", lambda e: e.bn_stats(out=st[:, 0:6], in_=y[:, 0:512]), r=[yk], w=[sk + "a"])
                S.add("dve", lambda e: e.bn_stats(out=st[:, 6:12], in_=y[:, 512:1024]), r=[yk], w=[sk + "b"])
                S.add("dve", lambda e: e.bn_aggr(out=st[:, 12:14], in_=st[:, 0:12]), r=[sk + "a", sk + "b"], w=[sk + "mv"])
                S.add("dve", lambda e: e.tensor_scalar(out=st[:, 14:15], in0=st[:, 13:14], scalar1=float(EPS1), scalar2=None, op0=ALU.add),
                      r=[sk + "mv"], w=[sk + "ve"])
                S.add("pool", lambda e: e.tensor_tensor(out=st[:, 15:16], in0=st[:, 14:15], in1=negh1[:, 0:1], op=ALU.pow),
                      r=[sk + "ve", "negh1"], w=[sk + "rs"])
                S.add("dve", lambda e: e.tensor_scalar(out=st[:, 16:17], in0=st[:, 12:13], scalar1=st[:, 15:16], scalar2=-1.0,
                                                        op0=ALU.mult, op1=ALU.mult), r=[sk + "mv", sk + "rs"], w=[sk + "nb"])
                zi = cnt["zn"] % 2
                cnt["zn"] += 1
                znb = zn if zi == 0 else wst[1]
                znk = "zn" if zi == 0 else "wst1"
                S.add("act", lambda e: e.activation(out=znb, in_=y, func=AF.Identity, scale=st[:, 15:16], bias=st[:, 16:17]),
                      r=[yk, sk + "rs", sk + "nb"], w=[znk])
                ga, ba = (0, 1) if which == 1 else (4, 5)
                S.add("pool", lambda e: e.tensor_tensor(out=y, in0=znb, in1=lnt[:, ga, :], op=ALU.mult), r=[znk, "lnt%d" % ga], w=[yk])
                S.add("pool", lambda e: e.tensor_tensor(out=y, in0=y, in1=lnt[:, ba, :], op=ALU.add), r=[yk, "lnt%d" % ba], w=[yk])
                if which == 1:
                    S.add("dve", lambda e: e.tensor_tensor(out=wst[0], in0=znb, in1=lnt[:, 2, :], op=ALU.mult), r=[znk, "lnt2"], w=["wst0"])
                    S.add("dve", lambda e: e.tensor_tensor(out=h2[:, tb, :], in0=wst[0], in1=lnt[:, 3, :], op=ALU.add),
                          r=["wst0", "lnt3"], w=["h2_%d" % tb])

            def transposes(c):
                for kc in range(8):
                    pb = 2 + kc % 2
                    for tb in range(4):
                        S.add("pe", lambda e, kc=kc, tb=tb, pb=pb: e.transpose(
                            out=bank_bf(pb)[:, tb * 128:(tb + 1) * 128], in_=h2[:, tb, kc * 128:(kc + 1) * 128], identity=ident_bf),
                            r=["h2_%d" % tb, "ident_bf"], w=["psT%d" % pb])
                    if cnt["ev"] % 2 == 0:
                        S.add("act", lambda e, kc=kc, pb=pb: e.copy(out=h2T[:, kc, :], in_=bank_bf(pb)[:, 0:512]), r=["psT%d" % pb], w=["h2T%d" % kc])
                    else:
                        S.add("dve", lambda e, kc=kc, pb=pb: e.tensor_copy(out=h2T[:, kc, :], in_=bank_bf(pb)[:, 0:512]), r=["psT%d" % pb], w=["h2T%d" % kc])
                    cnt["ev"] += 1

            def load_wgu(p):
                if p >= 8 * NJ:
                    return
                j = p % NJ
                wb_ = p % 3
                S.add("sp", lambda e, j=j, wb_=wb_: e.dma_start(out=wgu[wb_].rearrange("p k n -> p (k n)"), in_=wguS[j]),
                      r=["wguS%d" % j], w=["wgu%d" % wb_], dma="wgu%d" % wb_)

            def gu(c, hooks):
                H2T = ["h2T%d" % k for k in range(8)]
                for j in range(NJ):
                    load_wgu(c * NJ + j + 2)
                    wb_ = (c * NJ + j) % 3
                    pg = 4 + 2 * (j % 2)
                    for half in range(2):
                        for kc in range(8):
                            S.add("pe", lambda e, kc=kc, half=half, pg=pg, wb_=wb_: e.matmul(
                                bank(pg + half), lhsT=wgu[wb_][:, kc, half * 128:(half + 1) * 128], rhs=h2T[:, kc, :],
                                start=(kc == 0), stop=(kc == 7)), r=["wgu%d" % wb_, H2T[kc]], w=["psG%d" % (pg + half)])
                    sb_ = j % 2
                    S.add("act", lambda e, pg=pg, sb_=sb_: e.activation(out=sg[sb_], in_=bank(pg), func=AF.Silu), r=["psG%d" % pg], w=["sg%d" % sb_])
                    S.add("dve", lambda e, pg=pg, sb_=sb_, j=j: e.tensor_tensor(out=actT[:, j, :], in0=bank(pg + 1), in1=sg[sb_], op=ALU.mult),
                          r=["psG%d" % (pg + 1), "sg%d" % sb_], w=["actT%d" % j])
                    for f in hooks.get(j, ()):
                        f()

            def load_wdn(q):
                if q >= 16 * NJ:
                    return
                i = q % (2 * NJ)
                j, n = i % NJ, i // NJ
                wb_ = q % 3
                S.add("sp", lambda e, j=j, n=n, wb_=wb_: e.dma_start(out=wdn[wb_][:, 0:512], in_=wdnS[j][:, n * 512:(n + 1) * 512]),
                      r=["wdnS%d" % j], w=["wdn%d" % wb_], dma="wdn%d" % wb_)

            def down(c, mid):
                cb_ = c % 2
                for i in range(2 * NJ):
                    load_wdn(c * 2 * NJ + i + 2)
                    j, n = i % NJ, i // NJ
                    wb_ = (c * 2 * NJ + i) % 3
                    for tb in range(4):
                        S.add("pe", lambda e, j=j, tb=tb, wb_=wb_: e.matmul(
                            bank(4 + tb), lhsT=actT[:, j, tb * 128:(tb + 1) * 128], rhs=wdn[wb_][:, 0:512],
                            start=(j == 0), stop=(j == NJ - 1)), r=["actT%d" % j, "wdn%d" % wb_], w=["psG%d" % (4 + tb)])
                    if j == NJ - 1:
                        for tb in range(4):
                            S.add("dve", lambda e, tb=tb, n=n, cb_=cb_: e.tensor_tensor(
                                out=xt[cb_][:, tb, n * 512:(n + 1) * 512], in0=bank(4 + tb), in1=xt[cb_][:, tb, n * 512:(n + 1) * 512], op=ALU.add),
                                r=["psG%d" % (4 + tb), "xt%d_%d" % (cb_, tb)], w=["xt%d_%d" % (cb_, tb)])
                        if n == 0:
                            for f in mid:
                                f()

            def ln2_tile(c, tb):
                cb_ = c % 2
                if True:
                    layer_norm(c, tb, 2)
                    t0_ = c * 512 + tb * 128
                    S.add("pool", lambda e, tb=tb, t0_=t0_, cb_=cb_: e.dma_start(out=xdst[t0_:t0_ + 128, :], in_=xt[cb_][:, tb, :]),
                          r=["xt%d_%d" % (cb_, tb)], dma="xo%d_%d" % (cb_, tb))

            load_mx(0)
            load_x(0)
            load_wgu(0)
            load_wgu(1)
            load_wdn(0)
            load_wdn(1)
            wout(0)
            for tb in range(4):
                layer_norm(0, tb, 1)
            transposes(0)
            for c in range(8):
                hooks = {}
                if c + 1 < 8:
                    load_mx(c + 1)
                if c >= 1:
                    for tb in range(4):
                        hooks.setdefault(1 + 5 * tb, []).append(lambda c=c, tb=tb: ln2_tile(c - 1, tb))
                if c + 1 < 8:
                    hooks.setdefault(17, []).append(lambda c=c: load_x(c + 1))
                gu(c, hooks)
                mid = []
                if c + 1 < 8:
                    wout(c + 1)
                    layer_norm(c + 1, 0, 1)
                    layer_norm(c + 1, 1, 1)
                    mid = [lambda c=c: layer_norm(c + 1, 2, 1), lambda c=c: layer_norm(c + 1, 3, 1)]
                down(c, mid)
                if c + 1 < 8:
                    transposes(c + 1)
            for tb in range(4):
                ln2_tile(7, tb)
            S.barrier()

        for l in layers:
            if on("M"):
                phase_M(l)
            if on("P1"):
                phase_P1(l)
            if on("A"):
                phase_A(l)
            if on("B"):
                phase_B(l)
            if on("C"):
                phase_C(l)
            if on("P3"):
                phase_P3(l)

        n_ops = S.emit()
    return nc, n_ops


def kernel(**inputs):
    nc, _ = build_program(debug=False)
    cores = list(range(NCORES))
    maps = make_in_maps(inputs, cores)
    res = run_bass_kernel_spmd(nc, maps, core_ids=cores)
    return np.stack([np.asarray(r["y"], dtype=np.float32) for r in res.results], axis=0)


def make_in_maps(inputs, cores):
    f = lambda a: np.ascontiguousarray(np.asarray(a, dtype=np.float32))
    consts = make_consts()
    shared = {
        "w_ada": f(inputs["w_ada"]), "b_ada": f(inputs["b_ada"]), "w_in": f(inputs["w_in"]),
        "lam": f(inputs["lam"]).reshape(DEPTH, 128), "subln_g": f(inputs["subln_g"]), "sink": f(inputs["sink"]),
        "w_out": f(inputs["w_out"]), "ln_g": f(inputs["ln_g"]), "ln_b": f(inputs["ln_b"]),
        "w_gu": f(inputs["w_gu"]), "w_down": f(inputs["w_down"]),
    }
    shared.update(consts)
    maps = []
    x = np.asarray(inputs["x"], dtype=np.float32)
    c = np.asarray(inputs["c"], dtype=np.float32)
    for b in cores:
        m = dict(shared)
        m["x"] = np.ascontiguousarray(x[b])
        m["ccol"] = np.ascontiguousarray(c[b].reshape(8, 128).T)
        maps.append(m)
    return maps
```

```python
import contextlib
import math
import numpy as np
import ml_dtypes
import concourse.bass as bass
import concourse.mybir as mybir
from concourse.bass_utils import run_bass_kernel_spmd

F32 = mybir.dt.float32
BF16 = mybir.dt.bfloat16
AF = mybir.ActivationFunctionType
ALU = mybir.AluOpType
AX = mybir.AxisListType

T = 4096
D = 1024
DEPTH = 2
NCORES = 8
FFN = 2816
NJ = FFN // 128
INW = 2560
LN_EPS = 1e-5
ALPHA = (2 * DEPTH) ** 0.25
NEG = -1.0e30
POOLW = 45500

SL = (2.0 ** (-8.0 * np.arange(1, 17) / 16)).astype(np.float32)
S_C = SL[0:6]
S_A = SL[6:10]
S_B = SL[10:16]
B_PATTERNS = ((128, 1), (512, 4), (2048, 16))


class Sched:
    ENGS = ("pe", "act", "dve", "pool", "sp")

    def __init__(self, nc, n_dma_sems=48):
        self.nc = nc
        self.ops = []
        self.last_w = {}
        self.readers = {}
        self.dma_last = {}
        self.dma_slot = {}
        self.n_dma_sems = n_dma_sems
        self.barrier_deps = []
        self.n_bg = 4
        self.bg_slot = {}
        self.persist = set()

    def add(self, eng, fn, r=(), w=(), dma=None, bg=False):
        idx = len(self.ops)
        raw = set()
        other = set()
        for k in r:
            if k in self.last_w:
                raw.add(self.last_w[k])
        for k in w:
            if k in self.last_w:
                other.add(self.last_w[k])
            other.update(self.readers.get(k, ()))
        slot = None
        if dma is not None:
            if bg:
                if dma not in self.bg_slot:
                    self.bg_slot[dma] = self.n_dma_sems + len(self.bg_slot) % self.n_bg
                slot = self.bg_slot[dma]
                self.persist.update(w)
            else:
                if dma not in self.dma_slot:
                    self.dma_slot[dma] = len(self.dma_slot) % self.n_dma_sems
                slot = self.dma_slot[dma]
            if slot in self.dma_last:
                raw.add(self.dma_last[slot])
            self.dma_last[slot] = idx
        for k in r:
            self.readers.setdefault(k, []).append(idx)
        for k in w:
            self.last_w[k] = idx
            self.readers[k] = []
        deps = set() if bg else set(self.barrier_deps)
        for d in raw | other:
            p = self.ops[d]
            if p["slot"] is None and slot is None and p["eng"] == eng:
                if eng == "pe" or d not in raw:
                    continue
            deps.add(d)
        deps.discard(idx)
        self.ops.append(dict(eng=eng, fn=fn, deps=deps, slot=slot, sem=None, val=0))
        return idx

    def barrier(self, final=False):
        last = {}
        for i, op in enumerate(self.ops):
            if op["slot"] is not None and op["slot"] >= self.n_dma_sems and not final:
                continue
            key = ("dma", op["slot"]) if op["slot"] is not None else ("eng", op["eng"])
            last[key] = i
        self.barrier_deps = sorted(last.values())
        self.last_w = {k: v for k, v in self.last_w.items() if k in self.persist}
        self.readers = {k: v for k, v in self.readers.items() if k in self.persist}

    def emit(self):
        nc = self.nc
        ops = self.ops
        self.barrier(final=True)
        self.add("sp", None)
        needed = set()
        for op in ops:
            needed.update(op["deps"])
        with contextlib.ExitStack() as st:
            eng_sem = {e: st.enter_context(nc.semaphore("s_" + e)) for e in self.ENGS}
            nslots = self.n_dma_sems + self.n_bg
            dma_sems = [st.enter_context(nc.semaphore("d_%d" % i)) for i in range(nslots)]
            cnt_e = {e: 0 for e in self.ENGS}
            cnt_d = [0] * nslots
            for i, op in enumerate(ops):
                if op["slot"] is not None:
                    cnt_d[op["slot"]] += 16
                    op["sem"] = ("d", op["slot"])
                    op["val"] = cnt_d[op["slot"]]
                elif i in needed:
                    cnt_e[op["eng"]] += 1
                    op["sem"] = ("e", op["eng"])
                    op["val"] = cnt_e[op["eng"]]
            per_eng = {e: [] for e in self.ENGS}
            for op in ops:
                per_eng[op["eng"]].append(op)

            def semh(s):
                return eng_sem[s[1]] if s[0] == "e" else dma_sems[s[1]]

            def run(e_obj, ename):
                waited = {}
                for op in per_eng[ename]:
                    for d in sorted(op["deps"]):
                        p = ops[d]
                        if waited.get(p["sem"], 0) >= p["val"]:
                            continue
                        e_obj.wait_ge(semh(p["sem"]), p["val"])
                        waited[p["sem"]] = p["val"]
                    if op["fn"] is None:
                        continue
                    ins = op["fn"](e_obj)
                    if op["sem"] is not None:
                        ins.then_inc(semh(op["sem"]), 16 if op["sem"][0] == "d" else 1)

            with nc.Block() as block:
                @block.sync
                def _(e):
                    run(e, "sp")

                @block.tensor
                def _(e):
                    run(e, "pe")

                @block.scalar
                def _(e):
                    run(e, "act")

                @block.vector
                def _(e):
                    run(e, "dve")

                @block.gpsimd
                def _(e):
                    run(e, "pool")
        return len(ops)


def _hi_lo(v):
    v = np.asarray(v, np.float32)
    hi = v.astype(ml_dtypes.bfloat16).astype(np.float32)
    lo = (v - hi).astype(ml_dtypes.bfloat16).astype(np.float32)
    return hi, lo


def make_consts():
    c = {}
    c["c_ident"] = np.eye(128, dtype=np.float32)
    qaug = np.zeros((4, 8, T), np.float32)
    kaug = np.zeros((4, 2, 8, T), np.float32)
    dp = np.zeros((4, 2, 128, 128), np.float32)
    ii = (np.arange(T) % 512).astype(np.float32)
    jj = (np.arange(T) % 128).astype(np.float32)
    for h in range(4):
        m = np.float32(S_A[h])
        hi, lo = _hi_lo(m * ii)
        qaug[h, 0] = -hi
        qaug[h, 1] = -lo
        qaug[h, 2] = 1.0
        qaug[h, 3] = 1.0
        hi, lo = _hi_lo(m * jj)
        kaug[h, 0, 0] = 1.0
        kaug[h, 0, 1] = 1.0
        kaug[h, 0, 2] = hi
        kaug[h, 0, 3] = lo
        kaug[h, 1] = -kaug[h, 0]
        pj = np.arange(128, dtype=np.float32)[:, None]
        pi = np.arange(128, dtype=np.float32)[None, :]
        dmat = -2.0 * m * np.maximum(pj - pi, 0.0)
        dp[h, 0], dp[h, 1] = _hi_lo(dmat)
    c["c_qaug"] = qaug
    c["c_kaug"] = kaug
    c["c_dp"] = dp
    p = np.arange(128, dtype=np.float32)[:, None]
    i = np.arange(128, dtype=np.float32)[None, :]
    bb = np.zeros((1, 128, 7, 512), np.float32)
    for h in range(1):
        m = np.float32(1.0)
        k = 0
        for pi_, (win, dil) in enumerate(B_PATTERNS):
            d1 = np.abs(i - (p - 64.0))
            d2 = np.abs(i - (p + 64.0))
            t1 = np.where(d1 <= 64.0, -m * dil * d1, -1.0e32).astype(np.float32)
            t2 = np.where(d2 <= 64.0, -m * dil * d2, -1.0e32).astype(np.float32)
            t1f = t1.copy()
            t1f[0:64, :] = -1.0e32
            t2l = t2.copy()
            t2l[64:128, :] = -1.0e32
            mid = np.concatenate([t1, t2, t1, t2], 1)
            first = np.concatenate([t1f, t2, t1, t2], 1)
            last = np.concatenate([t1, t2, t1, t2l], 1)
            both = np.concatenate([t1f, t2, t1, t2l], 1)
            if pi_ < 2:
                bb[h, :, k] = mid
                bb[h, :, k + 1] = first
                bb[h, :, k + 2] = last
                k += 3
            else:
                bb[h, :, k] = both
                k += 1
    c["c_bbias"] = bb[0]
    cb = np.zeros((2, 128, 3, 384), np.float32)
    for g in range(2):
        for rep in range(3):
            m = np.float32(S_C[3 * g + rep])
            for kt in range(3):
                dist = np.abs(i - (p + 128.0 * (kt - 1)))
                cb[g, :, kt, rep * 128:(rep + 1) * 128] = np.where(dist <= 128.0, -m * dist, NEG)
    c["c_cbias"] = cb
    return c


CONST_SHAPES = {
    "c_ident": [128, 128], "c_qaug": [4, 8, T], "c_kaug": [4, 2, 8, T], "c_dp": [4, 2, 128, 128],
    "c_bbias": [128, 7, 512], "c_cbias": [2, 128, 3, 384],
}


def build_program(debug=False, phases=None, layers=(0, 1)):
    nc = bass.Bass("TRN2", target_bir_lowering=False)

    def din(name, shape, dt=F32):
        return nc.dram_tensor(name, shape, dt, kind="ExternalInput").ap()

    def dscr(name, shape, dt):
        return nc.dram_tensor(name, shape, dt, kind=("ExternalOutput" if debug else "Internal")).ap()

    x_in = din("x", [T, D])
    ccol = din("ccol", [128, 8])
    w_ada = din("w_ada", [DEPTH, D, 6 * D])
    b_ada = din("b_ada", [DEPTH, 6 * D])
    w_in = din("w_in", [DEPTH, D, INW])
    lam = din("lam", [DEPTH, 128])
    subln = din("subln_g", [DEPTH, 64])
    sink = din("sink", [DEPTH, 6])
    w_out = din("w_out", [DEPTH, D, D])
    ln_g = din("ln_g", [DEPTH, 2, D])
    ln_b = din("ln_b", [DEPTH, 2, D])
    w_gu = din("w_gu", [DEPTH, D, 2 * FFN])
    w_down = din("w_down", [DEPTH, FFN, D])
    cst = {k: din(k, v) for k, v in CONST_SHAPES.items()}
    y_out = nc.dram_tensor("y", [T, D], F32, kind="ExternalOutput").ap()

    qaT = dscr("qaT", [256, T], BF16)
    kaT = dscr("kaT", [256, T], BF16)
    qbT = dscr("qbT", [384, T], BF16)
    kbT = dscr("kbT", [384, T], BF16)
    qcT = dscr("qcT", [384, T], BF16)
    kcT = dscr("kcT", [128, T], BF16)
    vA = dscr("vA", [T, 260], BF16)
    vB = dscr("vB", [T, 390], BF16)
    vC = dscr("vC", [T, 130], BF16)
    mixT = dscr("mixT", [D, T], BF16)
    x1s = dscr("x1s", [T, D], F32)
    wguS = dscr("wguS", [NJ, 128, 8 * 256], BF16)
    wdnS = dscr("wdnS", [NJ, 128, D], BF16)
    dbg_mod = dscr("dbg_mod", [128, 4096 + 16], F32) if debug else None

    allp = phases is None

    def on(name):
        return allp or name in phases

    with nc.sbuf_tensor("pool", [128, POOLW], F32) as pool, nc.psum_tensor("ps", [128, 4096], F32) as ps:
        S = Sched(nc)

        class Mem:
            def __init__(self, base):
                self.off = base

            def f32(self, n, parts=None):
                v = pool[:, self.off:self.off + n]
                self.off += n
                assert self.off <= POOLW, self.off
                return v

            def bf16(self, n):
                nw = (n + 1) // 2
                v = pool[:, self.off:self.off + nw].bitcast(BF16)[:, 0:n]
                self.off += nw
                assert self.off <= POOLW, self.off
                return v

        def bank(i, n=512):
            return ps[:, i * 512:i * 512 + n]

        def bank_bf(i):
            return ps[:, i * 512:(i + 1) * 512].bitcast(BF16)

        PM = Mem(0)
        ident = PM.f32(128)
        ident_bf = PM.bf16(128)
        ones = PM.f32(128)
        cs_rep = PM.f32(1024).rearrange("p (k m) -> p k m", m=128)
        modcols = PM.f32(16)
        keepR = PM.f32(4096).rearrange("p (a n) -> p a n", n=1024)
        small = PM.f32(64)
        PBASE = PM.off

        S.add("sp", lambda e: e.dma_start(out=ident, in_=cst["c_ident"]), w=["ident"], dma="ident")
        S.add("pool", lambda e: e.dma_start(out=ident_bf, in_=cst["c_ident"]), w=["ident_bf"], dma="ident_bf")
        S.add("dve", lambda e: e.memset(ones, 1.0), w=["ones"])
        cs = small[:, 0:8]
        cs2 = small[:, 8:16]
        S.add("sp", lambda e: e.dma_start(out=cs, in_=ccol), w=["cs"], dma="cs")
        S.add("act", lambda e: e.activation(out=cs2, in_=cs, func=AF.Silu), r=["cs"], w=["cs2"])
        S.add("dve", lambda e: e.tensor_copy(out=cs_rep, in_=cs2.unsqueeze(2).to_broadcast([128, 8, 128])),
              r=["cs2"], w=["cs_rep"])
        S.barrier()

        def phase_M(l):
            M = Mem(PBASE)
            wst = [M.f32(3072) for _ in range(4)]
            bada = M.f32(6144)
            modR = M.f32(6144)
            S.add("sp", lambda e: e.dma_start(out=bada, in_=b_ada[l].partition_broadcast(128)), w=["bada"], dma="bada")
            ld = 0
            for half in range(2):
                for kc in range(8):
                    b = ld % 4
                    ld += 1
                    S.add("sp", lambda e, b=b, kc=kc, half=half: e.dma_start(
                        out=wst[b], in_=w_ada[l, kc * 128:(kc + 1) * 128, half * 3072:(half + 1) * 3072]),
                        w=["wst%d" % b], dma="wst%d" % b)
                    for n in range(6):
                        S.add("pe", lambda e, b=b, kc=kc, n=n: e.matmul(
                            bank(n), lhsT=cs_rep[:, kc, :], rhs=wst[b][:, n * 512:(n + 1) * 512],
                            start=(kc == 0), stop=(kc == 7)),
                            r=["wst%d" % b, "cs_rep"], w=["psM%d" % n])
                for n in range(6):
                    c0 = half * 3072 + n * 512
                    S.add("dve", lambda e, n=n, c0=c0: e.tensor_tensor(
                        out=modR[:, c0:c0 + 512], in0=bank(n), in1=bada[:, c0:c0 + 512], op=ALU.add),
                        r=["psM%d" % n, "bada"], w=["modR%d" % (c0 // 512)])
            allR = ["modR%d" % i for i in range(12)]
            for grp in range(4):
                for t4 in range(4):
                    idx = grp * 4 + t4
                    col0 = (1024 + idx * 128) if idx < 8 else ((idx - 8) * 128)
                    S.add("pe", lambda e, grp=grp, t4=t4, col0=col0: e.transpose(
                        out=bank(6 + grp % 2)[:, t4 * 128:(t4 + 1) * 128], in_=modR[:, col0:col0 + 128], identity=ident),
                        r=allR + ["ident"], w=["psT%d" % (grp % 2)])
                src = bank(6 + grp % 2).rearrange("p (a b) -> p a b", b=128)[:, :, 0]
                addv = 1.0 if grp < 2 else 0.0
                S.add("dve", lambda e, grp=grp, src=src, addv=addv: e.tensor_scalar(
                    out=modcols[:, grp * 4:(grp + 1) * 4], in0=src, scalar1=addv, scalar2=None, op0=ALU.add),
                    r=["psT%d" % (grp % 2)], w=["modcols"])
            S.add("dve", lambda e: e.tensor_scalar(out=keepR[:, 0, :], in0=modR[:, 2048:3072], scalar1=1.0,
                                                    scalar2=1.0 / ALPHA, op0=ALU.add, op1=ALU.mult), r=allR, w=["keep0"])
            S.add("dve", lambda e: e.tensor_scalar(out=keepR[:, 1, :], in0=modR[:, 4096:5120], scalar1=1.0,
                                                    scalar2=None, op0=ALU.add), r=allR, w=["keep1"])
            S.add("dve", lambda e: e.tensor_copy(out=keepR[:, 2, :], in_=modR[:, 3072:4096]), r=allR, w=["keep2"])
            S.add("dve", lambda e: e.tensor_scalar(out=keepR[:, 3, :], in0=modR[:, 5120:6144], scalar1=1.0,
                                                    scalar2=1.0 / ALPHA, op0=ALU.add, op1=ALU.mult), r=allR, w=["keep3"])
            if debug:
                S.add("sp", lambda e: e.dma_start(out=dbg_mod[:, 0:4096], in_=keepR.rearrange("p a n -> p (a n)")),
                      r=["keep0", "keep1", "keep2", "keep3"], dma="dbgm")
                S.add("sp", lambda e: e.dma_start(out=dbg_mod[:, 4096:4112], in_=modcols), r=["modcols"], dma="dbgm2")
            S.barrier()

        def phase_P1(l):
            M = Mem(PBASE)
            wbf = M.bf16(8 * INW).rearrange("p (k n) -> p k n", n=INW)
            xt = [M.f32(4096).rearrange("p (t n) -> p t n", n=1024) for _ in range(2)]
            hT = [M.bf16(4096).rearrange("p (k n) -> p k n", n=512) for _ in range(2)]
            stg = [M.bf16(512) for _ in range(4)]
            vst = [[M.bf16(4 * 65).rearrange("p (h d) -> p h d", d=65),
                    M.bf16(6 * 65).rearrange("p (h d) -> p h d", d=65),
                    M.bf16(2 * 65).rearrange("p (h d) -> p h d", d=65)] for _ in range(2)]
            xsrc = x_in if l == 0 else x1s
            for kc in range(8):
                for hf in range(2):
                    S.add("pool", lambda e, kc=kc, hf=hf: e.dma_start(
                        out=wbf[:, kc, hf * 1280:(hf + 1) * 1280],
                        in_=w_in[l, kc * 128:(kc + 1) * 128, hf * 1280:(hf + 1) * 1280]),
                        w=["wbf%d" % kc], dma="wbf%d_%d" % (kc, hf))
            for b in range(2):
                for gi in range(3):
                    S.add("pool", lambda e, b=b, gi=gi: e.memset(vst[b][gi], 1.0), w=["vst%d_%d" % (b, gi)])
            WB = ["wbf%d" % k for k in range(8)]

            def load_x(c):
                b = c % 2
                S.add("sp", lambda e, b=b, c=c: e.dma_start(
                    out=xt[b], in_=xsrc[c * 512:(c + 1) * 512, :].rearrange("(t p) n -> p t n", p=128)),
                    w=["xt%d" % b], dma="xt%d" % b)

            fm = []
            for i in range(2):
                fm.append((i * 128, qaT, i * 128, 32.0 ** -0.5))
            for i in range(2):
                fm.append((256 + i * 128, kaT, i * 128, 1.0))
            for i in range(3):
                fm.append((768 + i * 128, qbT, i * 128, 0.125))
            for i in range(3):
                fm.append((1152 + i * 128, kbT, i * 128, 1.0))
            for i in range(3):
                fm.append((1920 + i * 128, qcT, i * 128, 0.125))
            fm.append((2304, kcT, 0, 1.0))
            tm = [(512, 256, vA, 4), (1536, 384, vB, 6), (2432, 128, vC, 2)]

            load_x(0)
            ev = 0
            fmn = 0
            tmn = 0
            for c in range(8):
                if c + 1 < 8:
                    load_x(c + 1)
                b = c % 2
                for kc in range(8):
                    pb = kc % 2
                    for t4 in range(4):
                        S.add("pe", lambda e, b=b, kc=kc, t4=t4, pb=pb: e.transpose(
                            out=bank(pb)[:, t4 * 128:(t4 + 1) * 128], in_=xt[b][:, t4, kc * 128:(kc + 1) * 128],
                            identity=ident), r=["xt%d" % b, "ident"], w=["psT%d" % pb])
                    if ev % 2 == 0:
                        S.add("act", lambda e, b=b, kc=kc, pb=pb: e.activation(
                            out=hT[b][:, kc, :], in_=bank(pb), func=AF.Identity,
                            scale=modcols[:, kc:kc + 1], bias=modcols[:, 8 + kc:9 + kc]),
                            r=["psT%d" % pb, "modcols"], w=["hT%d_%d" % (b, kc)])
                    else:
                        S.add("dve", lambda e, b=b, kc=kc, pb=pb: e.tensor_scalar(
                            out=hT[b][:, kc, :], in0=bank(pb), scalar1=modcols[:, kc:kc + 1],
                            scalar2=modcols[:, 8 + kc:9 + kc], op0=ALU.mult, op1=ALU.add),
                            r=["psT%d" % pb, "modcols"], w=["hT%d_%d" % (b, kc)])
                    ev += 1
                HT = ["hT%d_%d" % (b, k) for k in range(8)]
                for (wc, dst, r0, scl) in fm:
                    pb = 2 + fmn % 3
                    sb = fmn % 4
                    fmn += 1
                    for kc in range(8):
                        S.add("pe", lambda e, kc=kc, wc=wc, pb=pb, b=b: e.matmul(
                            bank(pb), lhsT=wbf[:, kc, wc:wc + 128], rhs=hT[b][:, kc, :], start=(kc == 0), stop=(kc == 7)),
                            r=[WB[kc], HT[kc]], w=["psF%d" % pb])
                    if ev % 2 == 0:
                        S.add("act", lambda e, pb=pb, sb=sb, scl=scl: e.activation(
                            out=stg[sb], in_=bank(pb), func=AF.Copy, scale=float(scl)), r=["psF%d" % pb], w=["stg%d" % sb])
                    else:
                        S.add("dve", lambda e, pb=pb, sb=sb, scl=scl: e.tensor_scalar(
                            out=stg[sb], in0=bank(pb), scalar1=float(scl), scalar2=None, op0=ALU.mult),
                            r=["psF%d" % pb], w=["stg%d" % sb])
                    ev += 1
                    S.add("sp", lambda e, sb=sb, dst=dst, r0=r0, c=c: e.dma_start(
                        out=dst[r0:r0 + 128, c * 512:(c + 1) * 512], in_=stg[sb]), r=["stg%d" % sb], dma="stg%d" % sb)
                for t4 in range(4):
                    vb_ = tmn % 2
                    tmn += 1
                    for gi, (wc, ncol, dst, nh) in enumerate(tm):
                        pb = 5 + gi
                        for kc in range(8):
                            S.add("pe", lambda e, kc=kc, wc=wc, ncol=ncol, pb=pb, b=b, t4=t4: e.matmul(
                                bank(pb, ncol), lhsT=hT[b][:, kc, t4 * 128:(t4 + 1) * 128], rhs=wbf[:, kc, wc:wc + ncol],
                                start=(kc == 0), stop=(kc == 7)), r=[WB[kc], HT[kc]], w=["psV%d" % pb])
                        src = bank(pb, ncol).rearrange("p (h d) -> p h d", d=64)
                        if ev % 2 == 0:
                            S.add("act", lambda e, src=src, vb_=vb_, gi=gi: e.copy(out=vst[vb_][gi][:, :, 0:64], in_=src),
                                  r=["psV%d" % pb], w=["vst%d_%d" % (vb_, gi)])
                        else:
                            S.add("dve", lambda e, src=src, vb_=vb_, gi=gi: e.tensor_copy(out=vst[vb_][gi][:, :, 0:64], in_=src),
                                  r=["psV%d" % pb], w=["vst%d_%d" % (vb_, gi)])
                        ev += 1
                        t0 = c * 512 + t4 * 128
                        S.add("sp", lambda e, vb_=vb_, gi=gi, dst=dst, t0=t0: e.dma_start(
                            out=dst[t0:t0 + 128, :], in_=vst[vb_][gi].rearrange("p h d -> p (h d)")),
                            r=["vst%d_%d" % (vb_, gi)], dma="vst%d_%d" % (vb_, gi))
            S.barrier()

        def norm_store(osb_num, rl_in, rl_buf, bc_bank, obf, dst, keys_r, tag, n=512, bc_key=None):
            S.add("act", lambda e: e.activation(out=rl_buf[64:65, 0:n], in_=rl_in, func=AF.Ln), r=keys_r, w=["rl" + tag])
            S.add("act", lambda e: e.activation(out=rl_buf[64:65, 0:n], in_=rl_buf[64:65, 0:n], func=AF.Exp, scale=-1.0),
                  r=["rl" + tag], w=["rl" + tag])
            bck = bc_key if bc_key is not None else "bc" + tag
            S.add("pe", lambda e: e.matmul(bc_bank[0:64, 0:n], lhsT=ones[64:65, 0:64], rhs=rl_buf[64:65, 0:n], start=True, stop=True),
                  r=["rl" + tag, "ones"], w=[bck])
            S.add("dve", lambda e: e.tensor_tensor(out=obf[0:64, 0:n], in0=bc_bank[0:64, 0:n], in1=osb_num, op=ALU.mult),
                  r=keys_r + [bck], w=["obf" + tag])
            S.add("sp", lambda e: e.dma_start(out=dst, in_=obf[0:64, 0:n]), r=["obf" + tag], dma="obf" + tag)

        def phase_A(l):
            lam_init = 0.8 - 0.6 * math.exp(-0.3 * l)
            M = Mem(PBASE)
            qa = [M.bf16(2 * T).rearrange("p (m t) -> p m t", t=T) for _ in range(2)]
            ka = [M.bf16(4 * T).rearrange("p (s m t) -> p s m t", m=2, t=T) for _ in range(2)]
            vah = [M.bf16(32 * 128).rearrange("p (k n) -> p k n", n=128) for _ in range(2)]
            dpt = M.bf16(4 * 2 * 128).rearrange("p (h s i) -> p h s i", s=2, i=128)
            pT = [M.bf16(1024).rearrange("p (m t) -> p m t", t=512) for _ in range(3)]
            osb = M.f32(1024).rearrange("p (m t) -> p m t", t=512)
            rl = M.f32(1024)
            t0 = M.f32(512)
            t1 = M.f32(512)
            dd = M.f32(512)
            sq = M.f32(512)
            tmpv = M.f32(512)
            rstd = M.f32(512)
            negh = M.f32(512)
            obf = M.bf16(512)
            lamt = M.f32(128)
            lw = M.f32(64)
            sm = M.f32(16)
            S.add("sp", lambda e: e.dma_start(out=lamt, in_=lam[l].partition_broadcast(128)), w=["lamt"], dma="lamt")
            S.add("dve", lambda e: e.tensor_tensor(out=lw[:, 0:32], in0=lamt[:, 0:32], in1=lamt[:, 32:64], op=ALU.mult), r=["lamt"], w=["lw0"])
            S.add("dve", lambda e: e.tensor_tensor(out=lw[:, 32:64], in0=lamt[:, 64:96], in1=lamt[:, 96:128], op=ALU.mult), r=["lamt"], w=["lw1"])
            S.add("dve", lambda e: e.tensor_reduce(out=sm[:, 0:2], in_=lw.rearrange("p (a b) -> p a b", b=32), axis=AX.X, op=ALU.add),
                  r=["lw0", "lw1"], w=["sm01"])
            S.add("act", lambda e: e.activation(out=sm[:, 2:4], in_=sm[:, 0:2], func=AF.Exp), r=["sm01"], w=["sm23"])
            S.add("dve", lambda e: e.scalar_tensor_tensor(out=sm[:, 4:5], in0=sm[:, 3:4], scalar=-lam_init, in1=sm[:, 2:3],
                                                          op0=ALU.add, op1=ALU.subtract), r=["sm23"], w=["lamneg"])
            S.add("sp", lambda e: e.dma_start(out=sm[0:64, 8:9], in_=subln[l].rearrange("(p o) -> p o", o=1)), w=["gc0"], dma="gc0")
            S.add("dve", lambda e: e.tensor_scalar(out=sm[0:64, 9:10], in0=sm[0:64, 8:9], scalar1=float(1.0 - lam_init), scalar2=None,
                                                    op0=ALU.mult), r=["gc0"], w=["gcol"])
            lamneg = sm[0:64, 4:5]
            gcol = sm[0:64, 9:10]
            epsc = sm[:, 12:13]
            S.add("pool", lambda e: e.memset(sm[:, 12:13], LN_EPS), w=["epsc"])
            for b2 in range(2):
                for m in range(2):
                    S.add("dve", lambda e, b2=b2, m=m: e.memset(qa[b2][:, m, :], 0.0), w=["qa%d" % b2])
                    for s_ in range(2):
                        S.add("dve", lambda e, b2=b2, m=m, s_=s_: e.memset(ka[b2][:, s_, m, :], 0.0), w=["ka%d" % b2])
                S.add("dve", lambda e, b2=b2: e.memset(vah[b2].rearrange("p k n -> p (k n)"), 0.0), w=["va%d" % b2])
            for h in range(4):
                S.add("pool", lambda e, h=h: e.dma_start(out=dpt[:, h, :, :], in_=cst["c_dp"][h].rearrange("s p i -> p s i")),
                      w=["dpt"], dma="dpt%d" % h)
            def load_head(h):
                hb = h % 2
                for q4 in range(4):
                    S.add("sp", lambda e, q4=q4, hb=hb, h=h: e.dma_start(
                        out=vah[hb][:, q4 * 8:(q4 + 1) * 8, 0:65],
                        in_=vA[q4 * 1024:(q4 + 1) * 1024, h * 65:(h + 1) * 65].rearrange("(k p) n -> p k n", p=128)),
                        w=["va%d" % hb], dma="va%d_%d" % (hb, q4))
                for m in range(2):
                    r0 = h * 64 + m * 32
                    S.add("sp", lambda e, hb=hb, m=m, r0=r0: e.dma_start(out=qa[hb][0:32, m, :], in_=qaT[r0:r0 + 32, :]),
                          w=["qa%d" % hb], dma="qa%d_%d" % (hb, m))
                    S.add("pool", lambda e, hb=hb, m=m, h=h: e.dma_start(
                        out=qa[hb][32:40, m, :].rearrange("p (a b) -> p a b", b=2048),
                        in_=cst["c_qaug"][h].rearrange("p (a b) -> p a b", b=2048)), w=["qa%d" % hb], dma="qg%d_%d" % (hb, m))
                    for s_ in range(2):
                        S.add("sp", lambda e, hb=hb, m=m, r0=r0, s_=s_: e.dma_start(
                            out=ka[hb][0:32, s_, m, :], in_=kaT[r0:r0 + 32, :]), w=["ka%d" % hb], dma="ka%d_%d_%d" % (hb, m, s_))
                        S.add("pool", lambda e, hb=hb, m=m, h=h, s_=s_: e.dma_start(
                            out=ka[hb][32:40, s_, m, :].rearrange("p (a b) -> p a b", b=2048),
                            in_=cst["c_kaug"][h, s_].rearrange("p (a b) -> p a b", b=2048)), w=["ka%d" % hb],
                            dma="kg%d_%d_%d" % (hb, m, s_))

            EPI_AT = [1, 2, 12, 13, 14, 15, 16, 17, 18, 19]
            load_head(0)
            it = 0
            pend_pv = None
            pend_epi = []
            for h in range(4):
                hb = h % 2
                if pend_pv is not None:
                    pend_pv()
                    pend_pv = None
                if h + 1 < 4:
                    load_head(h + 1)
                if h == 0:
                    for j in range(NJ):
                        for gu in range(2):
                            src = w_gu[l][:, gu * FFN + j * 128:gu * FFN + (j + 1) * 128].rearrange("(k p) c -> p k c", p=128)
                            dst = wguS[j].rearrange("p (k n) -> p k n", n=256)[:, :, gu * 128:(gu + 1) * 128]
                            S.add("pool", lambda e, src=src, dst=dst: e.dma_start(out=dst, in_=src), w=["wguS%d" % j],
                                  dma="wguS%d" % ((2 * j + gu) % 4), bg=True)
                mh = float(S_A[h])
                QK = ["qa%d" % hb, "ka%d" % hb]
                for qc in range(8):
                    ab = (h * 8 + qc) % 2
                    acc = ps[:, (4 + 2 * ab) * 512:(6 + 2 * ab) * 512].rearrange("p (m t) -> p m t", t=512)
                    acck = "acc%d" % ab
                    q0 = qc * 512
                    for kb in range(32):
                        sb = it % 2
                        pb = it % 3
                        Sv = ps[:, sb * 1024:(sb + 1) * 1024].rearrange("p (m t) -> p m t", t=512)
                        sk = "S%d" % sb
                        dl = kb - 4 * qc
                        k0 = kb * 128
                        for m in range(2):
                            if dl < 0 or dl > 3:
                                s_ = 0 if dl < 0 else 1
                                S.add("pe", lambda e, m=m, s_=s_, k0=k0, Sv=Sv, hb=hb, q0=q0: e.matmul(
                                    Sv[:, m, :], lhsT=ka[hb][:, s_, m, k0:k0 + 128], rhs=qa[hb][:, m, q0:q0 + 512],
                                    start=True, stop=True), r=QK, w=[sk])
                            else:
                                c0 = 128 * dl
                                if dl > 0:
                                    S.add("pe", lambda e, m=m, k0=k0, Sv=Sv, hb=hb, q0=q0, c0=c0: e.matmul(
                                        Sv[:, m, 0:c0], lhsT=ka[hb][:, 1, m, k0:k0 + 128], rhs=qa[hb][:, m, q0:q0 + c0],
                                        start=True, stop=True), r=QK, w=[sk])
                                S.add("pe", lambda e, m=m, k0=k0, Sv=Sv, hb=hb, q0=q0, c0=c0: e.matmul(
                                    Sv[:, m, c0:512], lhsT=ka[hb][:, 0, m, k0:k0 + 128], rhs=qa[hb][:, m, q0 + c0:q0 + 512],
                                    start=True, stop=False), r=QK, w=[sk])
                                for hl in range(2):
                                    S.add("pe", lambda e, m=m, Sv=Sv, c0=c0, hl=hl, h=h: e.matmul(
                                        Sv[:, m, c0:c0 + 128], lhsT=ident_bf, rhs=dpt[:, h, hl, :],
                                        start=False, stop=(hl == 1)), r=["dpt", "ident_bf"], w=[sk])
                        if pend_pv is not None:
                            pend_pv()
                            pend_pv = None
                        if dl < 0 or dl > 3:
                            bias = -mh * abs(512 * qc - 128 * kb)
                            S.add("act", lambda e, Sv=Sv, pb=pb, bias=bias: e.activation(
                                out=pT[pb].rearrange("p m t -> p (m t)"), in_=Sv.rearrange("p m t -> p (m t)"),
                                func=AF.Exp, bias=float(bias), scale=1.0), r=[sk], w=["pT%d" % pb])
                        else:
                            c0 = 128 * dl
                            if dl > 0:
                                S.add("act", lambda e, Sv=Sv, pb=pb, c0=c0, mh=mh: e.activation(
                                    out=pT[pb][:, :, 0:c0], in_=Sv[:, :, 0:c0], func=AF.Exp, bias=float(-mh * c0), scale=1.0),
                                    r=[sk], w=["pT%d" % pb])
                            S.add("act", lambda e, Sv=Sv, pb=pb, c0=c0, mh=mh: e.activation(
                                out=pT[pb][:, :, c0:512], in_=Sv[:, :, c0:512], func=AF.Exp, bias=float(mh * c0), scale=1.0),
                                r=[sk], w=["pT%d" % pb])

                        def pv(pb=pb, kb=kb, acc=acc, acck=acck, hb=hb):
                            for m in range(2):
                                S.add("pe", lambda e, m=m: e.matmul(
                                    acc[:, m, :], lhsT=vah[hb][:, kb, :], rhs=pT[pb][:, m, :],
                                    start=(kb == 0), stop=(kb == 31)), r=["pT%d" % pb, "va%d" % hb], w=[acck])
                        pend_pv = pv
                        it += 1
                        if pend_epi and kb == EPI_AT[10 - len(pend_epi)]:
                            pend_epi.pop(0)()
                    def mk_epi(acc=acc, acck=acck, h=h, qc=qc):
                        st = []
                        st.append(lambda: S.add("dve", lambda e: e.tensor_copy(out=osb[0:65].rearrange("p m t -> p (m t)"),
                                                                                in_=acc[0:65].rearrange("p m t -> p (m t)")), r=[acck], w=["osb"]))
                        st.append(lambda: S.add("dve", lambda e: e.reciprocal(out=rl[64:65, :], in_=osb[64:65].rearrange("p m t -> p (m t)")),
                                                r=["osb"], w=["rl"]))
                        def bc():
                            for m in range(2):
                                S.add("pe", lambda e, m=m: e.matmul(acc[0:64, m, :], lhsT=ones[64:65, 0:64], rhs=rl[64:65, m * 512:(m + 1) * 512],
                                                                     start=True, stop=True), r=["rl", "ones"], w=[acck])
                        st.append(bc)
                        def mul():
                            S.add("dve", lambda e: e.tensor_tensor(out=t0[0:64], in0=acc[0:64, 0, :], in1=osb[0:64, 0, :], op=ALU.mult),
                                  r=[acck, "osb"], w=["t0"])
                            S.add("dve", lambda e: e.tensor_tensor(out=t1[0:64], in0=acc[0:64, 1, :], in1=osb[0:64, 1, :], op=ALU.mult),
                                  r=[acck, "osb"], w=["t1"])
                            S.add("dve", lambda e: e.scalar_tensor_tensor(out=dd[0:64], in0=t1[0:64], scalar=lamneg, in1=t0[0:64],
                                                                          op0=ALU.mult, op1=ALU.add), r=["t0", "t1", "lamneg"], w=["dd"])
                        st.append(mul)
                        st.append(lambda: S.add("dve", lambda e: e.tensor_tensor(out=sq[0:64], in0=dd[0:64], in1=dd[0:64], op=ALU.mult), r=["dd"], w=["sq"]))
                        st.append(lambda: S.add("pe", lambda e: e.matmul(acc[0:64, 0, :], lhsT=ones[0:64, 0:64], rhs=sq[0:64],
                                                                          start=True, stop=True), r=["sq", "ones"], w=[acck]))
                        st.append(lambda: S.add("act", lambda e: e.activation(out=tmpv[0:64], in_=acc[0:64, 0, :], func=AF.Ln, scale=1.0 / 64.0,
                                                                              bias=epsc[0:64, 0:1]), r=[acck, "epsc"], w=["tmpv"]))
                        st.append(lambda: S.add("act", lambda e: e.activation(out=rstd[0:64], in_=tmpv[0:64], func=AF.Exp, scale=-0.5),
                                                r=["tmpv"], w=["rstd"]))
                        st.append(lambda: S.add("dve", lambda e: e.scalar_tensor_tensor(out=obf[0:64], in0=dd[0:64], scalar=gcol, in1=rstd[0:64],
                                                                                        op0=ALU.mult, op1=ALU.mult),
                                                r=["dd", "rstd", "gcol"], w=["obfA"]))
                        st.append(lambda: S.add("sp", lambda e: e.dma_start(out=mixT[h * 64:(h + 1) * 64, qc * 512:(qc + 1) * 512], in_=obf[0:64]),
                                                r=["obfA"], dma="obfA"))
                        return st
                    while pend_epi:
                        pend_epi.pop(0)()
                    pend_epi = mk_epi()
            if pend_pv is not None:
                pend_pv()
            while pend_epi:
                pend_epi.pop(0)()
            S.barrier()

        def phase_B(l):
            M = Mem(PBASE)
            PAD = 64
            vb = M.bf16(3 * 33 * 390).rearrange("p (a t n) -> p a t n", a=3, t=33)
            qb = M.bf16(T)
            kb_ = M.bf16(T + 2 * PAD)
            qp = M.bf16(T)
            kp = M.bf16(T + 2 * PAD)
            bias = M.f32(7 * 512).rearrange("p (v n) -> p v n", n=512)
            accT = M.f32(T)
            Ssb = [M.f32(512) for _ in range(3)]
            pT = [M.bf16(512) for _ in range(3)]
            rl = [M.f32(512) for _ in range(2)]
            obf = [M.bf16(512) for _ in range(2)]
            S.add("sp", lambda e: e.dma_start(out=bias.rearrange("p v n -> p (v n)"), in_=cst["c_bbias"].rearrange("p v n -> p (v n)")),
                  w=["biasB"], dma="biasB")
            VBK = ["vb_%d" % i for i in range(84)]
            for a3 in range(3):
                for t3 in range(3):
                    S.add("dve", lambda e, a3=a3, t3=t3: e.memset(vb[:, a3, t3 * 11:(t3 + 1) * 11, :], 0.0), w=VBK)
            for (buf, key) in ((qb, "qb"), (kb_, "kb"), (qp, "qp"), (kp, "kp")):
                S.add("dve", lambda e, buf=buf: e.memset(buf, 0.0), w=[key])
            nd = 0
            for pi, (win, d) in enumerate(B_PATTERNS):
                Lc = T // d
                nt = Lc // 128
                for r in range(d):
                    srcB = bass.AP(vB.tensor, r * 390, [[d * 390, 64], [128 * d * 390, nt], [1, 390]])
                    tA = r * nt + 1
                    tB = r * nt
                    ntA = nt
                    if (64 + 128 * (nt - 1) + 63) * d + r >= T:
                        ntA = nt - 1
                    if ntA > 0:
                        srcA = bass.AP(vB.tensor, (64 * d + r) * 390, [[d * 390, 64], [128 * d * 390, ntA], [1, 390]])
                        S.add("sp", lambda e, srcA=srcA, pi=pi, tA=tA, ntA=ntA: e.dma_start(out=vb[0:64, pi, tA:tA + ntA, :], in_=srcA),
                              w=[VBK[nd]], dma="vb%d" % (nd % 4))
                        nd += 1
                    S.add("sp", lambda e, srcB=srcB, pi=pi, tB=tB, nt=nt: e.dma_start(out=vb[64:128, pi, tB:tB + nt, :], in_=srcB),
                          w=[VBK[nd]], dma="vb%d" % (nd % 4))
                    nd += 1
            blocks = [(0, 0), (0, 1), (1, 1), (1, 2)]
            def load_qk(h):
                S.add("sp", lambda e, h=h: e.dma_start(out=qb[0:64, :], in_=qbT[h * 64:(h + 1) * 64, :]), w=["qb"], dma="qb")
                S.add("sp", lambda e, h=h: e.dma_start(out=kb_[0:64, PAD:PAD + T], in_=kbT[h * 64:(h + 1) * 64, :]), w=["kb"], dma="kb")

            load_qk(0)
            for h in range(6):
                mh = float(S_B[h])
                for pi, (win, d) in enumerate(B_PATTERNS):
                    Lc = T // d
                    ntc = Lc // 128
                    ng = ntc // 2
                    if pi == 0:
                        qs_, ks_, qk_, kk_ = qb, kb_, "qb", "kb"
                    else:
                        qs_, ks_, qk_, kk_ = qp, kp, "qp", "kp"
                        S.add("dve", lambda e, d=d: e.tensor_copy(out=qp[0:64, :].rearrange("p (r j) -> p r j", r=d),
                                                                  in_=qb[0:64, :].rearrange("p (j r) -> p r j", r=d)), r=["qb"], w=["qp"])
                        S.add("act", lambda e, d=d: e.copy(out=kp[0:64, PAD:PAD + T].rearrange("p (r j) -> p r j", r=d),
                                                           in_=kb_[0:64, PAD:PAD + T].rearrange("p (j r) -> p r j", r=d)), r=["kb"], w=["kp"])
                        if pi == 2 and h + 1 < 6:
                            load_qk(h + 1)
                    groups = []
                    for r in range(d):
                        for gq in range(ng):
                            b0 = 2 * gq
                            tq = r * ntc + b0
                            if pi == 2:
                                var = 6
                            else:
                                var = 3 * pi + (1 if gq == 0 else (2 if gq == ng - 1 else 0))
                            groups.append((tq, var, 128 * b0 * d + r))

                    def emit_S(gi, groups=groups, qs_=qs_, ks_=ks_, qk_=qk_, kk_=kk_):
                        tq, var, s0 = groups[gi]
                        sbk = gi % 3
                        Sb = bank(sbk)
                        for bi, (qi, ci) in enumerate(blocks):
                            qs = 128 * (tq + qi)
                            ks = PAD + 128 * (tq + ci) - 64
                            S.add("pe", lambda e, bi=bi, qs=qs, ks=ks: e.matmul(
                                Sb[:, bi * 128:(bi + 1) * 128], lhsT=ks_[:, ks:ks + 128], rhs=qs_[:, qs:qs + 128],
                                start=True, stop=True), r=[qk_, kk_], w=["SB%d" % sbk])

                    def emit_rest(gi, groups=groups, h=h, pi=pi, d=d, mh=mh):
                        tq, var, s0 = groups[gi]
                        sbk = gi % 3
                        obk = gi % 2
                        Sb = bank(sbk)
                        ob = bank(3 + obk)
                        S.add("dve", lambda e: e.scalar_tensor_tensor(out=Ssb[sbk], in0=bias[:, var, :], scalar=mh, in1=Sb,
                                                                      op0=ALU.mult, op1=ALU.add),
                              r=["SB%d" % sbk, "biasB"], w=["Ssb%d" % sbk])
                        S.add("act", lambda e: e.activation(out=pT[sbk], in_=Ssb[sbk], func=AF.Exp), r=["Ssb%d" % sbk], w=["pTB%d" % sbk])
                        for bi, (qi, ci) in enumerate(blocks):
                            S.add("pe", lambda e, bi=bi, qi=qi, ci=ci: e.matmul(
                                ob[0:65, qi * 128:(qi + 1) * 128], lhsT=vb[:, pi, tq + ci, h * 65:(h + 1) * 65],
                                rhs=pT[sbk][:, bi * 128:(bi + 1) * 128], start=(bi % 2 == 0), stop=(bi % 2 == 1)),
                                r=["pTB%d" % sbk] + VBK, w=["oB%d" % obk])

                    def emit_tail(gi, groups=groups, pi=pi, d=d):
                        tq, var, s0 = groups[gi]
                        obk = gi % 2
                        ob = bank(3 + obk)
                        dst = accT[0:65, s0:s0 + 255 * d + 1:d]
                        if pi == 0:
                            S.add("act", lambda e: e.copy(out=dst, in_=ob[0:65, 0:256]), r=["oB%d" % obk], w=["accT"])
                        else:
                            S.add("dve", lambda e: e.tensor_tensor(out=dst, in0=ob[0:65, 0:256], in1=dst, op=ALU.add),
                                  r=["oB%d" % obk, "accT"], w=["accT"])

                    emit_S(0)
                    emit_S(1)
                    for gi in range(len(groups)):
                        if gi + 2 < len(groups):
                            emit_S(gi + 2)
                        emit_rest(gi)
                        if gi >= 1:
                            emit_tail(gi - 1)
                    emit_tail(len(groups) - 1)
                for c in range(8):
                    norm_store(accT[0:64, c * 512:(c + 1) * 512], accT[64:65, c * 512:(c + 1) * 512], rl[c % 2], bank(5 + c % 2), obf[c % 2],
                               mixT[256 + h * 64:256 + (h + 1) * 64, c * 512:(c + 1) * 512], ["accT"], "B%d" % (c % 2))
            S.barrier()

        def phase_C(l):
            M = Mem(PBASE)
            PADC = 128
            G = []
            for g in range(2):
                d_ = dict(
                    kc=M.bf16(T + 2 * PADC), qc=M.bf16(3 * T).rearrange("p (r t) -> p r t", t=T),
                    vc=M.bf16(32 * 65).rearrange("p (k n) -> p k n", n=65), cb=M.f32(3 * 384).rearrange("p (k n) -> p k n", n=384),
                    Ssb=M.f32(3 * 384).rearrange("p (k n) -> p k n", n=384), pT=M.bf16(3 * 384).rearrange("p (k n) -> p k n", n=384),
                    osbc=[M.f32(3 * 512).rearrange("p (r t) -> p r t", t=512) for _ in range(2)],
                    rlin=M.f32(512), rl=M.f32(512), obf=M.bf16(512))
                G.append(d_)
            es = M.f32(16)
            S.add("sp", lambda e: e.dma_start(out=es[64:65, 0:6], in_=sink[l:l + 1, :]), w=["es0"], dma="es0")
            S.add("act", lambda e: e.activation(out=es[64:65, 8:14], in_=es[64:65, 0:6], func=AF.Exp), r=["es0"], w=["es"])
            for g in range(2):
                d_ = G[g]
                S.add("dve", lambda e, d_=d_: e.memset(d_["kc"], 0.0), w=["kc%d" % g])
                S.add("dve", lambda e, d_=d_: e.memset(d_["qc"][64:128].rearrange("p r t -> p (r t)"), 0.0), w=["qcz%d" % g])
                S.add("sp", lambda e, g=g, d_=d_: e.dma_start(out=d_["kc"][0:64, PADC:PADC + T], in_=kcT[g * 64:(g + 1) * 64, :]),
                      w=["kc%d" % g], dma="kc%d" % g)
                for rep in range(3):
                    hq = 3 * g + rep
                    S.add("sp", lambda e, rep=rep, hq=hq, d_=d_: e.dma_start(out=d_["qc"][0:64, rep, :], in_=qcT[hq * 64:(hq + 1) * 64, :]),
                          w=["qc%d_%d" % (g, rep)], dma="qc%d_%d" % (g, rep))
                for q4 in range(4):
                    S.add("sp", lambda e, g=g, q4=q4, d_=d_: e.dma_start(
                        out=d_["vc"][:, q4 * 8:(q4 + 1) * 8, :],
                        in_=vC[q4 * 1024:(q4 + 1) * 1024, g * 65:(g + 1) * 65].rearrange("(k p) n -> p k n", p=128)),
                        w=["vc%d_%d" % (g, q4)], dma="vc%d_%d" % (g, q4))
                S.add("sp", lambda e, g=g, d_=d_: e.dma_start(out=d_["cb"].rearrange("p k n -> p (k n)"),
                                                              in_=cst["c_cbias"][g].rearrange("p k n -> p (k n)")), w=["cb%d" % g], dma="cb%d" % g)

            def kts_of(qb_):
                return [kt for kt in range(3) if 0 <= qb_ + kt - 1 <= 31]

            def c_S(g, qb_):
                d_ = G[g]
                for kt in kts_of(qb_):
                    kbk = qb_ + kt - 1
                    Sb = bank(3 * g + kt, 384)
                    S.add("pe", lambda e, Sb=Sb, kbk=kbk: e.matmul(
                        Sb, lhsT=d_["kc"][:, PADC + kbk * 128:PADC + (kbk + 1) * 128], rhs=d_["qc"][:, :, qb_ * 128:(qb_ + 1) * 128],
                        start=True, stop=True), r=["kc%d" % g, "qcz%d" % g] + ["qc%d_%d" % (g, r_) for r_ in range(3)], w=["SC%d_%d" % (g, kt)])

            def c_exp(g, qb_):
                d_ = G[g]
                kts = kts_of(qb_)
                for kt in kts:
                    Sb = bank(3 * g + kt, 384)
                    S.add("dve", lambda e, Sb=Sb, kt=kt: e.tensor_tensor(out=d_["Ssb"][:, kt, :], in0=Sb, in1=d_["cb"][:, kt, :], op=ALU.add),
                          r=["SC%d_%d" % (g, kt), "cb%d" % g], w=["SsbC%d" % g])
                lo, hi = kts[0], kts[-1] + 1
                S.add("act", lambda e: e.activation(out=d_["pT"][:, lo:hi, :], in_=d_["Ssb"][:, lo:hi, :], func=AF.Exp),
                      r=["SsbC%d" % g], w=["pTC%d" % g])

            def c_pv(g, qb_):
                d_ = G[g]
                kts = kts_of(qb_)
                lo, hi = kts[0], kts[-1] + 1
                for kt in kts:
                    kbk = qb_ + kt - 1
                    S.add("pe", lambda e, kt=kt, kbk=kbk: e.matmul(
                        bank(6 + g, 384)[0:65, :], lhsT=d_["vc"][:, kbk, :], rhs=d_["pT"][:, kt, :], start=(kt == lo), stop=(kt == hi - 1)),
                        r=["pTC%d" % g] + ["vc%d_%d" % (g, q4) for q4 in range(4)], w=["oC%d" % g])

            def c_tail(g, qb_):
                d_ = G[g]
                ch = qb_ // 4
                obk = ch % 2
                S.add("act", lambda e: e.copy(
                    out=d_["osbc"][obk][0:65, :, (qb_ % 4) * 128:(qb_ % 4 + 1) * 128],
                    in_=bank(6 + g, 384)[0:65, :].rearrange("p (r t) -> p r t", t=128)), r=["oC%d" % g], w=["osbc%d_%d" % (g, obk)])
                if qb_ % 4 == 3:
                    for rep in range(3):
                        hq = 3 * g + rep
                        S.add("dve", lambda e, rep=rep, hq=hq: e.tensor_scalar(
                            out=d_["rlin"][64:65, :], in0=d_["osbc"][obk][64:65, rep, :], scalar1=es[64:65, 8 + hq:9 + hq], scalar2=None, op0=ALU.add),
                            r=["osbc%d_%d" % (g, obk), "es"], w=["rlin%d" % g])
                        norm_store(d_["osbc"][obk][0:64, rep, :], d_["rlin"][64:65, :], d_["rl"], bank(6 + g), d_["obf"],
                                   mixT[640 + hq * 64:640 + (hq + 1) * 64, ch * 512:(ch + 1) * 512],
                                   ["rlin%d" % g, "osbc%d_%d" % (g, obk)], "C%d" % g, bc_key="oC%d" % g)

            c_S(0, 0)
            c_S(1, 0)
            for qb_ in range(32):
                for g in range(2):
                    c_exp(g, qb_)
                    if qb_ + 1 < 32:
                        c_S(g, qb_ + 1)
                for g in range(2):
                    c_pv(g, qb_)
                for g in range(2):
                    c_tail(g, qb_)
            S.barrier()

        def phase_P3(l):
            EPS1 = LN_EPS / (ALPHA * ALPHA)
            M = Mem(PBASE)
            wob = M.bf16(8 * 1024).rearrange("p (k n) -> p k n", n=1024)
            lnt = M.f32(6 * 1024).rearrange("p (a n) -> p a n", n=1024)
            mxT = M.bf16(8 * 512).rearrange("p (k t) -> p k t", t=512)
            xt = [M.f32(4096).rearrange("p (t n) -> p t n", n=1024) for _ in range(2)]
            zn = M.f32(1024)
            h2 = M.bf16(4 * 1024).rearrange("p (t n) -> p t n", n=1024)
            h2T = M.bf16(8 * 512).rearrange("p (k t) -> p k t", t=512)
            actT = M.bf16(NJ * 512).rearrange("p (j t) -> p j t", t=512)
            sg = [M.f32(512) for _ in range(2)]
            wgu = [M.bf16(8 * 256).rearrange("p (k n) -> p k n", n=256) for _ in range(3)]
            wdn = [M.bf16(1024) for _ in range(3)]
            wst = [M.f32(1024) for _ in range(2)]
            stt = M.f32(8 * 24).rearrange("p (s n) -> p s n", n=24)
            negh1 = M.f32(2)
            xdst = x1s if l == 0 else y_out
            S.add("pool", lambda e: e.memset(negh1, -0.5), w=["negh1"])
            for (a, src) in ((0, ln_g[l, 0]), (1, ln_b[l, 0]), (4, ln_g[l, 1]), (5, ln_b[l, 1])):
                S.add("sp", lambda e, a=a, src=src: e.dma_start(out=lnt[:, a, :], in_=src.partition_broadcast(128)), w=["lnt%d" % a], dma="lnt%d" % a)
            S.add("dve", lambda e: e.tensor_tensor(out=lnt[:, 2, :], in0=lnt[:, 0, :], in1=keepR[:, 1, :], op=ALU.mult), r=["lnt0"], w=["lnt2"])
            S.add("dve", lambda e: e.tensor_tensor(out=lnt[:, 3, :], in0=lnt[:, 1, :], in1=keepR[:, 1, :], op=ALU.mult), r=["lnt1"], w=["lnt3"])
            S.add("dve", lambda e: e.tensor_tensor(out=lnt[:, 3, :], in0=lnt[:, 3, :], in1=keepR[:, 2, :], op=ALU.add), r=["lnt3"], w=["lnt3"])
            for kc in range(8):
                b = kc % 2
                S.add("sp", lambda e, kc=kc, b=b: e.dma_start(out=wst[b], in_=w_out[l, kc * 128:(kc + 1) * 128, :]), w=["wst%d" % b], dma="wst%d" % b)
                S.add("dve", lambda e, kc=kc, b=b: e.tensor_tensor(out=wob[:, kc, :], in0=wst[b], in1=keepR[:, 0, :], op=ALU.mult),
                      r=["wst%d" % b], w=["wob%d" % kc])
            for j in range(NJ):
                b = j % 2
                wb_ = j % 3
                S.add("sp", lambda e, j=j, b=b: e.dma_start(out=wst[b], in_=w_down[l, j * 128:(j + 1) * 128, :]), w=["wst%d" % b], dma="wst%d" % b)
                S.add("dve", lambda e, b=b, wb_=wb_: e.tensor_tensor(out=wdn[wb_], in0=wst[b], in1=keepR[:, 3, :], op=ALU.mult),
                      r=["wst%d" % b], w=["wdn%d" % wb_])
                S.add("sp", lambda e, j=j, wb_=wb_: e.dma_start(out=wdnS[j], in_=wdn[wb_]), r=["wdn%d" % wb_], w=["wdnS%d" % j], dma="wdn%d" % wb_)
            WOB = ["wob%d" % k for k in range(8)]

            def load_mx(c):
                S.add("sp", lambda e, c=c: e.dma_start(out=mxT, in_=mixT[:, c * 512:(c + 1) * 512].rearrange("(k p) t -> p k t", p=128)),
                      w=["mxT"], dma="mxT")

            def load_x(c):
                cb_ = c % 2
                xsrc = x_in if l == 0 else x1s
                S.add("sp", lambda e, c=c, cb_=cb_: e.dma_start(
                    out=xt[cb_], in_=xsrc[c * 512:(c + 1) * 512, :].rearrange("(t p) n -> p t n", p=128)), w=["xt%d_%d" % (cb_, t_) for t_ in range(4)],
                    dma="xt%d" % cb_)

            cnt = {"tm": 0, "ev": 0, "zn": 0}

            def wout(c):
                cb_ = c % 2
                for tb in range(4):
                    for n in range(2):
                        pb = cnt["tm"] % 2
                        cnt["tm"] += 1
                        for kc in range(8):
                            S.add("pe", lambda e, tb=tb, n=n, kc=kc, pb=pb: e.matmul(
                                bank(pb), lhsT=mxT[:, kc, tb * 128:(tb + 1) * 128], rhs=wob[:, kc, n * 512:(n + 1) * 512],
                                start=(kc == 0), stop=(kc == 7)), r=["mxT", WOB[kc]], w=["psO%d" % pb])
                        S.add("dve", lambda e, tb=tb, n=n, pb=pb, cb_=cb_: e.tensor_tensor(
                            out=xt[cb_][:, tb, n * 512:(n + 1) * 512], in0=bank(pb), in1=xt[cb_][:, tb, n * 512:(n + 1) * 512], op=ALU.add),
                            r=["psO%d" % pb, "xt%d_%d" % (cb_, tb)], w=["xt%d_%d" % (cb_, tb)])

            def layer_norm(c, tb, which):
                cb_ = c % 2
                y = xt[cb_][:, tb, :]
                yk = "xt%d_%d" % (cb_, tb)
                st = stt[:, tb + 4 * (which - 1), :]
                sk = "st%d" % (tb + 4 * (which - 1))
                S.add("dve", lambda e: e.bn_stats(out=st[:, 0:6], in_=y[:, 0:512]), r=[yk], w=[sk + "a"])
                S.add("dve", lambda e: e.bn_stats(out=st[:, 6:12], in_=y[:, 512:1024]), r=[yk], w=[sk + "b"])
                S.add("dve", lambda e: e.bn_aggr(out=st[:, 12:14], in_=st[:, 0:12]), r=[sk + "a", sk + "b"], w=[sk + "mv"])
                S.add("dve", lambda e: e.tensor_scalar(out=st[:, 14:15], in0=st[:, 13:14], scalar1=float(EPS1), scalar2=None, op0=ALU.add),
                      r=[sk + "mv"], w=[sk + "ve"])
                S.add("pool", lambda e: e.tensor_tensor(out=st[:, 15:16], in0=st[:, 14:15], in1=negh1[:, 0:1], op=ALU.pow),
                      r=[sk + "ve", "negh1"], w=[sk + "rs"])
                S.add("dve", lambda e: e.tensor_scalar(out=st[:, 16:17], in0=st[:, 12:13], scalar1=st[:, 15:16], scalar2=-1.0,
                                                        op0=ALU.mult, op1=ALU.mult), r=[sk + "mv", sk + "rs"], w=[sk + "nb"])
                zi = cnt["zn"] % 2
                cnt["zn"] += 1
                znb = zn if zi == 0 else wst[1]
                znk = "zn" if zi == 0 else "wst1"
                S.add("act", lambda e: e.activation(out=znb, in_=y, func=AF.Identity, scale=st[:, 15:16], bias=st[:, 16:17]),
                      r=[yk, sk + "rs", sk + "nb"], w=[znk])
                ga, ba = (0, 1) if which == 1 else (4, 5)
                S.add("pool", lambda e: e.tensor_tensor(out=y, in0=znb, in1=lnt[:, ga, :], op=ALU.mult), r=[znk, "lnt%d" % ga], w=[yk])
                S.add("pool", lambda e: e.tensor_tensor(out=y, in0=y, in1=lnt[:, ba, :], op=ALU.add), r=[yk, "lnt%d" % ba], w=[yk])
                if which == 1:
                    S.add("dve", lambda e: e.tensor_tensor(out=wst[0], in0=znb, in1=lnt[:, 2, :], op=ALU.mult), r=[znk, "lnt2"], w=["wst0"])
                    S.add("dve", lambda e: e.tensor_tensor(out=h2[:, tb, :], in0=wst[0], in1=lnt[:, 3, :], op=ALU.add),
                          r=["wst0", "lnt3"], w=["h2_%d" % tb])

            def transposes(c):
                for kc in range(8):
                    pb = 2 + kc % 2
                    for tb in range(4):
                        S.add("pe", lambda e, kc=kc, tb=tb, pb=pb: e.transpose(
                            out=bank_bf(pb)[:, tb * 128:(tb + 1) * 128], in_=h2[:, tb, kc * 128:(kc + 1) * 128], identity=ident_bf),
                            r=["h2_%d" % tb, "ident_bf"], w=["psT%d" % pb])
                    if cnt["ev"] % 2 == 0:
                        S.add("act", lambda e, kc=kc, pb=pb: e.copy(out=h2T[:, kc, :], in_=bank_bf(pb)[:, 0:512]), r=["psT%d" % pb], w=["h2T%d" % kc])
                    else:
                        S.add("dve", lambda e, kc=kc, pb=pb: e.tensor_copy(out=h2T[:, kc, :], in_=bank_bf(pb)[:, 0:512]), r=["psT%d" % pb], w=["h2T%d" % kc])
                    cnt["ev"] += 1

            def load_wgu(p):
                if p >= 8 * NJ:
                    return
                j = p % NJ
                wb_ = p % 3
                S.add("sp", lambda e, j=j, wb_=wb_: e.dma_start(out=wgu[wb_].rearrange("p k n -> p (k n)"), in_=wguS[j]),
                      r=["wguS%d" % j], w=["wgu%d" % wb_], dma="wgu%d" % wb_)

            def gu(c, hooks):
                H2T = ["h2T%d" % k for k in range(8)]
                for j in range(NJ):
                    load_wgu(c * NJ + j + 2)
                    wb_ = (c * NJ + j) % 3
                    pg = 4 + 2 * (j % 2)
                    for half in range(2):
                        for kc in range(8):
                            S.add("pe", lambda e, kc=kc, half=half, pg=pg, wb_=wb_: e.matmul(
                                bank(pg + half), lhsT=wgu[wb_][:, kc, half * 128:(half + 1) * 128], rhs=h2T[:, kc, :],
                                start=(kc == 0), stop=(kc == 7)), r=["wgu%d" % wb_, H2T[kc]], w=["psG%d" % (pg + half)])
                    sb_ = j % 2
                    S.add("act", lambda e, pg=pg, sb_=sb_: e.activation(out=sg[sb_], in_=bank(pg), func=AF.Silu), r=["psG%d" % pg], w=["sg%d" % sb_])
                    S.add("dve", lambda e, pg=pg, sb_=sb_, j=j: e.tensor_tensor(out=actT[:, j, :], in0=bank(pg + 1), in1=sg[sb_], op=ALU.mult),
                          r=["psG%d" % (pg + 1), "sg%d" % sb_], w=["actT%d" % j])
                    for f in hooks.get(j, ()):
                        f()

            def load_wdn(q):
                if q >= 16 * NJ:
                    return
                i = q % (2 * NJ)
                j, n = i % NJ, i // NJ
                wb_ = q % 3
                S.add("sp", lambda e, j=j, n=n, wb_=wb_: e.dma_start(out=wdn[wb_][:, 0:512], in_=wdnS[j][:, n * 512:(n + 1) * 512]),
                      r=["wdnS%d" % j], w=["wdn%d" % wb_], dma="wdn%d" % wb_)

            def down(c, mid):
                cb_ = c % 2
                for i in range(2 * NJ):
                    load_wdn(c * 2 * NJ + i + 2)
                    j, n = i % NJ, i // NJ
                    wb_ = (c * 2 * NJ + i) % 3
                    for tb in range(4):
                        S.add("pe", lambda e, j=j, tb=tb, wb_=wb_: e.matmul(
                            bank(4 + tb), lhsT=actT[:, j, tb * 128:(tb + 1) * 128], rhs=wdn[wb_][:, 0:512],
                            start=(j == 0), stop=(j == NJ - 1)), r=["actT%d" % j, "wdn%d" % wb_], w=["psG%d" % (4 + tb)])
                    if j == NJ - 1:
                        for tb in range(4):
                            S.add("dve", lambda e, tb=tb, n=n, cb_=cb_: e.tensor_tensor(
                                out=xt[cb_][:, tb, n * 512:(n + 1) * 512], in0=bank(4 + tb), in1=xt[cb_][:, tb, n * 512:(n + 1) * 512], op=ALU.add),
                                r=["psG%d" % (4 + tb), "xt%d_%d" % (cb_, tb)], w=["xt%d_%d" % (cb_, tb)])
                        if n == 0:
                            for f in mid:
                                f()

            def ln2_tile(c, tb):
                cb_ = c % 2
                if True:
                    layer_norm(c, tb, 2)
                    t0_ = c * 512 + tb * 128
                    S.add("pool", lambda e, tb=tb, t0_=t0_, cb_=cb_: e.dma_start(out=xdst[t0_:t0_ + 128, :], in_=xt[cb_][:, tb, :]),
                          r=["xt%d_%d" % (cb_, tb)], dma="xo%d_%d" % (cb_, tb))

            load_mx(0)
            load_x(0)
            load_wgu(0)
            load_wgu(1)
            load_wdn(0)
            load_wdn(1)
            wout(0)
            for tb in range(4):
                layer_norm(0, tb, 1)
            transposes(0)
            for c in range(8):
                hooks = {}
                if c + 1 < 8:
                    load_mx(c + 1)
                if c >= 1:
                    for tb in range(4):
                        hooks.setdefault(3 * tb, []).append(lambda c=c, tb=tb: ln2_tile(c - 1, tb))
                if c + 1 < 8:
                    hooks.setdefault(10, []).append(lambda c=c: load_x(c + 1))
                gu(c, hooks)
                mid = []
                if c + 1 < 8:
                    wout(c + 1)
                    layer_norm(c + 1, 0, 1)
                    layer_norm(c + 1, 1, 1)
                    mid = [lambda c=c: layer_norm(c + 1, 2, 1), lambda c=c: layer_norm(c + 1, 3, 1)]
                down(c, mid)
                if c + 1 < 8:
                    transposes(c + 1)
            for tb in range(4):
                ln2_tile(7, tb)
            S.barrier()

        for l in layers:
            if on("M"):
                phase_M(l)
            if on("P1"):
                phase_P1(l)
            if on("A"):
                phase_A(l)
            if on("B"):
                phase_B(l)
            if on("C"):
                phase_C(l)
            if on("P3"):
                phase_P3(l)

        n_ops = S.emit()
    return nc, n_ops


def kernel(**inputs):
    nc, _ = build_program(debug=False)
    cores = list(range(NCORES))
    maps = make_in_maps(inputs, cores)
    res = run_bass_kernel_spmd(nc, maps, core_ids=cores)
    return np.stack([np.asarray(r["y"], dtype=np.float32) for r in res.results], axis=0)


def make_in_maps(inputs, cores):
    f = lambda a: np.ascontiguousarray(np.asarray(a, dtype=np.float32))
    consts = make_consts()
    shared = {
        "w_ada": f(inputs["w_ada"]), "b_ada": f(inputs["b_ada"]), "w_in": f(inputs["w_in"]),
        "lam": f(inputs["lam"]).reshape(DEPTH, 128), "subln_g": f(inputs["subln_g"]), "sink": f(inputs["sink"]),
        "w_out": f(inputs["w_out"]), "ln_g": f(inputs["ln_g"]), "ln_b": f(inputs["ln_b"]),
        "w_gu": f(inputs["w_gu"]), "w_down": f(inputs["w_down"]),
    }
    shared.update(consts)
    maps = []
    x = np.asarray(inputs["x"], dtype=np.float32)
    c = np.asarray(inputs["c"], dtype=np.float32)
    for b in cores:
        m = dict(shared)
        m["x"] = np.ascontiguousarray(x[b])
        m["ccol"] = np.ascontiguousarray(c[b].reshape(8, 128).T)
        maps.append(m)
    return maps
```

```python
import contextlib
import math
import numpy as np
import ml_dtypes
import concourse.bass as bass
import concourse.mybir as mybir
from concourse.bass_utils import run_bass_kernel_spmd

F32 = mybir.dt.float32
BF16 = mybir.dt.bfloat16
AF = mybir.ActivationFunctionType
ALU = mybir.AluOpType
AX = mybir.AxisListType

T = 4096
D = 1024
DEPTH = 2
NCORES = 8
FFN = 2816
NJ = FFN // 128
INW = 2560
LN_EPS = 1e-5
ALPHA = (2 * DEPTH) ** 0.25
NEG = -1.0e30
POOLW = 50000

SL = (2.0 ** (-8.0 * np.arange(1, 17) / 16)).astype(np.float32)
S_C = SL[0:6]
S_A = SL[6:10]
S_B = SL[10:16]
B_PATTERNS = ((128, 1), (512, 4), (2048, 16))


class Sched:
    ENGS = ("pe", "act", "dve", "pool", "sp")

    def __init__(self, nc, n_dma_sems=48):
        self.nc = nc
        self.ops = []
        self.last_w = {}
        self.readers = {}
        self.dma_last = {}
        self.dma_slot = {}
        self.n_dma_sems = n_dma_sems
        self.barrier_deps = []
        self.n_bg = 4
        self.bg_slot = {}
        self.persist = set()

    def add(self, eng, fn, r=(), w=(), dma=None, bg=False):
        idx = len(self.ops)
        raw = set()
        other = set()
        for k in r:
            if k in self.last_w:
                raw.add(self.last_w[k])
        for k in w:
            if k in self.last_w:
                other.add(self.last_w[k])
            other.update(self.readers.get(k, ()))
        slot = None
        if dma is not None:
            if bg:
                if dma not in self.bg_slot:
                    self.bg_slot[dma] = self.n_dma_sems + len(self.bg_slot) % self.n_bg
                slot = self.bg_slot[dma]
                self.persist.update(w)
            else:
                if dma not in self.dma_slot:
                    self.dma_slot[dma] = len(self.dma_slot) % self.n_dma_sems
                slot = self.dma_slot[dma]
            if slot in self.dma_last:
                raw.add(self.dma_last[slot])
            self.dma_last[slot] = idx
        for k in r:
            self.readers.setdefault(k, []).append(idx)
        for k in w:
            self.last_w[k] = idx
            self.readers[k] = []
        deps = set() if bg else set(self.barrier_deps)
        for d in raw | other:
            p = self.ops[d]
            if p["slot"] is None and slot is None and p["eng"] == eng:
                if eng == "pe" or d not in raw:
                    continue
            deps.add(d)
        deps.discard(idx)
        self.ops.append(dict(eng=eng, fn=fn, deps=deps, slot=slot, sem=None, val=0))
        return idx

    def barrier(self, final=False):
        last = {}
        for i, op in enumerate(self.ops):
            if op["slot"] is not None and op["slot"] >= self.n_dma_sems and not final:
                continue
            key = ("dma", op["slot"]) if op["slot"] is not None else ("eng", op["eng"])
            last[key] = i
        self.barrier_deps = sorted(last.values())
        self.last_w = {k: v for k, v in self.last_w.items() if k in self.persist}
        self.readers = {k: v for k, v in self.readers.items() if k in self.persist}

    def emit(self):
        nc = self.nc
        ops = self.ops
        self.barrier(final=True)
        self.add("sp", None)
        needed = set()
        for op in ops:
            needed.update(op["deps"])
        with contextlib.ExitStack() as st:
            eng_sem = {e: st.enter_context(nc.semaphore("s_" + e)) for e in self.ENGS}
            nslots = self.n_dma_sems + self.n_bg
            dma_sems = [st.enter_context(nc.semaphore("d_%d" % i)) for i in range(nslots)]
            cnt_e = {e: 0 for e in self.ENGS}
            cnt_d = [0] * nslots
            for i, op in enumerate(ops):
                if op["slot"] is not None:
                    cnt_d[op["slot"]] += 16
                    op["sem"] = ("d", op["slot"])
                    op["val"] = cnt_d[op["slot"]]
                elif i in needed:
                    cnt_e[op["eng"]] += 1
                    op["sem"] = ("e", op["eng"])
                    op["val"] = cnt_e[op["eng"]]
            per_eng = {e: [] for e in self.ENGS}
            for op in ops:
                per_eng[op["eng"]].append(op)

            def semh(s):
                return eng_sem[s[1]] if s[0] == "e" else dma_sems[s[1]]

            def run(e_obj, ename):
                waited = {}
                for op in per_eng[ename]:
                    for d in sorted(op["deps"]):
                        p = ops[d]
                        if waited.get(p["sem"], 0) >= p["val"]:
                            continue
                        e_obj.wait_ge(semh(p["sem"]), p["val"])
                        waited[p["sem"]] = p["val"]
                    if op["fn"] is None:
                        continue
                    ins = op["fn"](e_obj)
                    if op["sem"] is not None:
                        ins.then_inc(semh(op["sem"]), 16 if op["sem"][0] == "d" else 1)

            with nc.Block() as block:
                @block.sync
                def _(e):
                    run(e, "sp")

                @block.tensor
                def _(e):
                    run(e, "pe")

                @block.scalar
                def _(e):
                    run(e, "act")

                @block.vector
                def _(e):
                    run(e, "dve")

                @block.gpsimd
                def _(e):
                    run(e, "pool")
        return len(ops)


def _hi_lo(v):
    v = np.asarray(v, np.float32)
    hi = v.astype(ml_dtypes.bfloat16).astype(np.float32)
    lo = (v - hi).astype(ml_dtypes.bfloat16).astype(np.float32)
    return hi, lo


def make_consts():
    c = {}
    c["c_ident"] = np.eye(128, dtype=np.float32)
    qaug = np.zeros((4, 8, T), np.float32)
    kaug = np.zeros((4, 2, 8, T), np.float32)
    dp = np.zeros((4, 2, 128, 128), np.float32)
    ii = (np.arange(T) % 512).astype(np.float32)
    jj = (np.arange(T) % 128).astype(np.float32)
    for h in range(4):
        m = np.float32(S_A[h])
        hi, lo = _hi_lo(m * ii)
        qaug[h, 0] = -hi
        qaug[h, 1] = -lo
        qaug[h, 2] = 1.0
        qaug[h, 3] = 1.0
        hi, lo = _hi_lo(m * jj)
        kaug[h, 0, 0] = 1.0
        kaug[h, 0, 1] = 1.0
        kaug[h, 0, 2] = hi
        kaug[h, 0, 3] = lo
        kaug[h, 1] = -kaug[h, 0]
        pj = np.arange(128, dtype=np.float32)[:, None]
        pi = np.arange(128, dtype=np.float32)[None, :]
        dmat = -2.0 * m * np.maximum(pj - pi, 0.0)
        dp[h, 0], dp[h, 1] = _hi_lo(dmat)
    c["c_qaug"] = qaug
    c["c_kaug"] = kaug
    c["c_dp"] = dp
    p = np.arange(128, dtype=np.float32)[:, None]
    i = np.arange(128, dtype=np.float32)[None, :]
    bb = np.zeros((1, 128, 7, 512), np.float32)
    for h in range(1):
        m = np.float32(1.0)
        k = 0
        for pi_, (win, dil) in enumerate(B_PATTERNS):
            d1 = np.abs(i - (p - 64.0))
            d2 = np.abs(i - (p + 64.0))
            t1 = np.where(d1 <= 64.0, -m * dil * d1, -1.0e32).astype(np.float32)
            t2 = np.where(d2 <= 64.0, -m * dil * d2, -1.0e32).astype(np.float32)
            t1f = t1.copy()
            t1f[0:64, :] = -1.0e32
            t2l = t2.copy()
            t2l[64:128, :] = -1.0e32
            mid = np.concatenate([t1, t2, t1, t2], 1)
            first = np.concatenate([t1f, t2, t1, t2], 1)
            last = np.concatenate([t1, t2, t1, t2l], 1)
            both = np.concatenate([t1f, t2, t1, t2l], 1)
            if pi_ < 2:
                bb[h, :, k] = mid
                bb[h, :, k + 1] = first
                bb[h, :, k + 2] = last
                k += 3
            else:
                bb[h, :, k] = both
                k += 1
    c["c_bbias"] = bb[0]
    cb = np.zeros((2, 128, 3, 384), np.float32)
    for g in range(2):
        for rep in range(3):
            m = np.float32(S_C[3 * g + rep])
            for kt in range(3):
                dist = np.abs(i - (p + 128.0 * (kt - 1)))
                cb[g, :, kt, rep * 128:(rep + 1) * 128] = np.where(dist <= 128.0, -m * dist, NEG)
    c["c_cbias"] = cb
    return c


CONST_SHAPES = {
    "c_ident": [128, 128], "c_qaug": [4, 8, T], "c_kaug": [4, 2, 8, T], "c_dp": [4, 2, 128, 128],
    "c_bbias": [128, 7, 512], "c_cbias": [2, 128, 3, 384],
}


def build_program(debug=False, phases=None, layers=(0, 1)):
    nc = bass.Bass("TRN2", target_bir_lowering=False)

    def din(name, shape, dt=F32):
        return nc.dram_tensor(name, shape, dt, kind="ExternalInput").ap()

    def dscr(name, shape, dt):
        return nc.dram_tensor(name, shape, dt, kind=("ExternalOutput" if debug else "Internal")).ap()

    x_in = din("x", [T, D])
    ccol = din("ccol", [128, 8])
    w_ada = din("w_ada", [DEPTH, D, 6 * D])
    b_ada = din("b_ada", [DEPTH, 6 * D])
    w_in = din("w_in", [DEPTH, D, INW])
    lam = din("lam", [DEPTH, 128])
    subln = din("subln_g", [DEPTH, 64])
    sink = din("sink", [DEPTH, 6])
    w_out = din("w_out", [DEPTH, D, D])
    ln_g = din("ln_g", [DEPTH, 2, D])
    ln_b = din("ln_b", [DEPTH, 2, D])
    w_gu = din("w_gu", [DEPTH, D, 2 * FFN])
    w_down = din("w_down", [DEPTH, FFN, D])
    cst = {k: din(k, v) for k, v in CONST_SHAPES.items()}
    y_out = nc.dram_tensor("y", [T, D], F32, kind="ExternalOutput").ap()

    qaT = dscr("qaT", [256, T], BF16)
    kaT = dscr("kaT", [256, T], BF16)
    qbT = dscr("qbT", [384, T], BF16)
    kbT = dscr("kbT", [384, T], BF16)
    qcT = dscr("qcT", [384, T], BF16)
    kcT = dscr("kcT", [128, T], BF16)
    vA = dscr("vA", [T, 260], BF16)
    vB = dscr("vB", [T, 390], BF16)
    vC = dscr("vC", [T, 130], BF16)
    mixT = dscr("mixT", [D, T], BF16)
    x1s = dscr("x1s", [T, D], F32)
    wguS = dscr("wguS", [NJ, 128, 8 * 256], BF16)
    wdnS = dscr("wdnS", [NJ, 128, D], BF16)
    dbg_mod = dscr("dbg_mod", [128, 4096 + 16], F32) if debug else None

    allp = phases is None

    def on(name):
        return allp or name in phases

    with nc.sbuf_tensor("pool", [128, POOLW], F32) as pool, nc.psum_tensor("ps", [128, 4096], F32) as ps:
        S = Sched(nc)

        class Mem:
            def __init__(self, base):
                self.off = base

            def f32(self, n, parts=None):
                v = pool[:, self.off:self.off + n]
                self.off += n
                assert self.off <= POOLW, self.off
                return v

            def bf16(self, n):
                nw = (n + 1) // 2
                v = pool[:, self.off:self.off + nw].bitcast(BF16)[:, 0:n]
                self.off += nw
                assert self.off <= POOLW, self.off
                return v

        def bank(i, n=512):
            return ps[:, i * 512:i * 512 + n]

        def bank_bf(i):
            return ps[:, i * 512:(i + 1) * 512].bitcast(BF16)

        PM = Mem(0)
        ident = PM.f32(128)
        ident_bf = PM.bf16(128)
        ones = PM.f32(128)
        cs_rep = PM.f32(1024).rearrange("p (k m) -> p k m", m=128)
        modcols = PM.f32(16)
        keepR = PM.f32(4096).rearrange("p (a n) -> p a n", n=1024)
        small = PM.f32(64)
        PBASE = PM.off

        S.add("sp", lambda e: e.dma_start(out=ident, in_=cst["c_ident"]), w=["ident"], dma="ident")
        S.add("pool", lambda e: e.dma_start(out=ident_bf, in_=cst["c_ident"]), w=["ident_bf"], dma="ident_bf")
        S.add("dve", lambda e: e.memset(ones, 1.0), w=["ones"])
        cs = small[:, 0:8]
        cs2 = small[:, 8:16]
        S.add("sp", lambda e: e.dma_start(out=cs, in_=ccol), w=["cs"], dma="cs")
        S.add("act", lambda e: e.activation(out=cs2, in_=cs, func=AF.Silu), r=["cs"], w=["cs2"])
        S.add("dve", lambda e: e.tensor_copy(out=cs_rep, in_=cs2.unsqueeze(2).to_broadcast([128, 8, 128])),
              r=["cs2"], w=["cs_rep"])
        S.barrier()

        def phase_M(l):
            M = Mem(PBASE)
            wst = [M.f32(3072) for _ in range(4)]
            bada = M.f32(6144)
            modR = M.f32(6144)
            S.add("sp", lambda e: e.dma_start(out=bada, in_=b_ada[l].partition_broadcast(128)), w=["bada"], dma="bada")
            ld = 0
            for half in range(2):
                for kc in range(8):
                    b = ld % 4
                    ld += 1
                    S.add("sp", lambda e, b=b, kc=kc, half=half: e.dma_start(
                        out=wst[b], in_=w_ada[l, kc * 128:(kc + 1) * 128, half * 3072:(half + 1) * 3072]),
                        w=["wst%d" % b], dma="wst%d" % b)
                    for n in range(6):
                        S.add("pe", lambda e, b=b, kc=kc, n=n: e.matmul(
                            bank(n), lhsT=cs_rep[:, kc, :], rhs=wst[b][:, n * 512:(n + 1) * 512],
                            start=(kc == 0), stop=(kc == 7)),
                            r=["wst%d" % b, "cs_rep"], w=["psM%d" % n])
                for n in range(6):
                    c0 = half * 3072 + n * 512
                    S.add("dve", lambda e, n=n, c0=c0: e.tensor_tensor(
                        out=modR[:, c0:c0 + 512], in0=bank(n), in1=bada[:, c0:c0 + 512], op=ALU.add),
                        r=["psM%d" % n, "bada"], w=["modR%d" % (c0 // 512)])
            allR = ["modR%d" % i for i in range(12)]
            for grp in range(4):
                for t4 in range(4):
                    idx = grp * 4 + t4
                    col0 = (1024 + idx * 128) if idx < 8 else ((idx - 8) * 128)
                    S.add("pe", lambda e, grp=grp, t4=t4, col0=col0: e.transpose(
                        out=bank(6 + grp % 2)[:, t4 * 128:(t4 + 1) * 128], in_=modR[:, col0:col0 + 128], identity=ident),
                        r=allR + ["ident"], w=["psT%d" % (grp % 2)])
                src = bank(6 + grp % 2).rearrange("p (a b) -> p a b", b=128)[:, :, 0]
                addv = 1.0 if grp < 2 else 0.0
                S.add("dve", lambda e, grp=grp, src=src, addv=addv: e.tensor_scalar(
                    out=modcols[:, grp * 4:(grp + 1) * 4], in0=src, scalar1=addv, scalar2=None, op0=ALU.add),
                    r=["psT%d" % (grp % 2)], w=["modcols"])
            S.add("dve", lambda e: e.tensor_scalar(out=keepR[:, 0, :], in0=modR[:, 2048:3072], scalar1=1.0,
                                                    scalar2=1.0 / ALPHA, op0=ALU.add, op1=ALU.mult), r=allR, w=["keep0"])
            S.add("dve", lambda e: e.tensor_scalar(out=keepR[:, 1, :], in0=modR[:, 4096:5120], scalar1=1.0,
                                                    scalar2=None, op0=ALU.add), r=allR, w=["keep1"])
            S.add("dve", lambda e: e.tensor_copy(out=keepR[:, 2, :], in_=modR[:, 3072:4096]), r=allR, w=["keep2"])
            S.add("dve", lambda e: e.tensor_scalar(out=keepR[:, 3, :], in0=modR[:, 5120:6144], scalar1=1.0,
                                                    scalar2=1.0 / ALPHA, op0=ALU.add, op1=ALU.mult), r=allR, w=["keep3"])
            if debug:
                S.add("sp", lambda e: e.dma_start(out=dbg_mod[:, 0:4096], in_=keepR.rearrange("p a n -> p (a n)")),
                      r=["keep0", "keep1", "keep2", "keep3"], dma="dbgm")
                S.add("sp", lambda e: e.dma_start(out=dbg_mod[:, 4096:4112], in_=modcols), r=["modcols"], dma="dbgm2")
            S.barrier()

        def phase_P1(l):
            M = Mem(PBASE)
            wbf = M.bf16(8 * INW).rearrange("p (k n) -> p k n", n=INW)
            xt = [M.f32(4096).rearrange("p (t n) -> p t n", n=1024) for _ in range(2)]
            hT = [M.bf16(4096).rearrange("p (k n) -> p k n", n=512) for _ in range(2)]
            stg = [M.bf16(512) for _ in range(4)]
            vst = [[M.bf16(4 * 65).rearrange("p (h d) -> p h d", d=65),
                    M.bf16(6 * 65).rearrange("p (h d) -> p h d", d=65),
                    M.bf16(2 * 65).rearrange("p (h d) -> p h d", d=65)] for _ in range(2)]
            xsrc = x_in if l == 0 else x1s
            for kc in range(8):
                for hf in range(2):
                    S.add("pool", lambda e, kc=kc, hf=hf: e.dma_start(
                        out=wbf[:, kc, hf * 1280:(hf + 1) * 1280],
                        in_=w_in[l, kc * 128:(kc + 1) * 128, hf * 1280:(hf + 1) * 1280]),
                        w=["wbf%d" % kc], dma="wbf%d_%d" % (kc, hf))
            for b in range(2):
                for gi in range(3):
                    S.add("pool", lambda e, b=b, gi=gi: e.memset(vst[b][gi], 1.0), w=["vst%d_%d" % (b, gi)])
            WB = ["wbf%d" % k for k in range(8)]

            def load_x(c):
                b = c % 2
                S.add("sp", lambda e, b=b, c=c: e.dma_start(
                    out=xt[b], in_=xsrc[c * 512:(c + 1) * 512, :].rearrange("(t p) n -> p t n", p=128)),
                    w=["xt%d" % b], dma="xt%d" % b)

            fm = []
            for i in range(2):
                fm.append((i * 128, qaT, i * 128, 32.0 ** -0.5))
            for i in range(2):
                fm.append((256 + i * 128, kaT, i * 128, 1.0))
            for i in range(3):
                fm.append((768 + i * 128, qbT, i * 128, 0.125))
            for i in range(3):
                fm.append((1152 + i * 128, kbT, i * 128, 1.0))
            for i in range(3):
                fm.append((1920 + i * 128, qcT, i * 128, 0.125))
            fm.append((2304, kcT, 0, 1.0))
            tm = [(512, 256, vA, 4), (1536, 384, vB, 6), (2432, 128, vC, 2)]

            load_x(0)
            ev = 0
            fmn = 0
            tmn = 0
            for c in range(8):
                if c + 1 < 8:
                    load_x(c + 1)
                b = c % 2
                for kc in range(8):
                    pb = kc % 2
                    for t4 in range(4):
                        S.add("pe", lambda e, b=b, kc=kc, t4=t4, pb=pb: e.transpose(
                            out=bank(pb)[:, t4 * 128:(t4 + 1) * 128], in_=xt[b][:, t4, kc * 128:(kc + 1) * 128],
                            identity=ident), r=["xt%d" % b, "ident"], w=["psT%d" % pb])
                    if ev % 2 == 0:
                        S.add("act", lambda e, b=b, kc=kc, pb=pb: e.activation(
                            out=hT[b][:, kc, :], in_=bank(pb), func=AF.Identity,
                            scale=modcols[:, kc:kc + 1], bias=modcols[:, 8 + kc:9 + kc]),
                            r=["psT%d" % pb, "modcols"], w=["hT%d_%d" % (b, kc)])
                    else:
                        S.add("dve", lambda e, b=b, kc=kc, pb=pb: e.tensor_scalar(
                            out=hT[b][:, kc, :], in0=bank(pb), scalar1=modcols[:, kc:kc + 1],
                            scalar2=modcols[:, 8 + kc:9 + kc], op0=ALU.mult, op1=ALU.add),
                            r=["psT%d" % pb, "modcols"], w=["hT%d_%d" % (b, kc)])
                    ev += 1
                HT = ["hT%d_%d" % (b, k) for k in range(8)]
                for (wc, dst, r0, scl) in fm:
                    pb = 2 + fmn % 3
                    sb = fmn % 4
                    fmn += 1
                    for kc in range(8):
                        S.add("pe", lambda e, kc=kc, wc=wc, pb=pb, b=b: e.matmul(
                            bank(pb), lhsT=wbf[:, kc, wc:wc + 128], rhs=hT[b][:, kc, :], start=(kc == 0), stop=(kc == 7)),
                            r=[WB[kc], HT[kc]], w=["psF%d" % pb])
                    if ev % 2 == 0:
                        S.add("act", lambda e, pb=pb, sb=sb, scl=scl: e.activation(
                            out=stg[sb], in_=bank(pb), func=AF.Copy, scale=float(scl)), r=["psF%d" % pb], w=["stg%d" % sb])
                    else:
                        S.add("dve", lambda e, pb=pb, sb=sb, scl=scl: e.tensor_scalar(
                            out=stg[sb], in0=bank(pb), scalar1=float(scl), scalar2=None, op0=ALU.mult),
                            r=["psF%d" % pb], w=["stg%d" % sb])
                    ev += 1
                    S.add("sp", lambda e, sb=sb, dst=dst, r0=r0, c=c: e.dma_start(
                        out=dst[r0:r0 + 128, c * 512:(c + 1) * 512], in_=stg[sb]), r=["stg%d" % sb], dma="stg%d" % sb)
                for t4 in range(4):
                    vb_ = tmn % 2
                    tmn += 1
                    for gi, (wc, ncol, dst, nh) in enumerate(tm):
                        pb = 5 + gi
                        for kc in range(8):
                            S.add("pe", lambda e, kc=kc, wc=wc, ncol=ncol, pb=pb, b=b, t4=t4: e.matmul(
                                bank(pb, ncol), lhsT=hT[b][:, kc, t4 * 128:(t4 + 1) * 128], rhs=wbf[:, kc, wc:wc + ncol],
                                start=(kc == 0), stop=(kc == 7)), r=[WB[kc], HT[kc]], w=["psV%d" % pb])
                        src = bank(pb, ncol).rearrange("p (h d) -> p h d", d=64)
                        if ev % 2 == 0:
                            S.add("act", lambda e, src=src, vb_=vb_, gi=gi: e.copy(out=vst[vb_][gi][:, :, 0:64], in_=src),
                                  r=["psV%d" % pb], w=["vst%d_%d" % (vb_, gi)])
                        else:
                            S.add("dve", lambda e, src=src, vb_=vb_, gi=gi: e.tensor_copy(out=vst[vb_][gi][:, :, 0:64], in_=src),
                                  r=["psV%d" % pb], w=["vst%d_%d" % (vb_, gi)])
                        ev += 1
                        t0 = c * 512 + t4 * 128
                        S.add("sp", lambda e, vb_=vb_, gi=gi, dst=dst, t0=t0: e.dma_start(
                            out=dst[t0:t0 + 128, :], in_=vst[vb_][gi].rearrange("p h d -> p (h d)")),
                            r=["vst%d_%d" % (vb_, gi)], dma="vst%d_%d" % (vb_, gi))
            S.barrier()

        def norm_store(osb_num, rl_in, rl_buf, bc_bank, obf, dst, keys_r, tag, n=512, bc_key=None):
            S.add("act", lambda e: e.activation(out=rl_buf[64:65, 0:n], in_=rl_in, func=AF.Ln), r=keys_r, w=["rl" + tag])
            S.add("act", lambda e: e.activation(out=rl_buf[64:65, 0:n], in_=rl_buf[64:65, 0:n], func=AF.Exp, scale=-1.0),
                  r=["rl" + tag], w=["rl" + tag])
            bck = bc_key if bc_key is not None else "bc" + tag
            S.add("pe", lambda e: e.matmul(bc_bank[0:64, 0:n], lhsT=ones[64:65, 0:64], rhs=rl_buf[64:65, 0:n], start=True, stop=True),
                  r=["rl" + tag, "ones"], w=[bck])
            S.add("dve", lambda e: e.tensor_tensor(out=obf[0:64, 0:n], in0=bc_bank[0:64, 0:n], in1=osb_num, op=ALU.mult),
                  r=keys_r + [bck], w=["obf" + tag])
            S.add("sp", lambda e: e.dma_start(out=dst, in_=obf[0:64, 0:n]), r=["obf" + tag], dma="obf" + tag)

        def phase_A(l):
            lam_init = 0.8 - 0.6 * math.exp(-0.3 * l)
            M = Mem(PBASE)
            qa = [M.bf16(2 * T).rearrange("p (m t) -> p m t", t=T) for _ in range(2)]
            ka = [M.bf16(4 * T).rearrange("p (s m t) -> p s m t", m=2, t=T) for _ in range(2)]
            vah = [M.bf16(32 * 128).rearrange("p (k n) -> p k n", n=128) for _ in range(2)]
            dpt = M.bf16(4 * 2 * 128).rearrange("p (h s i) -> p h s i", s=2, i=128)
            pT = [M.bf16(1024).rearrange("p (m t) -> p m t", t=512) for _ in range(3)]
            osb = M.f32(1024).rearrange("p (m t) -> p m t", t=512)
            rl = M.f32(1024)
            t0 = M.f32(512)
            t1 = M.f32(512)
            dd = M.f32(512)
            sq = M.f32(512)
            tmpv = M.f32(512)
            rstd = M.f32(512)
            negh = M.f32(512)
            obf = M.bf16(512)
            lamt = M.f32(128)
            lw = M.f32(64)
            sm = M.f32(16)
            S.add("sp", lambda e: e.dma_start(out=lamt, in_=lam[l].partition_broadcast(128)), w=["lamt"], dma="lamt")
            S.add("dve", lambda e: e.tensor_tensor(out=lw[:, 0:32], in0=lamt[:, 0:32], in1=lamt[:, 32:64], op=ALU.mult), r=["lamt"], w=["lw0"])
            S.add("dve", lambda e: e.tensor_tensor(out=lw[:, 32:64], in0=lamt[:, 64:96], in1=lamt[:, 96:128], op=ALU.mult), r=["lamt"], w=["lw1"])
            S.add("dve", lambda e: e.tensor_reduce(out=sm[:, 0:2], in_=lw.rearrange("p (a b) -> p a b", b=32), axis=AX.X, op=ALU.add),
                  r=["lw0", "lw1"], w=["sm01"])
            S.add("act", lambda e: e.activation(out=sm[:, 2:4], in_=sm[:, 0:2], func=AF.Exp), r=["sm01"], w=["sm23"])
            S.add("dve", lambda e: e.scalar_tensor_tensor(out=sm[:, 4:5], in0=sm[:, 3:4], scalar=-lam_init, in1=sm[:, 2:3],
                                                          op0=ALU.add, op1=ALU.subtract), r=["sm23"], w=["lamneg"])
            S.add("sp", lambda e: e.dma_start(out=sm[0:64, 8:9], in_=subln[l].rearrange("(p o) -> p o", o=1)), w=["gc0"], dma="gc0")
            S.add("dve", lambda e: e.tensor_scalar(out=sm[0:64, 9:10], in0=sm[0:64, 8:9], scalar1=float(1.0 - lam_init), scalar2=None,
                                                    op0=ALU.mult), r=["gc0"], w=["gcol"])
            lamneg = sm[0:64, 4:5]
            gcol = sm[0:64, 9:10]
            epsc = sm[:, 12:13]
            S.add("pool", lambda e: e.memset(sm[:, 12:13], LN_EPS), w=["epsc"])
            for b2 in range(2):
                for m in range(2):
                    S.add("dve", lambda e, b2=b2, m=m: e.memset(qa[b2][:, m, :], 0.0), w=["qa%d" % b2])
                    for s_ in range(2):
                        S.add("dve", lambda e, b2=b2, m=m, s_=s_: e.memset(ka[b2][:, s_, m, :], 0.0), w=["ka%d" % b2])
                S.add("dve", lambda e, b2=b2: e.memset(vah[b2].rearrange("p k n -> p (k n)"), 0.0), w=["va%d" % b2])
            for h in range(4):
                S.add("pool", lambda e, h=h: e.dma_start(out=dpt[:, h, :, :], in_=cst["c_dp"][h].rearrange("s p i -> p s i")),
                      w=["dpt"], dma="dpt%d" % h)
            def load_head(h):
                hb = h % 2
                for q4 in range(4):
                    S.add("sp", lambda e, q4=q4, hb=hb, h=h: e.dma_start(
                        out=vah[hb][:, q4 * 8:(q4 + 1) * 8, 0:65],
                        in_=vA[q4 * 1024:(q4 + 1) * 1024, h * 65:(h + 1) * 65].rearrange("(k p) n -> p k n", p=128)),
                        w=["va%d" % hb], dma="va%d_%d" % (hb, q4))
                for m in range(2):
                    r0 = h * 64 + m * 32
                    S.add("sp", lambda e, hb=hb, m=m, r0=r0: e.dma_start(out=qa[hb][0:32, m, :], in_=qaT[r0:r0 + 32, :]),
                          w=["qa%d" % hb], dma="qa%d_%d" % (hb, m))
                    S.add("pool", lambda e, hb=hb, m=m, h=h: e.dma_start(
                        out=qa[hb][32:40, m, :].rearrange("p (a b) -> p a b", b=2048),
                        in_=cst["c_qaug"][h].rearrange("p (a b) -> p a b", b=2048)), w=["qa%d" % hb], dma="qg%d_%d" % (hb, m))
                    for s_ in range(2):
                        S.add("sp", lambda e, hb=hb, m=m, r0=r0, s_=s_: e.dma_start(
                            out=ka[hb][0:32, s_, m, :], in_=kaT[r0:r0 + 32, :]), w=["ka%d" % hb], dma="ka%d_%d_%d" % (hb, m, s_))
                        S.add("pool", lambda e, hb=hb, m=m, h=h, s_=s_: e.dma_start(
                            out=ka[hb][32:40, s_, m, :].rearrange("p (a b) -> p a b", b=2048),
                            in_=cst["c_kaug"][h, s_].rearrange("p (a b) -> p a b", b=2048)), w=["ka%d" % hb],
                            dma="kg%d_%d_%d" % (hb, m, s_))

            EPI_AT = [1, 2, 12, 13, 14, 15, 16, 17, 18, 19]
            load_head(0)
            it = 0
            pend_pv = None
            pend_epi = []
            for h in range(4):
                hb = h % 2
                if pend_pv is not None:
                    pend_pv()
                    pend_pv = None
                if h + 1 < 4:
                    load_head(h + 1)
                if h == 0:
                    for j in range(NJ):
                        for gu in range(2):
                            src = w_gu[l][:, gu * FFN + j * 128:gu * FFN + (j + 1) * 128].rearrange("(k p) c -> p k c", p=128)
                            dst = wguS[j].rearrange("p (k n) -> p k n", n=256)[:, :, gu * 128:(gu + 1) * 128]
                            S.add("pool", lambda e, src=src, dst=dst: e.dma_start(out=dst, in_=src), w=["wguS%d" % j],
                                  dma="wguS%d" % ((2 * j + gu) % 4), bg=True)
                mh = float(S_A[h])
                QK = ["qa%d" % hb, "ka%d" % hb]
                for qc in range(8):
                    ab = (h * 8 + qc) % 2
                    acc = ps[:, (4 + 2 * ab) * 512:(6 + 2 * ab) * 512].rearrange("p (m t) -> p m t", t=512)
                    acck = "acc%d" % ab
                    q0 = qc * 512
                    for kb in range(32):
                        sb = it % 2
                        pb = it % 3
                        Sv = ps[:, sb * 1024:(sb + 1) * 1024].rearrange("p (m t) -> p m t", t=512)
                        sk = "S%d" % sb
                        dl = kb - 4 * qc
                        k0 = kb * 128
                        for m in range(2):
                            if dl < 0 or dl > 3:
                                s_ = 0 if dl < 0 else 1
                                S.add("pe", lambda e, m=m, s_=s_, k0=k0, Sv=Sv, hb=hb, q0=q0: e.matmul(
                                    Sv[:, m, :], lhsT=ka[hb][:, s_, m, k0:k0 + 128], rhs=qa[hb][:, m, q0:q0 + 512],
                                    start=True, stop=True), r=QK, w=[sk])
                            else:
                                c0 = 128 * dl
                                if dl > 0:
                                    S.add("pe", lambda e, m=m, k0=k0, Sv=Sv, hb=hb, q0=q0, c0=c0: e.matmul(
                                        Sv[:, m, 0:c0], lhsT=ka[hb][:, 1, m, k0:k0 + 128], rhs=qa[hb][:, m, q0:q0 + c0],
                                        start=True, stop=True), r=QK, w=[sk])
                                S.add("pe", lambda e, m=m, k0=k0, Sv=Sv, hb=hb, q0=q0, c0=c0: e.matmul(
                                    Sv[:, m, c0:512], lhsT=ka[hb][:, 0, m, k0:k0 + 128], rhs=qa[hb][:, m, q0 + c0:q0 + 512],
                                    start=True, stop=False), r=QK, w=[sk])
                                for hl in range(2):
                                    S.add("pe", lambda e, m=m, Sv=Sv, c0=c0, hl=hl, h=h: e.matmul(
                                        Sv[:, m, c0:c0 + 128], lhsT=ident_bf, rhs=dpt[:, h, hl, :],
                                        start=False, stop=(hl == 1)), r=["dpt", "ident_bf"], w=[sk])
                        if pend_pv is not None:
                            pend_pv()
                            pend_pv = None
                        if dl < 0 or dl > 3:
                            bias = -mh * abs(512 * qc - 128 * kb)
                            S.add("act", lambda e, Sv=Sv, pb=pb, bias=bias: e.activation(
                                out=pT[pb].rearrange("p m t -> p (m t)"), in_=Sv.rearrange("p m t -> p (m t)"),
                                func=AF.Exp, bias=float(bias), scale=1.0), r=[sk], w=["pT%d" % pb])
                        else:
                            c0 = 128 * dl
                            if dl > 0:
                                S.add("act", lambda e, Sv=Sv, pb=pb, c0=c0, mh=mh: e.activation(
                                    out=pT[pb][:, :, 0:c0], in_=Sv[:, :, 0:c0], func=AF.Exp, bias=float(-mh * c0), scale=1.0),
                                    r=[sk], w=["pT%d" % pb])
                            S.add("act", lambda e, Sv=Sv, pb=pb, c0=c0, mh=mh: e.activation(
                                out=pT[pb][:, :, c0:512], in_=Sv[:, :, c0:512], func=AF.Exp, bias=float(mh * c0), scale=1.0),
                                r=[sk], w=["pT%d" % pb])

                        def pv(pb=pb, kb=kb, acc=acc, acck=acck, hb=hb):
                            for m in range(2):
                                S.add("pe", lambda e, m=m: e.matmul(
                                    acc[:, m, :], lhsT=vah[hb][:, kb, :], rhs=pT[pb][:, m, :],
                                    start=(kb == 0), stop=(kb == 31)), r=["pT%d" % pb, "va%d" % hb], w=[acck])
                        pend_pv = pv
                        it += 1
                        if pend_epi and kb == EPI_AT[10 - len(pend_epi)]:
                            pend_epi.pop(0)()
                    def mk_epi(acc=acc, acck=acck, h=h, qc=qc):
                        st = []
                        st.append(lambda: S.add("dve", lambda e: e.tensor_copy(out=osb[0:65].rearrange("p m t -> p (m t)"),
                                                                                in_=acc[0:65].rearrange("p m t -> p (m t)")), r=[acck], w=["osb"]))
                        st.append(lambda: S.add("dve", lambda e: e.reciprocal(out=rl[64:65, :], in_=osb[64:65].rearrange("p m t -> p (m t)")),
                                                r=["osb"], w=["rl"]))
                        def bc():
                            for m in range(2):
                                S.add("pe", lambda e, m=m: e.matmul(acc[0:64, m, :], lhsT=ones[64:65, 0:64], rhs=rl[64:65, m * 512:(m + 1) * 512],
                                                                     start=True, stop=True), r=["rl", "ones"], w=[acck])
                        st.append(bc)
                        def mul():
                            S.add("dve", lambda e: e.tensor_tensor(out=t0[0:64], in0=acc[0:64, 0, :], in1=osb[0:64, 0, :], op=ALU.mult),
                                  r=[acck, "osb"], w=["t0"])
                            S.add("dve", lambda e: e.tensor_tensor(out=t1[0:64], in0=acc[0:64, 1, :], in1=osb[0:64, 1, :], op=ALU.mult),
                                  r=[acck, "osb"], w=["t1"])
                            S.add("dve", lambda e: e.scalar_tensor_tensor(out=dd[0:64], in0=t1[0:64], scalar=lamneg, in1=t0[0:64],
                                                                          op0=ALU.mult, op1=ALU.add), r=["t0", "t1", "lamneg"], w=["dd"])
                        st.append(mul)
                        st.append(lambda: S.add("dve", lambda e: e.tensor_tensor(out=sq[0:64], in0=dd[0:64], in1=dd[0:64], op=ALU.mult), r=["dd"], w=["sq"]))
                        st.append(lambda: S.add("pe", lambda e: e.matmul(acc[0:64, 0, :], lhsT=ones[0:64, 0:64], rhs=sq[0:64],
                                                                          start=True, stop=True), r=["sq", "ones"], w=[acck]))
                        st.append(lambda: S.add("act", lambda e: e.activation(out=tmpv[0:64], in_=acc[0:64, 0, :], func=AF.Ln, scale=1.0 / 64.0,
                                                                              bias=epsc[0:64, 0:1]), r=[acck, "epsc"], w=["tmpv"]))
                        st.append(lambda: S.add("act", lambda e: e.activation(out=rstd[0:64], in_=tmpv[0:64], func=AF.Exp, scale=-0.5),
                                                r=["tmpv"], w=["rstd"]))
                        st.append(lambda: S.add("dve", lambda e: e.scalar_tensor_tensor(out=obf[0:64], in0=dd[0:64], scalar=gcol, in1=rstd[0:64],
                                                                                        op0=ALU.mult, op1=ALU.mult),
                                                r=["dd", "rstd", "gcol"], w=["obfA"]))
                        st.append(lambda: S.add("sp", lambda e: e.dma_start(out=mixT[h * 64:(h + 1) * 64, qc * 512:(qc + 1) * 512], in_=obf[0:64]),
                                                r=["obfA"], dma="obfA"))
                        return st
                    while pend_epi:
                        pend_epi.pop(0)()
                    pend_epi = mk_epi()
            if pend_pv is not None:
                pend_pv()
            while pend_epi:
                pend_epi.pop(0)()
            S.barrier()

        def phase_B(l):
            M = Mem(PBASE)
            PAD = 64
            vb = M.bf16(3 * 33 * 390).rearrange("p (a t n) -> p a t n", a=3, t=33)
            qbs = [M.bf16(T) for _ in range(2)]
            kbs_ = [M.bf16(T + 2 * PAD) for _ in range(2)]
            qp = M.bf16(T)
            kp = M.bf16(T + 2 * PAD)
            bias = M.f32(7 * 512).rearrange("p (v n) -> p v n", n=512)
            accT = M.f32(T)
            Ssb = [M.f32(512) for _ in range(4)]
            pT = [M.bf16(512) for _ in range(4)]
            rl = [M.f32(512) for _ in range(2)]
            obf = [M.bf16(512) for _ in range(2)]
            S.add("sp", lambda e: e.dma_start(out=bias.rearrange("p v n -> p (v n)"), in_=cst["c_bbias"].rearrange("p v n -> p (v n)")),
                  w=["biasB"], dma="biasB")
            VBK = ["vb_%d" % i for i in range(84)]
            for a3 in range(3):
                for t3 in range(3):
                    S.add("dve", lambda e, a3=a3, t3=t3: e.memset(vb[:, a3, t3 * 11:(t3 + 1) * 11, :], 0.0), w=VBK)
            for (buf, key) in ((qbs[0], "qb0"), (kbs_[0], "kb0"), (qbs[1], "qb1"), (kbs_[1], "kb1"), (qp, "qp"), (kp, "kp")):
                S.add("dve", lambda e, buf=buf: e.memset(buf, 0.0), w=[key])
            nd = 0
            for pi, (win, d) in enumerate(B_PATTERNS):
                Lc = T // d
                nt = Lc // 128
                for r in range(d):
                    srcB = bass.AP(vB.tensor, r * 390, [[d * 390, 64], [128 * d * 390, nt], [1, 390]])
                    tA = r * nt + 1
                    tB = r * nt
                    ntA = nt
                    if (64 + 128 * (nt - 1) + 63) * d + r >= T:
                        ntA = nt - 1
                    if ntA > 0:
                        srcA = bass.AP(vB.tensor, (64 * d + r) * 390, [[d * 390, 64], [128 * d * 390, ntA], [1, 390]])
                        S.add("sp", lambda e, srcA=srcA, pi=pi, tA=tA, ntA=ntA: e.dma_start(out=vb[0:64, pi, tA:tA + ntA, :], in_=srcA),
                              w=[VBK[nd]], dma="vb%d" % (nd % 4))
                        nd += 1
                    S.add("sp", lambda e, srcB=srcB, pi=pi, tB=tB, nt=nt: e.dma_start(out=vb[64:128, pi, tB:tB + nt, :], in_=srcB),
                          w=[VBK[nd]], dma="vb%d" % (nd % 4))
                    nd += 1
            blocks = [(0, 0), (0, 1), (1, 1), (1, 2)]
            def load_qk(h):
                S.add("sp", lambda e, h=h: e.dma_start(out=qbs[h % 2][0:64, :], in_=qbT[h * 64:(h + 1) * 64, :]), w=["qb%d" % (h % 2)], dma="qb%d" % (h % 2))
                S.add("sp", lambda e, h=h: e.dma_start(out=kbs_[h % 2][0:64, PAD:PAD + T], in_=kbT[h * 64:(h + 1) * 64, :]), w=["kb%d" % (h % 2)], dma="kb%d" % (h % 2))

            load_qk(0)
            for h in range(6):
                mh = float(S_B[h])
                qb, kb_ = qbs[h % 2], kbs_[h % 2]
                qbk, kbk_ = "qb%d" % (h % 2), "kb%d" % (h % 2)
                if h + 1 < 6:
                    load_qk(h + 1)
                for pi, (win, d) in enumerate(B_PATTERNS):
                    Lc = T // d
                    ntc = Lc // 128
                    ng = ntc // 2
                    if pi == 0:
                        qs_, ks_, qk_, kk_ = qb, kb_, qbk, kbk_
                    else:
                        qs_, ks_, qk_, kk_ = qp, kp, "qp", "kp"
                        S.add("dve", lambda e, d=d, qb=qb: e.tensor_copy(out=qp[0:64, :].rearrange("p (r j) -> p r j", r=d),
                                                                         in_=qb[0:64, :].rearrange("p (j r) -> p r j", r=d)), r=[qbk], w=["qp"])
                        S.add("act", lambda e, d=d, kb_=kb_: e.copy(out=kp[0:64, PAD:PAD + T].rearrange("p (r j) -> p r j", r=d),
                                                                    in_=kb_[0:64, PAD:PAD + T].rearrange("p (j r) -> p r j", r=d)), r=[kbk_], w=["kp"])
                    groups = []
                    for r in range(d):
                        for gq in range(ng):
                            b0 = 2 * gq
                            tq = r * ntc + b0
                            if pi == 2:
                                var = 6
                            else:
                                var = 3 * pi + (1 if gq == 0 else (2 if gq == ng - 1 else 0))
                            groups.append((tq, var, 128 * b0 * d + r))

                    def emit_S(gi, groups=groups, qs_=qs_, ks_=ks_, qk_=qk_, kk_=kk_):
                        tq, var, s0 = groups[gi]
                        sbk = gi % 4
                        Sb = bank(sbk)
                        for bi, (qi, ci) in enumerate(blocks):
                            qs = 128 * (tq + qi)
                            ks = PAD + 128 * (tq + ci) - 64
                            S.add("pe", lambda e, bi=bi, qs=qs, ks=ks: e.matmul(
                                Sb[:, bi * 128:(bi + 1) * 128], lhsT=ks_[:, ks:ks + 128], rhs=qs_[:, qs:qs + 128],
                                start=True, stop=True), r=[qk_, kk_], w=["SB%d" % sbk])

                    def emit_rest(gi, groups=groups, h=h, pi=pi, d=d, mh=mh):
                        tq, var, s0 = groups[gi]
                        sbk = gi % 4
                        obk = gi % 2
                        Sb = bank(sbk)
                        ob = bank(4 + obk)
                        S.add("dve", lambda e: e.scalar_tensor_tensor(out=Ssb[sbk], in0=bias[:, var, :], scalar=mh, in1=Sb,
                                                                      op0=ALU.mult, op1=ALU.add),
                              r=["SB%d" % sbk, "biasB"], w=["Ssb%d" % sbk])
                        S.add("act", lambda e: e.activation(out=pT[sbk], in_=Ssb[sbk], func=AF.Exp), r=["Ssb%d" % sbk], w=["pTB%d" % sbk])
                        for bi, (qi, ci) in enumerate(blocks):
                            S.add("pe", lambda e, bi=bi, qi=qi, ci=ci: e.matmul(
                                ob[0:65, qi * 128:(qi + 1) * 128], lhsT=vb[:, pi, tq + ci, h * 65:(h + 1) * 65],
                                rhs=pT[sbk][:, bi * 128:(bi + 1) * 128], start=(bi % 2 == 0), stop=(bi % 2 == 1)),
                                r=["pTB%d" % sbk] + VBK, w=["oB%d" % obk])

                    def emit_tail(gi, groups=groups, pi=pi, d=d):
                        tq, var, s0 = groups[gi]
                        obk = gi % 2
                        ob = bank(4 + obk)
                        dst = accT[0:65, s0:s0 + 255 * d + 1:d]
                        if pi == 0:
                            S.add("act", lambda e: e.copy(out=dst, in_=ob[0:65, 0:256]), r=["oB%d" % obk], w=["accT"])
                        else:
                            S.add("dve", lambda e: e.tensor_tensor(out=dst, in0=ob[0:65, 0:256], in1=dst, op=ALU.add),
                                  r=["oB%d" % obk, "accT"], w=["accT"])

                    emit_S(0)
                    emit_S(1)
                    emit_S(2)
                    for gi in range(len(groups)):
                        if gi + 3 < len(groups):
                            emit_S(gi + 3)
                        emit_rest(gi)
                        if gi >= 1:
                            emit_tail(gi - 1)
                    emit_tail(len(groups) - 1)
                for c in range(8):
                    norm_store(accT[0:64, c * 512:(c + 1) * 512], accT[64:65, c * 512:(c + 1) * 512], rl[c % 2], bank(6 + c % 2), obf[c % 2],
                               mixT[256 + h * 64:256 + (h + 1) * 64, c * 512:(c + 1) * 512], ["accT"], "B%d" % (c % 2))
            S.barrier()

        def phase_C(l):
            M = Mem(PBASE)
            PADC = 128
            G = []
            for g in range(2):
                d_ = dict(
                    kc=M.bf16(T + 2 * PADC), qc=M.bf16(3 * T).rearrange("p (r t) -> p r t", t=T),
                    vc=M.bf16(32 * 65).rearrange("p (k n) -> p k n", n=65), cb=M.f32(3 * 384).rearrange("p (k n) -> p k n", n=384),
                    Ssb=M.f32(3 * 384).rearrange("p (k n) -> p k n", n=384), pT=M.bf16(3 * 384).rearrange("p (k n) -> p k n", n=384),
                    osbc=[M.f32(3 * 512).rearrange("p (r t) -> p r t", t=512) for _ in range(2)],
                    rlin=M.f32(512), rl=M.f32(512), obf=M.bf16(512))
                G.append(d_)
            es = M.f32(16)
            S.add("sp", lambda e: e.dma_start(out=es[64:65, 0:6], in_=sink[l:l + 1, :]), w=["es0"], dma="es0")
            S.add("act", lambda e: e.activation(out=es[64:65, 8:14], in_=es[64:65, 0:6], func=AF.Exp), r=["es0"], w=["es"])
            for g in range(2):
                d_ = G[g]
                S.add("dve", lambda e, d_=d_: e.memset(d_["kc"], 0.0), w=["kc%d" % g])
                S.add("dve", lambda e, d_=d_: e.memset(d_["qc"][64:128].rearrange("p r t -> p (r t)"), 0.0), w=["qcz%d" % g])
                S.add("sp", lambda e, g=g, d_=d_: e.dma_start(out=d_["kc"][0:64, PADC:PADC + T], in_=kcT[g * 64:(g + 1) * 64, :]),
                      w=["kc%d" % g], dma="kc%d" % g)
                for rep in range(3):
                    hq = 3 * g + rep
                    S.add("sp", lambda e, rep=rep, hq=hq, d_=d_: e.dma_start(out=d_["qc"][0:64, rep, :], in_=qcT[hq * 64:(hq + 1) * 64, :]),
                          w=["qc%d_%d" % (g, rep)], dma="qc%d_%d" % (g, rep))
                for q4 in range(4):
                    S.add("sp", lambda e, g=g, q4=q4, d_=d_: e.dma_start(
                        out=d_["vc"][:, q4 * 8:(q4 + 1) * 8, :],
                        in_=vC[q4 * 1024:(q4 + 1) * 1024, g * 65:(g + 1) * 65].rearrange("(k p) n -> p k n", p=128)),
                        w=["vc%d_%d" % (g, q4)], dma="vc%d_%d" % (g, q4))
                S.add("sp", lambda e, g=g, d_=d_: e.dma_start(out=d_["cb"].rearrange("p k n -> p (k n)"),
                                                              in_=cst["c_cbias"][g].rearrange("p k n -> p (k n)")), w=["cb%d" % g], dma="cb%d" % g)

            def kts_of(qb_):
                return [kt for kt in range(3) if 0 <= qb_ + kt - 1 <= 31]

            def c_S(g, qb_):
                d_ = G[g]
                for kt in kts_of(qb_):
                    kbk = qb_ + kt - 1
                    Sb = bank(3 * g + kt, 384)
                    S.add("pe", lambda e, Sb=Sb, kbk=kbk: e.matmul(
                        Sb, lhsT=d_["kc"][:, PADC + kbk * 128:PADC + (kbk + 1) * 128], rhs=d_["qc"][:, :, qb_ * 128:(qb_ + 1) * 128],
                        start=True, stop=True), r=["kc%d" % g, "qcz%d" % g] + ["qc%d_%d" % (g, r_) for r_ in range(3)], w=["SC%d_%d" % (g, kt)])

            def c_exp(g, qb_):
                d_ = G[g]
                kts = kts_of(qb_)
                for kt in kts:
                    Sb = bank(3 * g + kt, 384)
                    S.add("dve", lambda e, Sb=Sb, kt=kt: e.tensor_tensor(out=d_["Ssb"][:, kt, :], in0=Sb, in1=d_["cb"][:, kt, :], op=ALU.add),
                          r=["SC%d_%d" % (g, kt), "cb%d" % g], w=["SsbC%d" % g])
                lo, hi = kts[0], kts[-1] + 1
                S.add("act", lambda e: e.activation(out=d_["pT"][:, lo:hi, :], in_=d_["Ssb"][:, lo:hi, :], func=AF.Exp),
                      r=["SsbC%d" % g], w=["pTC%d" % g])

            def c_pv(g, qb_):
                d_ = G[g]
                kts = kts_of(qb_)
                lo, hi = kts[0], kts[-1] + 1
                for kt in kts:
                    kbk = qb_ + kt - 1
                    S.add("pe", lambda e, kt=kt, kbk=kbk: e.matmul(
                        bank(6 + g, 384)[0:65, :], lhsT=d_["vc"][:, kbk, :], rhs=d_["pT"][:, kt, :], start=(kt == lo), stop=(kt == hi - 1)),
                        r=["pTC%d" % g] + ["vc%d_%d" % (g, q4) for q4 in range(4)], w=["oC%d" % g])

            def c_tail(g, qb_):
                d_ = G[g]
                ch = qb_ // 4
                obk = ch % 2
                S.add("act", lambda e: e.copy(
                    out=d_["osbc"][obk][0:65, :, (qb_ % 4) * 128:(qb_ % 4 + 1) * 128],
                    in_=bank(6 + g, 384)[0:65, :].rearrange("p (r t) -> p r t", t=128)), r=["oC%d" % g], w=["osbc%d_%d" % (g, obk)])
                if qb_ % 4 == 3:
                    for rep in range(3):
                        hq = 3 * g + rep
                        S.add("dve", lambda e, rep=rep, hq=hq: e.tensor_scalar(
                            out=d_["rlin"][64:65, :], in0=d_["osbc"][obk][64:65, rep, :], scalar1=es[64:65, 8 + hq:9 + hq], scalar2=None, op0=ALU.add),
                            r=["osbc%d_%d" % (g, obk), "es"], w=["rlin%d" % g])
                        norm_store(d_["osbc"][obk][0:64, rep, :], d_["rlin"][64:65, :], d_["rl"], bank(6 + g), d_["obf"],
                                   mixT[640 + hq * 64:640 + (hq + 1) * 64, ch * 512:(ch + 1) * 512],
                                   ["rlin%d" % g, "osbc%d_%d" % (g, obk)], "C%d" % g, bc_key="oC%d" % g)

            c_S(0, 0)
            c_S(1, 0)
            for qb_ in range(32):
                for g in range(2):
                    c_exp(g, qb_)
                    if qb_ + 1 < 32:
                        c_S(g, qb_ + 1)
                for g in range(2):
                    c_pv(g, qb_)
                for g in range(2):
                    c_tail(g, qb_)
            S.barrier()

        def phase_P3(l):
            EPS1 = LN_EPS / (ALPHA * ALPHA)
            M = Mem(PBASE)
            wob = M.bf16(8 * 1024).rearrange("p (k n) -> p k n", n=1024)
            lnt = M.f32(6 * 1024).rearrange("p (a n) -> p a n", n=1024)
            mxT = M.bf16(8 * 512).rearrange("p (k t) -> p k t", t=512)
            xt = [M.f32(4096).rearrange("p (t n) -> p t n", n=1024) for _ in range(2)]
            zn = M.f32(1024)
            h2 = M.bf16(4 * 1024).rearrange("p (t n) -> p t n", n=1024)
            h2T = M.bf16(8 * 512).rearrange("p (k t) -> p k t", t=512)
            actT = M.bf16(NJ * 512).rearrange("p (j t) -> p j t", t=512)
            sg = [M.f32(512) for _ in range(2)]
            wgu = [M.bf16(8 * 256).rearrange("p (k n) -> p k n", n=256) for _ in range(3)]
            wdn = [M.bf16(1024) for _ in range(3)]
            wst = [M.f32(1024) for _ in range(2)]
            stt = M.f32(8 * 24).rearrange("p (s n) -> p s n", n=24)
            negh1 = M.f32(2)
            xdst = x1s if l == 0 else y_out
            S.add("pool", lambda e: e.memset(negh1, -0.5), w=["negh1"])
            for (a, src) in ((0, ln_g[l, 0]), (1, ln_b[l, 0]), (4, ln_g[l, 1]), (5, ln_b[l, 1])):
                S.add("sp", lambda e, a=a, src=src: e.dma_start(out=lnt[:, a, :], in_=src.partition_broadcast(128)), w=["lnt%d" % a], dma="lnt%d" % a)
            S.add("dve", lambda e: e.tensor_tensor(out=lnt[:, 2, :], in0=lnt[:, 0, :], in1=keepR[:, 1, :], op=ALU.mult), r=["lnt0"], w=["lnt2"])
            S.add("dve", lambda e: e.tensor_tensor(out=lnt[:, 3, :], in0=lnt[:, 1, :], in1=keepR[:, 1, :], op=ALU.mult), r=["lnt1"], w=["lnt3"])
            S.add("dve", lambda e: e.tensor_tensor(out=lnt[:, 3, :], in0=lnt[:, 3, :], in1=keepR[:, 2, :], op=ALU.add), r=["lnt3"], w=["lnt3"])
            for kc in range(8):
                b = kc % 2
                S.add("sp", lambda e, kc=kc, b=b: e.dma_start(out=wst[b], in_=w_out[l, kc * 128:(kc + 1) * 128, :]), w=["wst%d" % b], dma="wst%d" % b)
                S.add("dve", lambda e, kc=kc, b=b: e.tensor_tensor(out=wob[:, kc, :], in0=wst[b], in1=keepR[:, 0, :], op=ALU.mult),
                      r=["wst%d" % b], w=["wob%d" % kc])
            for j in range(NJ):
                b = j % 2
                wb_ = j % 3
                S.add("sp", lambda e, j=j, b=b: e.dma_start(out=wst[b], in_=w_down[l, j * 128:(j + 1) * 128, :]), w=["wst%d" % b], dma="wst%d" % b)
                S.add("dve", lambda e, b=b, wb_=wb_: e.tensor_tensor(out=wdn[wb_], in0=wst[b], in1=keepR[:, 3, :], op=ALU.mult),
                      r=["wst%d" % b], w=["wdn%d" % wb_])
                S.add("sp", lambda e, j=j, wb_=wb_: e.dma_start(out=wdnS[j], in_=wdn[wb_]), r=["wdn%d" % wb_], w=["wdnS%d" % j], dma="wdn%d" % wb_)
            WOB = ["wob%d" % k for k in range(8)]

            def load_mx(c):
                S.add("sp", lambda e, c=c: e.dma_start(out=mxT, in_=mixT[:, c * 512:(c + 1) * 512].rearrange("(k p) t -> p k t", p=128)),
                      w=["mxT"], dma="mxT")

            def load_x(c):
                cb_ = c % 2
                xsrc = x_in if l == 0 else x1s
                S.add("sp", lambda e, c=c, cb_=cb_: e.dma_start(
                    out=xt[cb_], in_=xsrc[c * 512:(c + 1) * 512, :].rearrange("(t p) n -> p t n", p=128)), w=["xt%d_%d" % (cb_, t_) for t_ in range(4)],
                    dma="xt%d" % cb_)

            cnt = {"tm": 0, "ev": 0, "zn": 0}

            def wout(c):
                cb_ = c % 2
                for tb in range(4):
                    for n in range(2):
                        pb = cnt["tm"] % 2
                        cnt["tm"] += 1
                        for kc in range(8):
                            S.add("pe", lambda e, tb=tb, n=n, kc=kc, pb=pb: e.matmul(
                                bank(pb), lhsT=mxT[:, kc, tb * 128:(tb + 1) * 128], rhs=wob[:, kc, n * 512:(n + 1) * 512],
                                start=(kc == 0), stop=(kc == 7)), r=["mxT", WOB[kc]], w=["psO%d" % pb])
                        S.add("dve", lambda e, tb=tb, n=n, pb=pb, cb_=cb_: e.tensor_tensor(
                            out=xt[cb_][:, tb, n * 512:(n + 1) * 512], in0=bank(pb), in1=xt[cb_][:, tb, n * 512:(n + 1) * 512], op=ALU.add),
                            r=["psO%d" % pb, "xt%d_%d" % (cb_, tb)], w=["xt%d_%d" % (cb_, tb)])

            def layer_norm(c, tb, which):
                cb_ = c % 2
                y = xt[cb_][:, tb, :]
                yk = "xt%d_%d" % (cb_, tb)
                st = stt[:, tb + 4 * (which - 1), :]
                sk = "st%d" % (tb + 4 * (which - 1))
                S.add("dve", lambda e: e.bn_stats(out=st[:, 0:6], in_=y[:, 0:512]), r=[yk], w=[sk + "a"])
                S.add("dve", lambda e: e.bn_stats(out=st[:, 6:12], in_=y[:, 512:1024]), r=[yk], w=[sk + "b"])
                S.add("dve", lambda e: e.bn_aggr(out=st[:, 12:14], in_=st[:, 0:12]), r=[sk + "a", sk + "b"], w=[sk + "mv"])
                S.add("dve", lambda e: e.tensor_scalar(out=st[:, 14:15], in0=st[:, 13:14], scalar1=float(EPS1), scalar2=None, op0=ALU.add),
                      r=[sk + "mv"], w=[sk + "ve"])
                S.add("pool", lambda e: e.tensor_tensor(out=st[:, 15:16], in0=st[:, 14:15], in1=negh1[:, 0:1], op=ALU.pow),
                      r=[sk + "ve", "negh1"], w=[sk + "rs"])
                S.add("dve", lambda e: e.tensor_scalar(out=st[:, 16:17], in0=st[:, 12:13], scalar1=st[:, 15:16], scalar2=-1.0,
                                                        op0=ALU.mult, op1=ALU.mult), r=[sk + "mv", sk + "rs"], w=[sk + "nb"])
                zi = cnt["zn"] % 2
                cnt["zn"] += 1
                znb = zn if zi == 0 else wst[1]
                znk = "zn" if zi == 0 else "wst1"
                S.add("act", lambda e: e.activation(out=znb, in_=y, func=AF.Identity, scale=st[:, 15:16], bias=st[:, 16:17]),
                      r=[yk, sk + "rs", sk + "nb"], w=[znk])
                ga, ba = (0, 1) if which == 1 else (4, 5)
                S.add("pool", lambda e: e.tensor_tensor(out=y, in0=znb, in1=lnt[:, ga, :], op=ALU.mult), r=[znk, "lnt%d" % ga], w=[yk])
                S.add("pool", lambda e: e.tensor_tensor(out=y, in0=y, in1=lnt[:, ba, :], op=ALU.add), r=[yk, "lnt%d" % ba], w=[yk])
                if which == 1:
                    S.add("dve", lambda e: e.tensor_tensor(out=wst[0], in0=znb, in1=lnt[:, 2, :], op=ALU.mult), r=[znk, "lnt2"], w=["wst0"])
                    S.add("dve", lambda e: e.tensor_tensor(out=h2[:, tb, :], in0=wst[0], in1=lnt[:, 3, :], op=ALU.add),
                          r=["wst0", "lnt3"], w=["h2_%d" % tb])

            def transposes(c):
                for kc in range(8):
                    pb = 2 + kc % 2
                    for tb in range(4):
                        S.add("pe", lambda e, kc=kc, tb=tb, pb=pb: e.transpose(
                            out=bank_bf(pb)[:, tb * 128:(tb + 1) * 128], in_=h2[:, tb, kc * 128:(kc + 1) * 128], identity=ident_bf),
                            r=["h2_%d" % tb, "ident_bf"], w=["psT%d" % pb])
                    if cnt["ev"] % 2 == 0:
                        S.add("act", lambda e, kc=kc, pb=pb: e.copy(out=h2T[:, kc, :], in_=bank_bf(pb)[:, 0:512]), r=["psT%d" % pb], w=["h2T%d" % kc])
                    else:
                        S.add("dve", lambda e, kc=kc, pb=pb: e.tensor_copy(out=h2T[:, kc, :], in_=bank_bf(pb)[:, 0:512]), r=["psT%d" % pb], w=["h2T%d" % kc])
                    cnt["ev"] += 1

            def load_wgu(p):
                if p >= 8 * NJ:
                    return
                j = p % NJ
                wb_ = p % 3
                S.add("sp", lambda e, j=j, wb_=wb_: e.dma_start(out=wgu[wb_].rearrange("p k n -> p (k n)"), in_=wguS[j]),
                      r=["wguS%d" % j], w=["wgu%d" % wb_], dma="wgu%d" % wb_)

            def gu(c, hooks):
                H2T = ["h2T%d" % k for k in range(8)]
                for j in range(NJ):
                    load_wgu(c * NJ + j + 2)
                    wb_ = (c * NJ + j) % 3
                    pg = 4 + 2 * (j % 2)
                    for half in range(2):
                        for kc in range(8):
                            S.add("pe", lambda e, kc=kc, half=half, pg=pg, wb_=wb_: e.matmul(
                                bank(pg + half), lhsT=wgu[wb_][:, kc, half * 128:(half + 1) * 128], rhs=h2T[:, kc, :],
                                start=(kc == 0), stop=(kc == 7)), r=["wgu%d" % wb_, H2T[kc]], w=["psG%d" % (pg + half)])
                    sb_ = j % 2
                    S.add("act", lambda e, pg=pg, sb_=sb_: e.activation(out=sg[sb_], in_=bank(pg), func=AF.Silu), r=["psG%d" % pg], w=["sg%d" % sb_])
                    S.add("dve", lambda e, pg=pg, sb_=sb_, j=j: e.tensor_tensor(out=actT[:, j, :], in0=bank(pg + 1), in1=sg[sb_], op=ALU.mult),
                          r=["psG%d" % (pg + 1), "sg%d" % sb_], w=["actT%d" % j])
                    for f in hooks.get(j, ()):
                        f()

            def load_wdn(q):
                if q >= 16 * NJ:
                    return
                i = q % (2 * NJ)
                j, n = i % NJ, i // NJ
                wb_ = q % 3
                S.add("sp", lambda e, j=j, n=n, wb_=wb_: e.dma_start(out=wdn[wb_][:, 0:512], in_=wdnS[j][:, n * 512:(n + 1) * 512]),
                      r=["wdnS%d" % j], w=["wdn%d" % wb_], dma="wdn%d" % wb_)

            def down(c, mid):
                cb_ = c % 2
                for i in range(2 * NJ):
                    load_wdn(c * 2 * NJ + i + 2)
                    j, n = i % NJ, i // NJ
                    wb_ = (c * 2 * NJ + i) % 3
                    for tb in range(4):
                        S.add("pe", lambda e, j=j, tb=tb, wb_=wb_: e.matmul(
                            bank(4 + tb), lhsT=actT[:, j, tb * 128:(tb + 1) * 128], rhs=wdn[wb_][:, 0:512],
                            start=(j == 0), stop=(j == NJ - 1)), r=["actT%d" % j, "wdn%d" % wb_], w=["psG%d" % (4 + tb)])
                    if j == NJ - 1:
                        for tb in range(4):
                            S.add("dve", lambda e, tb=tb, n=n, cb_=cb_: e.tensor_tensor(
                                out=xt[cb_][:, tb, n * 512:(n + 1) * 512], in0=bank(4 + tb), in1=xt[cb_][:, tb, n * 512:(n + 1) * 512], op=ALU.add),
                                r=["psG%d" % (4 + tb), "xt%d_%d" % (cb_, tb)], w=["xt%d_%d" % (cb_, tb)])
                        if n == 0:
                            for f in mid:
                                f()

            def ln2_tile(c, tb):
                cb_ = c % 2
                if True:
                    layer_norm(c, tb, 2)
                    t0_ = c * 512 + tb * 128
                    S.add("pool", lambda e, tb=tb, t0_=t0_, cb_=cb_: e.dma_start(out=xdst[t0_:t0_ + 128, :], in_=xt[cb_][:, tb, :]),
                          r=["xt%d_%d" % (cb_, tb)], dma="xo%d_%d" % (cb_, tb))

            load_mx(0)
            load_x(0)
            load_wgu(0)
            load_wgu(1)
            load_wdn(0)
            load_wdn(1)
            wout(0)
            for tb in range(4):
                layer_norm(0, tb, 1)
            transposes(0)
            for c in range(8):
                hooks = {}
                if c + 1 < 8:
                    load_mx(c + 1)
                if c >= 1:
                    for tb in range(4):
                        hooks.setdefault(3 * tb, []).append(lambda c=c, tb=tb: ln2_tile(c - 1, tb))
                if c + 1 < 8:
                    hooks.setdefault(10, []).append(lambda c=c: load_x(c + 1))
                gu(c, hooks)
                mid = []
                if c + 1 < 8:
                    wout(c + 1)
                    layer_norm(c + 1, 0, 1)
                    layer_norm(c + 1, 1, 1)
                    mid = [lambda c=c: layer_norm(c + 1, 2, 1), lambda c=c: layer_norm(c + 1, 3, 1)]
                down(c, mid)
                if c + 1 < 8:
                    transposes(c + 1)
            for tb in range(4):
                ln2_tile(7, tb)
            S.barrier()

        for l in layers:
            if on("M"):
                phase_M(l)
            if on("P1"):
                phase_P1(l)
            if on("A"):
                phase_A(l)
            if on("B"):
                phase_B(l)
            if on("C"):
                phase_C(l)
            if on("P3"):
                phase_P3(l)

        n_ops = S.emit()
    return nc, n_ops


def kernel(**inputs):
    nc, _ = build_program(debug=False)
    cores = list(range(NCORES))
    maps = make_in_maps(inputs, cores)
    res = run_bass_kernel_spmd(nc, maps, core_ids=cores)
    return np.stack([np.asarray(r["y"], dtype=np.float32) for r in res.results], axis=0)


def make_in_maps(inputs, cores):
    f = lambda a: np.ascontiguousarray(np.asarray(a, dtype=np.float32))
    consts = make_consts()
    shared = {
        "w_ada": f(inputs["w_ada"]), "b_ada": f(inputs["b_ada"]), "w_in": f(inputs["w_in"]),
        "lam": f(inputs["lam"]).reshape(DEPTH, 128), "subln_g": f(inputs["subln_g"]), "sink": f(inputs["sink"]),
        "w_out": f(inputs["w_out"]), "ln_g": f(inputs["ln_g"]), "ln_b": f(inputs["ln_b"]),
        "w_gu": f(inputs["w_gu"]), "w_down": f(inputs["w_down"]),
    }
    shared.update(consts)
    maps = []
    x = np.asarray(inputs["x"], dtype=np.float32)
    c = np.asarray(inputs["c"], dtype=np.float32)
    for b in cores:
        m = dict(shared)
        m["x"] = np.ascontiguousarray(x[b])
        m["ccol"] = np.ascontiguousarray(c[b].reshape(8, 128).T)
        maps.append(m)
    return maps
```

```python
import contextlib
import math
import numpy as np
import ml_dtypes
import concourse.bass as bass
import concourse.mybir as mybir
from concourse.bass_utils import run_bass_kernel_spmd

F32 = mybir.dt.float32
BF16 = mybir.dt.bfloat16
AF = mybir.ActivationFunctionType
ALU = mybir.AluOpType
AX = mybir.AxisListType

T = 4096
D = 1024
DEPTH = 2
NCORES = 8
FFN = 2816
NJ = FFN // 128
INW = 2560
LN_EPS = 1e-5
ALPHA = (2 * DEPTH) ** 0.25
NEG = -1.0e30
POOLW = 50000

SL = (2.0 ** (-8.0 * np.arange(1, 17) / 16)).astype(np.float32)
S_C = SL[0:6]
S_A = SL[6:10]
S_B = SL[10:16]
B_PATTERNS = ((128, 1), (512, 4), (2048, 16))


class Sched:
    ENGS = ("pe", "act", "dve", "pool", "sp")

    def __init__(self, nc, n_dma_sems=48):
        self.nc = nc
        self.ops = []
        self.last_w = {}
        self.readers = {}
        self.dma_last = {}
        self.dma_slot = {}
        self.n_dma_sems = n_dma_sems
        self.barrier_deps = []
        self.n_bg = 4
        self.bg_slot = {}
        self.persist = set()

    def add(self, eng, fn, r=(), w=(), dma=None, bg=False):
        idx = len(self.ops)
        raw = set()
        other = set()
        for k in r:
            if k in self.last_w:
                raw.add(self.last_w[k])
        for k in w:
            if k in self.last_w:
                other.add(self.last_w[k])
            other.update(self.readers.get(k, ()))
        slot = None
        if dma is not None:
            if bg:
                if dma not in self.bg_slot:
                    self.bg_slot[dma] = self.n_dma_sems + len(self.bg_slot) % self.n_bg
                slot = self.bg_slot[dma]
                self.persist.update(w)
            else:
                if dma not in self.dma_slot:
                    self.dma_slot[dma] = len(self.dma_slot) % self.n_dma_sems
                slot = self.dma_slot[dma]
            if slot in self.dma_last:
                raw.add(self.dma_last[slot])
            self.dma_last[slot] = idx
        for k in r:
            self.readers.setdefault(k, []).append(idx)
        for k in w:
            self.last_w[k] = idx
            self.readers[k] = []
        deps = set() if bg else set(self.barrier_deps)
        for d in raw | other:
            p = self.ops[d]
            if p["slot"] is None and slot is None and p["eng"] == eng:
                if eng == "pe" or d not in raw:
                    continue
            deps.add(d)
        deps.discard(idx)
        self.ops.append(dict(eng=eng, fn=fn, deps=deps, slot=slot, sem=None, val=0))
        return idx

    def barrier(self, final=False):
        last = {}
        for i, op in enumerate(self.ops):
            if op["slot"] is not None and op["slot"] >= self.n_dma_sems and not final:
                continue
            key = ("dma", op["slot"]) if op["slot"] is not None else ("eng", op["eng"])
            last[key] = i
        self.barrier_deps = sorted(last.values())
        self.last_w = {k: v for k, v in self.last_w.items() if k in self.persist}
        self.readers = {k: v for k, v in self.readers.items() if k in self.persist}

    def emit(self):
        nc = self.nc
        ops = self.ops
        self.barrier(final=True)
        self.add("sp", None)
        needed = set()
        for op in ops:
            needed.update(op["deps"])
        with contextlib.ExitStack() as st:
            eng_sem = {e: st.enter_context(nc.semaphore("s_" + e)) for e in self.ENGS}
            nslots = self.n_dma_sems + self.n_bg
            dma_sems = [st.enter_context(nc.semaphore("d_%d" % i)) for i in range(nslots)]
            cnt_e = {e: 0 for e in self.ENGS}
            cnt_d = [0] * nslots
            for i, op in enumerate(ops):
                if op["slot"] is not None:
                    cnt_d[op["slot"]] += 16
                    op["sem"] = ("d", op["slot"])
                    op["val"] = cnt_d[op["slot"]]
                elif i in needed:
                    cnt_e[op["eng"]] += 1
                    op["sem"] = ("e", op["eng"])
                    op["val"] = cnt_e[op["eng"]]
            per_eng = {e: [] for e in self.ENGS}
            for op in ops:
                per_eng[op["eng"]].append(op)

            def semh(s):
                return eng_sem[s[1]] if s[0] == "e" else dma_sems[s[1]]

            def run(e_obj, ename):
                waited = {}
                for op in per_eng[ename]:
                    for d in sorted(op["deps"]):
                        p = ops[d]
                        if waited.get(p["sem"], 0) >= p["val"]:
                            continue
                        e_obj.wait_ge(semh(p["sem"]), p["val"])
                        waited[p["sem"]] = p["val"]
                    if op["fn"] is None:
                        continue
                    ins = op["fn"](e_obj)
                    if op["sem"] is not None:
                        ins.then_inc(semh(op["sem"]), 16 if op["sem"][0] == "d" else 1)

            with nc.Block() as block:
                @block.sync
                def _(e):
                    run(e, "sp")

                @block.tensor
                def _(e):
                    run(e, "pe")

                @block.scalar
                def _(e):
                    run(e, "act")

                @block.vector
                def _(e):
                    run(e, "dve")

                @block.gpsimd
                def _(e):
                    run(e, "pool")
        return len(ops)


def _hi_lo(v):
    v = np.asarray(v, np.float32)
    hi = v.astype(ml_dtypes.bfloat16).astype(np.float32)
    lo = (v - hi).astype(ml_dtypes.bfloat16).astype(np.float32)
    return hi, lo


def make_consts():
    c = {}
    c["c_ident"] = np.eye(128, dtype=np.float32)
    qaug = np.zeros((4, 8, T), np.float32)
    kaug = np.zeros((4, 2, 8, T), np.float32)
    dp = np.zeros((4, 2, 128, 128), np.float32)
    ii = (np.arange(T) % 512).astype(np.float32)
    jj = (np.arange(T) % 128).astype(np.float32)
    for h in range(4):
        m = np.float32(S_A[h])
        hi, lo = _hi_lo(m * ii)
        qaug[h, 0] = -hi
        qaug[h, 1] = -lo
        qaug[h, 2] = 1.0
        qaug[h, 3] = 1.0
        hi, lo = _hi_lo(m * jj)
        kaug[h, 0, 0] = 1.0
        kaug[h, 0, 1] = 1.0
        kaug[h, 0, 2] = hi
        kaug[h, 0, 3] = lo
        kaug[h, 1] = -kaug[h, 0]
        pj = np.arange(128, dtype=np.float32)[:, None]
        pi = np.arange(128, dtype=np.float32)[None, :]
        dmat = -2.0 * m * np.maximum(pj - pi, 0.0)
        dp[h, 0], dp[h, 1] = _hi_lo(dmat)
    c["c_qaug"] = qaug
    c["c_kaug"] = kaug
    c["c_dp"] = dp
    p = np.arange(128, dtype=np.float32)[:, None]
    i = np.arange(128, dtype=np.float32)[None, :]
    bb = np.zeros((1, 128, 7, 512), np.float32)
    for h in range(1):
        m = np.float32(1.0)
        k = 0
        for pi_, (win, dil) in enumerate(B_PATTERNS):
            d1 = np.abs(i - (p - 64.0))
            d2 = np.abs(i - (p + 64.0))
            t1 = np.where(d1 <= 64.0, -m * dil * d1, -1.0e32).astype(np.float32)
            t2 = np.where(d2 <= 64.0, -m * dil * d2, -1.0e32).astype(np.float32)
            t1f = t1.copy()
            t1f[0:64, :] = -1.0e32
            t2l = t2.copy()
            t2l[64:128, :] = -1.0e32
            mid = np.concatenate([t1, t2, t1, t2], 1)
            first = np.concatenate([t1f, t2, t1, t2], 1)
            last = np.concatenate([t1, t2, t1, t2l], 1)
            both = np.concatenate([t1f, t2, t1, t2l], 1)
            if pi_ < 2:
                bb[h, :, k] = mid
                bb[h, :, k + 1] = first
                bb[h, :, k + 2] = last
                k += 3
            else:
                bb[h, :, k] = both
                k += 1
    c["c_bbias"] = bb[0]
    cb = np.zeros((2, 128, 3, 384), np.float32)
    for g in range(2):
        for rep in range(3):
            m = np.float32(S_C[3 * g + rep])
            for kt in range(3):
                dist = np.abs(i - (p + 128.0 * (kt - 1)))
                cb[g, :, kt, rep * 128:(rep + 1) * 128] = np.where(dist <= 128.0, -m * dist, NEG)
    c["c_cbias"] = cb
    return c


CONST_SHAPES = {
    "c_ident": [128, 128], "c_qaug": [4, 8, T], "c_kaug": [4, 2, 8, T], "c_dp": [4, 2, 128, 128],
    "c_bbias": [128, 7, 512], "c_cbias": [2, 128, 3, 384],
}


def build_program(debug=False, phases=None, layers=(0, 1)):
    nc = bass.Bass("TRN2", target_bir_lowering=False)

    def din(name, shape, dt=F32):
        return nc.dram_tensor(name, shape, dt, kind="ExternalInput").ap()

    def dscr(name, shape, dt):
        return nc.dram_tensor(name, shape, dt, kind=("ExternalOutput" if debug else "Internal")).ap()

    x_in = din("x", [T, D])
    ccol = din("ccol", [128, 8])
    w_ada = din("w_ada", [DEPTH, D, 6 * D])
    b_ada = din("b_ada", [DEPTH, 6 * D])
    w_in = din("w_in", [DEPTH, D, INW])
    lam = din("lam", [DEPTH, 128])
    subln = din("subln_g", [DEPTH, 64])
    sink = din("sink", [DEPTH, 6])
    w_out = din("w_out", [DEPTH, D, D])
    ln_g = din("ln_g", [DEPTH, 2, D])
    ln_b = din("ln_b", [DEPTH, 2, D])
    w_gu = din("w_gu", [DEPTH, D, 2 * FFN])
    w_down = din("w_down", [DEPTH, FFN, D])
    cst = {k: din(k, v) for k, v in CONST_SHAPES.items()}
    y_out = nc.dram_tensor("y", [T, D], F32, kind="ExternalOutput").ap()

    qaT = dscr("qaT", [256, T], BF16)
    kaT = dscr("kaT", [256, T], BF16)
    qbT = dscr("qbT", [384, T], BF16)
    kbT = dscr("kbT", [384, T], BF16)
    qcT = dscr("qcT", [384, T], BF16)
    kcT = dscr("kcT", [128, T], BF16)
    vA = dscr("vA", [T, 260], BF16)
    vB = dscr("vB", [T, 390], BF16)
    vC = dscr("vC", [T, 130], BF16)
    mixT = dscr("mixT", [D, T], BF16)
    x1s = dscr("x1s", [T, D], F32)
    wguS = dscr("wguS", [NJ, 128, 8 * 256], BF16)
    wdnS = dscr("wdnS", [NJ, 128, D], BF16)
    dbg_mod = dscr("dbg_mod", [128, 4096 + 16], F32) if debug else None

    allp = phases is None

    def on(name):
        return allp or name in phases

    with nc.sbuf_tensor("pool", [128, POOLW], F32) as pool, nc.psum_tensor("ps", [128, 4096], F32) as ps:
        S = Sched(nc)

        class Mem:
            def __init__(self, base):
                self.off = base

            def f32(self, n, parts=None):
                v = pool[:, self.off:self.off + n]
                self.off += n
                assert self.off <= POOLW, self.off
                return v

            def bf16(self, n):
                nw = (n + 1) // 2
                v = pool[:, self.off:self.off + nw].bitcast(BF16)[:, 0:n]
                self.off += nw
                assert self.off <= POOLW, self.off
                return v

        def bank(i, n=512):
            return ps[:, i * 512:i * 512 + n]

        def bank_bf(i):
            return ps[:, i * 512:(i + 1) * 512].bitcast(BF16)

        PM = Mem(0)
        ident = PM.f32(128)
        ident_bf = PM.bf16(128)
        ones = PM.f32(128)
        cs_rep = PM.f32(1024).rearrange("p (k m) -> p k m", m=128)
        modcols = PM.f32(16)
        keepR = PM.f32(4096).rearrange("p (a n) -> p a n", n=1024)
        small = PM.f32(64)
        PBASE = PM.off

        S.add("sp", lambda e: e.dma_start(out=ident, in_=cst["c_ident"]), w=["ident"], dma="ident")
        S.add("pool", lambda e: e.dma_start(out=ident_bf, in_=cst["c_ident"]), w=["ident_bf"], dma="ident_bf")
        S.add("dve", lambda e: e.memset(ones, 1.0), w=["ones"])
        cs = small[:, 0:8]
        cs2 = small[:, 8:16]
        S.add("sp", lambda e: e.dma_start(out=cs, in_=ccol), w=["cs"], dma="cs")
        S.add("act", lambda e: e.activation(out=cs2, in_=cs, func=AF.Silu), r=["cs"], w=["cs2"])
        S.add("dve", lambda e: e.tensor_copy(out=cs_rep, in_=cs2.unsqueeze(2).to_broadcast([128, 8, 128])),
              r=["cs2"], w=["cs_rep"])
        S.barrier()

        def phase_M(l):
            M = Mem(PBASE)
            wst = [M.f32(3072) for _ in range(4)]
            bada = M.f32(6144)
            modR = M.f32(6144)
            S.add("sp", lambda e: e.dma_start(out=bada, in_=b_ada[l].partition_broadcast(128)), w=["bada"], dma="bada")
            ld = 0
            for half in range(2):
                for kc in range(8):
                    b = ld % 4
                    ld += 1
                    S.add("sp", lambda e, b=b, kc=kc, half=half: e.dma_start(
                        out=wst[b], in_=w_ada[l, kc * 128:(kc + 1) * 128, half * 3072:(half + 1) * 3072]),
                        w=["wst%d" % b], dma="wst%d" % b)
                    for n in range(6):
                        S.add("pe", lambda e, b=b, kc=kc, n=n: e.matmul(
                            bank(n), lhsT=cs_rep[:, kc, :], rhs=wst[b][:, n * 512:(n + 1) * 512],
                            start=(kc == 0), stop=(kc == 7)),
                            r=["wst%d" % b, "cs_rep"], w=["psM%d" % n])
                for n in range(6):
                    c0 = half * 3072 + n * 512
                    S.add("dve", lambda e, n=n, c0=c0: e.tensor_tensor(
                        out=modR[:, c0:c0 + 512], in0=bank(n), in1=bada[:, c0:c0 + 512], op=ALU.add),
                        r=["psM%d" % n, "bada"], w=["modR%d" % (c0 // 512)])
            allR = ["modR%d" % i for i in range(12)]
            for grp in range(4):
                for t4 in range(4):
                    idx = grp * 4 + t4
                    col0 = (1024 + idx * 128) if idx < 8 else ((idx - 8) * 128)
                    S.add("pe", lambda e, grp=grp, t4=t4, col0=col0: e.transpose(
                        out=bank(6 + grp % 2)[:, t4 * 128:(t4 + 1) * 128], in_=modR[:, col0:col0 + 128], identity=ident),
                        r=allR + ["ident"], w=["psT%d" % (grp % 2)])
                src = bank(6 + grp % 2).rearrange("p (a b) -> p a b", b=128)[:, :, 0]
                addv = 1.0 if grp < 2 else 0.0
                S.add("dve", lambda e, grp=grp, src=src, addv=addv: e.tensor_scalar(
                    out=modcols[:, grp * 4:(grp + 1) * 4], in0=src, scalar1=addv, scalar2=None, op0=ALU.add),
                    r=["psT%d" % (grp % 2)], w=["modcols"])
            S.add("dve", lambda e: e.tensor_scalar(out=keepR[:, 0, :], in0=modR[:, 2048:3072], scalar1=1.0,
                                                    scalar2=1.0 / ALPHA, op0=ALU.add, op1=ALU.mult), r=allR, w=["keep0"])
            S.add("dve", lambda e: e.tensor_scalar(out=keepR[:, 1, :], in0=modR[:, 4096:5120], scalar1=1.0,
                                                    scalar2=None, op0=ALU.add), r=allR, w=["keep1"])
            S.add("dve", lambda e: e.tensor_copy(out=keepR[:, 2, :], in_=modR[:, 3072:4096]), r=allR, w=["keep2"])
            S.add("dve", lambda e: e.tensor_scalar(out=keepR[:, 3, :], in0=modR[:, 5120:6144], scalar1=1.0,
                                                    scalar2=1.0 / ALPHA, op0=ALU.add, op1=ALU.mult), r=allR, w=["keep3"])
            if debug:
                S.add("sp", lambda e: e.dma_start(out=dbg_mod[:, 0:4096], in_=keepR.rearrange("p a n -> p (a n)")),
                      r=["keep0", "keep1", "keep2", "keep3"], dma="dbgm")
                S.add("sp", lambda e: e.dma_start(out=dbg_mod[:, 4096:4112], in_=modcols), r=["modcols"], dma="dbgm2")
            S.barrier()

        def phase_P1(l):
            M = Mem(PBASE)
            wbf = M.bf16(8 * INW).rearrange("p (k n) -> p k n", n=INW)
            xt = [M.f32(4096).rearrange("p (t n) -> p t n", n=1024) for _ in range(2)]
            hT = [M.bf16(4096).rearrange("p (k n) -> p k n", n=512) for _ in range(2)]
            stg = [M.bf16(512) for _ in range(4)]
            vst = [[M.bf16(4 * 65).rearrange("p (h d) -> p h d", d=65),
                    M.bf16(6 * 65).rearrange("p (h d) -> p h d", d=65),
                    M.bf16(2 * 65).rearrange("p (h d) -> p h d", d=65)] for _ in range(2)]
            xsrc = x_in if l == 0 else x1s
            for kc in range(8):
                for hf in range(2):
                    S.add("pool", lambda e, kc=kc, hf=hf: e.dma_start(
                        out=wbf[:, kc, hf * 1280:(hf + 1) * 1280],
                        in_=w_in[l, kc * 128:(kc + 1) * 128, hf * 1280:(hf + 1) * 1280]),
                        w=["wbf%d" % kc], dma="wbf%d_%d" % (kc, hf))
            for b in range(2):
                for gi in range(3):
                    S.add("pool", lambda e, b=b, gi=gi: e.memset(vst[b][gi], 1.0), w=["vst%d_%d" % (b, gi)])
            WB = ["wbf%d" % k for k in range(8)]

            def load_x(c):
                b = c % 2
                S.add("sp", lambda e, b=b, c=c: e.dma_start(
                    out=xt[b], in_=xsrc[c * 512:(c + 1) * 512, :].rearrange("(t p) n -> p t n", p=128)),
                    w=["xt%d" % b], dma="xt%d" % b)

            fm = []
            for i in range(2):
                fm.append((i * 128, qaT, i * 128, 32.0 ** -0.5))
            for i in range(2):
                fm.append((256 + i * 128, kaT, i * 128, 1.0))
            for i in range(3):
                fm.append((768 + i * 128, qbT, i * 128, 0.125))
            for i in range(3):
                fm.append((1152 + i * 128, kbT, i * 128, 1.0))
            for i in range(3):
                fm.append((1920 + i * 128, qcT, i * 128, 0.125))
            fm.append((2304, kcT, 0, 1.0))
            tm = [(512, 256, vA, 4), (1536, 384, vB, 6), (2432, 128, vC, 2)]

            load_x(0)
            ev = 0
            fmn = 0
            tmn = 0
            for c in range(8):
                if c + 1 < 8:
                    load_x(c + 1)
                b = c % 2
                for kc in range(8):
                    pb = kc % 2
                    for t4 in range(4):
                        S.add("pe", lambda e, b=b, kc=kc, t4=t4, pb=pb: e.transpose(
                            out=bank(pb)[:, t4 * 128:(t4 + 1) * 128], in_=xt[b][:, t4, kc * 128:(kc + 1) * 128],
                            identity=ident), r=["xt%d" % b, "ident"], w=["psT%d" % pb])
                    if ev % 2 == 0:
                        S.add("act", lambda e, b=b, kc=kc, pb=pb: e.activation(
                            out=hT[b][:, kc, :], in_=bank(pb), func=AF.Identity,
                            scale=modcols[:, kc:kc + 1], bias=modcols[:, 8 + kc:9 + kc]),
                            r=["psT%d" % pb, "modcols"], w=["hT%d_%d" % (b, kc)])
                    else:
                        S.add("dve", lambda e, b=b, kc=kc, pb=pb: e.tensor_scalar(
                            out=hT[b][:, kc, :], in0=bank(pb), scalar1=modcols[:, kc:kc + 1],
                            scalar2=modcols[:, 8 + kc:9 + kc], op0=ALU.mult, op1=ALU.add),
                            r=["psT%d" % pb, "modcols"], w=["hT%d_%d" % (b, kc)])
                    ev += 1
                HT = ["hT%d_%d" % (b, k) for k in range(8)]
                for (wc, dst, r0, scl) in fm:
                    pb = 2 + fmn % 3
                    sb = fmn % 4
                    fmn += 1
                    for kc in range(8):
                        S.add("pe", lambda e, kc=kc, wc=wc, pb=pb, b=b: e.matmul(
                            bank(pb), lhsT=wbf[:, kc, wc:wc + 128], rhs=hT[b][:, kc, :], start=(kc == 0), stop=(kc == 7)),
                            r=[WB[kc], HT[kc]], w=["psF%d" % pb])
                    if ev % 2 == 0:
                        S.add("act", lambda e, pb=pb, sb=sb, scl=scl: e.activation(
                            out=stg[sb], in_=bank(pb), func=AF.Copy, scale=float(scl)), r=["psF%d" % pb], w=["stg%d" % sb])
                    else:
                        S.add("dve", lambda e, pb=pb, sb=sb, scl=scl: e.tensor_scalar(
                            out=stg[sb], in0=bank(pb), scalar1=float(scl), scalar2=None, op0=ALU.mult),
                            r=["psF%d" % pb], w=["stg%d" % sb])
                    ev += 1
                    S.add("sp", lambda e, sb=sb, dst=dst, r0=r0, c=c: e.dma_start(
                        out=dst[r0:r0 + 128, c * 512:(c + 1) * 512], in_=stg[sb]), r=["stg%d" % sb], dma="stg%d" % sb)
                for t4 in range(4):
                    vb_ = tmn % 2
                    tmn += 1
                    for gi, (wc, ncol, dst, nh) in enumerate(tm):
                        pb = 5 + gi
                        for kc in range(8):
                            S.add("pe", lambda e, kc=kc, wc=wc, ncol=ncol, pb=pb, b=b, t4=t4: e.matmul(
                                bank(pb, ncol), lhsT=hT[b][:, kc, t4 * 128:(t4 + 1) * 128], rhs=wbf[:, kc, wc:wc + ncol],
                                start=(kc == 0), stop=(kc == 7)), r=[WB[kc], HT[kc]], w=["psV%d" % pb])
                        src = bank(pb, ncol).rearrange("p (h d) -> p h d", d=64)
                        if ev % 2 == 0:
                            S.add("act", lambda e, src=src, vb_=vb_, gi=gi: e.copy(out=vst[vb_][gi][:, :, 0:64], in_=src),
                                  r=["psV%d" % pb], w=["vst%d_%d" % (vb_, gi)])
                        else:
                            S.add("dve", lambda e, src=src, vb_=vb_, gi=gi: e.tensor_copy(out=vst[vb_][gi][:, :, 0:64], in_=src),
                                  r=["psV%d" % pb], w=["vst%d_%d" % (vb_, gi)])
                        ev += 1
                        t0 = c * 512 + t4 * 128
                        S.add("sp", lambda e, vb_=vb_, gi=gi, dst=dst, t0=t0: e.dma_start(
                            out=dst[t0:t0 + 128, :], in_=vst[vb_][gi].rearrange("p h d -> p (h d)")),
                            r=["vst%d_%d" % (vb_, gi)], dma="vst%d_%d" % (vb_, gi))
            S.barrier()

        def norm_store(osb_num, rl_in, rl_buf, bc_bank, obf, dst, keys_r, tag, n=512, bc_key=None):
            S.add("act", lambda e: e.activation(out=rl_buf[64:65, 0:n], in_=rl_in, func=AF.Ln), r=keys_r, w=["rl" + tag])
            S.add("act", lambda e: e.activation(out=rl_buf[64:65, 0:n], in_=rl_buf[64:65, 0:n], func=AF.Exp, scale=-1.0),
                  r=["rl" + tag], w=["rl" + tag])
            bck = bc_key if bc_key is not None else "bc" + tag
            S.add("pe", lambda e: e.matmul(bc_bank[0:64, 0:n], lhsT=ones[64:65, 0:64], rhs=rl_buf[64:65, 0:n], start=True, stop=True),
                  r=["rl" + tag, "ones"], w=[bck])
            S.add("dve", lambda e: e.tensor_tensor(out=obf[0:64, 0:n], in0=bc_bank[0:64, 0:n], in1=osb_num, op=ALU.mult),
                  r=keys_r + [bck], w=["obf" + tag])
            S.add("sp", lambda e: e.dma_start(out=dst, in_=obf[0:64, 0:n]), r=["obf" + tag], dma="obf" + tag)

        def phase_A(l):
            lam_init = 0.8 - 0.6 * math.exp(-0.3 * l)
            M = Mem(PBASE)
            qa = [M.bf16(2 * T).rearrange("p (m t) -> p m t", t=T) for _ in range(2)]
            ka = [M.bf16(4 * T).rearrange("p (s m t) -> p s m t", m=2, t=T) for _ in range(2)]
            vah = [M.bf16(32 * 128).rearrange("p (k n) -> p k n", n=128) for _ in range(2)]
            dpt = M.bf16(4 * 2 * 128).rearrange("p (h s i) -> p h s i", s=2, i=128)
            pT = [M.bf16(1024).rearrange("p (m t) -> p m t", t=512) for _ in range(3)]
            osb = M.f32(1024).rearrange("p (m t) -> p m t", t=512)
            rl = M.f32(1024)
            t0 = M.f32(512)
            t1 = M.f32(512)
            dd = M.f32(512)
            sq = M.f32(512)
            tmpv = M.f32(512)
            rstd = M.f32(512)
            negh = M.f32(512)
            obf = M.bf16(512)
            lamt = M.f32(128)
            lw = M.f32(64)
            sm = M.f32(16)
            S.add("sp", lambda e: e.dma_start(out=lamt, in_=lam[l].partition_broadcast(128)), w=["lamt"], dma="lamt")
            S.add("dve", lambda e: e.tensor_tensor(out=lw[:, 0:32], in0=lamt[:, 0:32], in1=lamt[:, 32:64], op=ALU.mult), r=["lamt"], w=["lw0"])
            S.add("dve", lambda e: e.tensor_tensor(out=lw[:, 32:64], in0=lamt[:, 64:96], in1=lamt[:, 96:128], op=ALU.mult), r=["lamt"], w=["lw1"])
            S.add("dve", lambda e: e.tensor_reduce(out=sm[:, 0:2], in_=lw.rearrange("p (a b) -> p a b", b=32), axis=AX.X, op=ALU.add),
                  r=["lw0", "lw1"], w=["sm01"])
            S.add("act", lambda e: e.activation(out=sm[:, 2:4], in_=sm[:, 0:2], func=AF.Exp), r=["sm01"], w=["sm23"])
            S.add("dve", lambda e: e.scalar_tensor_tensor(out=sm[:, 4:5], in0=sm[:, 3:4], scalar=-lam_init, in1=sm[:, 2:3],
                                                          op0=ALU.add, op1=ALU.subtract), r=["sm23"], w=["lamneg"])
            S.add("sp", lambda e: e.dma_start(out=sm[0:64, 8:9], in_=subln[l].rearrange("(p o) -> p o", o=1)), w=["gc0"], dma="gc0")
            S.add("dve", lambda e: e.tensor_scalar(out=sm[0:64, 9:10], in0=sm[0:64, 8:9], scalar1=float(1.0 - lam_init), scalar2=None,
                                                    op0=ALU.mult), r=["gc0"], w=["gcol"])
            lamneg = sm[0:64, 4:5]
            gcol = sm[0:64, 9:10]
            epsc = sm[:, 12:13]
            S.add("pool", lambda e: e.memset(sm[:, 12:13], LN_EPS), w=["epsc"])
            for b2 in range(2):
                for m in range(2):
                    S.add("dve", lambda e, b2=b2, m=m: e.memset(qa[b2][:, m, :], 0.0), w=["qa%d" % b2])
                    for s_ in range(2):
                        S.add("dve", lambda e, b2=b2, m=m, s_=s_: e.memset(ka[b2][:, s_, m, :], 0.0), w=["ka%d" % b2])
                S.add("dve", lambda e, b2=b2: e.memset(vah[b2].rearrange("p k n -> p (k n)"), 0.0), w=["va%d" % b2])
            for h in range(4):
                S.add("pool", lambda e, h=h: e.dma_start(out=dpt[:, h, :, :], in_=cst["c_dp"][h].rearrange("s p i -> p s i")),
                      w=["dpt"], dma="dpt%d" % h)
            def load_head(h):
                hb = h % 2
                for q4 in range(4):
                    S.add("sp", lambda e, q4=q4, hb=hb, h=h: e.dma_start(
                        out=vah[hb][:, q4 * 8:(q4 + 1) * 8, 0:65],
                        in_=vA[q4 * 1024:(q4 + 1) * 1024, h * 65:(h + 1) * 65].rearrange("(k p) n -> p k n", p=128)),
                        w=["va%d" % hb], dma="va%d_%d" % (hb, q4))
                for m in range(2):
                    r0 = h * 64 + m * 32
                    S.add("sp", lambda e, hb=hb, m=m, r0=r0: e.dma_start(out=qa[hb][0:32, m, :], in_=qaT[r0:r0 + 32, :]),
                          w=["qa%d" % hb], dma="qa%d_%d" % (hb, m))
                    S.add("pool", lambda e, hb=hb, m=m, h=h: e.dma_start(
                        out=qa[hb][32:40, m, :].rearrange("p (a b) -> p a b", b=2048),
                        in_=cst["c_qaug"][h].rearrange("p (a b) -> p a b", b=2048)), w=["qa%d" % hb], dma="qg%d_%d" % (hb, m))
                    for s_ in range(2):
                        S.add("sp", lambda e, hb=hb, m=m, r0=r0, s_=s_: e.dma_start(
                            out=ka[hb][0:32, s_, m, :], in_=kaT[r0:r0 + 32, :]), w=["ka%d" % hb], dma="ka%d_%d_%d" % (hb, m, s_))
                        S.add("pool", lambda e, hb=hb, m=m, h=h, s_=s_: e.dma_start(
                            out=ka[hb][32:40, s_, m, :].rearrange("p (a b) -> p a b", b=2048),
                            in_=cst["c_kaug"][h, s_].rearrange("p (a b) -> p a b", b=2048)), w=["ka%d" % hb],
                            dma="kg%d_%d_%d" % (hb, m, s_))

            EPI_AT = [1, 2, 12, 13, 14, 15, 16, 17, 18, 19]
            load_head(0)
            it = 0
            pend_pv = None
            pend_epi = []
            for h in range(4):
                hb = h % 2
                if pend_pv is not None:
                    pend_pv()
                    pend_pv = None
                if h + 1 < 4:
                    load_head(h + 1)
                if h == 0:
                    for j in range(NJ):
                        for gu in range(2):
                            src = w_gu[l][:, gu * FFN + j * 128:gu * FFN + (j + 1) * 128].rearrange("(k p) c -> p k c", p=128)
                            dst = wguS[j].rearrange("p (k n) -> p k n", n=256)[:, :, gu * 128:(gu + 1) * 128]
                            S.add("pool", lambda e, src=src, dst=dst: e.dma_start(out=dst, in_=src), w=["wguS%d" % j],
                                  dma="wguS%d" % ((2 * j + gu) % 4), bg=True)
                mh = float(S_A[h])
                QK = ["qa%d" % hb, "ka%d" % hb]
                for qc in range(8):
                    ab = (h * 8 + qc) % 2
                    acc = ps[:, (4 + 2 * ab) * 512:(6 + 2 * ab) * 512].rearrange("p (m t) -> p m t", t=512)
                    acck = "acc%d" % ab
                    q0 = qc * 512
                    for kb in range(32):
                        sb = it % 2
                        pb = it % 3
                        Sv = ps[:, sb * 1024:(sb + 1) * 1024].rearrange("p (m t) -> p m t", t=512)
                        sk = "S%d" % sb
                        dl = kb - 4 * qc
                        k0 = kb * 128
                        for m in range(2):
                            if dl < 0 or dl > 3:
                                s_ = 0 if dl < 0 else 1
                                S.add("pe", lambda e, m=m, s_=s_, k0=k0, Sv=Sv, hb=hb, q0=q0: e.matmul(
                                    Sv[:, m, :], lhsT=ka[hb][:, s_, m, k0:k0 + 128], rhs=qa[hb][:, m, q0:q0 + 512],
                                    start=True, stop=True), r=QK, w=[sk])
                            else:
                                c0 = 128 * dl
                                if dl > 0:
                                    S.add("pe", lambda e, m=m, k0=k0, Sv=Sv, hb=hb, q0=q0, c0=c0: e.matmul(
                                        Sv[:, m, 0:c0], lhsT=ka[hb][:, 1, m, k0:k0 + 128], rhs=qa[hb][:, m, q0:q0 + c0],
                                        start=True, stop=True), r=QK, w=[sk])
                                S.add("pe", lambda e, m=m, k0=k0, Sv=Sv, hb=hb, q0=q0, c0=c0: e.matmul(
                                    Sv[:, m, c0:512], lhsT=ka[hb][:, 0, m, k0:k0 + 128], rhs=qa[hb][:, m, q0 + c0:q0 + 512],
                                    start=True, stop=False), r=QK, w=[sk])
                                for hl in range(2):
                                    S.add("pe", lambda e, m=m, Sv=Sv, c0=c0, hl=hl, h=h: e.matmul(
                                        Sv[:, m, c0:c0 + 128], lhsT=ident_bf, rhs=dpt[:, h, hl, :],
                                        start=False, stop=(hl == 1)), r=["dpt", "ident_bf"], w=[sk])
                        if pend_pv is not None:
                            pend_pv()
                            pend_pv = None
                        if dl < 0 or dl > 3:
                            bias = -mh * abs(512 * qc - 128 * kb)
                            S.add("act", lambda e, Sv=Sv, pb=pb, bias=bias: e.activation(
                                out=pT[pb].rearrange("p m t -> p (m t)"), in_=Sv.rearrange("p m t -> p (m t)"),
                                func=AF.Exp, bias=float(bias), scale=1.0), r=[sk], w=["pT%d" % pb])
                        else:
                            c0 = 128 * dl
                            if dl > 0:
                                S.add("act", lambda e, Sv=Sv, pb=pb, c0=c0, mh=mh: e.activation(
                                    out=pT[pb][:, :, 0:c0], in_=Sv[:, :, 0:c0], func=AF.Exp, bias=float(-mh * c0), scale=1.0),
                                    r=[sk], w=["pT%d" % pb])
                            S.add("act", lambda e, Sv=Sv, pb=pb, c0=c0, mh=mh: e.activation(
                                out=pT[pb][:, :, c0:512], in_=Sv[:, :, c0:512], func=AF.Exp, bias=float(mh * c0), scale=1.0),
                                r=[sk], w=["pT%d" % pb])

                        def pv(pb=pb, kb=kb, acc=acc, acck=acck, hb=hb):
                            for m in range(2):
                                S.add("pe", lambda e, m=m: e.matmul(
                                    acc[:, m, :], lhsT=vah[hb][:, kb, :], rhs=pT[pb][:, m, :],
                                    start=(kb == 0), stop=(kb == 31)), r=["pT%d" % pb, "va%d" % hb], w=[acck])
                        pend_pv = pv
                        it += 1
                        if pend_epi and kb == EPI_AT[10 - len(pend_epi)]:
                            pend_epi.pop(0)()
                    def mk_epi(acc=acc, acck=acck, h=h, qc=qc):
                        st = []
                        st.append(lambda: S.add("dve", lambda e: e.tensor_copy(out=osb[0:65].rearrange("p m t -> p (m t)"),
                                                                                in_=acc[0:65].rearrange("p m t -> p (m t)")), r=[acck], w=["osb"]))
                        st.append(lambda: S.add("dve", lambda e: e.reciprocal(out=rl[64:65, :], in_=osb[64:65].rearrange("p m t -> p (m t)")),
                                                r=["osb"], w=["rl"]))
                        def bc():
                            for m in range(2):
                                S.add("pe", lambda e, m=m: e.matmul(acc[0:64, m, :], lhsT=ones[64:65, 0:64], rhs=rl[64:65, m * 512:(m + 1) * 512],
                                                                     start=True, stop=True), r=["rl", "ones"], w=[acck])
                        st.append(bc)
                        def mul():
                            S.add("dve", lambda e: e.tensor_tensor(out=t0[0:64], in0=acc[0:64, 0, :], in1=osb[0:64, 0, :], op=ALU.mult),
                                  r=[acck, "osb"], w=["t0"])
                            S.add("dve", lambda e: e.tensor_tensor(out=t1[0:64], in0=acc[0:64, 1, :], in1=osb[0:64, 1, :], op=ALU.mult),
                                  r=[acck, "osb"], w=["t1"])
                            S.add("dve", lambda e: e.scalar_tensor_tensor(out=dd[0:64], in0=t1[0:64], scalar=lamneg, in1=t0[0:64],
                                                                          op0=ALU.mult, op1=ALU.add), r=["t0", "t1", "lamneg"], w=["dd"])
                        st.append(mul)
                        st.append(lambda: S.add("dve", lambda e: e.tensor_tensor(out=sq[0:64], in0=dd[0:64], in1=dd[0:64], op=ALU.mult), r=["dd"], w=["sq"]))
                        st.append(lambda: S.add("pe", lambda e: e.matmul(acc[0:64, 0, :], lhsT=ones[0:64, 0:64], rhs=sq[0:64],
                                                                          start=True, stop=True), r=["sq", "ones"], w=[acck]))
                        st.append(lambda: S.add("act", lambda e: e.activation(out=tmpv[0:64], in_=acc[0:64, 0, :], func=AF.Ln, scale=1.0 / 64.0,
                                                                              bias=epsc[0:64, 0:1]), r=[acck, "epsc"], w=["tmpv"]))
                        st.append(lambda: S.add("act", lambda e: e.activation(out=rstd[0:64], in_=tmpv[0:64], func=AF.Exp, scale=-0.5),
                                                r=["tmpv"], w=["rstd"]))
                        st.append(lambda: S.add("dve", lambda e: e.scalar_tensor_tensor(out=obf[0:64], in0=dd[0:64], scalar=gcol, in1=rstd[0:64],
                                                                                        op0=ALU.mult, op1=ALU.mult),
                                                r=["dd", "rstd", "gcol"], w=["obfA"]))
                        st.append(lambda: S.add("sp", lambda e: e.dma_start(out=mixT[h * 64:(h + 1) * 64, qc * 512:(qc + 1) * 512], in_=obf[0:64]),
                                                r=["obfA"], dma="obfA"))
                        return st
                    while pend_epi:
                        pend_epi.pop(0)()
                    pend_epi = mk_epi()
            if pend_pv is not None:
                pend_pv()
            while pend_epi:
                pend_epi.pop(0)()
            S.barrier()

        def phase_B(l):
            M = Mem(PBASE)
            PAD = 64
            vb = M.bf16(3 * 33 * 390).rearrange("p (a t n) -> p a t n", a=3, t=33)
            qbs = [M.bf16(T) for _ in range(2)]
            kbs_ = [M.bf16(T + 2 * PAD) for _ in range(2)]
            qp = M.bf16(T)
            kp = M.bf16(T + 2 * PAD)
            bias = M.f32(7 * 512).rearrange("p (v n) -> p v n", n=512)
            accT = M.f32(T)
            Ssb = [M.f32(512) for _ in range(4)]
            pT = [M.bf16(512) for _ in range(4)]
            rl = [M.f32(512) for _ in range(2)]
            obf = [M.bf16(512) for _ in range(2)]
            S.add("sp", lambda e: e.dma_start(out=bias.rearrange("p v n -> p (v n)"), in_=cst["c_bbias"].rearrange("p v n -> p (v n)")),
                  w=["biasB"], dma="biasB")
            VBK = ["vb_%d" % i for i in range(84)]
            for a3 in range(3):
                for t3 in range(3):
                    S.add("dve", lambda e, a3=a3, t3=t3: e.memset(vb[:, a3, t3 * 11:(t3 + 1) * 11, :], 0.0), w=VBK)
            for (buf, key) in ((qbs[0], "qb0"), (kbs_[0], "kb0"), (qbs[1], "qb1"), (kbs_[1], "kb1"), (qp, "qp"), (kp, "kp")):
                S.add("dve", lambda e, buf=buf: e.memset(buf, 0.0), w=[key])
            nd = 0
            for pi, (win, d) in enumerate(B_PATTERNS):
                Lc = T // d
                nt = Lc // 128
                for r in range(d):
                    srcB = bass.AP(vB.tensor, r * 390, [[d * 390, 64], [128 * d * 390, nt], [1, 390]])
                    tA = r * nt + 1
                    tB = r * nt
                    ntA = nt
                    if (64 + 128 * (nt - 1) + 63) * d + r >= T:
                        ntA = nt - 1
                    if ntA > 0:
                        srcA = bass.AP(vB.tensor, (64 * d + r) * 390, [[d * 390, 64], [128 * d * 390, ntA], [1, 390]])
                        S.add("sp", lambda e, srcA=srcA, pi=pi, tA=tA, ntA=ntA: e.dma_start(out=vb[0:64, pi, tA:tA + ntA, :], in_=srcA),
                              w=[VBK[nd]], dma="vb%d" % (nd % 4))
                        nd += 1
                    S.add("sp", lambda e, srcB=srcB, pi=pi, tB=tB, nt=nt: e.dma_start(out=vb[64:128, pi, tB:tB + nt, :], in_=srcB),
                          w=[VBK[nd]], dma="vb%d" % (nd % 4))
                    nd += 1
            blocks = [(0, 0), (0, 1), (1, 1), (1, 2)]
            def load_qk(h):
                S.add("sp", lambda e, h=h: e.dma_start(out=qbs[h % 2][0:64, :], in_=qbT[h * 64:(h + 1) * 64, :]), w=["qb%d" % (h % 2)], dma="qb%d" % (h % 2))
                S.add("sp", lambda e, h=h: e.dma_start(out=kbs_[h % 2][0:64, PAD:PAD + T], in_=kbT[h * 64:(h + 1) * 64, :]), w=["kb%d" % (h % 2)], dma="kb%d" % (h % 2))

            load_qk(0)
            for h in range(6):
                mh = float(S_B[h])
                qb, kb_ = qbs[h % 2], kbs_[h % 2]
                qbk, kbk_ = "qb%d" % (h % 2), "kb%d" % (h % 2)
                if h + 1 < 6:
                    load_qk(h + 1)
                for pi, (win, d) in enumerate(B_PATTERNS):
                    Lc = T // d
                    ntc = Lc // 128
                    ng = ntc // 2
                    if pi == 0:
                        qs_, ks_, qk_, kk_ = qb, kb_, qbk, kbk_
                    else:
                        qs_, ks_, qk_, kk_ = qp, kp, "qp", "kp"
                        S.add("dve", lambda e, d=d, qb=qb: e.tensor_copy(out=qp[0:64, :].rearrange("p (r j) -> p r j", r=d),
                                                                         in_=qb[0:64, :].rearrange("p (j r) -> p r j", r=d)), r=[qbk], w=["qp"])
                        S.add("act", lambda e, d=d, kb_=kb_: e.copy(out=kp[0:64, PAD:PAD + T].rearrange("p (r j) -> p r j", r=d),
                                                                    in_=kb_[0:64, PAD:PAD + T].rearrange("p (j r) -> p r j", r=d)), r=[kbk_], w=["kp"])
                    groups = []
                    for r in range(d):
                        for gq in range(ng):
                            b0 = 2 * gq
                            tq = r * ntc + b0
                            if pi == 2:
                                var = 6
                            else:
                                var = 3 * pi + (1 if gq == 0 else (2 if gq == ng - 1 else 0))
                            groups.append((tq, var, 128 * b0 * d + r))

                    def emit_S(gi, groups=groups, qs_=qs_, ks_=ks_, qk_=qk_, kk_=kk_):
                        tq, var, s0 = groups[gi]
                        sbk = gi % 4
                        Sb = bank(sbk)
                        for bi, (qi, ci) in enumerate(blocks):
                            qs = 128 * (tq + qi)
                            ks = PAD + 128 * (tq + ci) - 64
                            S.add("pe", lambda e, bi=bi, qs=qs, ks=ks: e.matmul(
                                Sb[:, bi * 128:(bi + 1) * 128], lhsT=ks_[:, ks:ks + 128], rhs=qs_[:, qs:qs + 128],
                                start=True, stop=True), r=[qk_, kk_], w=["SB%d" % sbk])

                    def emit_rest(gi, groups=groups, h=h, pi=pi, d=d, mh=mh):
                        tq, var, s0 = groups[gi]
                        sbk = gi % 4
                        obk = gi % 2
                        Sb = bank(sbk)
                        ob = bank(4 + obk)
                        S.add("dve", lambda e: e.scalar_tensor_tensor(out=Ssb[sbk], in0=bias[:, var, :], scalar=mh, in1=Sb,
                                                                      op0=ALU.mult, op1=ALU.add),
                              r=["SB%d" % sbk, "biasB"], w=["Ssb%d" % sbk])
                        S.add("act", lambda e: e.activation(out=pT[sbk], in_=Ssb[sbk], func=AF.Exp), r=["Ssb%d" % sbk], w=["pTB%d" % sbk])
                        for bi, (qi, ci) in enumerate(blocks):
                            S.add("pe", lambda e, bi=bi, qi=qi, ci=ci: e.matmul(
                                ob[0:65, qi * 128:(qi + 1) * 128], lhsT=vb[:, pi, tq + ci, h * 65:(h + 1) * 65],
                                rhs=pT[sbk][:, bi * 128:(bi + 1) * 128], start=(bi % 2 == 0), stop=(bi % 2 == 1)),
                                r=["pTB%d" % sbk] + VBK, w=["oB%d" % obk])

                    def emit_tail(gi, groups=groups, pi=pi, d=d):
                        tq, var, s0 = groups[gi]
                        obk = gi % 2
                        ob = bank(4 + obk)
                        dst = accT[0:65, s0:s0 + 255 * d + 1:d]
                        if pi == 0:
                            S.add("act", lambda e: e.copy(out=dst, in_=ob[0:65, 0:256]), r=["oB%d" % obk], w=["accT"])
                        else:
                            S.add("dve", lambda e: e.tensor_tensor(out=dst, in0=ob[0:65, 0:256], in1=dst, op=ALU.add),
                                  r=["oB%d" % obk, "accT"], w=["accT"])

                    emit_S(0)
                    emit_S(1)
                    emit_S(2)
                    for gi in range(len(groups)):
                        if gi + 3 < len(groups):
                            emit_S(gi + 3)
                        emit_rest(gi)
                        if gi >= 1:
                            emit_tail(gi - 1)
                    emit_tail(len(groups) - 1)
                for c in range(8):
                    norm_store(accT[0:64, c * 512:(c + 1) * 512], accT[64:65, c * 512:(c + 1) * 512], rl[c % 2], bank(6 + c % 2), obf[c % 2],
                               mixT[256 + h * 64:256 + (h + 1) * 64, c * 512:(c + 1) * 512], ["accT"], "B%d" % (c % 2))
            S.barrier()

        def phase_C(l):
            M = Mem(PBASE)
            PADC = 128
            G = []
            for g in range(2):
                d_ = dict(
                    kc=M.bf16(T + 2 * PADC), qc=M.bf16(3 * T).rearrange("p (r t) -> p r t", t=T),
                    vc=M.bf16(32 * 65).rearrange("p (k n) -> p k n", n=65), cb=M.f32(3 * 384).rearrange("p (k n) -> p k n", n=384),
                    Ssb=M.f32(3 * 384).rearrange("p (k n) -> p k n", n=384), pT=M.bf16(3 * 384).rearrange("p (k n) -> p k n", n=384),
                    osbc=[M.f32(3 * 512).rearrange("p (r t) -> p r t", t=512) for _ in range(2)],
                    rlin=M.f32(512), rl=M.f32(512), obf=M.bf16(512))
                G.append(d_)
            es = M.f32(16)
            S.add("sp", lambda e: e.dma_start(out=es[64:65, 0:6], in_=sink[l:l + 1, :]), w=["es0"], dma="es0")
            S.add("act", lambda e: e.activation(out=es[64:65, 8:14], in_=es[64:65, 0:6], func=AF.Exp), r=["es0"], w=["es"])
            for g in range(2):
                d_ = G[g]
                S.add("dve", lambda e, d_=d_: e.memset(d_["kc"], 0.0), w=["kc%d" % g])
                S.add("dve", lambda e, d_=d_: e.memset(d_["qc"][64:128].rearrange("p r t -> p (r t)"), 0.0), w=["qcz%d" % g])
                S.add("sp", lambda e, g=g, d_=d_: e.dma_start(out=d_["kc"][0:64, PADC:PADC + T], in_=kcT[g * 64:(g + 1) * 64, :]),
                      w=["kc%d" % g], dma="kc%d" % g)
                for rep in range(3):
                    hq = 3 * g + rep
                    S.add("sp", lambda e, rep=rep, hq=hq, d_=d_: e.dma_start(out=d_["qc"][0:64, rep, :], in_=qcT[hq * 64:(hq + 1) * 64, :]),
                          w=["qc%d_%d" % (g, rep)], dma="qc%d_%d" % (g, rep))
                for q4 in range(4):
                    S.add("sp", lambda e, g=g, q4=q4, d_=d_: e.dma_start(
                        out=d_["vc"][:, q4 * 8:(q4 + 1) * 8, :],
                        in_=vC[q4 * 1024:(q4 + 1) * 1024, g * 65:(g + 1) * 65].rearrange("(k p) n -> p k n", p=128)),
                        w=["vc%d_%d" % (g, q4)], dma="vc%d_%d" % (g, q4))
                S.add("sp", lambda e, g=g, d_=d_: e.dma_start(out=d_["cb"].rearrange("p k n -> p (k n)"),
                                                              in_=cst["c_cbias"][g].rearrange("p k n -> p (k n)")), w=["cb%d" % g], dma="cb%d" % g)

            def kts_of(qb_):
                return [kt for kt in range(3) if 0 <= qb_ + kt - 1 <= 31]

            def c_S(g, qb_):
                d_ = G[g]
                for kt in kts_of(qb_):
                    kbk = qb_ + kt - 1
                    Sb = bank(3 * g + kt, 384)
                    S.add("pe", lambda e, Sb=Sb, kbk=kbk: e.matmul(
                        Sb, lhsT=d_["kc"][:, PADC + kbk * 128:PADC + (kbk + 1) * 128], rhs=d_["qc"][:, :, qb_ * 128:(qb_ + 1) * 128],
                        start=True, stop=True), r=["kc%d" % g, "qcz%d" % g] + ["qc%d_%d" % (g, r_) for r_ in range(3)], w=["SC%d_%d" % (g, kt)])

            def c_exp(g, qb_):
                d_ = G[g]
                kts = kts_of(qb_)
                for kt in kts:
                    Sb = bank(3 * g + kt, 384)
                    S.add("dve", lambda e, Sb=Sb, kt=kt: e.tensor_tensor(out=d_["Ssb"][:, kt, :], in0=Sb, in1=d_["cb"][:, kt, :], op=ALU.add),
                          r=["SC%d_%d" % (g, kt), "cb%d" % g], w=["SsbC%d" % g])
                lo, hi = kts[0], kts[-1] + 1
                S.add("act", lambda e: e.activation(out=d_["pT"][:, lo:hi, :], in_=d_["Ssb"][:, lo:hi, :], func=AF.Exp),
                      r=["SsbC%d" % g], w=["pTC%d" % g])

            def c_pv(g, qb_):
                d_ = G[g]
                kts = kts_of(qb_)
                lo, hi = kts[0], kts[-1] + 1
                for kt in kts:
                    kbk = qb_ + kt - 1
                    S.add("pe", lambda e, kt=kt, kbk=kbk: e.matmul(
                        bank(6 + g, 384)[0:65, :], lhsT=d_["vc"][:, kbk, :], rhs=d_["pT"][:, kt, :], start=(kt == lo), stop=(kt == hi - 1)),
                        r=["pTC%d" % g] + ["vc%d_%d" % (g, q4) for q4 in range(4)], w=["oC%d" % g])

            def c_tail(g, qb_):
                d_ = G[g]
                ch = qb_ // 4
                obk = ch % 2
                S.add("act", lambda e: e.copy(
                    out=d_["osbc"][obk][0:65, :, (qb_ % 4) * 128:(qb_ % 4 + 1) * 128],
                    in_=bank(6 + g, 384)[0:65, :].rearrange("p (r t) -> p r t", t=128)), r=["oC%d" % g], w=["osbc%d_%d" % (g, obk)])
                if qb_ % 4 == 3:
                    for rep in range(3):
                        hq = 3 * g + rep
                        S.add("dve", lambda e, rep=rep, hq=hq: e.tensor_scalar(
                            out=d_["rlin"][64:65, :], in0=d_["osbc"][obk][64:65, rep, :], scalar1=es[64:65, 8 + hq:9 + hq], scalar2=None, op0=ALU.add),
                            r=["osbc%d_%d" % (g, obk), "es"], w=["rlin%d" % g])
                        norm_store(d_["osbc"][obk][0:64, rep, :], d_["rlin"][64:65, :], d_["rl"], bank(6 + g), d_["obf"],
                                   mixT[640 + hq * 64:640 + (hq + 1) * 64, ch * 512:(ch + 1) * 512],
                                   ["rlin%d" % g, "osbc%d_%d" % (g, obk)], "C%d" % g, bc_key="oC%d" % g)

            wst_c = [M.f32(1024) for _ in range(2)]
            wdb_c = [M.bf16(1024) for _ in range(2)]
            wprep = []
            for j in range(NJ):
                def prep(j=j):
                    b = j % 2
                    S.add("sp", lambda e: e.dma_start(out=wst_c[b], in_=w_down[l, j * 128:(j + 1) * 128, :]), w=["wstc%d" % b], dma="wstc%d" % b)
                    S.add("pool", lambda e: e.tensor_tensor(out=wdb_c[b], in0=wst_c[b], in1=keepR[:, 3, :], op=ALU.mult),
                          r=["wstc%d" % b], w=["wdbc%d" % b])
                    S.add("pool", lambda e: e.dma_start(out=wdnS[j], in_=wdb_c[b]), r=["wdbc%d" % b], w=["wdnS%d" % j], dma="wdbc%d" % b)
                wprep.append(prep)
            c_S(0, 0)
            c_S(1, 0)
            for qb_ in range(32):
                if qb_ < NJ:
                    wprep[qb_]()
                for g in range(2):
                    c_exp(g, qb_)
                    if qb_ + 1 < 32:
                        c_S(g, qb_ + 1)
                for g in range(2):
                    c_pv(g, qb_)
                for g in range(2):
                    c_tail(g, qb_)
            S.barrier()

        def phase_P3(l):
            EPS1 = LN_EPS / (ALPHA * ALPHA)
            M = Mem(PBASE)
            wob = M.bf16(8 * 1024).rearrange("p (k n) -> p k n", n=1024)
            lnt = M.f32(6 * 1024).rearrange("p (a n) -> p a n", n=1024)
            mxT = M.bf16(8 * 512).rearrange("p (k t) -> p k t", t=512)
            xt = [M.f32(4096).rearrange("p (t n) -> p t n", n=1024) for _ in range(2)]
            zn = M.f32(1024)
            h2 = M.bf16(4 * 1024).rearrange("p (t n) -> p t n", n=1024)
            h2T = M.bf16(8 * 512).rearrange("p (k t) -> p k t", t=512)
            actT = M.bf16(NJ * 512).rearrange("p (j t) -> p j t", t=512)
            sg = [M.f32(512) for _ in range(2)]
            wgu = [M.bf16(8 * 256).rearrange("p (k n) -> p k n", n=256) for _ in range(3)]
            wdn = [M.bf16(1024) for _ in range(3)]
            wst = [M.f32(1024) for _ in range(2)]
            stt = M.f32(8 * 24).rearrange("p (s n) -> p s n", n=24)
            negh1 = M.f32(2)
            xdst = x1s if l == 0 else y_out
            S.add("pool", lambda e: e.memset(negh1, -0.5), w=["negh1"])
            for (a, src) in ((0, ln_g[l, 0]), (1, ln_b[l, 0]), (4, ln_g[l, 1]), (5, ln_b[l, 1])):
                S.add("sp", lambda e, a=a, src=src: e.dma_start(out=lnt[:, a, :], in_=src.partition_broadcast(128)), w=["lnt%d" % a], dma="lnt%d" % a)
            S.add("dve", lambda e: e.tensor_tensor(out=lnt[:, 2, :], in0=lnt[:, 0, :], in1=keepR[:, 1, :], op=ALU.mult), r=["lnt0"], w=["lnt2"])
            S.add("dve", lambda e: e.tensor_tensor(out=lnt[:, 3, :], in0=lnt[:, 1, :], in1=keepR[:, 1, :], op=ALU.mult), r=["lnt1"], w=["lnt3"])
            S.add("dve", lambda e: e.tensor_tensor(out=lnt[:, 3, :], in0=lnt[:, 3, :], in1=keepR[:, 2, :], op=ALU.add), r=["lnt3"], w=["lnt3"])
            for kc in range(8):
                b = kc % 2
                S.add("sp", lambda e, kc=kc, b=b: e.dma_start(out=wst[b], in_=w_out[l, kc * 128:(kc + 1) * 128, :]), w=["wst%d" % b], dma="wst%d" % b)
                S.add("dve", lambda e, kc=kc, b=b: e.tensor_tensor(out=wob[:, kc, :], in0=wst[b], in1=keepR[:, 0, :], op=ALU.mult),
                      r=["wst%d" % b], w=["wob%d" % kc])
            WOB = ["wob%d" % k for k in range(8)]

            def load_mx(c):
                S.add("sp", lambda e, c=c: e.dma_start(out=mxT, in_=mixT[:, c * 512:(c + 1) * 512].rearrange("(k p) t -> p k t", p=128)),
                      w=["mxT"], dma="mxT")

            def load_x(c):
                cb_ = c % 2
                xsrc = x_in if l == 0 else x1s
                S.add("sp", lambda e, c=c, cb_=cb_: e.dma_start(
                    out=xt[cb_], in_=xsrc[c * 512:(c + 1) * 512, :].rearrange("(t p) n -> p t n", p=128)), w=["xt%d_%d" % (cb_, t_) for t_ in range(4)],
                    dma="xt%d" % cb_)

            cnt = {"tm": 0, "ev": 0, "zn": 0}

            def wout(c):
                cb_ = c % 2
                for tb in range(4):
                    for n in range(2):
                        pb = cnt["tm"] % 2
                        cnt["tm"] += 1
                        for kc in range(8):
                            S.add("pe", lambda e, tb=tb, n=n, kc=kc, pb=pb: e.matmul(
                                bank(pb), lhsT=mxT[:, kc, tb * 128:(tb + 1) * 128], rhs=wob[:, kc, n * 512:(n + 1) * 512],
                                start=(kc == 0), stop=(kc == 7)), r=["mxT", WOB[kc]], w=["psO%d" % pb])
                        S.add("dve", lambda e, tb=tb, n=n, pb=pb, cb_=cb_: e.tensor_tensor(
                            out=xt[cb_][:, tb, n * 512:(n + 1) * 512], in0=bank(pb), in1=xt[cb_][:, tb, n * 512:(n + 1) * 512], op=ALU.add),
                            r=["psO%d" % pb, "xt%d_%d" % (cb_, tb)], w=["xt%d_%d" % (cb_, tb)])

            def layer_norm(c, tb, which):
                cb_ = c % 2
                y = xt[cb_][:, tb, :]
                yk = "xt%d_%d" % (cb_, tb)
                st = stt[:, tb + 4 * (which - 1), :]
                sk = "st%d" % (tb + 4 * (which - 1))
                S.add("dve", lambda e: e.bn_stats(out=st[:, 0:6], in_=y[:, 0:512]), r=[yk], w=[sk + "a"])
                S.add("dve", lambda e: e.bn_stats(out=st[:, 6:12], in_=y[:, 512:1024]), r=[yk], w=[sk + "b"])
                S.add("dve", lambda e: e.bn_aggr(out=st[:, 12:14], in_=st[:, 0:12]), r=[sk + "a", sk + "b"], w=[sk + "mv"])
                S.add("dve", lambda e: e.tensor_scalar(out=st[:, 14:15], in0=st[:, 13:14], scalar1=float(EPS1), scalar2=None, op0=ALU.add),
                      r=[sk + "mv"], w=[sk + "ve"])
                S.add("pool", lambda e: e.tensor_tensor(out=st[:, 15:16], in0=st[:, 14:15], in1=negh1[:, 0:1], op=ALU.pow),
                      r=[sk + "ve", "negh1"], w=[sk + "rs"])
                S.add("dve", lambda e: e.tensor_scalar(out=st[:, 16:17], in0=st[:, 12:13], scalar1=st[:, 15:16], scalar2=-1.0,
                                                        op0=ALU.mult, op1=ALU.mult), r=[sk + "mv", sk + "rs"], w=[sk + "nb"])
                zi = cnt["zn"] % 2
                cnt["zn"] += 1
                znb = zn if zi == 0 else wst[1]
                znk = "zn" if zi == 0 else "wst1"
                S.add("act", lambda e: e.activation(out=znb, in_=y, func=AF.Identity, scale=st[:, 15:16], bias=st[:, 16:17]),
                      r=[yk, sk + "rs", sk + "nb"], w=[znk])
                ga, ba = (0, 1) if which == 1 else (4, 5)
                S.add("pool", lambda e: e.tensor_tensor(out=y, in0=znb, in1=lnt[:, ga, :], op=ALU.mult), r=[znk, "lnt%d" % ga], w=[yk])
                S.add("pool", lambda e: e.tensor_tensor(out=y, in0=y, in1=lnt[:, ba, :], op=ALU.add), r=[yk, "lnt%d" % ba], w=[yk])
                if which == 1:
                    S.add("dve", lambda e: e.tensor_tensor(out=wst[0], in0=znb, in1=lnt[:, 2, :], op=ALU.mult), r=[znk, "lnt2"], w=["wst0"])
                    S.add("dve", lambda e: e.tensor_tensor(out=h2[:, tb, :], in0=wst[0], in1=lnt[:, 3, :], op=ALU.add),
                          r=["wst0", "lnt3"], w=["h2_%d" % tb])

            def transposes(c):
                for kc in range(8):
                    pb = 2 + kc % 2
                    for tb in range(4):
                        S.add("pe", lambda e, kc=kc, tb=tb, pb=pb: e.transpose(
                            out=bank_bf(pb)[:, tb * 128:(tb + 1) * 128], in_=h2[:, tb, kc * 128:(kc + 1) * 128], identity=ident_bf),
                            r=["h2_%d" % tb, "ident_bf"], w=["psT%d" % pb])
                    if cnt["ev"] % 2 == 0:
                        S.add("act", lambda e, kc=kc, pb=pb: e.copy(out=h2T[:, kc, :], in_=bank_bf(pb)[:, 0:512]), r=["psT%d" % pb], w=["h2T%d" % kc])
                    else:
                        S.add("dve", lambda e, kc=kc, pb=pb: e.tensor_copy(out=h2T[:, kc, :], in_=bank_bf(pb)[:, 0:512]), r=["psT%d" % pb], w=["h2T%d" % kc])
                    cnt["ev"] += 1

            def load_wgu(p):
                if p >= 8 * NJ:
                    return
                j = p % NJ
                wb_ = p % 3
                S.add("sp", lambda e, j=j, wb_=wb_: e.dma_start(out=wgu[wb_].rearrange("p k n -> p (k n)"), in_=wguS[j]),
                      r=["wguS%d" % j], w=["wgu%d" % wb_], dma="wgu%d" % wb_)

            def gu(c, hooks):
                H2T = ["h2T%d" % k for k in range(8)]
                for j in range(NJ):
                    load_wgu(c * NJ + j + 2)
                    wb_ = (c * NJ + j) % 3
                    pg = 4 + 2 * (j % 2)
                    for half in range(2):
                        for kc in range(8):
                            S.add("pe", lambda e, kc=kc, half=half, pg=pg, wb_=wb_: e.matmul(
                                bank(pg + half), lhsT=wgu[wb_][:, kc, half * 128:(half + 1) * 128], rhs=h2T[:, kc, :],
                                start=(kc == 0), stop=(kc == 7)), r=["wgu%d" % wb_, H2T[kc]], w=["psG%d" % (pg + half)])
                    sb_ = j % 2
                    S.add("act", lambda e, pg=pg, sb_=sb_: e.activation(out=sg[sb_], in_=bank(pg), func=AF.Silu), r=["psG%d" % pg], w=["sg%d" % sb_])
                    S.add("dve", lambda e, pg=pg, sb_=sb_, j=j: e.tensor_tensor(out=actT[:, j, :], in0=bank(pg + 1), in1=sg[sb_], op=ALU.mult),
                          r=["psG%d" % (pg + 1), "sg%d" % sb_], w=["actT%d" % j])
                    for f in hooks.get(j, ()):
                        f()

            def load_wdn(q):
                if q >= 16 * NJ:
                    return
                i = q % (2 * NJ)
                j, n = i % NJ, i // NJ
                wb_ = q % 3
                S.add("sp", lambda e, j=j, n=n, wb_=wb_: e.dma_start(out=wdn[wb_][:, 0:512], in_=wdnS[j][:, n * 512:(n + 1) * 512]),
                      r=["wdnS%d" % j], w=["wdn%d" % wb_], dma="wdn%d" % wb_)

            def down(c, mid):
                cb_ = c % 2
                for i in range(2 * NJ):
                    load_wdn(c * 2 * NJ + i + 2)
                    j, n = i % NJ, i // NJ
                    wb_ = (c * 2 * NJ + i) % 3
                    for tb in range(4):
                        S.add("pe", lambda e, j=j, tb=tb, wb_=wb_: e.matmul(
                            bank(4 + tb), lhsT=actT[:, j, tb * 128:(tb + 1) * 128], rhs=wdn[wb_][:, 0:512],
                            start=(j == 0), stop=(j == NJ - 1)), r=["actT%d" % j, "wdn%d" % wb_], w=["psG%d" % (4 + tb)])
                    if j == NJ - 1:
                        for tb in range(4):
                            S.add("dve", lambda e, tb=tb, n=n, cb_=cb_: e.tensor_tensor(
                                out=xt[cb_][:, tb, n * 512:(n + 1) * 512], in0=bank(4 + tb), in1=xt[cb_][:, tb, n * 512:(n + 1) * 512], op=ALU.add),
                                r=["psG%d" % (4 + tb), "xt%d_%d" % (cb_, tb)], w=["xt%d_%d" % (cb_, tb)])
                        if n == 0:
                            for f in mid:
                                f()

            def ln2_tile(c, tb):
                cb_ = c % 2
                if True:
                    layer_norm(c, tb, 2)
                    t0_ = c * 512 + tb * 128
                    S.add("pool", lambda e, tb=tb, t0_=t0_, cb_=cb_: e.dma_start(out=xdst[t0_:t0_ + 128, :], in_=xt[cb_][:, tb, :]),
                          r=["xt%d_%d" % (cb_, tb)], dma="xo%d_%d" % (cb_, tb))

            load_mx(0)
            load_x(0)
            load_wgu(0)
            load_wgu(1)
            load_wdn(0)
            load_wdn(1)
            wout(0)
            for tb in range(4):
                layer_norm(0, tb, 1)
            transposes(0)
            for c in range(8):
                hooks = {}
                if c + 1 < 8:
                    load_mx(c + 1)
                if c >= 1:
                    for tb in range(4):
                        hooks.setdefault(3 * tb, []).append(lambda c=c, tb=tb: ln2_tile(c - 1, tb))
                if c + 1 < 8:
                    hooks.setdefault(10, []).append(lambda c=c: load_x(c + 1))
                gu(c, hooks)
                mid = []
                if c + 1 < 8:
                    wout(c + 1)
                    layer_norm(c + 1, 0, 1)
                    layer_norm(c + 1, 1, 1)
                    mid = [lambda c=c: layer_norm(c + 1, 2, 1), lambda c=c: layer_norm(c + 1, 3, 1)]
                down(c, mid)
                if c + 1 < 8:
                    transposes(c + 1)
            for tb in range(4):
                ln2_tile(7, tb)
            S.barrier()

        for l in layers:
            if on("M"):
                phase_M(l)
            if on("P1"):
                phase_P1(l)
            if on("A"):
                phase_A(l)
            if on("B"):
                phase_B(l)
            if on("C"):
                phase_C(l)
            if on("P3"):
                phase_P3(l)

        n_ops = S.emit()
    return nc, n_ops


def kernel(**inputs):
    nc, _ = build_program(debug=False)
    cores = list(range(NCORES))
    maps = make_in_maps(inputs, cores)
    res = run_bass_kernel_spmd(nc, maps, core_ids=cores)
    return np.stack([np.asarray(r["y"], dtype=np.float32) for r in res.results], axis=0)


def make_in_maps(inputs, cores):
    f = lambda a: np.ascontiguousarray(np.asarray(a, dtype=np.float32))
    consts = make_consts()
    shared = {
        "w_ada": f(inputs["w_ada"]), "b_ada": f(inputs["b_ada"]), "w_in": f(inputs["w_in"]),
        "lam": f(inputs["lam"]).reshape(DEPTH, 128), "subln_g": f(inputs["subln_g"]), "sink": f(inputs["sink"]),
        "w_out": f(inputs["w_out"]), "ln_g": f(inputs["ln_g"]), "ln_b": f(inputs["ln_b"]),
        "w_gu": f(inputs["w_gu"]), "w_down": f(inputs["w_down"]),
    }
    shared.update(consts)
    maps = []
    x = np.asarray(inputs["x"], dtype=np.float32)
    c = np.asarray(inputs["c"], dtype=np.float32)
    for b in cores:
        m = dict(shared)
        m["x"] = np.ascontiguousarray(x[b])
        m["ccol"] = np.ascontiguousarray(c[b].reshape(8, 128).T)
        maps.append(m)
    return maps
```

```python
import contextlib
import math
import numpy as np
import ml_dtypes
import concourse.bass as bass
import concourse.mybir as mybir
from concourse.bass_utils import run_bass_kernel_spmd

F32 = mybir.dt.float32
BF16 = mybir.dt.bfloat16
AF = mybir.ActivationFunctionType
ALU = mybir.AluOpType
AX = mybir.AxisListType

T = 4096
D = 1024
DEPTH = 2
NCORES = 8
FFN = 2816
NJ = FFN // 128
INW = 2560
LN_EPS = 1e-5
ALPHA = (2 * DEPTH) ** 0.25
NEG = -1.0e30
POOLW = 50000

SL = (2.0 ** (-8.0 * np.arange(1, 17) / 16)).astype(np.float32)
S_C = SL[0:6]
S_A = SL[6:10]
S_B = SL[10:16]
B_PATTERNS = ((128, 1), (512, 4), (2048, 16))


class Sched:
    ENGS = ("pe", "act", "dve", "pool", "sp")

    def __init__(self, nc, n_dma_sems=48):
        self.nc = nc
        self.ops = []
        self.last_w = {}
        self.readers = {}
        self.dma_last = {}
        self.dma_slot = {}
        self.n_dma_sems = n_dma_sems
        self.barrier_deps = []
        self.n_bg = 4
        self.bg_slot = {}
        self.persist = set()

    def add(self, eng, fn, r=(), w=(), dma=None, bg=False):
        idx = len(self.ops)
        raw = set()
        other = set()
        for k in r:
            if k in self.last_w:
                raw.add(self.last_w[k])
        for k in w:
            if k in self.last_w:
                other.add(self.last_w[k])
            other.update(self.readers.get(k, ()))
        slot = None
        if dma is not None:
            if bg:
                if dma not in self.bg_slot:
                    self.bg_slot[dma] = self.n_dma_sems + len(self.bg_slot) % self.n_bg
                slot = self.bg_slot[dma]
                self.persist.update(w)
            else:
                if dma not in self.dma_slot:
                    self.dma_slot[dma] = len(self.dma_slot) % self.n_dma_sems
                slot = self.dma_slot[dma]
            if slot in self.dma_last:
                raw.add(self.dma_last[slot])
            self.dma_last[slot] = idx
        for k in r:
            self.readers.setdefault(k, []).append(idx)
        for k in w:
            self.last_w[k] = idx
            self.readers[k] = []
        deps = set(self.barrier_deps)
        for d in raw | other:
            p = self.ops[d]
            if p["slot"] is None and slot is None and p["eng"] == eng:
                if eng == "pe" or d not in raw:
                    continue
            deps.add(d)
        deps.discard(idx)
        self.ops.append(dict(eng=eng, fn=fn, deps=deps, slot=slot, sem=None, val=0))
        return idx

    def barrier(self, final=False):
        last = {}
        for i, op in enumerate(self.ops):
            if op["slot"] is not None and op["slot"] >= self.n_dma_sems and not final:
                continue
            key = ("dma", op["slot"]) if op["slot"] is not None else ("eng", op["eng"])
            last[key] = i
        self.barrier_deps = sorted(last.values())
        self.last_w = {k: v for k, v in self.last_w.items() if k in self.persist}
        self.readers = {k: v for k, v in self.readers.items() if k in self.persist}

    def emit(self):
        nc = self.nc
        ops = self.ops
        self.barrier(final=True)
        self.add("sp", None)
        needed = set()
        for op in ops:
            needed.update(op["deps"])
        with contextlib.ExitStack() as st:
            eng_sem = {e: st.enter_context(nc.semaphore("s_" + e)) for e in self.ENGS}
            nslots = self.n_dma_sems + self.n_bg
            dma_sems = [st.enter_context(nc.semaphore("d_%d" % i)) for i in range(nslots)]
            cnt_e = {e: 0 for e in self.ENGS}
            cnt_d = [0] * nslots
            for i, op in enumerate(ops):
                if op["slot"] is not None:
                    cnt_d[op["slot"]] += 16
                    op["sem"] = ("d", op["slot"])
                    op["val"] = cnt_d[op["slot"]]
                elif i in needed:
                    cnt_e[op["eng"]] += 1
                    op["sem"] = ("e", op["eng"])
                    op["val"] = cnt_e[op["eng"]]
            per_eng = {e: [] for e in self.ENGS}
            for op in ops:
                per_eng[op["eng"]].append(op)

            def semh(s):
                return eng_sem[s[1]] if s[0] == "e" else dma_sems[s[1]]

            def run(e_obj, ename):
                waited = {}
                for op in per_eng[ename]:
                    for d in sorted(op["deps"]):
                        p = ops[d]
                        if waited.get(p["sem"], 0) >= p["val"]:
                            continue
                        e_obj.wait_ge(semh(p["sem"]), p["val"])
                        waited[p["sem"]] = p["val"]
                    if op["fn"] is None:
                        continue
                    ins = op["fn"](e_obj)
                    if op["sem"] is not None:
                        ins.then_inc(semh(op["sem"]), 16 if op["sem"][0] == "d" else 1)

            with nc.Block() as block:
                @block.sync
                def _(e):
                    run(e, "sp")

                @block.tensor
                def _(e):
                    run(e, "pe")

                @block.scalar
                def _(e):
                    run(e, "act")

                @block.vector
                def _(e):
                    run(e, "dve")

                @block.gpsimd
                def _(e):
                    run(e, "pool")
        return len(ops)


def _hi_lo(v):
    v = np.asarray(v, np.float32)
    hi = v.astype(ml_dtypes.bfloat16).astype(np.float32)
    lo = (v - hi).astype(ml_dtypes.bfloat16).astype(np.float32)
    return hi, lo


def make_consts():
    c = {}
    c["c_ident"] = np.eye(128, dtype=np.float32)
    qaug = np.zeros((4, 8, T), np.float32)
    kaug = np.zeros((4, 2, 8, T), np.float32)
    dp = np.zeros((4, 2, 128, 128), np.float32)
    ii = (np.arange(T) % 512).astype(np.float32)
    jj = (np.arange(T) % 128).astype(np.float32)
    for h in range(4):
        m = np.float32(S_A[h])
        hi, lo = _hi_lo(m * ii)
        qaug[h, 0] = -hi
        qaug[h, 1] = -lo
        qaug[h, 2] = 1.0
        qaug[h, 3] = 1.0
        hi, lo = _hi_lo(m * jj)
        kaug[h, 0, 0] = 1.0
        kaug[h, 0, 1] = 1.0
        kaug[h, 0, 2] = hi
        kaug[h, 0, 3] = lo
        kaug[h, 1] = -kaug[h, 0]
        pj = np.arange(128, dtype=np.float32)[:, None]
        pi = np.arange(128, dtype=np.float32)[None, :]
        dmat = -2.0 * m * np.maximum(pj - pi, 0.0)
        dp[h, 0], dp[h, 1] = _hi_lo(dmat)
    c["c_qaug"] = qaug
    c["c_kaug"] = kaug
    c["c_dp"] = dp
    p = np.arange(128, dtype=np.float32)[:, None]
    i = np.arange(128, dtype=np.float32)[None, :]
    bb = np.zeros((1, 128, 7, 512), np.float32)
    for h in range(1):
        m = np.float32(1.0)
        k = 0
        for pi_, (win, dil) in enumerate(B_PATTERNS):
            d1 = np.abs(i - (p - 64.0))
            d2 = np.abs(i - (p + 64.0))
            t1 = np.where(d1 <= 64.0, -m * dil * d1, -1.0e32).astype(np.float32)
            t2 = np.where(d2 <= 64.0, -m * dil * d2, -1.0e32).astype(np.float32)
            t1f = t1.copy()
            t1f[0:64, :] = -1.0e32
            t2l = t2.copy()
            t2l[64:128, :] = -1.0e32
            mid = np.concatenate([t1, t2, t1, t2], 1)
            first = np.concatenate([t1f, t2, t1, t2], 1)
            last = np.concatenate([t1, t2, t1, t2l], 1)
            both = np.concatenate([t1f, t2, t1, t2l], 1)
            if pi_ < 2:
                bb[h, :, k] = mid
                bb[h, :, k + 1] = first
                bb[h, :, k + 2] = last
                k += 3
            else:
                bb[h, :, k] = both
                k += 1
    c["c_bbias"] = bb[0]
    cb = np.zeros((2, 128, 3, 384), np.float32)
    for g in range(2):
        for rep in range(3):
            m = np.float32(S_C[3 * g + rep])
            for kt in range(3):
                dist = np.abs(i - (p + 128.0 * (kt - 1)))
                cb[g, :, kt, rep * 128:(rep + 1) * 128] = np.where(dist <= 128.0, -m * dist, NEG)
    c["c_cbias"] = cb
    return c


CONST_SHAPES = {
    "c_ident": [128, 128], "c_qaug": [4, 8, T], "c_kaug": [4, 2, 8, T], "c_dp": [4, 2, 128, 128],
    "c_bbias": [128, 7, 512], "c_cbias": [2, 128, 3, 384],
}


def build_program(debug=False, phases=None, layers=(0, 1)):
    nc = bass.Bass("TRN2", target_bir_lowering=False)

    def din(name, shape, dt=F32):
        return nc.dram_tensor(name, shape, dt, kind="ExternalInput").ap()

    def dscr(name, shape, dt):
        return nc.dram_tensor(name, shape, dt, kind=("ExternalOutput" if debug else "Internal")).ap()

    x_in = din("x", [T, D])
    ccol = din("ccol", [128, 8])
    w_ada = din("w_ada", [DEPTH, D, 6 * D])
    b_ada = din("b_ada", [DEPTH, 6 * D])
    w_in = din("w_in", [DEPTH, D, INW])
    lam = din("lam", [DEPTH, 128])
    subln = din("subln_g", [DEPTH, 64])
    sink = din("sink", [DEPTH, 6])
    w_out = din("w_out", [DEPTH, D, D])
    ln_g = din("ln_g", [DEPTH, 2, D])
    ln_b = din("ln_b", [DEPTH, 2, D])
    w_gu = din("w_gu", [DEPTH, D, 2 * FFN])
    w_down = din("w_down", [DEPTH, FFN, D])
    cst = {k: din(k, v) for k, v in CONST_SHAPES.items()}
    y_out = nc.dram_tensor("y", [T, D], F32, kind="ExternalOutput").ap()

    qaT = dscr("qaT", [256, T], BF16)
    kaT = dscr("kaT", [256, T], BF16)
    qbT = dscr("qbT", [384, T], BF16)
    kbT = dscr("kbT", [384, T], BF16)
    qcT = dscr("qcT", [384, T], BF16)
    kcT = dscr("kcT", [128, T], BF16)
    vA = dscr("vA", [T, 260], BF16)
    vB = dscr("vB", [T, 390], BF16)
    vC = dscr("vC", [T, 130], BF16)
    mixT = dscr("mixT", [D, T], BF16)
    x1s = dscr("x1s", [T, D], F32)
    wguS = dscr("wguS", [NJ, 128, 8 * 256], BF16)
    wdnS = dscr("wdnS", [NJ, 128, D], BF16)
    dbg_mod = dscr("dbg_mod", [128, 4096 + 16], F32) if debug else None

    allp = phases is None

    def on(name):
        return allp or name in phases

    with nc.sbuf_tensor("pool", [128, POOLW], F32) as pool, nc.psum_tensor("ps", [128, 4096], F32) as ps:
        S = Sched(nc)

        class Mem:
            def __init__(self, base):
                self.off = base

            def f32(self, n, parts=None):
                v = pool[:, self.off:self.off + n]
                self.off += n
                assert self.off <= POOLW, self.off
                return v

            def bf16(self, n):
                nw = (n + 1) // 2
                v = pool[:, self.off:self.off + nw].bitcast(BF16)[:, 0:n]
                self.off += nw
                assert self.off <= POOLW, self.off
                return v

        def bank(i, n=512):
            return ps[:, i * 512:i * 512 + n]

        def bank_bf(i):
            return ps[:, i * 512:(i + 1) * 512].bitcast(BF16)

        PM = Mem(0)
        ident = PM.f32(128)
        ident_bf = PM.bf16(128)
        ones = PM.f32(128)
        cs_rep = PM.f32(1024).rearrange("p (k m) -> p k m", m=128)
        modcols = PM.f32(16)
        keepR = PM.f32(4096).rearrange("p (a n) -> p a n", n=1024)
        small = PM.f32(64)
        PBASE = PM.off
        WBF_OFF = 38000
        WOB_OFF = 45000

        S.add("sp", lambda e: e.dma_start(out=ident, in_=cst["c_ident"]), w=["ident"], dma="ident")
        S.add("pool", lambda e: e.dma_start(out=ident_bf, in_=cst["c_ident"]), w=["ident_bf"], dma="ident_bf")
        S.add("dve", lambda e: e.memset(ones, 1.0), w=["ones"])
        cs = small[:, 0:8]
        cs2 = small[:, 8:16]
        S.add("sp", lambda e: e.dma_start(out=cs, in_=ccol), w=["cs"], dma="cs")
        S.add("act", lambda e: e.activation(out=cs2, in_=cs, func=AF.Silu), r=["cs"], w=["cs2"])
        S.add("dve", lambda e: e.tensor_copy(out=cs_rep, in_=cs2.unsqueeze(2).to_broadcast([128, 8, 128])),
              r=["cs2"], w=["cs_rep"])
        S.barrier()

        def phase_M(l):
            M = Mem(PBASE)
            wbf_m = Mem(WBF_OFF).bf16(8 * INW).rearrange("p (k n) -> p k n", n=INW)
            for kc in range(8):
                for hf in range(2):
                    S.add("pool", lambda e, kc=kc, hf=hf: e.dma_start(
                        out=wbf_m[:, kc, hf * 1280:(hf + 1) * 1280],
                        in_=w_in[l, kc * 128:(kc + 1) * 128, hf * 1280:(hf + 1) * 1280]),
                        w=["wbf%d" % kc], dma="wbf%d_%d" % (kc, hf), bg=True)
            wst = [M.f32(3072) for _ in range(4)]
            bada = M.f32(6144)
            modR = M.f32(6144)
            S.add("sp", lambda e: e.dma_start(out=bada, in_=b_ada[l].partition_broadcast(128)), w=["bada"], dma="bada")
            ld = 0
            for half in range(2):
                for kc in range(8):
                    b = ld % 4
                    ld += 1
                    S.add("sp", lambda e, b=b, kc=kc, half=half: e.dma_start(
                        out=wst[b], in_=w_ada[l, kc * 128:(kc + 1) * 128, half * 3072:(half + 1) * 3072]),
                        w=["wst%d" % b], dma="wst%d" % b)
                    for n in range(6):
                        S.add("pe", lambda e, b=b, kc=kc, n=n: e.matmul(
                            bank(n), lhsT=cs_rep[:, kc, :], rhs=wst[b][:, n * 512:(n + 1) * 512],
                            start=(kc == 0), stop=(kc == 7)),
                            r=["wst%d" % b, "cs_rep"], w=["psM%d" % n])
                for n in range(6):
                    c0 = half * 3072 + n * 512
                    S.add("dve", lambda e, n=n, c0=c0: e.tensor_tensor(
                        out=modR[:, c0:c0 + 512], in0=bank(n), in1=bada[:, c0:c0 + 512], op=ALU.add),
                        r=["psM%d" % n, "bada"], w=["modR%d" % (c0 // 512)])
            allR = ["modR%d" % i for i in range(12)]
            for grp in range(4):
                for t4 in range(4):
                    idx = grp * 4 + t4
                    col0 = (1024 + idx * 128) if idx < 8 else ((idx - 8) * 128)
                    S.add("pe", lambda e, grp=grp, t4=t4, col0=col0: e.transpose(
                        out=bank(6 + grp % 2)[:, t4 * 128:(t4 + 1) * 128], in_=modR[:, col0:col0 + 128], identity=ident),
                        r=allR + ["ident"], w=["psT%d" % (grp % 2)])
                src = bank(6 + grp % 2).rearrange("p (a b) -> p a b", b=128)[:, :, 0]
                addv = 1.0 if grp < 2 else 0.0
                S.add("dve", lambda e, grp=grp, src=src, addv=addv: e.tensor_scalar(
                    out=modcols[:, grp * 4:(grp + 1) * 4], in0=src, scalar1=addv, scalar2=None, op0=ALU.add),
                    r=["psT%d" % (grp % 2)], w=["modcols"])
            S.add("dve", lambda e: e.tensor_scalar(out=keepR[:, 0, :], in0=modR[:, 2048:3072], scalar1=1.0,
                                                    scalar2=1.0 / ALPHA, op0=ALU.add, op1=ALU.mult), r=allR, w=["keep0"])
            S.add("dve", lambda e: e.tensor_scalar(out=keepR[:, 1, :], in0=modR[:, 4096:5120], scalar1=1.0,
                                                    scalar2=None, op0=ALU.add), r=allR, w=["keep1"])
            S.add("dve", lambda e: e.tensor_copy(out=keepR[:, 2, :], in_=modR[:, 3072:4096]), r=allR, w=["keep2"])
            S.add("dve", lambda e: e.tensor_scalar(out=keepR[:, 3, :], in0=modR[:, 5120:6144], scalar1=1.0,
                                                    scalar2=1.0 / ALPHA, op0=ALU.add, op1=ALU.mult), r=allR, w=["keep3"])
            if debug:
                S.add("sp", lambda e: e.dma_start(out=dbg_mod[:, 0:4096], in_=keepR.rearrange("p a n -> p (a n)")),
                      r=["keep0", "keep1", "keep2", "keep3"], dma="dbgm")
                S.add("sp", lambda e: e.dma_start(out=dbg_mod[:, 4096:4112], in_=modcols), r=["modcols"], dma="dbgm2")
            S.barrier()

        def phase_P1(l):
            M = Mem(PBASE)
            wbf = Mem(WBF_OFF).bf16(8 * INW).rearrange("p (k n) -> p k n", n=INW)
            xt = [M.f32(4096).rearrange("p (t n) -> p t n", n=1024) for _ in range(2)]
            hT = [M.bf16(4096).rearrange("p (k n) -> p k n", n=512) for _ in range(2)]
            stg = [M.bf16(512) for _ in range(4)]
            vst = [[M.bf16(4 * 65).rearrange("p (h d) -> p h d", d=65),
                    M.bf16(6 * 65).rearrange("p (h d) -> p h d", d=65),
                    M.bf16(2 * 65).rearrange("p (h d) -> p h d", d=65)] for _ in range(2)]
            xsrc = x_in if l == 0 else x1s
            for b in range(2):
                for gi in range(3):
                    S.add("pool", lambda e, b=b, gi=gi: e.memset(vst[b][gi], 1.0), w=["vst%d_%d" % (b, gi)])
            WB = ["wbf%d" % k for k in range(8)]

            def load_x(c):
                b = c % 2
                S.add("sp", lambda e, b=b, c=c: e.dma_start(
                    out=xt[b], in_=xsrc[c * 512:(c + 1) * 512, :].rearrange("(t p) n -> p t n", p=128)),
                    w=["xt%d" % b], dma="xt%d" % b)

            fm = []
            for i in range(2):
                fm.append((i * 128, qaT, i * 128, 32.0 ** -0.5))
            for i in range(2):
                fm.append((256 + i * 128, kaT, i * 128, 1.0))
            for i in range(3):
                fm.append((768 + i * 128, qbT, i * 128, 0.125))
            for i in range(3):
                fm.append((1152 + i * 128, kbT, i * 128, 1.0))
            for i in range(3):
                fm.append((1920 + i * 128, qcT, i * 128, 0.125))
            fm.append((2304, kcT, 0, 1.0))
            tm = [(512, 256, vA, 4), (1536, 384, vB, 6), (2432, 128, vC, 2)]

            load_x(0)
            ev = 0
            fmn = 0
            tmn = 0
            for c in range(8):
                if c + 1 < 8:
                    load_x(c + 1)
                b = c % 2
                for kc in range(8):
                    pb = kc % 2
                    for t4 in range(4):
                        S.add("pe", lambda e, b=b, kc=kc, t4=t4, pb=pb: e.transpose(
                            out=bank(pb)[:, t4 * 128:(t4 + 1) * 128], in_=xt[b][:, t4, kc * 128:(kc + 1) * 128],
                            identity=ident), r=["xt%d" % b, "ident"], w=["psT%d" % pb])
                    if ev % 2 == 0:
                        S.add("act", lambda e, b=b, kc=kc, pb=pb: e.activation(
                            out=hT[b][:, kc, :], in_=bank(pb), func=AF.Identity,
                            scale=modcols[:, kc:kc + 1], bias=modcols[:, 8 + kc:9 + kc]),
                            r=["psT%d" % pb, "modcols"], w=["hT%d_%d" % (b, kc)])
                    else:
                        S.add("dve", lambda e, b=b, kc=kc, pb=pb: e.tensor_scalar(
                            out=hT[b][:, kc, :], in0=bank(pb), scalar1=modcols[:, kc:kc + 1],
                            scalar2=modcols[:, 8 + kc:9 + kc], op0=ALU.mult, op1=ALU.add),
                            r=["psT%d" % pb, "modcols"], w=["hT%d_%d" % (b, kc)])
                    ev += 1
                HT = ["hT%d_%d" % (b, k) for k in range(8)]
                for (wc, dst, r0, scl) in fm:
                    pb = 2 + fmn % 3
                    sb = fmn % 4
                    fmn += 1
                    for kc in range(8):
                        S.add("pe", lambda e, kc=kc, wc=wc, pb=pb, b=b: e.matmul(
                            bank(pb), lhsT=wbf[:, kc, wc:wc + 128], rhs=hT[b][:, kc, :], start=(kc == 0), stop=(kc == 7)),
                            r=[WB[kc], HT[kc]], w=["psF%d" % pb])
                    if ev % 2 == 0:
                        S.add("act", lambda e, pb=pb, sb=sb, scl=scl: e.activation(
                            out=stg[sb], in_=bank(pb), func=AF.Copy, scale=float(scl)), r=["psF%d" % pb], w=["stg%d" % sb])
                    else:
                        S.add("dve", lambda e, pb=pb, sb=sb, scl=scl: e.tensor_scalar(
                            out=stg[sb], in0=bank(pb), scalar1=float(scl), scalar2=None, op0=ALU.mult),
                            r=["psF%d" % pb], w=["stg%d" % sb])
                    ev += 1
                    S.add("sp", lambda e, sb=sb, dst=dst, r0=r0, c=c: e.dma_start(
                        out=dst[r0:r0 + 128, c * 512:(c + 1) * 512], in_=stg[sb]), r=["stg%d" % sb], dma="stg%d" % sb)
                for t4 in range(4):
                    vb_ = tmn % 2
                    tmn += 1
                    for gi, (wc, ncol, dst, nh) in enumerate(tm):
                        pb = 5 + gi
                        for kc in range(8):
                            S.add("pe", lambda e, kc=kc, wc=wc, ncol=ncol, pb=pb, b=b, t4=t4: e.matmul(
                                bank(pb, ncol), lhsT=hT[b][:, kc, t4 * 128:(t4 + 1) * 128], rhs=wbf[:, kc, wc:wc + ncol],
                                start=(kc == 0), stop=(kc == 7)), r=[WB[kc], HT[kc]], w=["psV%d" % pb])
                        src = bank(pb, ncol).rearrange("p (h d) -> p h d", d=64)
                        if ev % 2 == 0:
                            S.add("act", lambda e, src=src, vb_=vb_, gi=gi: e.copy(out=vst[vb_][gi][:, :, 0:64], in_=src),
                                  r=["psV%d" % pb], w=["vst%d_%d" % (vb_, gi)])
                        else:
                            S.add("dve", lambda e, src=src, vb_=vb_, gi=gi: e.tensor_copy(out=vst[vb_][gi][:, :, 0:64], in_=src),
                                  r=["psV%d" % pb], w=["vst%d_%d" % (vb_, gi)])
                        ev += 1
                        t0 = c * 512 + t4 * 128
                        S.add("sp", lambda e, vb_=vb_, gi=gi, dst=dst, t0=t0: e.dma_start(
                            out=dst[t0:t0 + 128, :], in_=vst[vb_][gi].rearrange("p h d -> p (h d)")),
                            r=["vst%d_%d" % (vb_, gi)], dma="vst%d_%d" % (vb_, gi))
            S.barrier()

        def norm_store(osb_num, rl_in, rl_buf, bc_bank, obf, dst, keys_r, tag, n=512, bc_key=None):
            S.add("act", lambda e: e.activation(out=rl_buf[64:65, 0:n], in_=rl_in, func=AF.Ln), r=keys_r, w=["rl" + tag])
            S.add("act", lambda e: e.activation(out=rl_buf[64:65, 0:n], in_=rl_buf[64:65, 0:n], func=AF.Exp, scale=-1.0),
                  r=["rl" + tag], w=["rl" + tag])
            bck = bc_key if bc_key is not None else "bc" + tag
            S.add("pe", lambda e: e.matmul(bc_bank[0:64, 0:n], lhsT=ones[64:65, 0:64], rhs=rl_buf[64:65, 0:n], start=True, stop=True),
                  r=["rl" + tag, "ones"], w=[bck])
            S.add("dve", lambda e: e.tensor_tensor(out=obf[0:64, 0:n], in0=bc_bank[0:64, 0:n], in1=osb_num, op=ALU.mult),
                  r=keys_r + [bck], w=["obf" + tag])
            S.add("sp", lambda e: e.dma_start(out=dst, in_=obf[0:64, 0:n]), r=["obf" + tag], dma="obf" + tag)

        def phase_A(l):
            lam_init = 0.8 - 0.6 * math.exp(-0.3 * l)
            M = Mem(PBASE)
            qa = [M.bf16(2 * T).rearrange("p (m t) -> p m t", t=T) for _ in range(2)]
            ka = [M.bf16(4 * T).rearrange("p (s m t) -> p s m t", m=2, t=T) for _ in range(2)]
            vah = [M.bf16(32 * 128).rearrange("p (k n) -> p k n", n=128) for _ in range(2)]
            dpt = M.bf16(4 * 2 * 128).rearrange("p (h s i) -> p h s i", s=2, i=128)
            pT = [M.bf16(1024).rearrange("p (m t) -> p m t", t=512) for _ in range(3)]
            osb = M.f32(1024).rearrange("p (m t) -> p m t", t=512)
            rl = M.f32(1024)
            t0 = M.f32(512)
            t1 = M.f32(512)
            dd = M.f32(512)
            sq = M.f32(512)
            tmpv = M.f32(512)
            rstd = M.f32(512)
            negh = M.f32(512)
            obf = M.bf16(512)
            lamt = M.f32(128)
            lw = M.f32(64)
            sm = M.f32(16)
            S.add("sp", lambda e: e.dma_start(out=lamt, in_=lam[l].partition_broadcast(128)), w=["lamt"], dma="lamt")
            S.add("dve", lambda e: e.tensor_tensor(out=lw[:, 0:32], in0=lamt[:, 0:32], in1=lamt[:, 32:64], op=ALU.mult), r=["lamt"], w=["lw0"])
            S.add("dve", lambda e: e.tensor_tensor(out=lw[:, 32:64], in0=lamt[:, 64:96], in1=lamt[:, 96:128], op=ALU.mult), r=["lamt"], w=["lw1"])
            S.add("dve", lambda e: e.tensor_reduce(out=sm[:, 0:2], in_=lw.rearrange("p (a b) -> p a b", b=32), axis=AX.X, op=ALU.add),
                  r=["lw0", "lw1"], w=["sm01"])
            S.add("act", lambda e: e.activation(out=sm[:, 2:4], in_=sm[:, 0:2], func=AF.Exp), r=["sm01"], w=["sm23"])
            S.add("dve", lambda e: e.scalar_tensor_tensor(out=sm[:, 4:5], in0=sm[:, 3:4], scalar=-lam_init, in1=sm[:, 2:3],
                                                          op0=ALU.add, op1=ALU.subtract), r=["sm23"], w=["lamneg"])
            S.add("sp", lambda e: e.dma_start(out=sm[0:64, 8:9], in_=subln[l].rearrange("(p o) -> p o", o=1)), w=["gc0"], dma="gc0")
            S.add("dve", lambda e: e.tensor_scalar(out=sm[0:64, 9:10], in0=sm[0:64, 8:9], scalar1=float(1.0 - lam_init), scalar2=None,
                                                    op0=ALU.mult), r=["gc0"], w=["gcol"])
            lamneg = sm[0:64, 4:5]
            gcol = sm[0:64, 9:10]
            epsc = sm[:, 12:13]
            S.add("pool", lambda e: e.memset(sm[:, 12:13], LN_EPS), w=["epsc"])
            for b2 in range(2):
                for m in range(2):
                    S.add("dve", lambda e, b2=b2, m=m: e.memset(qa[b2][:, m, :], 0.0), w=["qa%d" % b2])
                    for s_ in range(2):
                        S.add("dve", lambda e, b2=b2, m=m, s_=s_: e.memset(ka[b2][:, s_, m, :], 0.0), w=["ka%d" % b2])
                S.add("dve", lambda e, b2=b2: e.memset(vah[b2].rearrange("p k n -> p (k n)"), 0.0), w=["va%d" % b2])
            for h in range(4):
                S.add("pool", lambda e, h=h: e.dma_start(out=dpt[:, h, :, :], in_=cst["c_dp"][h].rearrange("s p i -> p s i")),
                      w=["dpt"], dma="dpt%d" % h)
            def load_head(h):
                hb = h % 2
                for q4 in range(4):
                    S.add("sp", lambda e, q4=q4, hb=hb, h=h: e.dma_start(
                        out=vah[hb][:, q4 * 8:(q4 + 1) * 8, 0:65],
                        in_=vA[q4 * 1024:(q4 + 1) * 1024, h * 65:(h + 1) * 65].rearrange("(k p) n -> p k n", p=128)),
                        w=["va%d" % hb], dma="va%d_%d" % (hb, q4))
                for m in range(2):
                    r0 = h * 64 + m * 32
                    S.add("sp", lambda e, hb=hb, m=m, r0=r0: e.dma_start(out=qa[hb][0:32, m, :], in_=qaT[r0:r0 + 32, :]),
                          w=["qa%d" % hb], dma="qa%d_%d" % (hb, m))
                    S.add("pool", lambda e, hb=hb, m=m, h=h: e.dma_start(
                        out=qa[hb][32:40, m, :].rearrange("p (a b) -> p a b", b=2048),
                        in_=cst["c_qaug"][h].rearrange("p (a b) -> p a b", b=2048)), w=["qa%d" % hb], dma="qg%d_%d" % (hb, m))
                    for s_ in range(2):
                        S.add("sp", lambda e, hb=hb, m=m, r0=r0, s_=s_: e.dma_start(
                            out=ka[hb][0:32, s_, m, :], in_=kaT[r0:r0 + 32, :]), w=["ka%d" % hb], dma="ka%d_%d_%d" % (hb, m, s_))
                        S.add("pool", lambda e, hb=hb, m=m, h=h, s_=s_: e.dma_start(
                            out=ka[hb][32:40, s_, m, :].rearrange("p (a b) -> p a b", b=2048),
                            in_=cst["c_kaug"][h, s_].rearrange("p (a b) -> p a b", b=2048)), w=["ka%d" % hb],
                            dma="kg%d_%d_%d" % (hb, m, s_))

            EPI_AT = [1, 2, 12, 13, 14, 15, 16, 17, 18, 19]
            load_head(0)
            it = 0
            pend_pv = None
            pend_epi = []
            for h in range(4):
                hb = h % 2
                if pend_pv is not None:
                    pend_pv()
                    pend_pv = None
                if h + 1 < 4:
                    load_head(h + 1)
                if h == 0:
                    for j in range(NJ):
                        for gu in range(2):
                            src = w_gu[l][:, gu * FFN + j * 128:gu * FFN + (j + 1) * 128].rearrange("(k p) c -> p k c", p=128)
                            dst = wguS[j].rearrange("p (k n) -> p k n", n=256)[:, :, gu * 128:(gu + 1) * 128]
                            S.add("pool", lambda e, src=src, dst=dst: e.dma_start(out=dst, in_=src), w=["wguS%d" % j],
                                  dma="wguS%d" % ((2 * j + gu) % 4), bg=True)
                mh = float(S_A[h])
                QK = ["qa%d" % hb, "ka%d" % hb]
                for qc in range(8):
                    ab = (h * 8 + qc) % 2
                    acc = ps[:, (4 + 2 * ab) * 512:(6 + 2 * ab) * 512].rearrange("p (m t) -> p m t", t=512)
                    acck = "acc%d" % ab
                    q0 = qc * 512
                    for kb in range(32):
                        sb = it % 2
                        pb = it % 3
                        Sv = ps[:, sb * 1024:(sb + 1) * 1024].rearrange("p (m t) -> p m t", t=512)
                        sk = "S%d" % sb
                        dl = kb - 4 * qc
                        k0 = kb * 128
                        for m in range(2):
                            if dl < 0 or dl > 3:
                                s_ = 0 if dl < 0 else 1
                                S.add("pe", lambda e, m=m, s_=s_, k0=k0, Sv=Sv, hb=hb, q0=q0: e.matmul(
                                    Sv[:, m, :], lhsT=ka[hb][:, s_, m, k0:k0 + 128], rhs=qa[hb][:, m, q0:q0 + 512],
                                    start=True, stop=True), r=QK, w=[sk])
                            else:
                                c0 = 128 * dl
                                if dl > 0:
                                    S.add("pe", lambda e, m=m, k0=k0, Sv=Sv, hb=hb, q0=q0, c0=c0: e.matmul(
                                        Sv[:, m, 0:c0], lhsT=ka[hb][:, 1, m, k0:k0 + 128], rhs=qa[hb][:, m, q0:q0 + c0],
                                        start=True, stop=True), r=QK, w=[sk])
                                S.add("pe", lambda e, m=m, k0=k0, Sv=Sv, hb=hb, q0=q0, c0=c0: e.matmul(
                                    Sv[:, m, c0:512], lhsT=ka[hb][:, 0, m, k0:k0 + 128], rhs=qa[hb][:, m, q0 + c0:q0 + 512],
                                    start=True, stop=False), r=QK, w=[sk])
                                for hl in range(2):
                                    S.add("pe", lambda e, m=m, Sv=Sv, c0=c0, hl=hl, h=h: e.matmul(
                                        Sv[:, m, c0:c0 + 128], lhsT=ident_bf, rhs=dpt[:, h, hl, :],
                                        start=False, stop=(hl == 1)), r=["dpt", "ident_bf"], w=[sk])
                        if pend_pv is not None:
                            pend_pv()
                            pend_pv = None
                        if dl < 0 or dl > 3:
                            bias = -mh * abs(512 * qc - 128 * kb)
                            S.add("act", lambda e, Sv=Sv, pb=pb, bias=bias: e.activation(
                                out=pT[pb].rearrange("p m t -> p (m t)"), in_=Sv.rearrange("p m t -> p (m t)"),
                                func=AF.Exp, bias=float(bias), scale=1.0), r=[sk], w=["pT%d" % pb])
                        else:
                            c0 = 128 * dl
                            if dl > 0:
                                S.add("act", lambda e, Sv=Sv, pb=pb, c0=c0, mh=mh: e.activation(
                                    out=pT[pb][:, :, 0:c0], in_=Sv[:, :, 0:c0], func=AF.Exp, bias=float(-mh * c0), scale=1.0),
                                    r=[sk], w=["pT%d" % pb])
                            S.add("act", lambda e, Sv=Sv, pb=pb, c0=c0, mh=mh: e.activation(
                                out=pT[pb][:, :, c0:512], in_=Sv[:, :, c0:512], func=AF.Exp, bias=float(mh * c0), scale=1.0),
                                r=[sk], w=["pT%d" % pb])

                        def pv(pb=pb, kb=kb, acc=acc, acck=acck, hb=hb):
                            for m in range(2):
                                S.add("pe", lambda e, m=m: e.matmul(
                                    acc[:, m, :], lhsT=vah[hb][:, kb, :], rhs=pT[pb][:, m, :],
                                    start=(kb == 0), stop=(kb == 31)), r=["pT%d" % pb, "va%d" % hb], w=[acck])
                        pend_pv = pv
                        it += 1
                        if pend_epi and kb == EPI_AT[10 - len(pend_epi)]:
                            pend_epi.pop(0)()
                    def mk_epi(acc=acc, acck=acck, h=h, qc=qc):
                        st = []
                        st.append(lambda: S.add("dve", lambda e: e.tensor_copy(out=osb[0:65].rearrange("p m t -> p (m t)"),
                                                                                in_=acc[0:65].rearrange("p m t -> p (m t)")), r=[acck], w=["osb"]))
                        st.append(lambda: S.add("dve", lambda e: e.reciprocal(out=rl[64:65, :], in_=osb[64:65].rearrange("p m t -> p (m t)")),
                                                r=["osb"], w=["rl"]))
                        def bc():
                            for m in range(2):
                                S.add("pe", lambda e, m=m: e.matmul(acc[0:64, m, :], lhsT=ones[64:65, 0:64], rhs=rl[64:65, m * 512:(m + 1) * 512],
                                                                     start=True, stop=True), r=["rl", "ones"], w=[acck])
                        st.append(bc)
                        def mul():
                            S.add("dve", lambda e: e.tensor_tensor(out=t0[0:64], in0=acc[0:64, 0, :], in1=osb[0:64, 0, :], op=ALU.mult),
                                  r=[acck, "osb"], w=["t0"])
                            S.add("dve", lambda e: e.tensor_tensor(out=t1[0:64], in0=acc[0:64, 1, :], in1=osb[0:64, 1, :], op=ALU.mult),
                                  r=[acck, "osb"], w=["t1"])
                            S.add("dve", lambda e: e.scalar_tensor_tensor(out=dd[0:64], in0=t1[0:64], scalar=lamneg, in1=t0[0:64],
                                                                          op0=ALU.mult, op1=ALU.add), r=["t0", "t1", "lamneg"], w=["dd"])
                        st.append(mul)
                        st.append(lambda: S.add("dve", lambda e: e.tensor_tensor(out=sq[0:64], in0=dd[0:64], in1=dd[0:64], op=ALU.mult), r=["dd"], w=["sq"]))
                        st.append(lambda: S.add("pe", lambda e: e.matmul(acc[0:64, 0, :], lhsT=ones[0:64, 0:64], rhs=sq[0:64],
                                                                          start=True, stop=True), r=["sq", "ones"], w=[acck]))
                        st.append(lambda: S.add("act", lambda e: e.activation(out=tmpv[0:64], in_=acc[0:64, 0, :], func=AF.Ln, scale=1.0 / 64.0,
                                                                              bias=epsc[0:64, 0:1]), r=[acck, "epsc"], w=["tmpv"]))
                        st.append(lambda: S.add("act", lambda e: e.activation(out=rstd[0:64], in_=tmpv[0:64], func=AF.Exp, scale=-0.5),
                                                r=["tmpv"], w=["rstd"]))
                        st.append(lambda: S.add("dve", lambda e: e.scalar_tensor_tensor(out=obf[0:64], in0=dd[0:64], scalar=gcol, in1=rstd[0:64],
                                                                                        op0=ALU.mult, op1=ALU.mult),
                                                r=["dd", "rstd", "gcol"], w=["obfA"]))
                        st.append(lambda: S.add("sp", lambda e: e.dma_start(out=mixT[h * 64:(h + 1) * 64, qc * 512:(qc + 1) * 512], in_=obf[0:64]),
                                                r=["obfA"], dma="obfA"))
                        return st
                    while pend_epi:
                        pend_epi.pop(0)()
                    pend_epi = mk_epi()
            if pend_pv is not None:
                pend_pv()
            while pend_epi:
                pend_epi.pop(0)()
            S.barrier()

        def phase_B(l):
            M = Mem(PBASE)
            PAD = 64
            vb = M.bf16(3 * 33 * 390).rearrange("p (a t n) -> p a t n", a=3, t=33)
            qbs = [M.bf16(T) for _ in range(2)]
            kbs_ = [M.bf16(T + 2 * PAD) for _ in range(2)]
            qp = M.bf16(T)
            kp = M.bf16(T + 2 * PAD)
            bias = M.f32(7 * 512).rearrange("p (v n) -> p v n", n=512)
            accT = M.f32(T)
            Ssb = [M.f32(512) for _ in range(4)]
            pT = [M.bf16(512) for _ in range(4)]
            rl = [M.f32(512) for _ in range(2)]
            obf = [M.bf16(512) for _ in range(2)]
            S.add("sp", lambda e: e.dma_start(out=bias.rearrange("p v n -> p (v n)"), in_=cst["c_bbias"].rearrange("p v n -> p (v n)")),
                  w=["biasB"], dma="biasB")
            VBK = ["vb_%d" % i for i in range(84)]
            for a3 in range(3):
                for t3 in range(3):
                    S.add("dve", lambda e, a3=a3, t3=t3: e.memset(vb[:, a3, t3 * 11:(t3 + 1) * 11, :], 0.0), w=VBK)
            for (buf, key) in ((qbs[0], "qb0"), (kbs_[0], "kb0"), (qbs[1], "qb1"), (kbs_[1], "kb1"), (qp, "qp"), (kp, "kp")):
                S.add("dve", lambda e, buf=buf: e.memset(buf, 0.0), w=[key])
            nd = 0
            for pi, (win, d) in enumerate(B_PATTERNS):
                Lc = T // d
                nt = Lc // 128
                for r in range(d):
                    srcB = bass.AP(vB.tensor, r * 390, [[d * 390, 64], [128 * d * 390, nt], [1, 390]])
                    tA = r * nt + 1
                    tB = r * nt
                    ntA = nt
                    if (64 + 128 * (nt - 1) + 63) * d + r >= T:
                        ntA = nt - 1
                    if ntA > 0:
                        srcA = bass.AP(vB.tensor, (64 * d + r) * 390, [[d * 390, 64], [128 * d * 390, ntA], [1, 390]])
                        S.add("sp", lambda e, srcA=srcA, pi=pi, tA=tA, ntA=ntA: e.dma_start(out=vb[0:64, pi, tA:tA + ntA, :], in_=srcA),
                              w=[VBK[nd]], dma="vb%d" % (nd % 4))
                        nd += 1
                    S.add("sp", lambda e, srcB=srcB, pi=pi, tB=tB, nt=nt: e.dma_start(out=vb[64:128, pi, tB:tB + nt, :], in_=srcB),
                          w=[VBK[nd]], dma="vb%d" % (nd % 4))
                    nd += 1
            blocks = [(0, 0), (0, 1), (1, 1), (1, 2)]
            def load_qk(h):
                S.add("sp", lambda e, h=h: e.dma_start(out=qbs[h % 2][0:64, :], in_=qbT[h * 64:(h + 1) * 64, :]), w=["qb%d" % (h % 2)], dma="qb%d" % (h % 2))
                S.add("sp", lambda e, h=h: e.dma_start(out=kbs_[h % 2][0:64, PAD:PAD + T], in_=kbT[h * 64:(h + 1) * 64, :]), w=["kb%d" % (h % 2)], dma="kb%d" % (h % 2))

            load_qk(0)
            for h in range(6):
                mh = float(S_B[h])
                qb, kb_ = qbs[h % 2], kbs_[h % 2]
                qbk, kbk_ = "qb%d" % (h % 2), "kb%d" % (h % 2)
                if h + 1 < 6:
                    load_qk(h + 1)
                for pi, (win, d) in enumerate(B_PATTERNS):
                    Lc = T // d
                    ntc = Lc // 128
                    ng = ntc // 2
                    if pi == 0:
                        qs_, ks_, qk_, kk_ = qb, kb_, qbk, kbk_
                    else:
                        qs_, ks_, qk_, kk_ = qp, kp, "qp", "kp"
                        S.add("dve", lambda e, d=d, qb=qb: e.tensor_copy(out=qp[0:64, :].rearrange("p (r j) -> p r j", r=d),
                                                                         in_=qb[0:64, :].rearrange("p (j r) -> p r j", r=d)), r=[qbk], w=["qp"])
                        S.add("act", lambda e, d=d, kb_=kb_: e.copy(out=kp[0:64, PAD:PAD + T].rearrange("p (r j) -> p r j", r=d),
                                                                    in_=kb_[0:64, PAD:PAD + T].rearrange("p (j r) -> p r j", r=d)), r=[kbk_], w=["kp"])
                    groups = []
                    for r in range(d):
                        for gq in range(ng):
                            b0 = 2 * gq
                            tq = r * ntc + b0
                            if pi == 2:
                                var = 6
                            else:
                                var = 3 * pi + (1 if gq == 0 else (2 if gq == ng - 1 else 0))
                            groups.append((tq, var, 128 * b0 * d + r))

                    def emit_S(gi, groups=groups, qs_=qs_, ks_=ks_, qk_=qk_, kk_=kk_):
                        tq, var, s0 = groups[gi]
                        sbk = gi % 4
                        Sb = bank(sbk)
                        for bi, (qi, ci) in enumerate(blocks):
                            qs = 128 * (tq + qi)
                            ks = PAD + 128 * (tq + ci) - 64
                            S.add("pe", lambda e, bi=bi, qs=qs, ks=ks: e.matmul(
                                Sb[:, bi * 128:(bi + 1) * 128], lhsT=ks_[:, ks:ks + 128], rhs=qs_[:, qs:qs + 128],
                                start=True, stop=True), r=[qk_, kk_], w=["SB%d" % sbk])

                    def emit_rest(gi, groups=groups, h=h, pi=pi, d=d, mh=mh):
                        tq, var, s0 = groups[gi]
                        sbk = gi % 4
                        obk = gi % 2
                        Sb = bank(sbk)
                        ob = bank(4 + obk)
                        S.add("dve", lambda e: e.scalar_tensor_tensor(out=Ssb[sbk], in0=bias[:, var, :], scalar=mh, in1=Sb,
                                                                      op0=ALU.mult, op1=ALU.add),
                              r=["SB%d" % sbk, "biasB"], w=["Ssb%d" % sbk])
                        S.add("act", lambda e: e.activation(out=pT[sbk], in_=Ssb[sbk], func=AF.Exp), r=["Ssb%d" % sbk], w=["pTB%d" % sbk])
                        for bi, (qi, ci) in enumerate(blocks):
                            S.add("pe", lambda e, bi=bi, qi=qi, ci=ci: e.matmul(
                                ob[0:65, qi * 128:(qi + 1) * 128], lhsT=vb[:, pi, tq + ci, h * 65:(h + 1) * 65],
                                rhs=pT[sbk][:, bi * 128:(bi + 1) * 128], start=(bi % 2 == 0), stop=(bi % 2 == 1)),
                                r=["pTB%d" % sbk] + VBK, w=["oB%d" % obk])

                    def emit_tail(gi, groups=groups, pi=pi, d=d):
                        tq, var, s0 = groups[gi]
                        obk = gi % 2
                        ob = bank(4 + obk)
                        dst = accT[0:65, s0:s0 + 255 * d + 1:d]
                        if pi == 0:
                            S.add("act", lambda e: e.copy(out=dst, in_=ob[0:65, 0:256]), r=["oB%d" % obk], w=["accT"])
                        else:
                            S.add("dve", lambda e: e.tensor_tensor(out=dst, in0=ob[0:65, 0:256], in1=dst, op=ALU.add),
                                  r=["oB%d" % obk, "accT"], w=["accT"])

                    emit_S(0)
                    emit_S(1)
                    emit_S(2)
                    for gi in range(len(groups)):
                        if gi + 3 < len(groups):
                            emit_S(gi + 3)
                        emit_rest(gi)
                        if gi >= 1:
                            emit_tail(gi - 1)
                    emit_tail(len(groups) - 1)
                for c in range(8):
                    norm_store(accT[0:64, c * 512:(c + 1) * 512], accT[64:65, c * 512:(c + 1) * 512], rl[c % 2], bank(6 + c % 2), obf[c % 2],
                               mixT[256 + h * 64:256 + (h + 1) * 64, c * 512:(c + 1) * 512], ["accT"], "B%d" % (c % 2))
            S.barrier()

        def phase_C(l):
            M = Mem(PBASE)
            PADC = 128
            G = []
            for g in range(2):
                d_ = dict(
                    kc=M.bf16(T + 2 * PADC), qc=M.bf16(3 * T).rearrange("p (r t) -> p r t", t=T),
                    vc=M.bf16(32 * 65).rearrange("p (k n) -> p k n", n=65), cb=M.f32(3 * 384).rearrange("p (k n) -> p k n", n=384),
                    Ssb=M.f32(3 * 384).rearrange("p (k n) -> p k n", n=384), pT=M.bf16(3 * 384).rearrange("p (k n) -> p k n", n=384),
                    osbc=[M.f32(3 * 512).rearrange("p (r t) -> p r t", t=512) for _ in range(2)],
                    rlin=M.f32(512), rl=M.f32(512), obf=M.bf16(512))
                G.append(d_)
            es = M.f32(16)
            S.add("sp", lambda e: e.dma_start(out=es[64:65, 0:6], in_=sink[l:l + 1, :]), w=["es0"], dma="es0")
            S.add("act", lambda e: e.activation(out=es[64:65, 8:14], in_=es[64:65, 0:6], func=AF.Exp), r=["es0"], w=["es"])
            for g in range(2):
                d_ = G[g]
                S.add("dve", lambda e, d_=d_: e.memset(d_["kc"], 0.0), w=["kc%d" % g])
                S.add("dve", lambda e, d_=d_: e.memset(d_["qc"][64:128].rearrange("p r t -> p (r t)"), 0.0), w=["qcz%d" % g])
                S.add("sp", lambda e, g=g, d_=d_: e.dma_start(out=d_["kc"][0:64, PADC:PADC + T], in_=kcT[g * 64:(g + 1) * 64, :]),
                      w=["kc%d" % g], dma="kc%d" % g)
                for rep in range(3):
                    hq = 3 * g + rep
                    S.add("sp", lambda e, rep=rep, hq=hq, d_=d_: e.dma_start(out=d_["qc"][0:64, rep, :], in_=qcT[hq * 64:(hq + 1) * 64, :]),
                          w=["qc%d_%d" % (g, rep)], dma="qc%d_%d" % (g, rep))
                for q4 in range(4):
                    S.add("sp", lambda e, g=g, q4=q4, d_=d_: e.dma_start(
                        out=d_["vc"][:, q4 * 8:(q4 + 1) * 8, :],
                        in_=vC[q4 * 1024:(q4 + 1) * 1024, g * 65:(g + 1) * 65].rearrange("(k p) n -> p k n", p=128)),
                        w=["vc%d_%d" % (g, q4)], dma="vc%d_%d" % (g, q4))
                S.add("sp", lambda e, g=g, d_=d_: e.dma_start(out=d_["cb"].rearrange("p k n -> p (k n)"),
                                                              in_=cst["c_cbias"][g].rearrange("p k n -> p (k n)")), w=["cb%d" % g], dma="cb%d" % g)

            def kts_of(qb_):
                return [kt for kt in range(3) if 0 <= qb_ + kt - 1 <= 31]

            def c_S(g, qb_):
                d_ = G[g]
                for kt in kts_of(qb_):
                    kbk = qb_ + kt - 1
                    Sb = bank(3 * g + kt, 384)
                    S.add("pe", lambda e, Sb=Sb, kbk=kbk: e.matmul(
                        Sb, lhsT=d_["kc"][:, PADC + kbk * 128:PADC + (kbk + 1) * 128], rhs=d_["qc"][:, :, qb_ * 128:(qb_ + 1) * 128],
                        start=True, stop=True), r=["kc%d" % g, "qcz%d" % g] + ["qc%d_%d" % (g, r_) for r_ in range(3)], w=["SC%d_%d" % (g, kt)])

            def c_exp(g, qb_):
                d_ = G[g]
                kts = kts_of(qb_)
                for kt in kts:
                    Sb = bank(3 * g + kt, 384)
                    S.add("dve", lambda e, Sb=Sb, kt=kt: e.tensor_tensor(out=d_["Ssb"][:, kt, :], in0=Sb, in1=d_["cb"][:, kt, :], op=ALU.add),
                          r=["SC%d_%d" % (g, kt), "cb%d" % g], w=["SsbC%d" % g])
                lo, hi = kts[0], kts[-1] + 1
                S.add("act", lambda e: e.activation(out=d_["pT"][:, lo:hi, :], in_=d_["Ssb"][:, lo:hi, :], func=AF.Exp),
                      r=["SsbC%d" % g], w=["pTC%d" % g])

            def c_pv(g, qb_):
                d_ = G[g]
                kts = kts_of(qb_)
                lo, hi = kts[0], kts[-1] + 1
                for kt in kts:
                    kbk = qb_ + kt - 1
                    S.add("pe", lambda e, kt=kt, kbk=kbk: e.matmul(
                        bank(6 + g, 384)[0:65, :], lhsT=d_["vc"][:, kbk, :], rhs=d_["pT"][:, kt, :], start=(kt == lo), stop=(kt == hi - 1)),
                        r=["pTC%d" % g] + ["vc%d_%d" % (g, q4) for q4 in range(4)], w=["oC%d" % g])

            def c_tail(g, qb_):
                d_ = G[g]
                ch = qb_ // 4
                obk = ch % 2
                S.add("act", lambda e: e.copy(
                    out=d_["osbc"][obk][0:65, :, (qb_ % 4) * 128:(qb_ % 4 + 1) * 128],
                    in_=bank(6 + g, 384)[0:65, :].rearrange("p (r t) -> p r t", t=128)), r=["oC%d" % g], w=["osbc%d_%d" % (g, obk)])
                if qb_ % 4 == 3:
                    for rep in range(3):
                        hq = 3 * g + rep
                        S.add("dve", lambda e, rep=rep, hq=hq: e.tensor_scalar(
                            out=d_["rlin"][64:65, :], in0=d_["osbc"][obk][64:65, rep, :], scalar1=es[64:65, 8 + hq:9 + hq], scalar2=None, op0=ALU.add),
                            r=["osbc%d_%d" % (g, obk), "es"], w=["rlin%d" % g])
                        norm_store(d_["osbc"][obk][0:64, rep, :], d_["rlin"][64:65, :], d_["rl"], bank(6 + g), d_["obf"],
                                   mixT[640 + hq * 64:640 + (hq + 1) * 64, ch * 512:(ch + 1) * 512],
                                   ["rlin%d" % g, "osbc%d_%d" % (g, obk)], "C%d" % g, bc_key="oC%d" % g)

            wst_c = [M.f32(1024) for _ in range(2)]
            wdb_c = [M.bf16(1024) for _ in range(2)]
            wob_c = Mem(WOB_OFF).bf16(8 * 1024).rearrange("p (k n) -> p k n", n=1024)
            wprep = []
            for kc in range(8):
                def prep_o(kc=kc):
                    b = kc % 2
                    S.add("sp", lambda e: e.dma_start(out=wst_c[b], in_=w_out[l, kc * 128:(kc + 1) * 128, :]), w=["wstc%d" % b], dma="wstc%d" % b)
                    S.add("pool", lambda e: e.tensor_tensor(out=wob_c[:, kc, :], in0=wst_c[b], in1=keepR[:, 0, :], op=ALU.mult),
                          r=["wstc%d" % b], w=["wobc%d" % kc])
                wprep.append(prep_o)
            for j in range(NJ):
                def prep(j=j):
                    b = j % 2
                    S.add("sp", lambda e: e.dma_start(out=wst_c[b], in_=w_down[l, j * 128:(j + 1) * 128, :]), w=["wstc%d" % b], dma="wstc%d" % b)
                    S.add("pool", lambda e: e.tensor_tensor(out=wdb_c[b], in0=wst_c[b], in1=keepR[:, 3, :], op=ALU.mult),
                          r=["wstc%d" % b], w=["wdbc%d" % b])
                    S.add("pool", lambda e: e.dma_start(out=wdnS[j], in_=wdb_c[b]), r=["wdbc%d" % b], w=["wdnS%d" % j], dma="wdbc%d" % b)
                wprep.append(prep)
            c_S(0, 0)
            c_S(1, 0)
            for qb_ in range(32):
                if qb_ < len(wprep):
                    wprep[qb_]()
                for g in range(2):
                    c_exp(g, qb_)
                    if qb_ + 1 < 32:
                        c_S(g, qb_ + 1)
                for g in range(2):
                    c_pv(g, qb_)
                for g in range(2):
                    c_tail(g, qb_)
            S.barrier()

        def phase_P3(l):
            EPS1 = LN_EPS / (ALPHA * ALPHA)
            M = Mem(PBASE)
            wob = Mem(WOB_OFF).bf16(8 * 1024).rearrange("p (k n) -> p k n", n=1024)
            lnt = M.f32(6 * 1024).rearrange("p (a n) -> p a n", n=1024)
            mxT = M.bf16(8 * 512).rearrange("p (k t) -> p k t", t=512)
            xt = [M.f32(4096).rearrange("p (t n) -> p t n", n=1024) for _ in range(2)]
            zn = M.f32(1024)
            h2 = M.bf16(4 * 1024).rearrange("p (t n) -> p t n", n=1024)
            h2T = M.bf16(8 * 512).rearrange("p (k t) -> p k t", t=512)
            actT = M.bf16(NJ * 512).rearrange("p (j t) -> p j t", t=512)
            sg = [M.f32(512) for _ in range(2)]
            wgu = [M.bf16(8 * 256).rearrange("p (k n) -> p k n", n=256) for _ in range(3)]
            wdn = [M.bf16(1024) for _ in range(3)]
            wst = [M.f32(1024) for _ in range(2)]
            stt = M.f32(8 * 24).rearrange("p (s n) -> p s n", n=24)
            negh1 = M.f32(2)
            xdst = x1s if l == 0 else y_out
            S.add("pool", lambda e: e.memset(negh1, -0.5), w=["negh1"])
            for (a, src) in ((0, ln_g[l, 0]), (1, ln_b[l, 0]), (4, ln_g[l, 1]), (5, ln_b[l, 1])):
                S.add("sp", lambda e, a=a, src=src: e.dma_start(out=lnt[:, a, :], in_=src.partition_broadcast(128)), w=["lnt%d" % a], dma="lnt%d" % a)
            S.add("dve", lambda e: e.tensor_tensor(out=lnt[:, 2, :], in0=lnt[:, 0, :], in1=keepR[:, 1, :], op=ALU.mult), r=["lnt0"], w=["lnt2"])
            S.add("dve", lambda e: e.tensor_tensor(out=lnt[:, 3, :], in0=lnt[:, 1, :], in1=keepR[:, 1, :], op=ALU.mult), r=["lnt1"], w=["lnt3"])
            S.add("dve", lambda e: e.tensor_tensor(out=lnt[:, 3, :], in0=lnt[:, 3, :], in1=keepR[:, 2, :], op=ALU.add), r=["lnt3"], w=["lnt3"])
            WOB = ["wob%d" % k for k in range(8)]

            def load_mx(c):
                S.add("sp", lambda e, c=c: e.dma_start(out=mxT, in_=mixT[:, c * 512:(c + 1) * 512].rearrange("(k p) t -> p k t", p=128)),
                      w=["mxT"], dma="mxT")

            def load_x(c):
                cb_ = c % 2
                xsrc = x_in if l == 0 else x1s
                S.add("sp", lambda e, c=c, cb_=cb_: e.dma_start(
                    out=xt[cb_], in_=xsrc[c * 512:(c + 1) * 512, :].rearrange("(t p) n -> p t n", p=128)), w=["xt%d_%d" % (cb_, t_) for t_ in range(4)],
                    dma="xt%d" % cb_)

            cnt = {"tm": 0, "ev": 0, "zn": 0}

            def wout(c):
                cb_ = c % 2
                for tb in range(4):
                    for n in range(2):
                        pb = cnt["tm"] % 2
                        cnt["tm"] += 1
                        for kc in range(8):
                            S.add("pe", lambda e, tb=tb, n=n, kc=kc, pb=pb: e.matmul(
                                bank(pb), lhsT=mxT[:, kc, tb * 128:(tb + 1) * 128], rhs=wob[:, kc, n * 512:(n + 1) * 512],
                                start=(kc == 0), stop=(kc == 7)), r=["mxT", WOB[kc]], w=["psO%d" % pb])
                        S.add("dve", lambda e, tb=tb, n=n, pb=pb, cb_=cb_: e.tensor_tensor(
                            out=xt[cb_][:, tb, n * 512:(n + 1) * 512], in0=bank(pb), in1=xt[cb_][:, tb, n * 512:(n + 1) * 512], op=ALU.add),
                            r=["psO%d" % pb, "xt%d_%d" % (cb_, tb)], w=["xt%d_%d" % (cb_, tb)])

            def layer_norm(c, tb, which):
                cb_ = c % 2
                y = xt[cb_][:, tb, :]
                yk = "xt%d_%d" % (cb_, tb)
                st = stt[:, tb + 4 * (which - 1), :]
                sk = "st%d" % (tb + 4 * (which - 1))
                S.add("dve", lambda e: e.bn_stats(out=st[:, 0:6], in_=y[:, 0:512]), r=[yk], w=[sk + "a"])
                S.add("dve", lambda e: e.bn_stats(out=st[:, 6:12], in_=y[:, 512:1024]), r=[yk], w=[sk + "b"])
                S.add("dve", lambda e: e.bn_aggr(out=st[:, 12:14], in_=st[:, 0:12]), r=[sk + "a", sk + "b"], w=[sk + "mv"])
                S.add("dve", lambda e: e.tensor_scalar(out=st[:, 14:15], in0=st[:, 13:14], scalar1=float(EPS1), scalar2=None, op0=ALU.add),
                      r=[sk + "mv"], w=[sk + "ve"])
                S.add("pool", lambda e: e.tensor_tensor(out=st[:, 15:16], in0=st[:, 14:15], in1=negh1[:, 0:1], op=ALU.pow),
                      r=[sk + "ve", "negh1"], w=[sk + "rs"])
                S.add("dve", lambda e: e.tensor_scalar(out=st[:, 16:17], in0=st[:, 12:13], scalar1=st[:, 15:16], scalar2=-1.0,
                                                        op0=ALU.mult, op1=ALU.mult), r=[sk + "mv", sk + "rs"], w=[sk + "nb"])
                zi = cnt["zn"] % 2
                cnt["zn"] += 1
                znb = zn if zi == 0 else wst[1]
                znk = "zn" if zi == 0 else "wst1"
                S.add("act", lambda e: e.activation(out=znb, in_=y, func=AF.Identity, scale=st[:, 15:16], bias=st[:, 16:17]),
                      r=[yk, sk + "rs", sk + "nb"], w=[znk])
                ga, ba = (0, 1) if which == 1 else (4, 5)
                S.add("pool", lambda e: e.tensor_tensor(out=y, in0=znb, in1=lnt[:, ga, :], op=ALU.mult), r=[znk, "lnt%d" % ga], w=[yk])
                S.add("pool", lambda e: e.tensor_tensor(out=y, in0=y, in1=lnt[:, ba, :], op=ALU.add), r=[yk, "lnt%d" % ba], w=[yk])
                if which == 1:
                    S.add("dve", lambda e: e.tensor_tensor(out=wst[0], in0=znb, in1=lnt[:, 2, :], op=ALU.mult), r=[znk, "lnt2"], w=["wst0"])
                    S.add("dve", lambda e: e.tensor_tensor(out=h2[:, tb, :], in0=wst[0], in1=lnt[:, 3, :], op=ALU.add),
                          r=["wst0", "lnt3"], w=["h2_%d" % tb])

            def transposes(c):
                for kc in range(8):
                    pb = 2 + kc % 2
                    for tb in range(4):
                        S.add("pe", lambda e, kc=kc, tb=tb, pb=pb: e.transpose(
                            out=bank_bf(pb)[:, tb * 128:(tb + 1) * 128], in_=h2[:, tb, kc * 128:(kc + 1) * 128], identity=ident_bf),
                            r=["h2_%d" % tb, "ident_bf"], w=["psT%d" % pb])
                    if cnt["ev"] % 2 == 0:
                        S.add("act", lambda e, kc=kc, pb=pb: e.copy(out=h2T[:, kc, :], in_=bank_bf(pb)[:, 0:512]), r=["psT%d" % pb], w=["h2T%d" % kc])
                    else:
                        S.add("dve", lambda e, kc=kc, pb=pb: e.tensor_copy(out=h2T[:, kc, :], in_=bank_bf(pb)[:, 0:512]), r=["psT%d" % pb], w=["h2T%d" % kc])
                    cnt["ev"] += 1

            def load_wgu(p):
                if p >= 8 * NJ:
                    return
                j = p % NJ
                wb_ = p % 3
                S.add("sp", lambda e, j=j, wb_=wb_: e.dma_start(out=wgu[wb_].rearrange("p k n -> p (k n)"), in_=wguS[j]),
                      r=["wguS%d" % j], w=["wgu%d" % wb_], dma="wgu%d" % wb_)

            def gu(c, hooks):
                H2T = ["h2T%d" % k for k in range(8)]
                for j in range(NJ):
                    load_wgu(c * NJ + j + 2)
                    wb_ = (c * NJ + j) % 3
                    pg = 4 + 2 * (j % 2)
                    for half in range(2):
                        for kc in range(8):
                            S.add("pe", lambda e, kc=kc, half=half, pg=pg, wb_=wb_: e.matmul(
                                bank(pg + half), lhsT=wgu[wb_][:, kc, half * 128:(half + 1) * 128], rhs=h2T[:, kc, :],
                                start=(kc == 0), stop=(kc == 7)), r=["wgu%d" % wb_, H2T[kc]], w=["psG%d" % (pg + half)])
                    sb_ = j % 2
                    S.add("act", lambda e, pg=pg, sb_=sb_: e.activation(out=sg[sb_], in_=bank(pg), func=AF.Silu), r=["psG%d" % pg], w=["sg%d" % sb_])
                    S.add("dve", lambda e, pg=pg, sb_=sb_, j=j: e.tensor_tensor(out=actT[:, j, :], in0=bank(pg + 1), in1=sg[sb_], op=ALU.mult),
                          r=["psG%d" % (pg + 1), "sg%d" % sb_], w=["actT%d" % j])
                    for f in hooks.get(j, ()):
                        f()

            def load_wdn(q):
                if q >= 16 * NJ:
                    return
                i = q % (2 * NJ)
                j, n = i % NJ, i // NJ
                wb_ = q % 3
                S.add("sp", lambda e, j=j, n=n, wb_=wb_: e.dma_start(out=wdn[wb_][:, 0:512], in_=wdnS[j][:, n * 512:(n + 1) * 512]),
                      r=["wdnS%d" % j], w=["wdn%d" % wb_], dma="wdn%d" % wb_)

            def down(c, mid):
                cb_ = c % 2
                for i in range(2 * NJ):
                    load_wdn(c * 2 * NJ + i + 2)
                    j, n = i % NJ, i // NJ
                    wb_ = (c * 2 * NJ + i) % 3
                    for tb in range(4):
                        S.add("pe", lambda e, j=j, tb=tb, wb_=wb_: e.matmul(
                            bank(4 + tb), lhsT=actT[:, j, tb * 128:(tb + 1) * 128], rhs=wdn[wb_][:, 0:512],
                            start=(j == 0), stop=(j == NJ - 1)), r=["actT%d" % j, "wdn%d" % wb_], w=["psG%d" % (4 + tb)])
                    if j == NJ - 1:
                        for tb in range(4):
                            S.add("dve", lambda e, tb=tb, n=n, cb_=cb_: e.tensor_tensor(
                                out=xt[cb_][:, tb, n * 512:(n + 1) * 512], in0=bank(4 + tb), in1=xt[cb_][:, tb, n * 512:(n + 1) * 512], op=ALU.add),
                                r=["psG%d" % (4 + tb), "xt%d_%d" % (cb_, tb)], w=["xt%d_%d" % (cb_, tb)])
                        if n == 0:
                            for f in mid:
                                f()

            def ln2_tile(c, tb):
                cb_ = c % 2
                if True:
                    layer_norm(c, tb, 2)
                    t0_ = c * 512 + tb * 128
                    S.add("pool", lambda e, tb=tb, t0_=t0_, cb_=cb_: e.dma_start(out=xdst[t0_:t0_ + 128, :], in_=xt[cb_][:, tb, :]),
                          r=["xt%d_%d" % (cb_, tb)], dma="xo%d_%d" % (cb_, tb))

            load_mx(0)
            load_x(0)
            load_wgu(0)
            load_wgu(1)
            load_wdn(0)
            load_wdn(1)
            wout(0)
            for tb in range(4):
                layer_norm(0, tb, 1)
            transposes(0)
            for c in range(8):
                hooks = {}
                if c + 1 < 8:
                    load_mx(c + 1)
                if c >= 1:
                    for tb in range(4):
                        hooks.setdefault(3 * tb, []).append(lambda c=c, tb=tb: ln2_tile(c - 1, tb))
                if c + 1 < 8:
                    hooks.setdefault(10, []).append(lambda c=c: load_x(c + 1))
                gu(c, hooks)
                mid = []
                if c + 1 < 8:
                    wout(c + 1)
                    layer_norm(c + 1, 0, 1)
                    layer_norm(c + 1, 1, 1)
                    mid = [lambda c=c: layer_norm(c + 1, 2, 1), lambda c=c: layer_norm(c + 1, 3, 1)]
                down(c, mid)
                if c + 1 < 8:
                    transposes(c + 1)
            for tb in range(4):
                ln2_tile(7, tb)
            S.barrier()

        for l in layers:
            if on("M"):
                phase_M(l)
            if on("P1"):
                phase_P1(l)
            if on("A"):
                phase_A(l)
            if on("B"):
                phase_B(l)
            if on("C"):
                phase_C(l)
            if on("P3"):
                phase_P3(l)

        n_ops = S.emit()
    return nc, n_ops


def kernel(**inputs):
    nc, _ = build_program(debug=False)
    cores = list(range(NCORES))
    maps = make_in_maps(inputs, cores)
    res = run_bass_kernel_spmd(nc, maps, core_ids=cores)
    return np.stack([np.asarray(r["y"], dtype=np.float32) for r in res.results], axis=0)


def make_in_maps(inputs, cores):
    f = lambda a: np.ascontiguousarray(np.asarray(a, dtype=np.float32))
    consts = make_consts()
    shared = {
        "w_ada": f(inputs["w_ada"]), "b_ada": f(inputs["b_ada"]), "w_in": f(inputs["w_in"]),
        "lam": f(inputs["lam"]).reshape(DEPTH, 128), "subln_g": f(inputs["subln_g"]), "sink": f(inputs["sink"]),
        "w_out": f(inputs["w_out"]), "ln_g": f(inputs["ln_g"]), "ln_b": f(inputs["ln_b"]),
        "w_gu": f(inputs["w_gu"]), "w_down": f(inputs["w_down"]),
    }
    shared.update(consts)
    maps = []
    x = np.asarray(inputs["x"], dtype=np.float32)
    c = np.asarray(inputs["c"], dtype=np.float32)
    for b in cores:
        m = dict(shared)
        m["x"] = np.ascontiguousarray(x[b])
        m["ccol"] = np.ascontiguousarray(c[b].reshape(8, 128).T)
        maps.append(m)
    return maps
```

```python
import contextlib
import math
import numpy as np
import ml_dtypes
import concourse.bass as bass
import concourse.mybir as mybir
from concourse.bass_utils import run_bass_kernel_spmd

F32 = mybir.dt.float32
BF16 = mybir.dt.bfloat16
AF = mybir.ActivationFunctionType
ALU = mybir.AluOpType
AX = mybir.AxisListType

T = 4096
D = 1024
DEPTH = 2
NCORES = 8
FFN = 2816
NJ = FFN // 128
INW = 2560
LN_EPS = 1e-5
ALPHA = (2 * DEPTH) ** 0.25
NEG = -1.0e30
POOLW = 50000

SL = (2.0 ** (-8.0 * np.arange(1, 17) / 16)).astype(np.float32)
S_C = SL[0:6]
S_A = SL[6:10]
S_B = SL[10:16]
B_PATTERNS = ((128, 1), (512, 4), (2048, 16))


class Sched:
    ENGS = ("pe", "act", "dve", "pool", "sp")

    def __init__(self, nc, n_dma_sems=48):
        self.nc = nc
        self.ops = []
        self.last_w = {}
        self.readers = {}
        self.dma_last = {}
        self.dma_slot = {}
        self.n_dma_sems = n_dma_sems
        self.barrier_deps = []
        self.n_bg = 4
        self.bg_slot = {}
        self.persist = set()

    def add(self, eng, fn, r=(), w=(), dma=None, bg=False):
        idx = len(self.ops)
        raw = set()
        other = set()
        for k in r:
            if k in self.last_w:
                raw.add(self.last_w[k])
        for k in w:
            if k in self.last_w:
                other.add(self.last_w[k])
            other.update(self.readers.get(k, ()))
        slot = None
        if dma is not None:
            if bg:
                if dma not in self.bg_slot:
                    self.bg_slot[dma] = self.n_dma_sems + len(self.bg_slot) % self.n_bg
                slot = self.bg_slot[dma]
                self.persist.update(w)
            else:
                if dma not in self.dma_slot:
                    self.dma_slot[dma] = len(self.dma_slot) % self.n_dma_sems
                slot = self.dma_slot[dma]
            if slot in self.dma_last:
                raw.add(self.dma_last[slot])
            self.dma_last[slot] = idx
        for k in r:
            self.readers.setdefault(k, []).append(idx)
        for k in w:
            self.last_w[k] = idx
            self.readers[k] = []
        deps = set(self.barrier_deps)
        for d in raw | other:
            p = self.ops[d]
            if p["slot"] is None and slot is None and p["eng"] == eng:
                if eng == "pe" or d not in raw:
                    continue
            deps.add(d)
        deps.discard(idx)
        self.ops.append(dict(eng=eng, fn=fn, deps=deps, slot=slot, sem=None, val=0))
        return idx

    def barrier(self, final=False):
        last = {}
        for i, op in enumerate(self.ops):
            if op["slot"] is not None and op["slot"] >= self.n_dma_sems and not final:
                continue
            key = ("dma", op["slot"]) if op["slot"] is not None else ("eng", op["eng"])
            last[key] = i
        self.barrier_deps = sorted(last.values())
        self.last_w = {k: v for k, v in self.last_w.items() if k in self.persist}
        self.readers = {k: v for k, v in self.readers.items() if k in self.persist}

    def emit(self):
        nc = self.nc
        ops = self.ops
        self.barrier(final=True)
        self.add("sp", None)
        needed = set()
        for op in ops:
            needed.update(op["deps"])
        with contextlib.ExitStack() as st:
            eng_sem = {e: st.enter_context(nc.semaphore("s_" + e)) for e in self.ENGS}
            nslots = self.n_dma_sems + self.n_bg
            dma_sems = [st.enter_context(nc.semaphore("d_%d" % i)) for i in range(nslots)]
            cnt_e = {e: 0 for e in self.ENGS}
            cnt_d = [0] * nslots
            for i, op in enumerate(ops):
                if op["slot"] is not None:
                    cnt_d[op["slot"]] += 16
                    op["sem"] = ("d", op["slot"])
                    op["val"] = cnt_d[op["slot"]]
                elif i in needed:
                    cnt_e[op["eng"]] += 1
                    op["sem"] = ("e", op["eng"])
                    op["val"] = cnt_e[op["eng"]]
            per_eng = {e: [] for e in self.ENGS}
            for op in ops:
                per_eng[op["eng"]].append(op)

            def semh(s):
                return eng_sem[s[1]] if s[0] == "e" else dma_sems[s[1]]

            def run(e_obj, ename):
                waited = {}
                for op in per_eng[ename]:
                    for d in sorted(op["deps"]):
                        p = ops[d]
                        if waited.get(p["sem"], 0) >= p["val"]:
                            continue
                        e_obj.wait_ge(semh(p["sem"]), p["val"])
                        waited[p["sem"]] = p["val"]
                    if op["fn"] is None:
                        continue
                    ins = op["fn"](e_obj)
                    if op["sem"] is not None:
                        ins.then_inc(semh(op["sem"]), 16 if op["sem"][0] == "d" else 1)

            with nc.Block() as block:
                @block.sync
                def _(e):
                    run(e, "sp")

                @block.tensor
                def _(e):
                    run(e, "pe")

                @block.scalar
                def _(e):
                    run(e, "act")

                @block.vector
                def _(e):
                    run(e, "dve")

                @block.gpsimd
                def _(e):
                    run(e, "pool")
        return len(ops)


def _hi_lo(v):
    v = np.asarray(v, np.float32)
    hi = v.astype(ml_dtypes.bfloat16).astype(np.float32)
    lo = (v - hi).astype(ml_dtypes.bfloat16).astype(np.float32)
    return hi, lo


def make_consts():
    c = {}
    c["c_ident"] = np.eye(128, dtype=np.float32)
    qaug = np.zeros((4, 8, T), np.float32)
    kaug = np.zeros((4, 2, 8, T), np.float32)
    dp = np.zeros((4, 2, 128, 128), np.float32)
    ii = (np.arange(T) % 512).astype(np.float32)
    jj = (np.arange(T) % 128).astype(np.float32)
    for h in range(4):
        m = np.float32(S_A[h])
        hi, lo = _hi_lo(m * ii)
        qaug[h, 0] = -hi
        qaug[h, 1] = -lo
        qaug[h, 2] = 1.0
        qaug[h, 3] = 1.0
        hi, lo = _hi_lo(m * jj)
        kaug[h, 0, 0] = 1.0
        kaug[h, 0, 1] = 1.0
        kaug[h, 0, 2] = hi
        kaug[h, 0, 3] = lo
        kaug[h, 1] = -kaug[h, 0]
        pj = np.arange(128, dtype=np.float32)[:, None]
        pi = np.arange(128, dtype=np.float32)[None, :]
        dmat = -2.0 * m * np.maximum(pj - pi, 0.0)
        dp[h, 0], dp[h, 1] = _hi_lo(dmat)
    c["c_qaug"] = qaug
    c["c_kaug"] = kaug
    c["c_dp"] = dp
    p = np.arange(128, dtype=np.float32)[:, None]
    i = np.arange(128, dtype=np.float32)[None, :]
    bb = np.zeros((1, 128, 7, 512), np.float32)
    for h in range(1):
        m = np.float32(1.0)
        k = 0
        for pi_, (win, dil) in enumerate(B_PATTERNS):
            d1 = np.abs(i - (p - 64.0))
            d2 = np.abs(i - (p + 64.0))
            t1 = np.where(d1 <= 64.0, -m * dil * d1, -1.0e32).astype(np.float32)
            t2 = np.where(d2 <= 64.0, -m * dil * d2, -1.0e32).astype(np.float32)
            t1f = t1.copy()
            t1f[0:64, :] = -1.0e32
            t2l = t2.copy()
            t2l[64:128, :] = -1.0e32
            mid = np.concatenate([t1, t2, t1, t2], 1)
            first = np.concatenate([t1f, t2, t1, t2], 1)
            last = np.concatenate([t1, t2, t1, t2l], 1)
            both = np.concatenate([t1f, t2, t1, t2l], 1)
            if pi_ < 2:
                bb[h, :, k] = mid
                bb[h, :, k + 1] = first
                bb[h, :, k + 2] = last
                k += 3
            else:
                bb[h, :, k] = both
                k += 1
    c["c_bbias"] = bb[0]
    cb = np.zeros((2, 128, 3, 384), np.float32)
    for g in range(2):
        for rep in range(3):
            m = np.float32(S_C[3 * g + rep])
            for kt in range(3):
                dist = np.abs(i - (p + 128.0 * (kt - 1)))
                cb[g, :, kt, rep * 128:(rep + 1) * 128] = np.where(dist <= 128.0, -m * dist, NEG)
    c["c_cbias"] = cb
    return c


CONST_SHAPES = {
    "c_ident": [128, 128], "c_qaug": [4, 8, T], "c_kaug": [4, 2, 8, T], "c_dp": [4, 2, 128, 128],
    "c_bbias": [128, 7, 512], "c_cbias": [2, 128, 3, 384],
}


def build_program(debug=False, phases=None, layers=(0, 1)):
    nc = bass.Bass("TRN2", target_bir_lowering=False)

    def din(name, shape, dt=F32):
        return nc.dram_tensor(name, shape, dt, kind="ExternalInput").ap()

    def dscr(name, shape, dt):
        return nc.dram_tensor(name, shape, dt, kind=("ExternalOutput" if debug else "Internal")).ap()

    x_in = din("x", [T, D])
    ccol = din("ccol", [128, 8])
    w_ada = din("w_ada", [DEPTH, D, 6 * D])
    b_ada = din("b_ada", [DEPTH, 6 * D])
    w_in = din("w_in", [DEPTH, D, INW])
    lam = din("lam", [DEPTH, 128])
    subln = din("subln_g", [DEPTH, 64])
    sink = din("sink", [DEPTH, 6])
    w_out = din("w_out", [DEPTH, D, D])
    ln_g = din("ln_g", [DEPTH, 2, D])
    ln_b = din("ln_b", [DEPTH, 2, D])
    w_gu = din("w_gu", [DEPTH, D, 2 * FFN])
    w_down = din("w_down", [DEPTH, FFN, D])
    cst = {k: din(k, v) for k, v in CONST_SHAPES.items()}
    y_out = nc.dram_tensor("y", [T, D], F32, kind="ExternalOutput").ap()

    qaT = dscr("qaT", [256, T], BF16)
    kaT = dscr("kaT", [256, T], BF16)
    qbT = dscr("qbT", [384, T], BF16)
    kbT = dscr("kbT", [384, T], BF16)
    qcT = dscr("qcT", [384, T], BF16)
    kcT = dscr("kcT", [128, T], BF16)
    vA = dscr("vA", [T, 260], BF16)
    vB = dscr("vB", [T, 390], BF16)
    vC = dscr("vC", [T, 130], BF16)
    mixT = dscr("mixT", [D, T], BF16)
    x1s = dscr("x1s", [T, D], F32)
    wguS = dscr("wguS", [NJ, 128, 8 * 256], BF16)
    wdnS = dscr("wdnS", [NJ, 128, D], BF16)
    dbg_mod = dscr("dbg_mod", [128, 4096 + 16], F32) if debug else None

    allp = phases is None

    def on(name):
        return allp or name in phases

    with nc.sbuf_tensor("pool", [128, POOLW], F32) as pool, nc.psum_tensor("ps", [128, 4096], F32) as ps:
        S = Sched(nc)

        class Mem:
            def __init__(self, base):
                self.off = base

            def f32(self, n, parts=None):
                v = pool[:, self.off:self.off + n]
                self.off += n
                assert self.off <= POOLW, self.off
                return v

            def bf16(self, n):
                nw = (n + 1) // 2
                v = pool[:, self.off:self.off + nw].bitcast(BF16)[:, 0:n]
                self.off += nw
                assert self.off <= POOLW, self.off
                return v

        def bank(i, n=512):
            return ps[:, i * 512:i * 512 + n]

        def bank_bf(i):
            return ps[:, i * 512:(i + 1) * 512].bitcast(BF16)

        PM = Mem(0)
        ident = PM.f32(128)
        ident_bf = PM.bf16(128)
        ones = PM.f32(128)
        cs_rep = PM.f32(1024).rearrange("p (k m) -> p k m", m=128)
        modcols = PM.f32(16)
        keepR = PM.f32(4096).rearrange("p (a n) -> p a n", n=1024)
        small = PM.f32(64)
        PBASE = PM.off
        WBF_OFF = 38000
        WOB_OFF = 45000

        S.add("sp", lambda e: e.dma_start(out=ident, in_=cst["c_ident"]), w=["ident"], dma="ident")
        S.add("pool", lambda e: e.dma_start(out=ident_bf, in_=cst["c_ident"]), w=["ident_bf"], dma="ident_bf")
        S.add("dve", lambda e: e.memset(ones, 1.0), w=["ones"])
        cs = small[:, 0:8]
        cs2 = small[:, 8:16]
        S.add("sp", lambda e: e.dma_start(out=cs, in_=ccol), w=["cs"], dma="cs")
        S.add("act", lambda e: e.activation(out=cs2, in_=cs, func=AF.Silu), r=["cs"], w=["cs2"])
        S.add("dve", lambda e: e.tensor_copy(out=cs_rep, in_=cs2.unsqueeze(2).to_broadcast([128, 8, 128])),
              r=["cs2"], w=["cs_rep"])
        S.barrier()

        def phase_M(l):
            M = Mem(PBASE)
            wbf_m = Mem(WBF_OFF).bf16(8 * INW).rearrange("p (k n) -> p k n", n=INW)
            for kc in range(8):
                for hf in range(2):
                    S.add("pool", lambda e, kc=kc, hf=hf: e.dma_start(
                        out=wbf_m[:, kc, hf * 1280:(hf + 1) * 1280],
                        in_=w_in[l, kc * 128:(kc + 1) * 128, hf * 1280:(hf + 1) * 1280]),
                        w=["wbf%d" % kc], dma="wbf%d_%d" % (kc, hf), bg=True)
            wst = [M.f32(3072) for _ in range(4)]
            bada = M.f32(6144)
            modR = M.f32(6144)
            S.add("sp", lambda e: e.dma_start(out=bada, in_=b_ada[l].partition_broadcast(128)), w=["bada"], dma="bada")
            ld = 0
            for half in range(2):
                for kc in range(8):
                    b = ld % 4
                    ld += 1
                    S.add("sp", lambda e, b=b, kc=kc, half=half: e.dma_start(
                        out=wst[b], in_=w_ada[l, kc * 128:(kc + 1) * 128, half * 3072:(half + 1) * 3072]),
                        w=["wst%d" % b], dma="wst%d" % b)
                    for n in range(6):
                        S.add("pe", lambda e, b=b, kc=kc, n=n: e.matmul(
                            bank(n), lhsT=cs_rep[:, kc, :], rhs=wst[b][:, n * 512:(n + 1) * 512],
                            start=(kc == 0), stop=(kc == 7)),
                            r=["wst%d" % b, "cs_rep"], w=["psM%d" % n])
                for n in range(6):
                    c0 = half * 3072 + n * 512
                    S.add("dve", lambda e, n=n, c0=c0: e.tensor_tensor(
                        out=modR[:, c0:c0 + 512], in0=bank(n), in1=bada[:, c0:c0 + 512], op=ALU.add),
                        r=["psM%d" % n, "bada"], w=["modR%d" % (c0 // 512)])
            allR = ["modR%d" % i for i in range(12)]
            for grp in range(4):
                for t4 in range(4):
                    idx = grp * 4 + t4
                    col0 = (1024 + idx * 128) if idx < 8 else ((idx - 8) * 128)
                    S.add("pe", lambda e, grp=grp, t4=t4, col0=col0: e.transpose(
                        out=bank(6 + grp % 2)[:, t4 * 128:(t4 + 1) * 128], in_=modR[:, col0:col0 + 128], identity=ident),
                        r=allR + ["ident"], w=["psT%d" % (grp % 2)])
                src = bank(6 + grp % 2).rearrange("p (a b) -> p a b", b=128)[:, :, 0]
                addv = 1.0 if grp < 2 else 0.0
                S.add("dve", lambda e, grp=grp, src=src, addv=addv: e.tensor_scalar(
                    out=modcols[:, grp * 4:(grp + 1) * 4], in0=src, scalar1=addv, scalar2=None, op0=ALU.add),
                    r=["psT%d" % (grp % 2)], w=["modcols"])
            S.add("dve", lambda e: e.tensor_scalar(out=keepR[:, 0, :], in0=modR[:, 2048:3072], scalar1=1.0,
                                                    scalar2=1.0 / ALPHA, op0=ALU.add, op1=ALU.mult), r=allR, w=["keep0"])
            S.add("dve", lambda e: e.tensor_scalar(out=keepR[:, 1, :], in0=modR[:, 4096:5120], scalar1=1.0,
                                                    scalar2=None, op0=ALU.add), r=allR, w=["keep1"])
            S.add("dve", lambda e: e.tensor_copy(out=keepR[:, 2, :], in_=modR[:, 3072:4096]), r=allR, w=["keep2"])
            S.add("dve", lambda e: e.tensor_scalar(out=keepR[:, 3, :], in0=modR[:, 5120:6144], scalar1=1.0,
                                                    scalar2=1.0 / ALPHA, op0=ALU.add, op1=ALU.mult), r=allR, w=["keep3"])
            if debug:
                S.add("sp", lambda e: e.dma_start(out=dbg_mod[:, 0:4096], in_=keepR.rearrange("p a n -> p (a n)")),
                      r=["keep0", "keep1", "keep2", "keep3"], dma="dbgm")
                S.add("sp", lambda e: e.dma_start(out=dbg_mod[:, 4096:4112], in_=modcols), r=["modcols"], dma="dbgm2")
            S.barrier()

        def phase_P1(l):
            M = Mem(PBASE)
            wbf = Mem(WBF_OFF).bf16(8 * INW).rearrange("p (k n) -> p k n", n=INW)
            xt = [M.f32(4096).rearrange("p (t n) -> p t n", n=1024) for _ in range(2)]
            hT = [M.bf16(4096).rearrange("p (k n) -> p k n", n=512) for _ in range(2)]
            stg = [M.bf16(512) for _ in range(4)]
            vst = [[M.bf16(4 * 65).rearrange("p (h d) -> p h d", d=65),
                    M.bf16(6 * 65).rearrange("p (h d) -> p h d", d=65),
                    M.bf16(2 * 65).rearrange("p (h d) -> p h d", d=65)] for _ in range(2)]
            xsrc = x_in if l == 0 else x1s
            for b in range(2):
                for gi in range(3):
                    S.add("pool", lambda e, b=b, gi=gi: e.memset(vst[b][gi], 1.0), w=["vst%d_%d" % (b, gi)])
            WB = ["wbf%d" % k for k in range(8)]

            def load_x(c):
                b = c % 2
                S.add("sp", lambda e, b=b, c=c: e.dma_start(
                    out=xt[b], in_=xsrc[c * 512:(c + 1) * 512, :].rearrange("(t p) n -> p t n", p=128)),
                    w=["xt%d" % b], dma="xt%d" % b)

            fm = []
            for i in range(2):
                fm.append((i * 128, qaT, i * 128, 32.0 ** -0.5))
            for i in range(2):
                fm.append((256 + i * 128, kaT, i * 128, 1.0))
            for i in range(3):
                fm.append((768 + i * 128, qbT, i * 128, 0.125))
            for i in range(3):
                fm.append((1152 + i * 128, kbT, i * 128, 1.0))
            for i in range(3):
                fm.append((1920 + i * 128, qcT, i * 128, 0.125))
            fm.append((2304, kcT, 0, 1.0))
            tm = [(512, 256, vA, 4), (1536, 384, vB, 6), (2432, 128, vC, 2)]

            load_x(0)
            ev = 0
            fmn = 0
            tmn = 0
            for c in range(8):
                if c + 1 < 8:
                    load_x(c + 1)
                b = c % 2
                for kc in range(8):
                    pb = kc % 2
                    for t4 in range(4):
                        S.add("pe", lambda e, b=b, kc=kc, t4=t4, pb=pb: e.transpose(
                            out=bank(pb)[:, t4 * 128:(t4 + 1) * 128], in_=xt[b][:, t4, kc * 128:(kc + 1) * 128],
                            identity=ident), r=["xt%d" % b, "ident"], w=["psT%d" % pb])
                    if ev % 2 == 0:
                        S.add("act", lambda e, b=b, kc=kc, pb=pb: e.activation(
                            out=hT[b][:, kc, :], in_=bank(pb), func=AF.Identity,
                            scale=modcols[:, kc:kc + 1], bias=modcols[:, 8 + kc:9 + kc]),
                            r=["psT%d" % pb, "modcols"], w=["hT%d_%d" % (b, kc)])
                    else:
                        S.add("dve", lambda e, b=b, kc=kc, pb=pb: e.tensor_scalar(
                            out=hT[b][:, kc, :], in0=bank(pb), scalar1=modcols[:, kc:kc + 1],
                            scalar2=modcols[:, 8 + kc:9 + kc], op0=ALU.mult, op1=ALU.add),
                            r=["psT%d" % pb, "modcols"], w=["hT%d_%d" % (b, kc)])
                    ev += 1
                HT = ["hT%d_%d" % (b, k) for k in range(8)]
                for (wc, dst, r0, scl) in fm:
                    pb = 2 + fmn % 3
                    sb = fmn % 4
                    fmn += 1
                    for kc in range(8):
                        S.add("pe", lambda e, kc=kc, wc=wc, pb=pb, b=b: e.matmul(
                            bank(pb), lhsT=wbf[:, kc, wc:wc + 128], rhs=hT[b][:, kc, :], start=(kc == 0), stop=(kc == 7)),
                            r=[WB[kc], HT[kc]], w=["psF%d" % pb])
                    if ev % 2 == 0:
                        S.add("act", lambda e, pb=pb, sb=sb, scl=scl: e.activation(
                            out=stg[sb], in_=bank(pb), func=AF.Copy, scale=float(scl)), r=["psF%d" % pb], w=["stg%d" % sb])
                    else:
                        S.add("dve", lambda e, pb=pb, sb=sb, scl=scl: e.tensor_scalar(
                            out=stg[sb], in0=bank(pb), scalar1=float(scl), scalar2=None, op0=ALU.mult),
                            r=["psF%d" % pb], w=["stg%d" % sb])
                    ev += 1
                    S.add("sp", lambda e, sb=sb, dst=dst, r0=r0, c=c: e.dma_start(
                        out=dst[r0:r0 + 128, c * 512:(c + 1) * 512], in_=stg[sb]), r=["stg%d" % sb], dma="stg%d" % sb)
                for t4 in range(4):
                    vb_ = tmn % 2
                    tmn += 1
                    for gi, (wc, ncol, dst, nh) in enumerate(tm):
                        pb = 5 + gi
                        for kc in range(8):
                            S.add("pe", lambda e, kc=kc, wc=wc, ncol=ncol, pb=pb, b=b, t4=t4: e.matmul(
                                bank(pb, ncol), lhsT=hT[b][:, kc, t4 * 128:(t4 + 1) * 128], rhs=wbf[:, kc, wc:wc + ncol],
                                start=(kc == 0), stop=(kc == 7)), r=[WB[kc], HT[kc]], w=["psV%d" % pb])
                        src = bank(pb, ncol).rearrange("p (h d) -> p h d", d=64)
                        if ev % 2 == 0:
                            S.add("act", lambda e, src=src, vb_=vb_, gi=gi: e.copy(out=vst[vb_][gi][:, :, 0:64], in_=src),
                                  r=["psV%d" % pb], w=["vst%d_%d" % (vb_, gi)])
                        else:
                            S.add("dve", lambda e, src=src, vb_=vb_, gi=gi: e.tensor_copy(out=vst[vb_][gi][:, :, 0:64], in_=src),
                                  r=["psV%d" % pb], w=["vst%d_%d" % (vb_, gi)])
                        ev += 1
                        t0 = c * 512 + t4 * 128
                        S.add("sp", lambda e, vb_=vb_, gi=gi, dst=dst, t0=t0: e.dma_start(
                            out=dst[t0:t0 + 128, :], in_=vst[vb_][gi].rearrange("p h d -> p (h d)")),
                            r=["vst%d_%d" % (vb_, gi)], dma="vst%d_%d" % (vb_, gi))
            S.barrier()

        def norm_store(osb_num, rl_in, rl_buf, bc_bank, obf, dst, keys_r, tag, n=512, bc_key=None):
            S.add("act", lambda e: e.activation(out=rl_buf[64:65, 0:n], in_=rl_in, func=AF.Ln), r=keys_r, w=["rl" + tag])
            S.add("act", lambda e: e.activation(out=rl_buf[64:65, 0:n], in_=rl_buf[64:65, 0:n], func=AF.Exp, scale=-1.0),
                  r=["rl" + tag], w=["rl" + tag])
            bck = bc_key if bc_key is not None else "bc" + tag
            S.add("pe", lambda e: e.matmul(bc_bank[0:64, 0:n], lhsT=ones[64:65, 0:64], rhs=rl_buf[64:65, 0:n], start=True, stop=True),
                  r=["rl" + tag, "ones"], w=[bck])
            S.add("dve", lambda e: e.tensor_tensor(out=obf[0:64, 0:n], in0=bc_bank[0:64, 0:n], in1=osb_num, op=ALU.mult),
                  r=keys_r + [bck], w=["obf" + tag])
            S.add("sp", lambda e: e.dma_start(out=dst, in_=obf[0:64, 0:n]), r=["obf" + tag], dma="obf" + tag)

        def phase_A(l):
            lam_init = 0.8 - 0.6 * math.exp(-0.3 * l)
            M = Mem(PBASE)
            qa = [M.bf16(2 * T).rearrange("p (m t) -> p m t", t=T) for _ in range(2)]
            ka = [M.bf16(4 * T).rearrange("p (s m t) -> p s m t", m=2, t=T) for _ in range(2)]
            vah = [M.bf16(32 * 128).rearrange("p (k n) -> p k n", n=128) for _ in range(2)]
            dpt = M.bf16(4 * 2 * 128).rearrange("p (h s i) -> p h s i", s=2, i=128)
            pT = [M.bf16(1024).rearrange("p (m t) -> p m t", t=512) for _ in range(3)]
            osb = M.f32(1024).rearrange("p (m t) -> p m t", t=512)
            rl = M.f32(1024)
            t0 = M.f32(512)
            t1 = M.f32(512)
            dd = M.f32(512)
            sq = M.f32(512)
            tmpv = M.f32(512)
            rstd = M.f32(512)
            negh = M.f32(512)
            obf = M.bf16(512)
            lamt = M.f32(128)
            lw = M.f32(64)
            sm = M.f32(16)
            S.add("sp", lambda e: e.dma_start(out=lamt, in_=lam[l].partition_broadcast(128)), w=["lamt"], dma="lamt")
            S.add("dve", lambda e: e.tensor_tensor(out=lw[:, 0:32], in0=lamt[:, 0:32], in1=lamt[:, 32:64], op=ALU.mult), r=["lamt"], w=["lw0"])
            S.add("dve", lambda e: e.tensor_tensor(out=lw[:, 32:64], in0=lamt[:, 64:96], in1=lamt[:, 96:128], op=ALU.mult), r=["lamt"], w=["lw1"])
            S.add("dve", lambda e: e.tensor_reduce(out=sm[:, 0:2], in_=lw.rearrange("p (a b) -> p a b", b=32), axis=AX.X, op=ALU.add),
                  r=["lw0", "lw1"], w=["sm01"])
            S.add("act", lambda e: e.activation(out=sm[:, 2:4], in_=sm[:, 0:2], func=AF.Exp), r=["sm01"], w=["sm23"])
            S.add("dve", lambda e: e.scalar_tensor_tensor(out=sm[:, 4:5], in0=sm[:, 3:4], scalar=-lam_init, in1=sm[:, 2:3],
                                                          op0=ALU.add, op1=ALU.subtract), r=["sm23"], w=["lamneg"])
            S.add("sp", lambda e: e.dma_start(out=sm[0:64, 8:9], in_=subln[l].rearrange("(p o) -> p o", o=1)), w=["gc0"], dma="gc0")
            S.add("dve", lambda e: e.tensor_scalar(out=sm[0:64, 9:10], in0=sm[0:64, 8:9], scalar1=float(1.0 - lam_init), scalar2=None,
                                                    op0=ALU.mult), r=["gc0"], w=["gcol"])
            lamneg = sm[0:64, 4:5]
            gcol = sm[0:64, 9:10]
            epsc = sm[:, 12:13]
            S.add("pool", lambda e: e.memset(sm[:, 12:13], LN_EPS), w=["epsc"])
            QAK = [["qa%d_%d" % (b2, i) for i in range(4)] for b2 in range(2)]
            KAK = [["ka%d_%d" % (b2, i) for i in range(8)] for b2 in range(2)]
            VAK = [["va%d_%d" % (b2, i) for i in range(4)] for b2 in range(2)]
            for b2 in range(2):
                for m in range(2):
                    S.add("dve", lambda e, b2=b2, m=m: e.memset(qa[b2][:, m, :], 0.0), w=QAK[b2])
                    for s_ in range(2):
                        S.add("dve", lambda e, b2=b2, m=m, s_=s_: e.memset(ka[b2][:, s_, m, :], 0.0), w=KAK[b2])
                S.add("dve", lambda e, b2=b2: e.memset(vah[b2].rearrange("p k n -> p (k n)"), 0.0), w=VAK[b2])
            for h in range(4):
                S.add("pool", lambda e, h=h: e.dma_start(out=dpt[:, h, :, :], in_=cst["c_dp"][h].rearrange("s p i -> p s i")),
                      w=["dpt"], dma="dpt%d" % h)
            def load_head(h):
                hb = h % 2
                for q4 in range(4):
                    S.add("sp", lambda e, q4=q4, hb=hb, h=h: e.dma_start(
                        out=vah[hb][:, q4 * 8:(q4 + 1) * 8, 0:65],
                        in_=vA[q4 * 1024:(q4 + 1) * 1024, h * 65:(h + 1) * 65].rearrange("(k p) n -> p k n", p=128)),
                        w=[VAK[hb][q4]], dma="va%d_%d" % (hb, q4))
                for m in range(2):
                    r0 = h * 64 + m * 32
                    S.add("sp", lambda e, hb=hb, m=m, r0=r0: e.dma_start(out=qa[hb][0:32, m, :], in_=qaT[r0:r0 + 32, :]),
                          w=[QAK[hb][2 * m]], dma="qa%d_%d" % (hb, m))
                    S.add("pool", lambda e, hb=hb, m=m, h=h: e.dma_start(
                        out=qa[hb][32:40, m, :].rearrange("p (a b) -> p a b", b=2048),
                        in_=cst["c_qaug"][h].rearrange("p (a b) -> p a b", b=2048)), w=[QAK[hb][2 * m + 1]], dma="qg%d_%d" % (hb, m))
                    for s_ in range(2):
                        S.add("sp", lambda e, hb=hb, m=m, r0=r0, s_=s_: e.dma_start(
                            out=ka[hb][0:32, s_, m, :], in_=kaT[r0:r0 + 32, :]), w=[KAK[hb][4 * m + 2 * s_]], dma="ka%d_%d_%d" % (hb, m, s_))
                        S.add("pool", lambda e, hb=hb, m=m, h=h, s_=s_: e.dma_start(
                            out=ka[hb][32:40, s_, m, :].rearrange("p (a b) -> p a b", b=2048),
                            in_=cst["c_kaug"][h, s_].rearrange("p (a b) -> p a b", b=2048)), w=[KAK[hb][4 * m + 2 * s_ + 1]],
                            dma="kg%d_%d_%d" % (hb, m, s_))

            EPI_AT = [1, 2, 12, 13, 14, 15, 16, 17, 18, 19]
            load_head(0)
            it = 0
            pend_pv = None
            pend_epi = []
            for h in range(4):
                hb = h % 2
                if pend_pv is not None:
                    pend_pv()
                    pend_pv = None
                if h + 1 < 4:
                    load_head(h + 1)
                if h == 0:
                    for j in range(NJ):
                        for gu in range(2):
                            src = w_gu[l][:, gu * FFN + j * 128:gu * FFN + (j + 1) * 128].rearrange("(k p) c -> p k c", p=128)
                            dst = wguS[j].rearrange("p (k n) -> p k n", n=256)[:, :, gu * 128:(gu + 1) * 128]
                            S.add("pool", lambda e, src=src, dst=dst: e.dma_start(out=dst, in_=src), w=["wguS%d" % j],
                                  dma="wguS%d" % ((2 * j + gu) % 4), bg=True)
                mh = float(S_A[h])
                QK = QAK[hb] + KAK[hb]
                for qc in range(8):
                    ab = (h * 8 + qc) % 2
                    acc = ps[:, (4 + 2 * ab) * 512:(6 + 2 * ab) * 512].rearrange("p (m t) -> p m t", t=512)
                    acck = "acc%d" % ab
                    q0 = qc * 512
                    for kb in range(32):
                        sb = it % 2
                        pb = it % 3
                        Sv = ps[:, sb * 1024:(sb + 1) * 1024].rearrange("p (m t) -> p m t", t=512)
                        sk = "S%d" % sb
                        dl = kb - 4 * qc
                        k0 = kb * 128
                        for m in range(2):
                            if dl < 0 or dl > 3:
                                s_ = 0 if dl < 0 else 1
                                S.add("pe", lambda e, m=m, s_=s_, k0=k0, Sv=Sv, hb=hb, q0=q0: e.matmul(
                                    Sv[:, m, :], lhsT=ka[hb][:, s_, m, k0:k0 + 128], rhs=qa[hb][:, m, q0:q0 + 512],
                                    start=True, stop=True), r=QK, w=[sk])
                            else:
                                c0 = 128 * dl
                                if dl > 0:
                                    S.add("pe", lambda e, m=m, k0=k0, Sv=Sv, hb=hb, q0=q0, c0=c0: e.matmul(
                                        Sv[:, m, 0:c0], lhsT=ka[hb][:, 1, m, k0:k0 + 128], rhs=qa[hb][:, m, q0:q0 + c0],
                                        start=True, stop=True), r=QK, w=[sk])
                                S.add("pe", lambda e, m=m, k0=k0, Sv=Sv, hb=hb, q0=q0, c0=c0: e.matmul(
                                    Sv[:, m, c0:512], lhsT=ka[hb][:, 0, m, k0:k0 + 128], rhs=qa[hb][:, m, q0 + c0:q0 + 512],
                                    start=True, stop=False), r=QK, w=[sk])
                                for hl in range(2):
                                    S.add("pe", lambda e, m=m, Sv=Sv, c0=c0, hl=hl, h=h: e.matmul(
                                        Sv[:, m, c0:c0 + 128], lhsT=ident_bf, rhs=dpt[:, h, hl, :],
                                        start=False, stop=(hl == 1)), r=["dpt", "ident_bf"], w=[sk])
                        if pend_pv is not None:
                            pend_pv()
                            pend_pv = None
                        if dl < 0 or dl > 3:
                            bias = -mh * abs(512 * qc - 128 * kb)
                            S.add("act", lambda e, Sv=Sv, pb=pb, bias=bias: e.activation(
                                out=pT[pb].rearrange("p m t -> p (m t)"), in_=Sv.rearrange("p m t -> p (m t)"),
                                func=AF.Exp, bias=float(bias), scale=1.0), r=[sk], w=["pT%d" % pb])
                        else:
                            c0 = 128 * dl
                            if dl > 0:
                                S.add("act", lambda e, Sv=Sv, pb=pb, c0=c0, mh=mh: e.activation(
                                    out=pT[pb][:, :, 0:c0], in_=Sv[:, :, 0:c0], func=AF.Exp, bias=float(-mh * c0), scale=1.0),
                                    r=[sk], w=["pT%d" % pb])
                            S.add("act", lambda e, Sv=Sv, pb=pb, c0=c0, mh=mh: e.activation(
                                out=pT[pb][:, :, c0:512], in_=Sv[:, :, c0:512], func=AF.Exp, bias=float(mh * c0), scale=1.0),
                                r=[sk], w=["pT%d" % pb])

                        def pv(pb=pb, kb=kb, acc=acc, acck=acck, hb=hb):
                            for m in range(2):
                                S.add("pe", lambda e, m=m: e.matmul(
                                    acc[:, m, :], lhsT=vah[hb][:, kb, :], rhs=pT[pb][:, m, :],
                                    start=(kb == 0), stop=(kb == 31)), r=["pT%d" % pb] + VAK[hb], w=[acck])
                        pend_pv = pv
                        it += 1
                        if pend_epi and kb == EPI_AT[10 - len(pend_epi)]:
                            pend_epi.pop(0)()
                    def mk_epi(acc=acc, acck=acck, h=h, qc=qc):
                        st = []
                        st.append(lambda: S.add("dve", lambda e: e.tensor_copy(out=osb[0:65].rearrange("p m t -> p (m t)"),
                                                                                in_=acc[0:65].rearrange("p m t -> p (m t)")), r=[acck], w=["osb"]))
                        st.append(lambda: S.add("dve", lambda e: e.reciprocal(out=rl[64:65, :], in_=osb[64:65].rearrange("p m t -> p (m t)")),
                                                r=["osb"], w=["rl"]))
                        def bc():
                            for m in range(2):
                                S.add("pe", lambda e, m=m: e.matmul(acc[0:64, m, :], lhsT=ones[64:65, 0:64], rhs=rl[64:65, m * 512:(m + 1) * 512],
                                                                     start=True, stop=True), r=["rl", "ones"], w=[acck])
                        st.append(bc)
                        def mul():
                            S.add("dve", lambda e: e.tensor_tensor(out=t0[0:64], in0=acc[0:64, 0, :], in1=osb[0:64, 0, :], op=ALU.mult),
                                  r=[acck, "osb"], w=["t0"])
                            S.add("dve", lambda e: e.tensor_tensor(out=t1[0:64], in0=acc[0:64, 1, :], in1=osb[0:64, 1, :], op=ALU.mult),
                                  r=[acck, "osb"], w=["t1"])
                            S.add("dve", lambda e: e.scalar_tensor_tensor(out=dd[0:64], in0=t1[0:64], scalar=lamneg, in1=t0[0:64],
                                                                          op0=ALU.mult, op1=ALU.add), r=["t0", "t1", "lamneg"], w=["dd"])
                        st.append(mul)
                        st.append(lambda: S.add("dve", lambda e: e.tensor_tensor(out=sq[0:64], in0=dd[0:64], in1=dd[0:64], op=ALU.mult), r=["dd"], w=["sq"]))
                        st.append(lambda: S.add("pe", lambda e: e.matmul(acc[0:64, 0, :], lhsT=ones[0:64, 0:64], rhs=sq[0:64],
                                                                          start=True, stop=True), r=["sq", "ones"], w=[acck]))
                        st.append(lambda: S.add("act", lambda e: e.activation(out=tmpv[0:64], in_=acc[0:64, 0, :], func=AF.Ln, scale=1.0 / 64.0,
                                                                              bias=epsc[0:64, 0:1]), r=[acck, "epsc"], w=["tmpv"]))
                        st.append(lambda: S.add("act", lambda e: e.activation(out=rstd[0:64], in_=tmpv[0:64], func=AF.Exp, scale=-0.5),
                                                r=["tmpv"], w=["rstd"]))
                        st.append(lambda: S.add("dve", lambda e: e.scalar_tensor_tensor(out=obf[0:64], in0=dd[0:64], scalar=gcol, in1=rstd[0:64],
                                                                                        op0=ALU.mult, op1=ALU.mult),
                                                r=["dd", "rstd", "gcol"], w=["obfA"]))
                        st.append(lambda: S.add("sp", lambda e: e.dma_start(out=mixT[h * 64:(h + 1) * 64, qc * 512:(qc + 1) * 512], in_=obf[0:64]),
                                                r=["obfA"], dma="obfA"))
                        return st
                    while pend_epi:
                        pend_epi.pop(0)()
                    pend_epi = mk_epi()
            if pend_pv is not None:
                pend_pv()
            while pend_epi:
                pend_epi.pop(0)()
            S.barrier()

        def phase_B(l):
            M = Mem(PBASE)
            PAD = 64
            vb = M.bf16(3 * 33 * 390).rearrange("p (a t n) -> p a t n", a=3, t=33)
            qbs = [M.bf16(T) for _ in range(2)]
            kbs_ = [M.bf16(T + 2 * PAD) for _ in range(2)]
            qp = M.bf16(T)
            kp = M.bf16(T + 2 * PAD)
            bias = M.f32(7 * 512).rearrange("p (v n) -> p v n", n=512)
            accT = M.f32(T)
            Ssb = [M.f32(512) for _ in range(4)]
            pT = [M.bf16(512) for _ in range(4)]
            rl = [M.f32(512) for _ in range(2)]
            obf = [M.bf16(512) for _ in range(2)]
            S.add("sp", lambda e: e.dma_start(out=bias.rearrange("p v n -> p (v n)"), in_=cst["c_bbias"].rearrange("p v n -> p (v n)")),
                  w=["biasB"], dma="biasB")
            VBK = ["vb_%d" % i for i in range(84)]
            for a3 in range(3):
                for t3 in range(3):
                    S.add("dve", lambda e, a3=a3, t3=t3: e.memset(vb[:, a3, t3 * 11:(t3 + 1) * 11, :], 0.0), w=VBK)
            for (buf, key) in ((qbs[0], "qb0"), (kbs_[0], "kb0"), (qbs[1], "qb1"), (kbs_[1], "kb1"), (qp, "qp"), (kp, "kp")):
                S.add("dve", lambda e, buf=buf: e.memset(buf, 0.0), w=[key])
            nd = 0
            for pi, (win, d) in enumerate(B_PATTERNS):
                Lc = T // d
                nt = Lc // 128
                for r in range(d):
                    srcB = bass.AP(vB.tensor, r * 390, [[d * 390, 64], [128 * d * 390, nt], [1, 390]])
                    tA = r * nt + 1
                    tB = r * nt
                    ntA = nt
                    if (64 + 128 * (nt - 1) + 63) * d + r >= T:
                        ntA = nt - 1
                    if ntA > 0:
                        srcA = bass.AP(vB.tensor, (64 * d + r) * 390, [[d * 390, 64], [128 * d * 390, ntA], [1, 390]])
                        S.add("sp", lambda e, srcA=srcA, pi=pi, tA=tA, ntA=ntA: e.dma_start(out=vb[0:64, pi, tA:tA + ntA, :], in_=srcA),
                              w=[VBK[nd]], dma="vb%d" % (nd % 4))
                        nd += 1
                    S.add("sp", lambda e, srcB=srcB, pi=pi, tB=tB, nt=nt: e.dma_start(out=vb[64:128, pi, tB:tB + nt, :], in_=srcB),
                          w=[VBK[nd]], dma="vb%d" % (nd % 4))
                    nd += 1
            blocks = [(0, 0), (0, 1), (1, 1), (1, 2)]
            def load_qk(h):
                S.add("sp", lambda e, h=h: e.dma_start(out=qbs[h % 2][0:64, :], in_=qbT[h * 64:(h + 1) * 64, :]), w=["qb%d" % (h % 2)], dma="qb%d" % (h % 2))
                S.add("sp", lambda e, h=h: e.dma_start(out=kbs_[h % 2][0:64, PAD:PAD + T], in_=kbT[h * 64:(h + 1) * 64, :]), w=["kb%d" % (h % 2)], dma="kb%d" % (h % 2))

            load_qk(0)
            for h in range(6):
                mh = float(S_B[h])
                qb, kb_ = qbs[h % 2], kbs_[h % 2]
                qbk, kbk_ = "qb%d" % (h % 2), "kb%d" % (h % 2)
                if h + 1 < 6:
                    load_qk(h + 1)
                for pi, (win, d) in enumerate(B_PATTERNS):
                    Lc = T // d
                    ntc = Lc // 128
                    ng = ntc // 2
                    if pi == 0:
                        qs_, ks_, qk_, kk_ = qb, kb_, qbk, kbk_
                    else:
                        qs_, ks_, qk_, kk_ = qp, kp, "qp", "kp"
                        S.add("dve", lambda e, d=d, qb=qb: e.tensor_copy(out=qp[0:64, :].rearrange("p (r j) -> p r j", r=d),
                                                                         in_=qb[0:64, :].rearrange("p (j r) -> p r j", r=d)), r=[qbk], w=["qp"])
                        S.add("act", lambda e, d=d, kb_=kb_: e.copy(out=kp[0:64, PAD:PAD + T].rearrange("p (r j) -> p r j", r=d),
                                                                    in_=kb_[0:64, PAD:PAD + T].rearrange("p (j r) -> p r j", r=d)), r=[kbk_], w=["kp"])
                    groups = []
                    for r in range(d):
                        for gq in range(ng):
                            b0 = 2 * gq
                            tq = r * ntc + b0
                            if pi == 2:
                                var = 6
                            else:
                                var = 3 * pi + (1 if gq == 0 else (2 if gq == ng - 1 else 0))
                            groups.append((tq, var, 128 * b0 * d + r))

                    def emit_S(gi, groups=groups, qs_=qs_, ks_=ks_, qk_=qk_, kk_=kk_):
                        tq, var, s0 = groups[gi]
                        sbk = gi % 4
                        Sb = bank(sbk)
                        for bi, (qi, ci) in enumerate(blocks):
                            qs = 128 * (tq + qi)
                            ks = PAD + 128 * (tq + ci) - 64
                            S.add("pe", lambda e, bi=bi, qs=qs, ks=ks: e.matmul(
                                Sb[:, bi * 128:(bi + 1) * 128], lhsT=ks_[:, ks:ks + 128], rhs=qs_[:, qs:qs + 128],
                                start=True, stop=True), r=[qk_, kk_], w=["SB%d" % sbk])

                    def emit_rest(gi, groups=groups, h=h, pi=pi, d=d, mh=mh):
                        tq, var, s0 = groups[gi]
                        sbk = gi % 4
                        obk = gi % 2
                        Sb = bank(sbk)
                        ob = bank(4 + obk)
                        S.add("dve", lambda e: e.scalar_tensor_tensor(out=Ssb[sbk], in0=bias[:, var, :], scalar=mh, in1=Sb,
                                                                      op0=ALU.mult, op1=ALU.add),
                              r=["SB%d" % sbk, "biasB"], w=["Ssb%d" % sbk])
                        S.add("act", lambda e: e.activation(out=pT[sbk], in_=Ssb[sbk], func=AF.Exp), r=["Ssb%d" % sbk], w=["pTB%d" % sbk])
                        for bi, (qi, ci) in enumerate(blocks):
                            S.add("pe", lambda e, bi=bi, qi=qi, ci=ci: e.matmul(
                                ob[0:65, qi * 128:(qi + 1) * 128], lhsT=vb[:, pi, tq + ci, h * 65:(h + 1) * 65],
                                rhs=pT[sbk][:, bi * 128:(bi + 1) * 128], start=(bi % 2 == 0), stop=(bi % 2 == 1)),
                                r=["pTB%d" % sbk] + VBK, w=["oB%d" % obk])

                    def emit_tail(gi, groups=groups, pi=pi, d=d):
                        tq, var, s0 = groups[gi]
                        obk = gi % 2
                        ob = bank(4 + obk)
                        dst = accT[0:65, s0:s0 + 255 * d + 1:d]
                        if pi == 0:
                            S.add("act", lambda e: e.copy(out=dst, in_=ob[0:65, 0:256]), r=["oB%d" % obk], w=["accT"])
                        else:
                            S.add("dve", lambda e: e.tensor_tensor(out=dst, in0=ob[0:65, 0:256], in1=dst, op=ALU.add),
                                  r=["oB%d" % obk, "accT"], w=["accT"])

                    emit_S(0)
                    emit_S(1)
                    emit_S(2)
                    for gi in range(len(groups)):
                        if gi + 3 < len(groups):
                            emit_S(gi + 3)
                        emit_rest(gi)
                        if gi >= 1:
                            emit_tail(gi - 1)
                    emit_tail(len(groups) - 1)
                for c in range(8):
                    norm_store(accT[0:64, c * 512:(c + 1) * 512], accT[64:65, c * 512:(c + 1) * 512], rl[c % 2], bank(6 + c % 2), obf[c % 2],
                               mixT[256 + h * 64:256 + (h + 1) * 64, c * 512:(c + 1) * 512], ["accT"], "B%d" % (c % 2))
            S.barrier()

        def phase_C(l):
            M = Mem(PBASE)
            PADC = 128
            G = []
            for g in range(2):
                d_ = dict(
                    kc=M.bf16(T + 2 * PADC), qc=M.bf16(3 * T).rearrange("p (r t) -> p r t", t=T),
                    vc=M.bf16(32 * 65).rearrange("p (k n) -> p k n", n=65), cb=M.f32(3 * 384).rearrange("p (k n) -> p k n", n=384),
                    Ssb=M.f32(3 * 384).rearrange("p (k n) -> p k n", n=384), pT=M.bf16(3 * 384).rearrange("p (k n) -> p k n", n=384),
                    osbc=[M.f32(3 * 512).rearrange("p (r t) -> p r t", t=512) for _ in range(2)],
                    rlin=M.f32(512), rl=M.f32(512), obf=M.bf16(512))
                G.append(d_)
            es = M.f32(16)
            S.add("sp", lambda e: e.dma_start(out=es[64:65, 0:6], in_=sink[l:l + 1, :]), w=["es0"], dma="es0")
            S.add("act", lambda e: e.activation(out=es[64:65, 8:14], in_=es[64:65, 0:6], func=AF.Exp), r=["es0"], w=["es"])
            for g in range(2):
                d_ = G[g]
                S.add("dve", lambda e, d_=d_: e.memset(d_["kc"], 0.0), w=["kc%d" % g])
                S.add("dve", lambda e, d_=d_: e.memset(d_["qc"][64:128].rearrange("p r t -> p (r t)"), 0.0), w=["qcz%d" % g])
                S.add("sp", lambda e, g=g, d_=d_: e.dma_start(out=d_["kc"][0:64, PADC:PADC + T], in_=kcT[g * 64:(g + 1) * 64, :]),
                      w=["kc%d" % g], dma="kc%d" % g)
                for rep in range(3):
                    hq = 3 * g + rep
                    S.add("sp", lambda e, rep=rep, hq=hq, d_=d_: e.dma_start(out=d_["qc"][0:64, rep, :], in_=qcT[hq * 64:(hq + 1) * 64, :]),
                          w=["qc%d_%d" % (g, rep)], dma="qc%d_%d" % (g, rep))
                for q4 in range(4):
                    S.add("sp", lambda e, g=g, q4=q4, d_=d_: e.dma_start(
                        out=d_["vc"][:, q4 * 8:(q4 + 1) * 8, :],
                        in_=vC[q4 * 1024:(q4 + 1) * 1024, g * 65:(g + 1) * 65].rearrange("(k p) n -> p k n", p=128)),
                        w=["vc%d_%d" % (g, q4)], dma="vc%d_%d" % (g, q4))
                S.add("sp", lambda e, g=g, d_=d_: e.dma_start(out=d_["cb"].rearrange("p k n -> p (k n)"),
                                                              in_=cst["c_cbias"][g].rearrange("p k n -> p (k n)")), w=["cb%d" % g], dma="cb%d" % g)

            def kts_of(qb_):
                return [kt for kt in range(3) if 0 <= qb_ + kt - 1 <= 31]

            def c_S(g, qb_):
                d_ = G[g]
                for kt in kts_of(qb_):
                    kbk = qb_ + kt - 1
                    Sb = bank(3 * g + kt, 384)
                    S.add("pe", lambda e, Sb=Sb, kbk=kbk: e.matmul(
                        Sb, lhsT=d_["kc"][:, PADC + kbk * 128:PADC + (kbk + 1) * 128], rhs=d_["qc"][:, :, qb_ * 128:(qb_ + 1) * 128],
                        start=True, stop=True), r=["kc%d" % g, "qcz%d" % g] + ["qc%d_%d" % (g, r_) for r_ in range(3)], w=["SC%d_%d" % (g, kt)])

            def c_exp(g, qb_):
                d_ = G[g]
                kts = kts_of(qb_)
                for kt in kts:
                    Sb = bank(3 * g + kt, 384)
                    S.add("dve", lambda e, Sb=Sb, kt=kt: e.tensor_tensor(out=d_["Ssb"][:, kt, :], in0=Sb, in1=d_["cb"][:, kt, :], op=ALU.add),
                          r=["SC%d_%d" % (g, kt), "cb%d" % g], w=["SsbC%d" % g])
                lo, hi = kts[0], kts[-1] + 1
                S.add("act", lambda e: e.activation(out=d_["pT"][:, lo:hi, :], in_=d_["Ssb"][:, lo:hi, :], func=AF.Exp),
                      r=["SsbC%d" % g], w=["pTC%d" % g])

            def c_pv(g, qb_):
                d_ = G[g]
                kts = kts_of(qb_)
                lo, hi = kts[0], kts[-1] + 1
                for kt in kts:
                    kbk = qb_ + kt - 1
                    S.add("pe", lambda e, kt=kt, kbk=kbk: e.matmul(
                        bank(6 + g, 384)[0:65, :], lhsT=d_["vc"][:, kbk, :], rhs=d_["pT"][:, kt, :], start=(kt == lo), stop=(kt == hi - 1)),
                        r=["pTC%d" % g] + ["vc%d_%d" % (g, q4) for q4 in range(4)], w=["oC%d" % g])

            def c_tail(g, qb_):
                d_ = G[g]
                ch = qb_ // 4
                obk = ch % 2
                S.add("act", lambda e: e.copy(
                    out=d_["osbc"][obk][0:65, :, (qb_ % 4) * 128:(qb_ % 4 + 1) * 128],
                    in_=bank(6 + g, 384)[0:65, :].rearrange("p (r t) -> p r t", t=128)), r=["oC%d" % g], w=["osbc%d_%d" % (g, obk)])
                if qb_ % 4 == 3:
                    for rep in range(3):
                        hq = 3 * g + rep
                        S.add("dve", lambda e, rep=rep, hq=hq: e.tensor_scalar(
                            out=d_["rlin"][64:65, :], in0=d_["osbc"][obk][64:65, rep, :], scalar1=es[64:65, 8 + hq:9 + hq], scalar2=None, op0=ALU.add),
                            r=["osbc%d_%d" % (g, obk), "es"], w=["rlin%d" % g])
                        norm_store(d_["osbc"][obk][0:64, rep, :], d_["rlin"][64:65, :], d_["rl"], bank(6 + g), d_["obf"],
                                   mixT[640 + hq * 64:640 + (hq + 1) * 64, ch * 512:(ch + 1) * 512],
                                   ["rlin%d" % g, "osbc%d_%d" % (g, obk)], "C%d" % g, bc_key="oC%d" % g)

            wst_c = [M.f32(1024) for _ in range(2)]
            wdb_c = [M.bf16(1024) for _ in range(2)]
            wob_c = Mem(WOB_OFF).bf16(8 * 1024).rearrange("p (k n) -> p k n", n=1024)
            wprep = []
            for kc in range(8):
                def prep_o(kc=kc):
                    b = kc % 2
                    S.add("sp", lambda e: e.dma_start(out=wst_c[b], in_=w_out[l, kc * 128:(kc + 1) * 128, :]), w=["wstc%d" % b], dma="wstc%d" % b)
                    S.add("pool", lambda e: e.tensor_tensor(out=wob_c[:, kc, :], in0=wst_c[b], in1=keepR[:, 0, :], op=ALU.mult),
                          r=["wstc%d" % b], w=["wobc%d" % kc])
                wprep.append(prep_o)
            for j in range(NJ):
                def prep(j=j):
                    b = j % 2
                    S.add("sp", lambda e: e.dma_start(out=wst_c[b], in_=w_down[l, j * 128:(j + 1) * 128, :]), w=["wstc%d" % b], dma="wstc%d" % b)
                    S.add("pool", lambda e: e.tensor_tensor(out=wdb_c[b], in0=wst_c[b], in1=keepR[:, 3, :], op=ALU.mult),
                          r=["wstc%d" % b], w=["wdbc%d" % b])
                    S.add("pool", lambda e: e.dma_start(out=wdnS[j], in_=wdb_c[b]), r=["wdbc%d" % b], w=["wdnS%d" % j], dma="wdbc%d" % b)
                wprep.append(prep)
            c_S(0, 0)
            c_S(1, 0)
            for qb_ in range(32):
                if qb_ < len(wprep):
                    wprep[qb_]()
                for g in range(2):
                    c_exp(g, qb_)
                    if qb_ + 1 < 32:
                        c_S(g, qb_ + 1)
                for g in range(2):
                    c_pv(g, qb_)
                for g in range(2):
                    c_tail(g, qb_)
            S.barrier()

        def phase_P3(l):
            EPS1 = LN_EPS / (ALPHA * ALPHA)
            M = Mem(PBASE)
            wob = Mem(WOB_OFF).bf16(8 * 1024).rearrange("p (k n) -> p k n", n=1024)
            lnt = M.f32(6 * 1024).rearrange("p (a n) -> p a n", n=1024)
            mxT = M.bf16(8 * 512).rearrange("p (k t) -> p k t", t=512)
            xt = [M.f32(4096).rearrange("p (t n) -> p t n", n=1024) for _ in range(2)]
            zn = M.f32(1024)
            h2 = M.bf16(4 * 1024).rearrange("p (t n) -> p t n", n=1024)
            h2T = M.bf16(8 * 512).rearrange("p (k t) -> p k t", t=512)
            actT = M.bf16(NJ * 512).rearrange("p (j t) -> p j t", t=512)
            sg = [M.f32(512) for _ in range(2)]
            wgu = [M.bf16(8 * 256).rearrange("p (k n) -> p k n", n=256) for _ in range(3)]
            wdn = [M.bf16(1024) for _ in range(3)]
            wst = [M.f32(1024) for _ in range(2)]
            stt = M.f32(8 * 24).rearrange("p (s n) -> p s n", n=24)
            negh1 = M.f32(2)
            xdst = x1s if l == 0 else y_out
            S.add("pool", lambda e: e.memset(negh1, -0.5), w=["negh1"])
            for (a, src) in ((0, ln_g[l, 0]), (1, ln_b[l, 0]), (4, ln_g[l, 1]), (5, ln_b[l, 1])):
                S.add("sp", lambda e, a=a, src=src: e.dma_start(out=lnt[:, a, :], in_=src.partition_broadcast(128)), w=["lnt%d" % a], dma="lnt%d" % a)
            S.add("dve", lambda e: e.tensor_tensor(out=lnt[:, 2, :], in0=lnt[:, 0, :], in1=keepR[:, 1, :], op=ALU.mult), r=["lnt0"], w=["lnt2"])
            S.add("dve", lambda e: e.tensor_tensor(out=lnt[:, 3, :], in0=lnt[:, 1, :], in1=keepR[:, 1, :], op=ALU.mult), r=["lnt1"], w=["lnt3"])
            S.add("dve", lambda e: e.tensor_tensor(out=lnt[:, 3, :], in0=lnt[:, 3, :], in1=keepR[:, 2, :], op=ALU.add), r=["lnt3"], w=["lnt3"])
            WOB = ["wob%d" % k for k in range(8)]

            def load_mx(c):
                S.add("sp", lambda e, c=c: e.dma_start(out=mxT, in_=mixT[:, c * 512:(c + 1) * 512].rearrange("(k p) t -> p k t", p=128)),
                      w=["mxT"], dma="mxT")

            def load_x(c):
                cb_ = c % 2
                xsrc = x_in if l == 0 else x1s
                S.add("sp", lambda e, c=c, cb_=cb_: e.dma_start(
                    out=xt[cb_], in_=xsrc[c * 512:(c + 1) * 512, :].rearrange("(t p) n -> p t n", p=128)), w=["xt%d_%d" % (cb_, t_) for t_ in range(4)],
                    dma="xt%d" % cb_)

            cnt = {"tm": 0, "ev": 0, "zn": 0}

            def wout(c):
                cb_ = c % 2
                for tb in range(4):
                    for n in range(2):
                        pb = cnt["tm"] % 2
                        cnt["tm"] += 1
                        for kc in range(8):
                            S.add("pe", lambda e, tb=tb, n=n, kc=kc, pb=pb: e.matmul(
                                bank(pb), lhsT=mxT[:, kc, tb * 128:(tb + 1) * 128], rhs=wob[:, kc, n * 512:(n + 1) * 512],
                                start=(kc == 0), stop=(kc == 7)), r=["mxT", WOB[kc]], w=["psO%d" % pb])
                        S.add("dve", lambda e, tb=tb, n=n, pb=pb, cb_=cb_: e.tensor_tensor(
                            out=xt[cb_][:, tb, n * 512:(n + 1) * 512], in0=bank(pb), in1=xt[cb_][:, tb, n * 512:(n + 1) * 512], op=ALU.add),
                            r=["psO%d" % pb, "xt%d_%d" % (cb_, tb)], w=["xt%d_%d" % (cb_, tb)])

            def layer_norm(c, tb, which):
                cb_ = c % 2
                y = xt[cb_][:, tb, :]
                yk = "xt%d_%d" % (cb_, tb)
                st = stt[:, tb + 4 * (which - 1), :]
                sk = "st%d" % (tb + 4 * (which - 1))
                S.add("dve", lambda e: e.bn_stats(out=st[:, 0:6], in_=y[:, 0:512]), r=[yk], w=[sk + "a"])
                S.add("dve", lambda e: e.bn_stats(out=st[:, 6:12], in_=y[:, 512:1024]), r=[yk], w=[sk + "b"])
                S.add("dve", lambda e: e.bn_aggr(out=st[:, 12:14], in_=st[:, 0:12]), r=[sk + "a", sk + "b"], w=[sk + "mv"])
                S.add("dve", lambda e: e.tensor_scalar(out=st[:, 14:15], in0=st[:, 13:14], scalar1=float(EPS1), scalar2=None, op0=ALU.add),
                      r=[sk + "mv"], w=[sk + "ve"])
                S.add("pool", lambda e: e.tensor_tensor(out=st[:, 15:16], in0=st[:, 14:15], in1=negh1[:, 0:1], op=ALU.pow),
                      r=[sk + "ve", "negh1"], w=[sk + "rs"])
                S.add("dve", lambda e: e.tensor_scalar(out=st[:, 16:17], in0=st[:, 12:13], scalar1=st[:, 15:16], scalar2=-1.0,
                                                        op0=ALU.mult, op1=ALU.mult), r=[sk + "mv", sk + "rs"], w=[sk + "nb"])
                zi = cnt["zn"] % 2
                cnt["zn"] += 1
                znb = zn if zi == 0 else wst[1]
                znk = "zn" if zi == 0 else "wst1"
                S.add("act", lambda e: e.activation(out=znb, in_=y, func=AF.Identity, scale=st[:, 15:16], bias=st[:, 16:17]),
                      r=[yk, sk + "rs", sk + "nb"], w=[znk])
                ga, ba = (0, 1) if which == 1 else (4, 5)
                S.add("pool", lambda e: e.tensor_tensor(out=y, in0=znb, in1=lnt[:, ga, :], op=ALU.mult), r=[znk, "lnt%d" % ga], w=[yk])
                S.add("pool", lambda e: e.tensor_tensor(out=y, in0=y, in1=lnt[:, ba, :], op=ALU.add), r=[yk, "lnt%d" % ba], w=[yk])
                if which == 1:
                    S.add("dve", lambda e: e.tensor_tensor(out=wst[0], in0=znb, in1=lnt[:, 2, :], op=ALU.mult), r=[znk, "lnt2"], w=["wst0"])
                    S.add("dve", lambda e: e.tensor_tensor(out=h2[:, tb, :], in0=wst[0], in1=lnt[:, 3, :], op=ALU.add),
                          r=["wst0", "lnt3"], w=["h2_%d" % tb])

            def transposes(c):
                for kc in range(8):
                    pb = 2 + kc % 2
                    for tb in range(4):
                        S.add("pe", lambda e, kc=kc, tb=tb, pb=pb: e.transpose(
                            out=bank_bf(pb)[:, tb * 128:(tb + 1) * 128], in_=h2[:, tb, kc * 128:(kc + 1) * 128], identity=ident_bf),
                            r=["h2_%d" % tb, "ident_bf"], w=["psT%d" % pb])
                    if cnt["ev"] % 2 == 0:
                        S.add("act", lambda e, kc=kc, pb=pb: e.copy(out=h2T[:, kc, :], in_=bank_bf(pb)[:, 0:512]), r=["psT%d" % pb], w=["h2T%d" % kc])
                    else:
                        S.add("dve", lambda e, kc=kc, pb=pb: e.tensor_copy(out=h2T[:, kc, :], in_=bank_bf(pb)[:, 0:512]), r=["psT%d" % pb], w=["h2T%d" % kc])
                    cnt["ev"] += 1

            def load_wgu(p):
                if p >= 8 * NJ:
                    return
                j = p % NJ
                wb_ = p % 3
                S.add("sp", lambda e, j=j, wb_=wb_: e.dma_start(out=wgu[wb_].rearrange("p k n -> p (k n)"), in_=wguS[j]),
                      r=["wguS%d" % j], w=["wgu%d" % wb_], dma="wgu%d" % wb_)

            def gu(c, hooks):
                H2T = ["h2T%d" % k for k in range(8)]
                for j in range(NJ):
                    load_wgu(c * NJ + j + 2)
                    wb_ = (c * NJ + j) % 3
                    pg = 4 + 2 * (j % 2)
                    for half in range(2):
                        for kc in range(8):
                            S.add("pe", lambda e, kc=kc, half=half, pg=pg, wb_=wb_: e.matmul(
                                bank(pg + half), lhsT=wgu[wb_][:, kc, half * 128:(half + 1) * 128], rhs=h2T[:, kc, :],
                                start=(kc == 0), stop=(kc == 7)), r=["wgu%d" % wb_, H2T[kc]], w=["psG%d" % (pg + half)])
                    sb_ = j % 2
                    S.add("act", lambda e, pg=pg, sb_=sb_: e.activation(out=sg[sb_], in_=bank(pg), func=AF.Silu), r=["psG%d" % pg], w=["sg%d" % sb_])
                    S.add("dve", lambda e, pg=pg, sb_=sb_, j=j: e.tensor_tensor(out=actT[:, j, :], in0=bank(pg + 1), in1=sg[sb_], op=ALU.mult),
                          r=["psG%d" % (pg + 1), "sg%d" % sb_], w=["actT%d" % j])
                    for f in hooks.get(j, ()):
                        f()

            def load_wdn(q):
                if q >= 16 * NJ:
                    return
                i = q % (2 * NJ)
                j, n = i % NJ, i // NJ
                wb_ = q % 3
                S.add("sp", lambda e, j=j, n=n, wb_=wb_: e.dma_start(out=wdn[wb_][:, 0:512], in_=wdnS[j][:, n * 512:(n + 1) * 512]),
                      r=["wdnS%d" % j], w=["wdn%d" % wb_], dma="wdn%d" % wb_)

            def down(c, mid):
                cb_ = c % 2
                for i in range(2 * NJ):
                    load_wdn(c * 2 * NJ + i + 2)
                    j, n = i % NJ, i // NJ
                    wb_ = (c * 2 * NJ + i) % 3
                    for tb in range(4):
                        S.add("pe", lambda e, j=j, tb=tb, wb_=wb_: e.matmul(
                            bank(4 + tb), lhsT=actT[:, j, tb * 128:(tb + 1) * 128], rhs=wdn[wb_][:, 0:512],
                            start=(j == 0), stop=(j == NJ - 1)), r=["actT%d" % j, "wdn%d" % wb_], w=["psG%d" % (4 + tb)])
                    if j == NJ - 1:
                        for tb in range(4):
                            S.add("dve", lambda e, tb=tb, n=n, cb_=cb_: e.tensor_tensor(
                                out=xt[cb_][:, tb, n * 512:(n + 1) * 512], in0=bank(4 + tb), in1=xt[cb_][:, tb, n * 512:(n + 1) * 512], op=ALU.add),
                                r=["psG%d" % (4 + tb), "xt%d_%d" % (cb_, tb)], w=["xt%d_%d" % (cb_, tb)])
                        if n == 0:
                            for f in mid:
                                f()

            def ln2_tile(c, tb):
                cb_ = c % 2
                if True:
                    layer_norm(c, tb, 2)
                    t0_ = c * 512 + tb * 128
                    S.add("pool", lambda e, tb=tb, t0_=t0_, cb_=cb_: e.dma_start(out=xdst[t0_:t0_ + 128, :], in_=xt[cb_][:, tb, :]),
                          r=["xt%d_%d" % (cb_, tb)], dma="xo%d_%d" % (cb_, tb))

            load_mx(0)
            load_x(0)
            load_wgu(0)
            load_wgu(1)
            load_wdn(0)
            load_wdn(1)
            wout(0)
            for tb in range(4):
                layer_norm(0, tb, 1)
            transposes(0)
            for c in range(8):
                hooks = {}
                if c + 1 < 8:
                    load_mx(c + 1)
                if c >= 1:
                    for tb in range(4):
                        hooks.setdefault(3 * tb, []).append(lambda c=c, tb=tb: ln2_tile(c - 1, tb))
                if c + 1 < 8:
                    hooks.setdefault(10, []).append(lambda c=c: load_x(c + 1))
                gu(c, hooks)
                mid = []
                if c + 1 < 8:
                    wout(c + 1)
                    layer_norm(c + 1, 0, 1)
                    layer_norm(c + 1, 1, 1)
                    mid = [lambda c=c: layer_norm(c + 1, 2, 1), lambda c=c: layer_norm(c + 1, 3, 1)]
                down(c, mid)
                if c + 1 < 8:
                    transposes(c + 1)
            for tb in range(4):
                ln2_tile(7, tb)
            S.barrier()

        for l in layers:
            if on("M"):
                phase_M(l)
            if on("P1"):
                phase_P1(l)
            if on("A"):
                phase_A(l)
            if on("B"):
                phase_B(l)
            if on("C"):
                phase_C(l)
            if on("P3"):
                phase_P3(l)

        n_ops = S.emit()
    return nc, n_ops


def kernel(**inputs):
    nc, _ = build_program(debug=False)
    cores = list(range(NCORES))
    maps = make_in_maps(inputs, cores)
    res = run_bass_kernel_spmd(nc, maps, core_ids=cores)
    return np.stack([np.asarray(r["y"], dtype=np.float32) for r in res.results], axis=0)


def make_in_maps(inputs, cores):
    f = lambda a: np.ascontiguousarray(np.asarray(a, dtype=np.float32))
    consts = make_consts()
    shared = {
        "w_ada": f(inputs["w_ada"]), "b_ada": f(inputs["b_ada"]), "w_in": f(inputs["w_in"]),
        "lam": f(inputs["lam"]).reshape(DEPTH, 128), "subln_g": f(inputs["subln_g"]), "sink": f(inputs["sink"]),
        "w_out": f(inputs["w_out"]), "ln_g": f(inputs["ln_g"]), "ln_b": f(inputs["ln_b"]),
        "w_gu": f(inputs["w_gu"]), "w_down": f(inputs["w_down"]),
    }
    shared.update(consts)
    maps = []
    x = np.asarray(inputs["x"], dtype=np.float32)
    c = np.asarray(inputs["c"], dtype=np.float32)
    for b in cores:
        m = dict(shared)
        m["x"] = np.ascontiguousarray(x[b])
        m["ccol"] = np.ascontiguousarray(c[b].reshape(8, 128).T)
        maps.append(m)
    return maps
```
